# Optimizing a Trainium2 kernel written in Bass

```python
import jax, jax.numpy as jnp
from jax import lax
import numpy as np

D_MODEL = 2048
BATCH = 4
SEQ = 2048
DEPTH = 4

CHUNK = 64
N_MEM = 256
D_MIX = D_MODEL
DN_HEAD_DIM = 128
DN_HEADS = (D_MIX // 2) // DN_HEAD_DIM
DN_WIDTH = DN_HEADS * DN_HEAD_DIM
DN_CONV = 4
MLA_NOPE = 128
MLA_ROPE = 64
MLA_V = 128
MLA_HEADS = (D_MIX - DN_WIDTH) // MLA_V
MLA_Q_RANK = 512
MLA_KV_RANK = 256
ROPE_BASE = 10000.0
Q_BLOCK = 128
XA_HEADS = 4
XA_HEAD_DIM = D_MODEL // XA_HEADS
D_FF = ((8 * D_MODEL // 3 + 255) // 256) * 256
FFN_CONV = 3
EPS = 1e-6

kernel_name = "hybrid_deltanet_mla_memxattn_convffn"


def _in_split_sizes():
    return [DN_WIDTH, DN_WIDTH, DN_WIDTH, DN_WIDTH,
            DN_HEADS, DN_HEADS,
            MLA_Q_RANK,
            MLA_KV_RANK + MLA_ROPE]


def _in_cols():
    return sum(_in_split_sizes())


def _split_points():
    return [int(v) for v in np.cumsum(_in_split_sizes())[:-1]]


def rms_norm(x, gain):
    xf = x.astype(jnp.float32)
    y = xf * lax.rsqrt(jnp.mean(xf * xf, axis=-1, keepdims=True) + EPS)
    return (y * gain.astype(jnp.float32)).astype(x.dtype)


def l2_norm(x):
    xf = x.astype(jnp.float32)
    return xf * lax.rsqrt(jnp.sum(xf * xf, axis=-1, keepdims=True) + EPS)


def causal_dwconv(x, w):
    K, C = w.shape
    return lax.conv_general_dilated(
        x, w[:, None, :].astype(x.dtype), window_strides=(1,),
        padding=[(K - 1, 0)], dimension_numbers=("NWC", "WIO", "NWC"),
        feature_group_count=C)


def rope_cos_sin(positions):
    inv = ROPE_BASE ** (-jnp.arange(0, MLA_ROPE, 2, dtype=jnp.float32) / MLA_ROPE)
    ang = positions.astype(jnp.float32)[..., None] * inv
    return jnp.cos(ang), jnp.sin(ang)


def apply_rope(x, cos, sin):
    xf = x.astype(jnp.float32)
    x1, x2 = jnp.split(xf, 2, axis=-1)
    return jnp.concatenate([x1 * cos - x2 * sin, x2 * cos + x1 * sin], axis=-1).astype(x.dtype)


def chunk_gated_delta_rule(q, k, v, g, beta):
    B, S, H, Dk = q.shape
    Dv = v.shape[-1]
    N = S // CHUNK
    f32 = jnp.float32

    def chunks(t):
        return jnp.moveaxis(t.reshape(B, N, CHUNK, H, *t.shape[3:]), 3, 2)

    q = chunks(q.astype(f32)) * (Dk ** -0.5)
    k = chunks(k.astype(f32))
    v = chunks(v.astype(f32))
    beta = chunks(beta.astype(f32))
    G = jnp.cumsum(chunks(g.astype(f32)), axis=-1)

    incl = np.tril(np.ones((CHUNK, CHUNK), dtype=bool))
    strict = np.tril(np.ones((CHUNK, CHUNK), dtype=bool), -1)
    decay = jnp.exp(jnp.where(incl, G[..., :, None] - G[..., None, :], -jnp.inf))

    kb = k * beta[..., None]
    lower = jnp.where(strict, jnp.einsum("bnhik,bnhjk->bnhij", kb, k) * decay, 0.0)
    a_mat = lower + np.eye(CHUNK, dtype=np.float32)
    rhs = jnp.concatenate([v * beta[..., None], kb * jnp.exp(G)[..., None]], axis=-1)
    sol = lax.linalg.triangular_solve(a_mat, rhs, left_side=True, lower=True,
                                      unit_diagonal=True)
    u, w = sol[..., :Dv], sol[..., Dv:]

    attn = jnp.einsum("bnhik,bnhjk->bnhij", q, k) * decay
    q_dec = q * jnp.exp(G)[..., None]
    k_dec = k * jnp.exp(G[..., -1:] - G)[..., None]
    g_last = jnp.exp(G[..., -1])

    def step(state, xs):
        u_c, w_c, a_c, q_c, k_c, gl_c = xs
        v_new = u_c - jnp.einsum("bhck,bhkv->bhcv", w_c, state)
        o_c = (jnp.einsum("bhck,bhkv->bhcv", q_c, state)
               + jnp.einsum("bhij,bhjv->bhiv", a_c, v_new))
        state = state * gl_c[..., None, None] + jnp.einsum("bhck,bhcv->bhkv", k_c, v_new)
        return state, o_c

    xs = tuple(jnp.moveaxis(t, 1, 0) for t in (u, w, attn, q_dec, k_dec, g_last))
    state0 = jnp.zeros((B, H, Dk, Dv), f32)
    _, o = lax.scan(step, state0, xs)
    o = jnp.moveaxis(o, 0, 1)
    return jnp.moveaxis(o, 2, 3).reshape(B, S, H, Dv)


def gated_deltanet(q_raw, k_raw, v_raw, z, b, a, conv_w, a_log, dt_bias, out_norm):
    B, S, _ = q_raw.shape
    qkv = jax.nn.silu(causal_dwconv(jnp.concatenate([q_raw, k_raw, v_raw], axis=-1), conv_w))
    q, k, v = jnp.split(qkv, 3, axis=-1)
    q = l2_norm(q.reshape(B, S, DN_HEADS, DN_HEAD_DIM))
    k = l2_norm(k.reshape(B, S, DN_HEADS, DN_HEAD_DIM))
    v = v.reshape(B, S, DN_HEADS, DN_HEAD_DIM)
    beta = jax.nn.sigmoid(b.astype(jnp.float32))
    g = -jnp.exp(a_log.astype(jnp.float32)) * jax.nn.softplus(
        a.astype(jnp.float32) + dt_bias.astype(jnp.float32))
    o = chunk_gated_delta_rule(q, k, v, g, beta)
    zf = z.reshape(B, S, DN_HEADS, DN_HEAD_DIM).astype(jnp.float32)
    o = rms_norm(o, out_norm) * jax.nn.silu(zf)
    return o.reshape(B, S, DN_WIDTH).astype(q_raw.dtype)


def mla_attention(q_lat, kv_lat, q_norm, w_qb, kv_norm, w_kvb, cos, sin):
    B, S, _ = q_lat.shape
    q = (rms_norm(q_lat, q_norm) @ w_qb).reshape(B, S, MLA_HEADS, MLA_NOPE + MLA_ROPE)
    q_nope = q[..., :MLA_NOPE]
    q_pe = apply_rope(q[..., MLA_NOPE:], cos[:, :, None, :], sin[:, :, None, :])
    c_kv = kv_lat[..., :MLA_KV_RANK]
    k_pe = apply_rope(kv_lat[..., MLA_KV_RANK:], cos, sin)
    kv = (rms_norm(c_kv, kv_norm) @ w_kvb).reshape(B, S, MLA_HEADS, MLA_NOPE + MLA_V)
    k_nope, v = kv[..., :MLA_NOPE], kv[..., MLA_NOPE:]
    scale = (MLA_NOPE + MLA_ROPE) ** -0.5
    outs = []
    for blk in range(S // Q_BLOCK):
        q0 = blk * Q_BLOCK
        kend = q0 + Q_BLOCK
        s = (jnp.einsum("bqhd,bkhd->bhqk", q_nope[:, q0:kend], k_nope[:, :kend])
             + jnp.einsum("bqhr,bkr->bhqk", q_pe[:, q0:kend], k_pe[:, :kend]))
        s = s.astype(jnp.float32) * scale
        q_chunk = (q0 + np.arange(Q_BLOCK)) // CHUNK
        k_chunk = np.arange(kend) // CHUNK
        mask = k_chunk[None, :] <= q_chunk[:, None]
        p = jax.nn.softmax(jnp.where(mask, s, -jnp.inf), axis=-1).astype(v.dtype)
        outs.append(jnp.einsum("bhqk,bkhd->bqhd", p, v[:, :kend]))
    o = jnp.concatenate(outs, axis=1)
    return o.reshape(B, S, MLA_HEADS * MLA_V)


def memory_cross_attention(h, mem_n, wq, wk, wv, wo):
    B, S, _ = h.shape
    M = mem_n.shape[1]
    q = (h @ wq).reshape(B, S, XA_HEADS, XA_HEAD_DIM)
    k = (mem_n @ wk).reshape(B, M, XA_HEADS, XA_HEAD_DIM)
    v = (mem_n @ wv).reshape(B, M, XA_HEADS, XA_HEAD_DIM)
    s = jnp.einsum("bqhd,bmhd->bhqm", q, k).astype(jnp.float32) * (XA_HEAD_DIM ** -0.5)
    p = jax.nn.softmax(s, axis=-1).astype(v.dtype)
    o = jnp.einsum("bhqm,bmhd->bqhd", p, v).reshape(B, S, XA_HEADS * XA_HEAD_DIM)
    return o @ wo


def conv_ffn(h, w_up, conv_w, conv_b, w_down):
    u = causal_dwconv(h @ w_up, conv_w) + conv_b
    gate, up = jnp.split(u, 2, axis=-1)
    return (jax.nn.silu(gate) * up) @ w_down


def setup_inputs(seed: int = 0) -> dict:
    key = jax.random.key(seed)
    ks = iter(jax.random.split(key, 40))
    f32 = jnp.float32

    def dense(shape, fan_in, scale=1.0):
        return jax.random.normal(next(ks), shape, f32) * (scale * fan_in ** -0.5)

    def gain(shape):
        return 1.0 + 0.01 * jax.random.normal(next(ks), shape, f32)

    x = jax.random.normal(next(ks), (BATCH, SEQ, D_MODEL), f32)
    mem = jax.random.normal(next(ks), (BATCH, N_MEM, D_MODEL), f32)
    offsets = jax.random.randint(next(ks), (BATCH, 1), 0, 64) * CHUNK
    positions = (offsets + jnp.arange(SEQ, dtype=jnp.int32)[None, :]).astype(jnp.int32)

    a_log = jnp.log(jax.random.uniform(next(ks), (DEPTH, DN_HEADS), f32, 1.0, 16.0))
    dt = jnp.exp(jax.random.uniform(next(ks), (DEPTH, DN_HEADS), f32,
                                    float(np.log(1e-3)), float(np.log(1e-1))))
    dt_bias = dt + jnp.log(-jnp.expm1(-dt))
    out_scale = 0.5
    return {
        "x": x,
        "mem": mem,
        "positions": positions,
        "norm_mix": gain((DEPTH, D_MODEL)),
        "w_in": dense((DEPTH, D_MODEL, _in_cols()), D_MODEL),
        "dn_conv": dense((DEPTH, DN_CONV, 3 * DN_WIDTH), DN_CONV),
        "dn_a_log": a_log,
        "dn_dt_bias": dt_bias,
        "dn_out_norm": gain((DEPTH, DN_HEAD_DIM)),
        "mla_q_norm": gain((DEPTH, MLA_Q_RANK)),
        "mla_w_qb": dense((DEPTH, MLA_Q_RANK, MLA_HEADS * (MLA_NOPE + MLA_ROPE)), MLA_Q_RANK),
        "mla_kv_norm": gain((DEPTH, MLA_KV_RANK)),
        "mla_w_kvb": dense((DEPTH, MLA_KV_RANK, MLA_HEADS * (MLA_NOPE + MLA_V)), MLA_KV_RANK),
        "w_out": dense((DEPTH, D_MIX, D_MODEL), D_MIX, out_scale),
        "mem_norm": gain((D_MODEL,)),
        "norm_xattn": gain((DEPTH, D_MODEL)),
        "xa_wq": dense((DEPTH, D_MODEL, XA_HEADS * XA_HEAD_DIM), D_MODEL),
        "xa_wk": dense((DEPTH, D_MODEL, XA_HEADS * XA_HEAD_DIM), D_MODEL),
        "xa_wv": dense((DEPTH, D_MODEL, XA_HEADS * XA_HEAD_DIM), D_MODEL),
        "xa_wo": dense((DEPTH, XA_HEADS * XA_HEAD_DIM, D_MODEL), D_MODEL, out_scale),
        "norm_ffn": gain((DEPTH, D_MODEL)),
        "ffn_w_up": dense((DEPTH, D_MODEL, 2 * D_FF), D_MODEL),
        "ffn_conv": dense((DEPTH, FFN_CONV, 2 * D_FF), FFN_CONV),
        "ffn_conv_bias": 0.01 * jax.random.normal(next(ks), (DEPTH, 2 * D_FF), f32),
        "ffn_w_down": dense((DEPTH, D_FF, D_MODEL), D_FF, out_scale),
        "norm_final": gain((D_MODEL,)),
    }


def reference(x, mem, positions, norm_mix, w_in, dn_conv, dn_a_log, dn_dt_bias,
              dn_out_norm, mla_q_norm, mla_w_qb, mla_kv_norm, mla_w_kvb, w_out,
              mem_norm, norm_xattn, xa_wq, xa_wk, xa_wv, xa_wo, norm_ffn,
              ffn_w_up, ffn_conv, ffn_conv_bias, ffn_w_down, norm_final):
    cos, sin = rope_cos_sin(positions)
    mem_n = rms_norm(mem, mem_norm)
    split_points = _split_points()
    h = x
    for l in range(DEPTH):
        u = rms_norm(h, norm_mix[l])
        proj = u @ w_in[l]
        dq, dk, dv, dz, db, da, mq, mkv = jnp.split(proj, split_points, axis=-1)
        o_dn = gated_deltanet(dq, dk, dv, dz, db, da, dn_conv[l], dn_a_log[l],
                              dn_dt_bias[l], dn_out_norm[l])
        o_mla = mla_attention(mq, mkv, mla_q_norm[l], mla_w_qb[l], mla_kv_norm[l],
                              mla_w_kvb[l], cos, sin)
        h = h + jnp.concatenate([o_dn.astype(h.dtype), o_mla.astype(h.dtype)], axis=-1) @ w_out[l]
        h = h + memory_cross_attention(rms_norm(h, norm_xattn[l]), mem_n, xa_wq[l],
                                       xa_wk[l], xa_wv[l], xa_wo[l])
        h = h + conv_ffn(rms_norm(h, norm_ffn[l]), ffn_w_up[l], ffn_conv[l],
                         ffn_conv_bias[l], ffn_w_down[l])
    return rms_norm(h, norm_final)
```

```python
import numpy as np
import concourse.bass as bass
import concourse.mybir as mybir
from concourse.bass_utils import run_bass_kernel_spmd

F32 = mybir.dt.float32
BF16 = mybir.dt.bfloat16
I32 = mybir.dt.int32
U8 = mybir.dt.uint8
AF = mybir.ActivationFunctionType
ALU = mybir.AluOpType
AX = mybir.AxisListType

ENGS = ["pe", "act", "dve", "pool", "sp"]

D = 2048
NKT = 16
DFF = 5632
NIN = 4944
EPS = 1e-6
NEG = -30000.0


class T:
    __slots__ = ("ap", "name", "w", "r", "parent")

    def __init__(self, ap, name="", parent=None):
        self.ap = ap
        self.name = name
        self.w = None
        self.r = []
        self.parent = parent


def _roots(ts):
    return [t.parent if t.parent is not None else t for t in ts]


class Prog:
    def __init__(self, nc):
        self.nc = nc
        self.streams = {e: [] for e in ENGS}
        self.cnt = {}
        self.seen = {e: {} for e in ENGS}
        self.sems = {}
        self.eng_sem = {e: ("E", e, 0) for e in ENGS}
        self.dma_rr = {e: 0 for e in ENGS}
        self.NDMA = 8
        self.n_ops = 0

    def _need(self, eng, dep):
        if dep is None:
            return
        key, val = dep
        if eng == "pe" and key[0] == "E" and key[1] == "pe":
            return
        if self.seen[eng].get(key, 0) >= val:
            return
        self.seen[eng][key] = val
        self.streams[eng].append(("wait", key, val))

    def _deps(self, eng, reads, writes):
        reads = _roots(reads)
        writes = _roots(writes)
        for t in reads:
            self._need(eng, t.w)
        for t in writes:
            self._need(eng, t.w)
            for d in t.r:
                self._need(eng, d)

    def _mark(self, stamp, reads, writes):
        reads = _roots(reads)
        writes = _roots(writes)
        for t in reads:
            t.r.append(stamp)
            if len(t.r) > 48:
                best = {}
                for k, v in t.r:
                    if best.get(k, 0) < v:
                        best[k] = v
                t.r = list(best.items())
        for t in writes:
            t.w = stamp
            t.r = []

    def op(self, eng, fn, reads=(), writes=()):
        self._deps(eng, reads, writes)
        key = self.eng_sem[eng]
        c = self.cnt.get(key, 0) + 1
        self.cnt[key] = c
        self.streams[eng].append(("op", fn, key))
        self._mark((key, c), reads, writes)
        self.n_ops += 1
        if c >= 16000:
            self.eng_sem[eng] = ("E", eng, key[2] + 1)

    def dma(self, q, out_t, out_ap, in_t, in_ap):
        reads = [in_t] if in_t is not None else []
        writes = [out_t] if out_t is not None else []
        self._deps(q, reads, writes)
        i = self.dma_rr[q]
        self.dma_rr[q] = (i + 1) % self.NDMA
        key = ("D", q, i)
        prev = self.cnt.get(key, 0)
        if prev:
            self._need(q, (key, prev))
        c = prev + 16
        self.cnt[key] = c

        def fn(e, out_ap=out_ap, in_ap=in_ap):
            return e.dma_start(out=out_ap, in_=in_ap)
        self.streams[q].append(("dma", fn, key))
        self._mark((key, c), reads, writes)
        self.n_ops += 1

    def barrier(self):
        for e in ENGS:
            self.wait_all(e)

    def wait_all(self, eng):
        for key, val in list(self.cnt.items()):
            if val:
                self._need(eng, (key, val))

    def emit(self):
        nc = self.nc
        for k in list(self.cnt.keys()):
            self.sems[k] = nc.alloc_semaphore("s_" + "_".join(str(x) for x in k))
        streams, sems = self.streams, self.sems

        def run(e, name):
            for item in streams[name]:
                if item[0] == "wait":
                    e.wait_ge(sems[item[1]], item[2])
                elif item[0] == "op":
                    item[1](e).then_inc(sems[item[2]], 1)
                else:
                    item[1](e).then_inc(sems[item[2]], 16)

        with nc.Block() as block:
            @block.tensor
            def _(e):
                run(e, "pe")

            @block.scalar
            def _(e):
                run(e, "act")

            @block.vector
            def _(e):
                run(e, "dve")

            @block.gpsimd
            def _(e):
                run(e, "pool")

            @block.sync
            def _(e):
                run(e, "sp")


class _Rec:
    def __getattr__(self, name):
        def mk(*args, **kw):
            return lambda e: getattr(e, name)(*args, **kw)
        return mk


R = _Rec()


class Arena:
    def __init__(self, nc, size):
        self.nc = nc
        slab = nc.alloc_sbuf_tensor("slab", [128, size], U8)
        self.base = nc.lookup_mloc(slab).addr
        self.size = size
        self.top = 0
        self.peak = 0

    def alloc(self, name, shape, dtype):
        nb = int(np.prod(shape[1:])) * (4 if dtype in (F32, I32) else 2)
        nb = (nb + 31) // 32 * 32
        off = self.top
        assert off + nb <= self.size, f"arena overflow {name}: {off}+{nb} > {self.size}"
        self.top += nb
        self.peak = max(self.peak, self.top)
        h = self.nc.alloc_sbuf_tensor_at(name, list(shape), dtype, offset=self.base + off)
        return T(h, name)

    def mark(self):
        return self.top

    def reset(self, m):
        self.top = m


class Cfg:
    def __init__(self, NTOK=2048, L=4, dbg=(), stop=None):
        self.NTOK = NTOK
        self.L = L
        self.TB = 512
        self.NTB = NTOK // 512
        self.NT = NTOK // 128
        self.dbg = dbg
        self.stop = stop


def build_program(cfg):
    nc = bass.Bass("TRN2", target_bir_lowering=False)
    P = Prog(nc)
    L, NTOK, NT, NTB = cfg.L, cfg.NTOK, cfg.NT, cfg.NTB

    def din(name, shape, dt=F32):
        return nc.dram_tensor(name, list(shape), dt, kind="ExternalInput").ap()

    x_d = din("x", [NTOK, D])
    mem_d = din("mem", [256, D])
    pos_d = din("pos", [128, NT], I32)
    invf_d = din("invf", [128, 32])
    norm_mix_d = din("norm_mix", [L, D])
    w_in_d = din("w_in", [L, D, NIN])
    dn_conv_d = din("dn_conv", [128, L * 24 * 4])
    a_log_d = din("dn_a_log", [L, 8])
    dt_bias_d = din("dn_dt_bias", [L, 8])
    out_norm_d = din("dn_out_norm", [L, 128])
    q_norm_d = din("mla_q_norm", [128, L * 4])
    w_qb_d = din("mla_w_qb", [L, 512, 1536])
    kv_norm_d = din("mla_kv_norm", [128, L * 2])
    w_kvb_d = din("mla_w_kvb", [L, 256, 2048])
    w_out_d = din("w_out", [L, D, D])
    mem_norm_d = din("mem_norm", [1, D])
    norm_x_d = din("norm_xattn", [L, D])
    wq_d = din("xa_wq", [L, D, D])
    wk_d = din("xa_wk", [L, D, D])
    wv_d = din("xa_wv", [L, D, D])
    wo_d = din("xa_wo", [L, D, D])
    norm_f_d = din("norm_ffn", [L, D])
    w_up_d = din("ffn_w_up", [L, D, 2 * DFF])
    f_conv_d = din("ffn_conv", [128, L * 88 * 3])
    f_bias_d = din("ffn_conv_bias", [128, L * 88])
    w_down_d = din("ffn_w_down", [L, DFF, D])
    norm_fin_d = din("norm_final", [1, D])
    y_d = nc.dram_tensor("y", [NTOK, D], F32, kind="ExternalOutput").ap()
    h_d = nc.dram_tensor("hscr", [NTOK, D], F32, kind="Internal").ap()
    WT = T(None, "weights")
    hT = [T(None, f"h{t}") for t in range(NT)]
    yT = [T(None, f"y{t}") for t in range(NT)]
    dbg_out = {}

    def dbg_dump(name, tile, ap, shape, dt=F32):
        if name not in cfg.dbg:
            return
        d = nc.dram_tensor("dbg_" + name, list(shape), dt, kind="ExternalOutput").ap()
        dbg_out[name] = d
        P.dma("sp", T(None), d, tile, ap)

    A = Arena(nc, 198 * 1024)
    PSB = [T(nc.alloc_psum_tensor(f"psb{i}", [128, 512], F32), f"psb{i}") for i in range(8)]
    LX = [T(PSB[b].ap[:, 0:256], f"lx{b}", parent=PSB[b]) for b in range(4)]
    LYa = [T(PSB[4 + b].ap[:, 0:128], f"lya{b}", parent=PSB[4 + b]) for b in range(4)]
    LYb = [T(PSB[4 + b].ap[:, 128:256], f"lyb{b}", parent=PSB[4 + b]) for b in range(4)]
    memn_scr = nc.dram_tensor("memn_scr", [128, 16 * 256], BF16, kind="Internal").ap()
    memn_T = T(None, "memn_scr")

    ident = A.alloc("ident", [128, 128], F32)
    ones_f = A.alloc("ones_f", [128, 128], F32)
    ones_b = A.alloc("ones_b", [128, 128], BF16)
    tri_incl = A.alloc("tri_incl", [128, 128], F32)
    mask2 = A.alloc("mask2", [128, 256], F32)
    negstrict = A.alloc("negstrict", [128, 128], F32)
    sel_last = A.alloc("sel_last", [128, 1], F32)
    cos_t = A.alloc("cos_t", [128, NT, 32], F32)
    sin_t = A.alloc("sin_t", [128, NT, 32], F32)
    qn_g = A.alloc("qn_g", [128, L, 4], F32)
    kvn_g = A.alloc("kvn_g", [128, L, 2], F32)
    dnc_w = A.alloc("dnc_w", [128, L, 24, 4], F32)
    fc_w = A.alloc("fc_w", [128, L, 88, 3], F32)
    fc_b = A.alloc("fc_b", [128, L, 88], F32)
    alog_bc = A.alloc("alog_bc", [128, L, 8], F32)
    dtb_bc = A.alloc("dtb_bc", [128, L, 8], F32)
    onorm_bc = A.alloc("onorm_bc", [128, L, 128], F32)
    KmT = A.alloc("KmT", [128, 16, 256], BF16)
    Vm = A.alloc("Vm", [128, 2, D], BF16)
    ckvT = A.alloc("ckvT", [128, 2, NTOK], BF16)
    kpeT = A.alloc("kpeT", [64, NTOK], BF16)
    Sst = A.alloc("Sst", [128, 8, 128], F32)
    S_T = [T(Sst.ap[:, h, :], f"S{h}") for h in range(8)]
    dn_halo = A.alloc("dn_halo", [128, 24, 3], F32)
    f_halo = A.alloc("f_halo", [128, 88, 2], F32)
    wbuf = [A.alloc(f"wbuf{i}", [128, 16, 512], BF16) for i in range(2)]
    stat = A.alloc("stat", [128, 16], F32)
    uT = A.alloc("uT", [128, 16, 512], BF16)
    wb_i = [0]

    def sp_load(tile, ap_out, src):
        P.dma("sp", tile, ap_out, WT, src)

    P.op("pool", R.memset(ident.ap[:], 1.0), writes=[ident])
    P.op("pool", R.affine_select(out=ident.ap[:], in_=ident.ap[:], pattern=[[-1, 128]], compare_op=ALU.is_equal,
                                          fill=0.0, base=0, channel_multiplier=1), reads=[ident], writes=[ident])
    P.op("pool", R.memset(ones_f.ap[:], 1.0), writes=[ones_f])
    P.op("pool", R.memset(ones_b.ap[:], 1.0), writes=[ones_b])
    P.op("pool", R.memset(tri_incl.ap[:], 1.0), writes=[tri_incl])
    P.op("pool", R.affine_select(out=tri_incl.ap[:], in_=tri_incl.ap[:], pattern=[[1, 128]], compare_op=ALU.is_ge,
                                          fill=0.0, base=0, channel_multiplier=-1), reads=[tri_incl], writes=[tri_incl])
    P.op("pool", R.memset(mask2.ap[:], 0.0), writes=[mask2])
    for hh in range(2):
        P.op("pool", R.affine_select(out=mask2.ap[:, hh * 128:(hh + 1) * 128], in_=mask2.ap[:, hh * 128:(hh + 1) * 128],
                                                      pattern=[[1, 128]], compare_op=ALU.is_ge, fill=NEG, base=0, channel_multiplier=-1),
             reads=[mask2], writes=[mask2])
    P.op("pool", R.memset(negstrict.ap[:], -1.0), writes=[negstrict])
    P.op("pool", R.affine_select(out=negstrict.ap[:], in_=negstrict.ap[:], pattern=[[1, 128]], compare_op=ALU.is_gt,
                                          fill=0.0, base=0, channel_multiplier=-1), reads=[negstrict], writes=[negstrict])
    P.op("pool", R.memset(sel_last.ap[:], 1.0), writes=[sel_last])
    P.op("pool", R.affine_select(out=sel_last.ap[:], in_=sel_last.ap[:], pattern=[[0, 1]], compare_op=ALU.is_equal,
                                          fill=0.0, base=-127, channel_multiplier=1), reads=[sel_last], writes=[sel_last])

    nc_allow = nc.allow_non_contiguous_dma(reason="tiny param loads")
    nc_allow.__enter__()
    sp_load(qn_g, qn_g.ap[:].rearrange("p l k -> p (l k)"), q_norm_d)
    sp_load(kvn_g, kvn_g.ap[:].rearrange("p l k -> p (l k)"), kv_norm_d)
    sp_load(dnc_w, dnc_w.ap[:].rearrange("p l c k -> p (l c k)"), dn_conv_d)
    sp_load(fc_w, fc_w.ap[:].rearrange("p l c k -> p (l c k)"), f_conv_d)
    sp_load(fc_b, fc_b.ap[:].rearrange("p l c -> p (l c)"), f_bias_d)
    for l in range(L):
        sp_load(alog_bc, alog_bc.ap[:, l, :], a_log_d[l:l + 1, :].partition_broadcast(128))
        sp_load(dtb_bc, dtb_bc.ap[:, l, :], dt_bias_d[l:l + 1, :].partition_broadcast(128))
        sp_load(onorm_bc, onorm_bc.ap[:, l, :], out_norm_d[l:l + 1, :].partition_broadcast(128))
    P.op("act", R.activation(out=alog_bc.ap[:].rearrange("p l h -> p (l h)"), in_=alog_bc.ap[:].rearrange("p l h -> p (l h)"), func=AF.Exp),
         reads=[alog_bc], writes=[alog_bc])

    m0 = A.mark()
    pos_i = A.alloc("pos_i", [128, NT], I32)
    pos_f = A.alloc("pos_f", [128, NT], F32)
    invf = A.alloc("invf", [128, 32], F32)
    ang = A.alloc("ang", [128, NT, 32], F32)
    kq = A.alloc("kq", [128, NT, 32], F32)
    ki = A.alloc("ki", [128, NT, 32], I32)
    rr = A.alloc("rr", [128, NT, 32], F32)
    sp_load(pos_i, pos_i.ap[:], pos_d)
    sp_load(invf, invf.ap[:], invf_d)
    P.op("dve", R.tensor_copy(out=pos_f.ap[:], in_=pos_i.ap[:]), reads=[pos_i], writes=[pos_f])
    for t in range(NT):
        P.op("dve", R.tensor_scalar(out=ang.ap[:, t, :], in0=invf.ap[:], scalar1=pos_f.ap[:, t:t + 1], scalar2=None, op0=ALU.mult),
             reads=[invf, pos_f], writes=[ang])
    TWO_PI = 2.0 * np.pi
    C1 = 6.28125
    C2 = TWO_PI - C1
    fl = lambda ap: ap[:].rearrange("p t j -> p (t j)")
    for which, tab in ((0, sin_t), (1, cos_t)):
        shift = 0.0 if which == 0 else np.pi / 2
        P.op("dve", R.tensor_scalar(out=fl(kq.ap), in0=fl(ang.ap), scalar1=float(shift), scalar2=float(1.0 / TWO_PI), op0=ALU.add, op1=ALU.mult),
             reads=[ang], writes=[kq])
        P.op("dve", R.tensor_copy(out=fl(ki.ap), in_=fl(kq.ap)), reads=[kq], writes=[ki])
        P.op("dve", R.tensor_copy(out=fl(kq.ap), in_=fl(ki.ap)), reads=[ki], writes=[kq])
        P.op("dve", R.scalar_tensor_tensor(out=fl(rr.ap), in0=fl(kq.ap), scalar=float(-C1), in1=fl(ang.ap), op0=ALU.mult, op1=ALU.add),
             reads=[kq, ang], writes=[rr])
        P.op("dve", R.scalar_tensor_tensor(out=fl(rr.ap), in0=fl(kq.ap), scalar=float(-C2), in1=fl(rr.ap), op0=ALU.mult, op1=ALU.add),
             reads=[kq, rr], writes=[rr])
        if which == 1:
            P.op("dve", R.tensor_scalar(out=fl(rr.ap), in0=fl(rr.ap), scalar1=float(shift), scalar2=None, op0=ALU.add),
                 reads=[rr], writes=[rr])
        P.op("dve", R.tensor_scalar(out=fl(rr.ap), in0=fl(rr.ap), scalar1=float(3.1415925), scalar2=float(-3.1415925), op0=ALU.min, op1=ALU.max),
             reads=[rr], writes=[rr])
        P.op("act", R.activation(out=fl(tab.ap), in_=fl(rr.ap), func=AF.Sin), reads=[rr], writes=[tab])
    dbg_dump("cos", cos_t, cos_t.ap[:], [128, NT, 32])
    dbg_dump("sin", sin_t, sin_t.ap[:], [128, NT, 32])
    P.barrier()
    A.reset(m0)

    evac_rr = [0]

    def evac_copy(out_t, out_ap, ps_t, ps_ap, eng=None):
        if eng is None:
            eng = "act" if evac_rr[0] % 2 == 0 else "dve"
            evac_rr[0] += 1
        if eng == "act":
            P.op("act", R.copy(out=out_ap, in_=ps_ap), reads=[ps_t], writes=[out_t])
        else:
            P.op("dve", R.tensor_copy(out=out_ap, in_=ps_ap), reads=[ps_t], writes=[out_t])

    def wload(src_ap, nkt, ncols):
        wb = wbuf[wb_i[0] % 2]
        wb_i[0] += 1
        P.dma("pool", wb, wb.ap[:, 0:nkt, 0:ncols], WT, src_ap.rearrange("(kt p) c -> p kt c", p=128))
        return wb

    def rstd_from_ss(ss_ap_fn, scale, tiles):
        P.op("dve", R.tensor_scalar(out=ss_ap_fn(), in0=ss_ap_fn(), scalar1=float(scale), scalar2=float(EPS), op0=ALU.mult, op1=ALU.add),
             reads=tiles, writes=tiles)
        P.op("act", R.activation(out=ss_ap_fn(), in_=ss_ap_fn(), func=AF.Sqrt), reads=tiles, writes=tiles)
        P.op("dve", R.reciprocal(out=ss_ap_fn(), in_=ss_ap_fn()), reads=tiles, writes=tiles)

    def norm_block(src_d, src_T, row0, gain_row_ap, ntile, dst_uT, to_y=None):
        mk = A.mark()
        hx = [A.alloc(f"hx{t}", [128, D], F32) for t in range(ntile)]
        g_mix = A.alloc("g_mix", [128, D], F32)
        junk = A.alloc("junk", [128, D], BF16)
        sp_load(g_mix, g_mix.ap[:], gain_row_ap.partition_broadcast(128))
        for t in range(ntile):
            P.dma("sp", hx[t], hx[t].ap[:], src_T[row0 // 128 + t], src_d[row0 + t * 128: row0 + (t + 1) * 128, :])
            P.op("act", R.activation(out=junk.ap[:], in_=hx[t].ap[:], func=AF.Square, accum_out=stat.ap[:, t:t + 1]),
                 reads=[hx[t]], writes=[junk, stat])
        rstd_from_ss(lambda: stat.ap[:, 0:ntile], 1.0 / D, [stat])
        for t in range(ntile):
            P.op("dve", R.scalar_tensor_tensor(out=hx[t].ap[:], in0=hx[t].ap[:], scalar=stat.ap[:, t:t + 1], in1=g_mix.ap[:],
                                                              op0=ALU.mult, op1=ALU.mult), reads=[hx[t], stat, g_mix], writes=[hx[t]])
            if to_y is not None:
                gt = row0 // 128 + t
                P.dma("sp", to_y[1][gt], to_y[0][gt * 128:(gt + 1) * 128, :], hx[t], hx[t].ap[:])
                continue
            for g in range(4):
                ps = PSB[4 + (g % 4)]
                for j in range(4):
                    kt = g * 4 + j
                    P.op("pe", R.transpose(out=ps.ap[:, j * 128:(j + 1) * 128], in_=hx[t].ap[:, kt * 128:(kt + 1) * 128], identity=ident.ap[:]),
                         reads=[hx[t], ident], writes=[ps])
                evac_copy(dst_uT, dst_uT.ap[:, g * 4:(g + 1) * 4, t * 128:(t + 1) * 128], ps, ps.ap[:].rearrange("p (a b) -> p a b", a=4))
        P.barrier()
        A.reset(mk)

    def residual_add_dense(actT, nkt_total, w_d2, tb):
        mk = A.mark()
        hx = [A.alloc(f"hxr{t}", [128, D], F32) for t in range(4)]
        for t in range(4):
            P.dma("sp", hx[t], hx[t].ap[:], hT[tb * 4 + t], h_d[(tb * 4 + t) * 128:(tb * 4 + t + 1) * 128, :])
        chunks = []
        k0 = 0
        while k0 < nkt_total:
            chunks.append((k0, min(16, nkt_total - k0)))
            k0 += 16
        for cb in range(4):
            for ci, (k0, nk) in enumerate(chunks):
                wb = wload(w_d2[k0 * 128:(k0 + nk) * 128, cb * 512:(cb + 1) * 512], nk, 512)
                for t in range(4):
                    ps = PSB[t]
                    for kk in range(nk):
                        kt = k0 + kk
                        P.op("pe", R.matmul(ps.ap[:], lhsT=actT.ap[:, kt, t * 128:(t + 1) * 128], rhs=wb.ap[:, kk, :],
                                                                                     start=(kt == 0), stop=(kt == nkt_total - 1)),
                             reads=[actT, wb], writes=[ps])
            for t in range(4):
                ps = PSB[t]
                P.op("dve", R.tensor_tensor(out=hx[t].ap[:, cb * 512:(cb + 1) * 512], in0=hx[t].ap[:, cb * 512:(cb + 1) * 512], in1=ps.ap[:], op=ALU.add),
                     reads=[hx[t], ps], writes=[hx[t]])
        for t in range(4):
            P.dma("sp", hT[tb * 4 + t], h_d[(tb * 4 + t) * 128:(tb * 4 + t + 1) * 128, :], hx[t], hx[t].ap[:])
        P.barrier()
        A.reset(mk)

    for t in range(NT):
        P.dma("sp", hT[t], h_d[t * 128:(t + 1) * 128, :], WT, x_d[t * 128:(t + 1) * 128, :])

    norm_block(mem_d, [WT, WT], 0, mem_norm_d[0:1, :], 2, uT)
    P.dma("sp", memn_T, memn_scr.rearrange("p (k m) -> p k m", k=16), uT, uT.ap[:, :, 0:256])
    P.barrier()

    for l in range(L):
        mk0 = A.mark()
        memnT = A.alloc("memnT", [128, 16, 256], BF16)
        P.dma("sp", memnT, memnT.ap[:], memn_T, memn_scr.rearrange("p (k m) -> p k m", k=16))
        if l == 0:
            dbg_dump("memnT", memnT, memnT.ap[:], [128, 16, 256], BF16)
        for cb in range(4):
            wb = wload(wk_d[l][:, cb * 512:(cb + 1) * 512], 16, 512)
            for c in range(4):
                ps = PSB[c]
                for kt in range(16):
                    P.op("pe", R.matmul(ps.ap[:, 0:256], lhsT=wb.ap[:, kt, c * 128:(c + 1) * 128], rhs=memnT.ap[:, kt, :],
                                                                         start=(kt == 0), stop=(kt == 15)), reads=[wb, memnT], writes=[ps])
                evac_copy(KmT, KmT.ap[:, cb * 4 + c, :], ps, ps.ap[:, 0:256])
        for cb in range(4):
            wb = wload(wv_d[l][:, cb * 512:(cb + 1) * 512], 16, 512)
            for m in range(2):
                ps = PSB[4 + m]
                for kt in range(16):
                    P.op("pe", R.matmul(ps.ap[:], lhsT=memnT.ap[:, kt, m * 128:(m + 1) * 128], rhs=wb.ap[:, kt, :],
                                                                         start=(kt == 0), stop=(kt == 15)), reads=[wb, memnT], writes=[ps])
                evac_copy(Vm, Vm.ap[:, m, cb * 512:(cb + 1) * 512], ps, ps.ap[:])
        for h in range(8):
            P.op("dve", R.memset(Sst.ap[:, h, :], 0.0), writes=[S_T[h]])
        P.op("dve", R.memset(dn_halo.ap[:], 0.0), writes=[dn_halo])
        P.op("dve", R.memset(f_halo.ap[:], 0.0), writes=[f_halo])
        P.barrier()
        A.reset(mk0)

        for tb in range(NTB):
            tok0 = tb * 512
            norm_block(h_d, hT, tok0, norm_mix_d[l:l + 1, :], 4, uT)
            if l == 0 and tb == 0:
                dbg_dump("uT0", uT, uT.ap[:], [128, 16, 512], BF16)
            mA = A.mark()
            oT = A.alloc("oT", [128, 16, 512], BF16)
            mB = A.mark()
            lat_q = A.alloc("lat_q", [128, 4, 512], F32)
            lat_kv = A.alloc("lat_kv", [128, 4, 320], F32)
            qlatT = A.alloc("qlatT", [128, 4, 512], BF16)
            qpe = A.alloc("qpe", [128, 512], F32)
            qpe_r = A.alloc("qpe_r", [128, 4, 512], F32)
            rtmp = A.alloc("rtmp", [128, 2, 256], F32)
            qpeT = A.alloc("qpeT", [64, 8, 512], BF16)
            KhT = A.alloc("KhT", [128, NTOK], BF16)
            Vh = A.alloc("Vh", [128, NT, 128], BF16)
            QhT = A.alloc("QhT", [128, 512], BF16)
            PT = [A.alloc(f"PT{i}", [128, 512], BF16) for i in range(3)]
            rec = A.alloc("rec", [128, 512], F32)
            junkm = A.alloc("junkm", [128, 512], BF16)
            wb1 = wload(w_in_d[l][:, 4112:4624], 16, 512)
            wb2 = wload(w_in_d[l][:, 4624:4944], 16, 320)
            for t in range(4):
                ps = PSB[t % 2]
                for kt in range(16):
                    P.op("pe", R.matmul(ps.ap[:], lhsT=uT.ap[:, kt, t * 128:(t + 1) * 128], rhs=wb1.ap[:, kt, :],
                                                                    start=(kt == 0), stop=(kt == 15)), reads=[uT, wb1], writes=[ps])
                P.op("act", R.copy(out=lat_q.ap[:, t, :], in_=ps.ap[:]), reads=[ps], writes=[lat_q])
                P.op("act", R.activation(out=junkm.ap[:], in_=lat_q.ap[:, t, :], func=AF.Square, accum_out=stat.ap[:, t:t + 1]),
                     reads=[lat_q], writes=[junkm, stat])
                ps2 = PSB[2 + t % 2]
                for kt in range(16):
                    P.op("pe", R.matmul(ps2.ap[:, 0:320], lhsT=uT.ap[:, kt, t * 128:(t + 1) * 128], rhs=wb2.ap[:, kt, 0:320],
                                                                      start=(kt == 0), stop=(kt == 15)), reads=[uT, wb2], writes=[ps2])
                P.op("dve", R.tensor_copy(out=lat_kv.ap[:, t, :], in_=ps2.ap[:, 0:320]), reads=[ps2], writes=[lat_kv])
                P.op("act", R.activation(out=junkm.ap[:, 0:256], in_=lat_kv.ap[:, t, 0:256], func=AF.Square, accum_out=stat.ap[:, 4 + t:5 + t]),
                     reads=[lat_kv], writes=[junkm, stat])
            rstd_from_ss(lambda: stat.ap[:, 0:4], 1.0 / 512, [stat])
            rstd_from_ss(lambda: stat.ap[:, 4:8], 1.0 / 256, [stat])
            if l == 0 and tb == 0:
                dbg_dump("lat_q", lat_q, lat_q.ap[:], [128, 4, 512])
                dbg_dump("lat_kv", lat_kv, lat_kv.ap[:], [128, 4, 320])
            wqb = wbuf[wb_i[0] % 2]
            wb_i[0] += 1
            wkvb = wbuf[wb_i[0] % 2]
            wb_i[0] += 1
            wqb_v = wqb.ap[:].rearrange("p k c -> p (k c)")[:, 0:4 * 1536].rearrange("p (k c) -> p k c", k=4)
            wkvb_v = wkvb.ap[:].rearrange("p k c -> p (k c)")[:, 0:2 * 2048].rearrange("p (k c) -> p k c", k=2)
            P.dma("pool", wqb, wqb_v, WT, w_qb_d[l].rearrange("(kt p) c -> p kt c", p=128))
            P.dma("pool", wkvb, wkvb_v, WT, w_kvb_d[l].rearrange("(kt p) c -> p kt c", p=128))
            wq4 = wqb_v.rearrange("p k (h d) -> p k h d", h=8)
            wkv4 = wkvb_v.rearrange("p k (h d) -> p k h d", h=8)
            for t in range(4):
                gt = tb * 4 + t
                P.op("dve", R.tensor_scalar(out=lat_q.ap[:, t, :], in0=lat_q.ap[:, t, :], scalar1=stat.ap[:, t:t + 1], scalar2=None, op0=ALU.mult),
                     reads=[lat_q, stat], writes=[lat_q])
                P.op("dve", R.tensor_scalar(out=lat_kv.ap[:, t, 0:256], in0=lat_kv.ap[:, t, 0:256], scalar1=stat.ap[:, 4 + t:5 + t], scalar2=None, op0=ALU.mult),
                     reads=[lat_kv, stat], writes=[lat_kv])
                x1 = lat_kv.ap[:, t, 256:288]
                x2 = lat_kv.ap[:, t, 288:320]
                cs = cos_t.ap[:, gt, :]
                sn = sin_t.ap[:, gt, :]
                r = rtmp.ap[:, 0, :]
                P.op("dve", R.tensor_tensor(out=r[:, 0:32], in0=x1, in1=cs, op=ALU.mult), reads=[lat_kv, cos_t], writes=[rtmp])
                P.op("dve", R.tensor_tensor(out=r[:, 32:64], in0=x2, in1=sn, op=ALU.mult), reads=[lat_kv, sin_t], writes=[rtmp])
                P.op("dve", R.tensor_tensor(out=r[:, 64:96], in0=x2, in1=cs, op=ALU.mult), reads=[lat_kv, cos_t], writes=[rtmp])
                P.op("dve", R.tensor_tensor(out=r[:, 96:128], in0=x1, in1=sn, op=ALU.mult), reads=[lat_kv, sin_t], writes=[rtmp])
                P.op("dve", R.tensor_tensor(out=x1, in0=r[:, 0:32], in1=r[:, 32:64], op=ALU.subtract), reads=[rtmp], writes=[lat_kv])
                P.op("dve", R.tensor_tensor(out=x2, in0=r[:, 64:96], in1=r[:, 96:128], op=ALU.add), reads=[rtmp], writes=[lat_kv])
                ps = PSB[4 + t % 2]
                for j in range(2):
                    P.op("pe", R.transpose(out=ps.ap[:, j * 128:(j + 1) * 128], in_=lat_kv.ap[:, t, j * 128:(j + 1) * 128], identity=ident.ap[:]),
                         reads=[lat_kv, ident], writes=[ps])
                P.op("pe", R.transpose(out=ps.ap[0:64, 256:384], in_=lat_kv.ap[:, t, 256:320], identity=ident.ap[:]),
                     reads=[lat_kv, ident], writes=[ps])
                for j in range(2):
                    P.op("act", R.activation(out=ckvT.ap[:, j, gt * 128:(gt + 1) * 128], in_=ps.ap[:, j * 128:(j + 1) * 128], func=AF.Copy,
                                                                          scale=kvn_g.ap[:, l, j:j + 1]), reads=[ps, kvn_g], writes=[ckvT])
                P.op("dve", R.tensor_copy(out=kpeT.ap[:, gt * 128:(gt + 1) * 128], in_=ps.ap[0:64, 256:384]), reads=[ps], writes=[kpeT])
            for kt in range(4):
                ps = PSB[6 + kt % 2]
                for t in range(4):
                    P.op("pe", R.transpose(out=ps.ap[:, t * 128:(t + 1) * 128], in_=lat_q.ap[:, t, kt * 128:(kt + 1) * 128], identity=ident.ap[:]),
                         reads=[lat_q, ident], writes=[ps])
                P.op("act", R.activation(out=qlatT.ap[:, kt, :], in_=ps.ap[:], func=AF.Copy, scale=qn_g.ap[:, l, kt:kt + 1]),
                     reads=[ps, qn_g], writes=[qlatT])
            if l == 0 and tb == 0:
                dbg_dump("qlatT", qlatT, qlatT.ap[:], [128, 4, 512], BF16)
                dbg_dump("ckvT", ckvT, ckvT.ap[:, :, 0:512], [128, 2, 512], BF16)
                dbg_dump("kpeT", kpeT, kpeT.ap[:, 0:512], [64, 512], BF16)
            for t in range(4):
                gt = tb * 4 + t
                ps = PSB[t % 2]
                for kt in range(4):
                    P.op("pe", R.matmul(ps.ap[:].rearrange("p (h d) -> p h d", h=8), lhsT=qlatT.ap[:, kt, t * 128:(t + 1) * 128],
                                                                    rhs=wq4[:, kt, :, 128:192], start=(kt == 0), stop=(kt == 3)), reads=[qlatT, wqb], writes=[ps])
                P.op("act", R.copy(out=qpe.ap[:], in_=ps.ap[:]), reads=[ps], writes=[qpe])
                xv = qpe.ap[:].rearrange("p (h d) -> p h d", h=8)
                ov = qpe_r.ap[:, t, :].rearrange("p (h d) -> p h d", h=8)
                cb_ = cos_t.ap[:, gt, :].unsqueeze(1).broadcast_to([128, 8, 32])
                sb_ = sin_t.ap[:, gt, :].unsqueeze(1).broadcast_to([128, 8, 32])
                r0 = rtmp.ap[:, 0, :].rearrange("p (h d) -> p h d", h=8)
                r1 = rtmp.ap[:, 1, :].rearrange("p (h d) -> p h d", h=8)
                P.op("dve", R.tensor_tensor(out=r0, in0=xv[:, :, 0:32], in1=cb_, op=ALU.mult), reads=[qpe, cos_t], writes=[rtmp])
                P.op("dve", R.tensor_tensor(out=r1, in0=xv[:, :, 32:64], in1=sb_, op=ALU.mult), reads=[qpe, sin_t], writes=[rtmp])
                P.op("dve", R.tensor_tensor(out=ov[:, :, 0:32], in0=r0, in1=r1, op=ALU.subtract), reads=[rtmp], writes=[qpe_r])
                P.op("dve", R.tensor_tensor(out=r0, in0=xv[:, :, 32:64], in1=cb_, op=ALU.mult), reads=[qpe, cos_t], writes=[rtmp])
                P.op("dve", R.tensor_tensor(out=r1, in0=xv[:, :, 0:32], in1=sb_, op=ALU.mult), reads=[qpe, sin_t], writes=[rtmp])
                P.op("dve", R.tensor_tensor(out=ov[:, :, 32:64], in0=r0, in1=r1, op=ALU.add), reads=[rtmp], writes=[qpe_r])
            for h in range(8):
                ps = PSB[4 + h % 2]
                for t in range(4):
                    P.op("pe", R.transpose(out=ps.ap[0:64, t * 128:(t + 1) * 128], in_=qpe_r.ap[:, t, h * 64:(h + 1) * 64], identity=ident.ap[:]),
                         reads=[qpe_r, ident], writes=[ps])
                evac_copy(qpeT, qpeT.ap[:, h, :], ps, ps.ap[0:64, :])
            if l == 0 and tb == 0:
                dbg_dump("qpeT", qpeT, qpeT.ap[:], [64, 8, 512], BF16)
            nkt_keys = (tb + 1) * 4
            sc = float(192 ** -0.5)
            for h in range(8):
                for nb in range(tb + 1):
                    ps = PSB[nb % 2]
                    for kt in range(2):
                        P.op("pe", R.matmul(ps.ap[:], lhsT=wkv4[:, kt, h, 0:128], rhs=ckvT.ap[:, kt, nb * 512:(nb + 1) * 512],
                                                                               start=(kt == 0), stop=(kt == 1)), reads=[wkvb, ckvT], writes=[ps])
                    evac_copy(KhT, KhT.ap[:, nb * 512:(nb + 1) * 512], ps, ps.ap[:])
                for g in range(0, nkt_keys, 4):
                    ps = PSB[2 + (g // 4) % 2]
                    for j in range(4):
                        kt_ = g + j
                        for kt in range(2):
                            P.op("pe", R.matmul(ps.ap[:, j * 128:(j + 1) * 128], lhsT=ckvT.ap[:, kt, kt_ * 128:(kt_ + 1) * 128],
                                                                                         rhs=wkv4[:, kt, h, 128:256], start=(kt == 0), stop=(kt == 1)),
                                 reads=[wkvb, ckvT], writes=[ps])
                    evac_copy(Vh, Vh.ap[:, g:g + 4, :], ps, ps.ap[:].rearrange("p (a b) -> p a b", a=4))
                ps = PSB[4]
                for kt in range(4):
                    P.op("pe", R.matmul(ps.ap[:], lhsT=wq4[:, kt, h, 0:128], rhs=qlatT.ap[:, kt, :], start=(kt == 0), stop=(kt == 3)),
                         reads=[wqb, qlatT], writes=[ps])
                evac_copy(QhT, QhT.ap[:], ps, ps.ap[:])
                if l == 0 and tb == 0 and h == 0:
                    dbg_dump("KhT", KhT, KhT.ap[:, 0:512], [128, 512], BF16)
                    dbg_dump("Vh", Vh, Vh.ap[:, 0:4, :], [128, 4, 128], BF16)
                    dbg_dump("QhT", QhT, QhT.ap[:], [128, 512], BF16)
                psO = PSB[5]
                psD = PSB[6]

                def scores(kt_, h=h):
                    ps = PSB[(kt_ % 2) * 7]
                    pt = PT[kt_ % 3]
                    P.op("pe", R.matmul(ps.ap[:], lhsT=KhT.ap[:, kt_ * 128:(kt_ + 1) * 128], rhs=QhT.ap[:], start=True, stop=False),
                         reads=[KhT, QhT], writes=[ps])
                    P.op("pe", R.matmul(ps.ap[:], lhsT=kpeT.ap[:, kt_ * 128:(kt_ + 1) * 128], rhs=qpeT.ap[:, h, :], start=False, stop=True),
                         reads=[kpeT, qpeT], writes=[ps])
                    P.op("act", R.activation(out=pt.ap[:], in_=ps.ap[:], func=AF.Exp, scale=sc), reads=[ps], writes=[pt])
                    j = kt_ - tb * 4
                    if j >= 0:
                        if j > 0:
                            P.op("dve", R.memset(pt.ap[:, 0:j * 128], 0.0), reads=[pt], writes=[pt])
                        P.op("dve", R.memset(pt.ap[64:128, j * 128:j * 128 + 64], 0.0), reads=[pt], writes=[pt])

                def pv(kt_):
                    pt = PT[kt_ % 3]
                    P.op("pe", R.matmul(psO.ap[:], lhsT=Vh.ap[:, kt_, :], rhs=pt.ap[:], start=(kt_ == 0), stop=(kt_ == nkt_keys - 1)),
                         reads=[Vh, pt], writes=[psO])
                    P.op("pe", R.matmul(psD.ap[:], lhsT=ones_b.ap[:], rhs=pt.ap[:], start=(kt_ == 0), stop=(kt_ == nkt_keys - 1)),
                         reads=[ones_b, pt], writes=[psD])
                scores(0)
                for kt_ in range(nkt_keys):
                    if kt_ + 1 < nkt_keys:
                        scores(kt_ + 1)
                    pv(kt_)
                P.op("dve", R.reciprocal(out=rec.ap[:], in_=psD.ap[:]), reads=[psD], writes=[rec])
                P.op("dve", R.tensor_tensor(out=oT.ap[:, 8 + h, :], in0=psO.ap[:], in1=rec.ap[:], op=ALU.mult), reads=[psO, rec], writes=[oT])
            if l == 0 and tb == 0:
                dbg_dump("oT_mla", oT, oT.ap[:, 8:16, :], [128, 8, 512], BF16)
            P.barrier()
            A.reset(mB)
            if cfg.stop == "mla":
                A.reset(mA)
                continue
            dn_section(P, A, nc, cfg, l, tb, locals())
            P.barrier()
            A.reset(mB)
            if l == 0 and tb == 0:
                dbg_dump("oT_dn", oT, oT.ap[:, 0:8, :], [128, 8, 512], BF16)
            if cfg.stop in ("dn", "dnpre"):
                P.barrier()
                A.reset(mA)
                continue
            residual_add_dense(oT, 16, w_out_d[l], tb)
            A.reset(mA)
            if cfg.stop == "mixer":
                continue
            norm_block(h_d, hT, tok0, norm_x_d[l:l + 1, :], 4, uT)
            mA = A.mark()
            oxT = A.alloc("oxT", [128, 16, 512], BF16)
            mX = A.mark()
            qxT = A.alloc("qxT", [128, 16, 512], BF16)
            PTx = [A.alloc(f"PTx{i}", [128, 512], BF16) for i in range(2)]
            recx = A.alloc("recx", [128, 512], F32)
            for cb in range(4):
                wb = wload(wq_d[l][:, cb * 512:(cb + 1) * 512], 16, 512)
                for c in range(4):
                    ps = PSB[c]
                    for kt in range(16):
                        P.op("pe", R.matmul(ps.ap[:], lhsT=wb.ap[:, kt, c * 128:(c + 1) * 128], rhs=uT.ap[:, kt, :],
                                                                             start=(kt == 0), stop=(kt == 15)), reads=[wb, uT], writes=[ps])
                    evac_copy(qxT, qxT.ap[:, cb * 4 + c, :], ps, ps.ap[:])
            scx = float(512 ** -0.5)
            for hd in range(4):
                for m in range(2):
                    ps = PSB[4 + m]
                    for c in range(4):
                        P.op("pe", R.matmul(ps.ap[:], lhsT=KmT.ap[:, hd * 4 + c, m * 128:(m + 1) * 128], rhs=qxT.ap[:, hd * 4 + c, :],
                                                                             start=(c == 0), stop=(c == 3)), reads=[KmT, qxT], writes=[ps])
                    P.op("act", R.activation(out=PTx[m].ap[:], in_=ps.ap[:], func=AF.Exp, scale=scx), reads=[ps], writes=[PTx[m]])
                psD = PSB[6]
                for m in range(2):
                    P.op("pe", R.matmul(psD.ap[:], lhsT=ones_b.ap[:], rhs=PTx[m].ap[:], start=(m == 0), stop=(m == 1)), reads=[ones_b, PTx[m]], writes=[psD])
                P.op("dve", R.reciprocal(out=recx.ap[:], in_=psD.ap[:]), reads=[psD], writes=[recx])
                for c in range(4):
                    ps = PSB[c]
                    for m in range(2):
                        P.op("pe", R.matmul(ps.ap[:], lhsT=Vm.ap[:, m, (hd * 4 + c) * 128:(hd * 4 + c + 1) * 128], rhs=PTx[m].ap[:],
                                                                             start=(m == 0), stop=(m == 1)), reads=[Vm, PTx[m]], writes=[ps])
                    P.op("dve", R.tensor_tensor(out=oxT.ap[:, hd * 4 + c, :], in0=ps.ap[:], in1=recx.ap[:], op=ALU.mult), reads=[ps, recx], writes=[oxT])
            P.barrier()
            A.reset(mX)
            residual_add_dense(oxT, 16, wo_d[l], tb)
            A.reset(mA)
            if cfg.stop == "xattn":
                continue
            norm_block(h_d, hT, tok0, norm_f_d[l:l + 1, :], 4, uT)
            mA = A.mark()
            aT = A.alloc("aT", [128, 44, 512], BF16)
            mF = A.mark()
            sg = A.alloc("sg", [128, 4, 512], F32)
            raw = [A.alloc(f"raw{i}", [128, 514], F32) for i in range(2)]
            cacc = [A.alloc(f"cacc{i}", [128, 512], F32) for i in range(2)]
            ri = 0
            for cb in range(11):
                for part in range(2):
                    col0 = part * DFF + cb * 512
                    wb = wload(w_up_d[l][:, col0:col0 + 512], 16, 512)
                    for c in range(4):
                        ct = (col0 // 128) + c
                        ps = PSB[c + 4 * part]
                        for kt in range(16):
                            P.op("pe", R.matmul(ps.ap[:], lhsT=wb.ap[:, kt, c * 128:(c + 1) * 128], rhs=uT.ap[:, kt, :],
                                                                                 start=(kt == 0), stop=(kt == 15)), reads=[wb, uT], writes=[ps])
                        rw = raw[ri % 2]
                        ca = cacc[ri % 2]
                        ri += 1
                        P.op("act", R.copy(out=rw.ap[:, 2:514], in_=ps.ap[:]), reads=[ps], writes=[rw])
                        P.op("dve", R.tensor_copy(out=rw.ap[:, 0:2], in_=f_halo.ap[:, ct, :]), reads=[f_halo], writes=[rw])
                        P.op("dve", R.tensor_copy(out=f_halo.ap[:, ct, :], in_=rw.ap[:, 512:514]), reads=[rw], writes=[f_halo])
                        P.op("dve", R.tensor_scalar(out=ca.ap[:], in0=rw.ap[:, 0:512], scalar1=fc_w.ap[:, l, ct, 0:1], scalar2=fc_b.ap[:, l, ct:ct + 1],
                                                                                  op0=ALU.mult, op1=ALU.add), reads=[rw, fc_w, fc_b], writes=[ca])
                        P.op("dve", R.scalar_tensor_tensor(out=ca.ap[:], in0=rw.ap[:, 1:513], scalar=fc_w.ap[:, l, ct, 1:2], in1=ca.ap[:],
                                                                                         op0=ALU.mult, op1=ALU.add), reads=[rw, fc_w, ca], writes=[ca])
                        P.op("dve", R.scalar_tensor_tensor(out=ca.ap[:], in0=rw.ap[:, 2:514], scalar=fc_w.ap[:, l, ct, 2:3], in1=ca.ap[:],
                                                                                         op0=ALU.mult, op1=ALU.add), reads=[rw, fc_w, ca], writes=[ca])
                        if part == 0:
                            P.op("act", R.activation(out=sg.ap[:, c, :], in_=ca.ap[:], func=AF.Silu), reads=[ca], writes=[sg])
                        else:
                            P.op("dve", R.tensor_tensor(out=aT.ap[:, cb * 4 + c, :], in0=ca.ap[:], in1=sg.ap[:, c, :], op=ALU.mult),
                                 reads=[ca, sg], writes=[aT])
            if l == 0 and tb == 0:
                dbg_dump("aT", aT, aT.ap[:], [128, 44, 512], BF16)
            P.barrier()
            A.reset(mF)
            residual_add_dense(aT, 44, w_down_d[l], tb)
            A.reset(mA)

    if cfg.stop is None:
        for tb in range(NTB):
            norm_block(h_d, hT, tb * 512, norm_fin_d[0:1, :], 4, None, to_y=(y_d, yT))
    else:
        for t in range(NT):
            P.dma("sp", yT[t], y_d[t * 128:(t + 1) * 128, :], hT[t], h_d[t * 128:(t + 1) * 128, :])
    P.wait_all("sp")
    P.emit()
    nc_allow.__exit__(None, None, None)
    return nc, P, A, dbg_out


def dn_section(P, A, nc, cfg, l, tb, env):
    uT = env["uT"]; PSB = env["PSB"]; wload = env["wload"]; w_in_d = env["w_in_d"]
    dnc_w = env["dnc_w"]; dn_halo = env["dn_halo"]; ones_f = env["ones_f"]
    sel_last = env["sel_last"]; tri_incl = env["tri_incl"]
    alog_bc = env["alog_bc"]; dtb_bc = env["dtb_bc"]
    dbg_dump = env["dbg_dump"]

    ba = A.alloc("ba", [128, 4, 16], F32)
    beta = A.alloc("beta", [128, 4, 8], F32)
    gg = A.alloc("gg", [128, 4, 8], F32)
    Gc = A.alloc("Gc", [128, 4, 8], F32)
    negG = A.alloc("negG", [128, 4, 8], F32)
    expG = A.alloc("expG", [128, 4, 8], F32)
    edec = A.alloc("edec", [128, 4, 8], F32)
    glast = A.alloc("glast", [128, 4, 8], F32)
    gsel = A.alloc("gsel", [128, 8], F32)
    wb = wload(w_in_d[l][:, 4096:4112], 16, 16)
    for t in range(4):
        ps = PSB[4 + t % 2]
        for kt in range(16):
            P.op("pe", R.matmul(ps.ap[:, 0:16], lhsT=uT.ap[:, kt, t * 128:(t + 1) * 128], rhs=wb.ap[:, kt, 0:16],
                                                            start=(kt == 0), stop=(kt == 15)), reads=[uT, wb], writes=[ps])
        P.op("dve", R.tensor_copy(out=ba.ap[:, t, :], in_=ps.ap[:, 0:16]), reads=[ps], writes=[ba])
    P.op("act", R.activation(out=beta.ap[:], in_=ba.ap[:, :, 0:8], func=AF.Sigmoid), reads=[ba], writes=[beta])
    for t in range(4):
        P.op("dve", R.tensor_tensor(out=gg.ap[:, t, :], in0=ba.ap[:, t, 8:16], in1=dtb_bc.ap[:, l, :], op=ALU.add), reads=[ba, dtb_bc], writes=[gg])
    P.op("act", R.activation(out=gg.ap[:], in_=gg.ap[:], func=AF.Exp), reads=[gg], writes=[gg])
    P.op("act", R.activation(out=gg.ap[:], in_=gg.ap[:], func=AF.Ln, bias=1.0), reads=[gg], writes=[gg])
    for t in range(4):
        P.op("dve", R.scalar_tensor_tensor(out=gg.ap[:, t, :], in0=gg.ap[:, t, :], scalar=-1.0, in1=alog_bc.ap[:, l, :], op0=ALU.mult, op1=ALU.mult),
             reads=[gg, alog_bc], writes=[gg])
    for t in range(4):
        ps = PSB[4 + t % 2]
        P.op("pe", R.matmul(ps.ap[:, 0:8], lhsT=tri_incl.ap[:], rhs=gg.ap[:, t, :], start=True, stop=True), reads=[tri_incl, gg], writes=[ps])
        P.op("dve", R.tensor_copy(out=Gc.ap[:, t, :], in_=ps.ap[:, 0:8]), reads=[ps], writes=[Gc])
        P.op("dve", R.tensor_scalar(out=gsel.ap[:], in0=Gc.ap[:, t, :], scalar1=sel_last.ap[:, 0:1], scalar2=None, op0=ALU.mult), reads=[Gc, sel_last], writes=[gsel])
        ps2 = PSB[6 + t % 2]
        P.op("pe", R.matmul(ps2.ap[:, 0:8], lhsT=ones_f.ap[:], rhs=gsel.ap[:], start=True, stop=True), reads=[ones_f, gsel], writes=[ps2])
        P.op("act", R.activation(out=glast.ap[:, t, :], in_=ps2.ap[:, 0:8], func=AF.Exp), reads=[ps2], writes=[glast])
        P.op("dve", R.tensor_tensor(out=edec.ap[:, t, :], in0=ps2.ap[:, 0:8], in1=Gc.ap[:, t, :], op=ALU.subtract), reads=[ps2, Gc], writes=[edec])
    P.op("act", R.activation(out=edec.ap[:], in_=edec.ap[:], func=AF.Exp), reads=[edec], writes=[edec])
    P.op("act", R.activation(out=expG.ap[:], in_=Gc.ap[:], func=AF.Exp), reads=[Gc], writes=[expG])
    P.op("dve", R.tensor_scalar(out=negG.ap[:], in0=Gc.ap[:], scalar1=-1.0, scalar2=None, op0=ALU.mult), reads=[Gc], writes=[negG])
    if l == 0 and tb == 0:
        dbg_dump("beta", beta, beta.ap[:], [128, 4, 8])
        dbg_dump("Gc", Gc, Gc.ap[:], [128, 4, 8])

    mH = A.mark()
    for hh in range(2):
        zs = A.alloc("zs", [128, 4, 512], F32)
        xc = A.alloc("xc", [128, 12, 512], F32)
        mR = A.mark()
        raw = [A.alloc(f"dnraw{i}", [128, 515], F32) for i in range(2)]
        ri = 0
        for part in range(3):
            col0 = part * 1024 + hh * 512
            wbk = wload(w_in_d[l][:, col0:col0 + 512], 16, 512)
            for c in range(4):
                ct = col0 // 128 + c
                ps = PSB[4 + c]
                for kt in range(16):
                    P.op("pe", R.matmul(ps.ap[:], lhsT=wbk.ap[:, kt, c * 128:(c + 1) * 128], rhs=uT.ap[:, kt, :],
                                                                           start=(kt == 0), stop=(kt == 15)), reads=[wbk, uT], writes=[ps])
                rw = raw[ri % 2]
                ri += 1
                xo = xc.ap[:, part * 4 + c, :]
                P.op("act", R.copy(out=rw.ap[:, 3:515], in_=ps.ap[:]), reads=[ps], writes=[rw])
                P.op("dve", R.tensor_copy(out=rw.ap[:, 0:3], in_=dn_halo.ap[:, ct, :]), reads=[dn_halo], writes=[rw])
                P.op("dve", R.tensor_copy(out=dn_halo.ap[:, ct, :], in_=rw.ap[:, 512:515]), reads=[rw], writes=[dn_halo])
                P.op("dve", R.tensor_scalar(out=xo, in0=rw.ap[:, 0:512], scalar1=dnc_w.ap[:, l, ct, 0:1], scalar2=None, op0=ALU.mult),
                     reads=[rw, dnc_w], writes=[xc])
                for k in range(1, 4):
                    P.op("dve", R.scalar_tensor_tensor(out=xo, in0=rw.ap[:, k:k + 512], scalar=dnc_w.ap[:, l, ct, k:k + 1], in1=xo,
                                                                                          op0=ALU.mult, op1=ALU.add), reads=[rw, dnc_w, xc], writes=[xc])
                P.op("act", R.activation(out=xo, in_=xo, func=AF.Silu), reads=[xc], writes=[xc])
        wbz = wload(w_in_d[l][:, 3072 + hh * 512:3072 + (hh + 1) * 512], 16, 512)
        for t in range(4):
            ps = PSB[4 + t % 4]
            for kt in range(16):
                P.op("pe", R.matmul(ps.ap[:], lhsT=uT.ap[:, kt, t * 128:(t + 1) * 128], rhs=wbz.ap[:, kt, :], start=(kt == 0), stop=(kt == 15)),
                     reads=[uT, wbz], writes=[ps])
            P.op("act", R.activation(out=zs.ap[:, t, :], in_=ps.ap[:], func=AF.Silu), reads=[ps], writes=[zs])
        if l == 0 and tb == 0 and hh == 0:
            dbg_dump("xc", xc, xc.ap[:], [128, 12, 512])
            dbg_dump("zs", zs, zs.ap[:], [128, 4, 512])
        P.barrier()
        A.reset(mR)
        if cfg.stop == "dnpre":
            A.reset(mH)
            continue
        dn_heads(P, A, nc, cfg, l, tb, hh, env, dict(xc=xc, zs=zs, beta=beta, Gc=Gc, negG=negG, expG=expG, edec=edec, glast=glast))
        P.barrier()
        A.reset(mH)


def dn_heads(P, A, nc, cfg, l, tb, hh, env, d):
    oT = env["oT"]; ident = env["ident"]; ones_f = env["ones_f"]; mask2 = env["mask2"]; negstrict = env["negstrict"]
    onorm_bc = env["onorm_bc"]; Sst = env["Sst"]; S_T = env["S_T"]; dbg_dump = env["dbg_dump"]
    LX = env["LX"]; LYa = env["LYa"]; LYb = env["LYb"]
    xc = d["xc"]; zs = d["zs"]; beta = d["beta"]; Gc = d["Gc"]; negG = d["negG"]; expG = d["expG"]; edec = d["edec"]; glast = d["glast"]
    mk = A.mark()
    NL = 4
    lanes = []
    for i in range(NL):
        lanes.append(dict(
            kv_tok=A.alloc(f"kv_tok{i}", [128, 256], F32),
            sq=A.alloc(f"sq{i}", [128, 256], F32),
            dg=A.alloc(f"dg{i}", [128, 256], F32),
            E=A.alloc(f"E{i}", [128, 256], F32),
            NA=A.alloc(f"NA{i}", [128, 256], F32),
            MN0=A.alloc(f"MN0_{i}", [128, 256], F32),
            MN1=A.alloc(f"MN1_{i}", [128, 256], F32),
            P0=A.alloc(f"P0_{i}", [128, 128], F32),
            P1=A.alloc(f"P1_{i}", [128, 128], F32),
            kdec=A.alloc(f"kdec{i}", [128, 128], F32),
            cf=A.alloc(f"cf{i}", [128, 16], F32),
            X=LX[i], Ya=LYa[i], Yb=LYb[i]))

    def chunk_local(ln, hl, t):
        h = hh * 4 + hl
        kv_tok, sq, dg, E, NA, kdec, cf = ln["kv_tok"], ln["sq"], ln["dg"], ln["E"], ln["NA"], ln["kdec"], ln["cf"]
        X, Ya, Yb = ln["X"], ln["Ya"], ln["Yb"]
        xq = xc.ap[:, 0 + hl, t * 128:(t + 1) * 128]
        xk = xc.ap[:, 4 + hl, t * 128:(t + 1) * 128]
        xv = xc.ap[:, 8 + hl, t * 128:(t + 1) * 128]
        P.op("pe", R.transpose(out=X.ap[:, 0:128], in_=xk, identity=ident.ap[:]), reads=[xc, ident], writes=[X])
        P.op("pe", R.transpose(out=X.ap[:, 128:256], in_=xv, identity=ident.ap[:]), reads=[xc, ident], writes=[X])
        P.op("pool", R.tensor_tensor(out=sq.ap[:, 0:128], in0=xq, in1=xq, op=ALU.mult), reads=[xc], writes=[sq])
        P.op("pool", R.tensor_tensor(out=sq.ap[:, 128:256], in0=xk, in1=xk, op=ALU.mult), reads=[xc], writes=[sq])
        yield
        P.op("act", R.copy(out=kv_tok.ap[:], in_=X.ap[:]), reads=[X], writes=[kv_tok])
        P.op("pe", R.matmul(Ya.ap[:, 0:8], lhsT=sq.ap[:, 0:128], rhs=ones_f.ap[:, 0:8], start=True, stop=True), reads=[sq, ones_f], writes=[Ya])
        P.op("pe", R.matmul(Ya.ap[:, 8:16], lhsT=sq.ap[:, 128:256], rhs=ones_f.ap[:, 0:8], start=True, stop=True), reads=[sq, ones_f], writes=[Ya])
        P.op("pe", R.matmul(X.ap[:, 0:128], lhsT=xk, rhs=xk, start=True, stop=True), reads=[xc], writes=[X])
        P.op("pe", R.matmul(X.ap[:, 128:256], lhsT=xk, rhs=xq, start=True, stop=True), reads=[xc], writes=[X])
        yield
        P.op("act", R.activation(out=cf.ap[:, 0:2], in_=Ya.ap[:, 0:16:8], func=AF.Ln, bias=float(EPS)), reads=[Ya], writes=[cf])
        P.op("act", R.activation(out=cf.ap[:, 0:2], in_=cf.ap[:, 0:2], func=AF.Exp, scale=-0.5), reads=[cf], writes=[cf])
        yield
        P.op("dve", R.tensor_tensor(out=cf.ap[:, 2:3], in0=cf.ap[:, 1:2], in1=beta.ap[:, t, h:h + 1], op=ALU.mult), reads=[cf, beta], writes=[cf])
        P.op("dve", R.tensor_scalar(out=cf.ap[:, 9:10], in0=cf.ap[:, 0:1], scalar1=float(128 ** -0.5), scalar2=None, op0=ALU.mult), reads=[cf], writes=[cf])
        P.op("dve", R.tensor_tensor(out=cf.ap[:, 6:7], in0=cf.ap[:, 1:2], in1=edec.ap[:, t, h:h + 1], op=ALU.mult), reads=[cf, edec], writes=[cf])
        yield
        P.op("act", R.activation(out=cf.ap[:, 3:4], in_=cf.ap[:, 2:3], func=AF.Ln), reads=[cf], writes=[cf])
        P.op("act", R.activation(out=cf.ap[:, 4:5], in_=cf.ap[:, 9:10], func=AF.Ln), reads=[cf], writes=[cf])
        P.op("dve", R.tensor_tensor(out=cf.ap[:, 5:6], in0=cf.ap[:, 2:3], in1=expG.ap[:, t, h:h + 1], op=ALU.mult), reads=[cf, expG], writes=[cf])
        P.op("dve", R.tensor_tensor(out=cf.ap[:, 7:8], in0=cf.ap[:, 9:10], in1=expG.ap[:, t, h:h + 1], op=ALU.mult), reads=[cf, expG], writes=[cf])
        yield
        P.op("dve", R.tensor_scalar(out=cf.ap[:, 3:5], in0=cf.ap[:, 3:5], scalar1=Gc.ap[:, t, h:h + 1], scalar2=None, op0=ALU.add), reads=[cf, Gc], writes=[cf])
        yield
        P.op("dve", R.tensor_scalar(out=dg.ap[:, 0:128], in0=ident.ap[:], scalar1=cf.ap[:, 3:4], scalar2=None, op0=ALU.mult), reads=[cf, ident], writes=[dg])
        P.op("dve", R.tensor_scalar(out=dg.ap[:, 128:256], in0=ident.ap[:], scalar1=cf.ap[:, 4:5], scalar2=None, op0=ALU.mult), reads=[cf, ident], writes=[dg])
        yield
        P.op("pe", R.matmul(Ya.ap[:], lhsT=ones_f.ap[:], rhs=dg.ap[:, 0:128], start=True, stop=True), reads=[ones_f, dg], writes=[Ya])
        P.op("pe", R.matmul(Yb.ap[:], lhsT=ones_f.ap[:], rhs=dg.ap[:, 128:256], start=True, stop=True), reads=[ones_f, dg], writes=[Yb])
        yield
        P.op("dve", R.tensor_tensor(out=E.ap[:, 0:128], in0=Ya.ap[:], in1=mask2.ap[:, 0:128], op=ALU.add), reads=[Ya, mask2], writes=[E])
        P.op("dve", R.tensor_tensor(out=E.ap[:, 128:256], in0=Yb.ap[:], in1=mask2.ap[:, 128:256], op=ALU.add), reads=[Yb, mask2], writes=[E])
        yield
        P.op("act", R.activation(out=E.ap[:], in_=E.ap[:], func=AF.Exp, bias=negG.ap[:, t, h:h + 1], scale=1.0), reads=[E, negG], writes=[E])
        yield
        P.op("dve", R.scalar_tensor_tensor(out=NA.ap[:], in0=X.ap[:], scalar=cf.ap[:, 1:2], in1=E.ap[:], op0=ALU.mult, op1=ALU.mult),
             reads=[X, cf, E], writes=[NA])
        yield
        cur = ln["MN0"]
        P.op("pool", R.tensor_tensor(out=cur.ap[:, 128:256], in0=NA.ap[:, 0:128], in1=negstrict.ap[:], op=ALU.mult), reads=[NA, negstrict], writes=[cur])
        yield
        P.op("pe", R.transpose(out=Ya.ap[:], in_=cur.ap[:, 128:256], identity=ident.ap[:]), reads=[cur, ident], writes=[Ya])
        pc = ln["P0"]
        P.op("dve", R.tensor_tensor(out=pc.ap[:], in0=cur.ap[:, 128:256], in1=ident.ap[:], op=ALU.add), reads=[cur, ident], writes=[pc])
        yield
        P.op("act", R.copy(out=cur.ap[:, 0:128], in_=Ya.ap[:]), reads=[Ya], writes=[cur])
        yield
        for j in range(1, 7):
            nxt = ln["MN1"] if j % 2 == 1 else ln["MN0"]
            pn = ln["P1"] if j % 2 == 1 else ln["P0"]
            last = (j == 6)
            P.op("pe", R.matmul(X.ap[:, 0:128], lhsT=cur.ap[:, 128:256], rhs=cur.ap[:, 0:128], start=True, stop=True), reads=[cur], writes=[X])
            if not last:
                P.op("pe", R.matmul(X.ap[:, 128:256], lhsT=cur.ap[:, 0:128], rhs=cur.ap[:, 128:256], start=True, stop=True), reads=[cur], writes=[X])
            yield
            if not last:
                P.op("act", R.copy(out=nxt.ap[:], in_=X.ap[:]), reads=[X], writes=[nxt])
            else:
                P.op("act", R.copy(out=nxt.ap[:, 0:128], in_=X.ap[:, 0:128]), reads=[X], writes=[nxt])
            yield
            Yp = Ya if j % 2 == 1 else Yb
            P.op("pe", R.matmul(Yp.ap[:], lhsT=nxt.ap[:, 0:128], rhs=pc.ap[:], start=True, stop=True), reads=[nxt, pc], writes=[Yp])
            yield
            P.op("dve", R.tensor_tensor(out=pn.ap[:], in0=Yp.ap[:], in1=pc.ap[:], op=ALU.add), reads=[Yp, pc], writes=[pn])
            cur = nxt
            pc = pn
        rhs_t = sq
        P.op("act", R.activation(out=rhs_t.ap[:, 0:128], in_=kv_tok.ap[:, 128:256], func=AF.Copy, scale=beta.ap[:, t, h:h + 1]), reads=[kv_tok, beta], writes=[rhs_t])
        P.op("act", R.activation(out=rhs_t.ap[:, 128:256], in_=kv_tok.ap[:, 0:128], func=AF.Copy, scale=cf.ap[:, 5:6]), reads=[kv_tok, cf], writes=[rhs_t])
        P.op("pool", R.tensor_scalar(out=kdec.ap[:], in0=kv_tok.ap[:, 0:128], scalar1=cf.ap[:, 6:7], scalar2=None, op0=ALU.mult), reads=[kv_tok, cf], writes=[kdec])
        yield
        P.op("pe", R.matmul(X.ap[:, 0:128], lhsT=pc.ap[:], rhs=rhs_t.ap[:, 0:128], start=True, stop=True), reads=[pc, rhs_t], writes=[X])
        P.op("pe", R.matmul(X.ap[:, 128:256], lhsT=rhs_t.ap[:, 128:256], rhs=pc.ap[:], start=True, stop=True), reads=[pc, rhs_t], writes=[X])
        yield
        uw = dg
        P.op("act", R.copy(out=uw.ap[:], in_=X.ap[:]), reads=[X], writes=[uw])
        if l == 0 and tb == 0 and h == 0 and t == 0:
            dbg_dump("dn_NA", NA, NA.ap[:], [128, 256])
            dbg_dump("dn_P", pc, pc.ap[:], [128, 128])
            dbg_dump("dn_uw", uw, uw.ap[:], [128, 256])
        yield

    def scan_step(ln, hl, t):
        h = hh * 4 + hl
        E, NA, kdec, cf = ln["E"], ln["NA"], ln["kdec"], ln["cf"]
        X, Ya, Yb = ln["X"], ln["Ya"], ln["Yb"]
        uw = ln["dg"]
        vo = E
        oo = ln["MN1"]
        xq = xc.ap[:, 0 + hl, t * 128:(t + 1) * 128]
        Sh = Sst.ap[:, h, :]
        ST = S_T[h]
        P.op("pe", R.matmul(X.ap[:, 0:128], lhsT=uw.ap[:, 128:256], rhs=Sh, start=True, stop=True), reads=[uw, ST], writes=[X])
        P.op("pe", R.matmul(X.ap[:, 128:256], lhsT=xq, rhs=Sh, start=True, stop=True), reads=[xc, ST], writes=[X])
        yield
        P.op("dve", R.tensor_tensor(out=vo.ap[:, 0:128], in0=uw.ap[:, 0:128], in1=X.ap[:, 0:128], op=ALU.subtract), reads=[uw, X], writes=[vo])
        P.op("act", R.activation(out=vo.ap[:, 128:256], in_=X.ap[:, 128:256], func=AF.Copy, scale=cf.ap[:, 7:8]), reads=[X, cf], writes=[vo])
        yield
        P.op("pe", R.matmul(Ya.ap[:], lhsT=NA.ap[:, 128:256], rhs=vo.ap[:, 0:128], start=True, stop=True), reads=[NA, vo], writes=[Ya])
        P.op("pe", R.matmul(Yb.ap[:], lhsT=kdec.ap[:], rhs=vo.ap[:, 0:128], start=True, stop=True), reads=[kdec, vo], writes=[Yb])
        yield
        P.op("dve", R.scalar_tensor_tensor(out=Sh, in0=Sh, scalar=glast.ap[:, t, h:h + 1], in1=Yb.ap[:], op0=ALU.mult, op1=ALU.add),
             reads=[ST, glast, Yb], writes=[ST])
        P.op("dve", R.tensor_tensor(out=oo.ap[:, 0:128], in0=vo.ap[:, 128:256], in1=Ya.ap[:], op=ALU.add), reads=[vo, Ya], writes=[oo])
        yield
        P.op("act", R.activation(out=oo.ap[:, 128:256], in_=oo.ap[:, 0:128], func=AF.Square, accum_out=cf.ap[:, 8:9]), reads=[oo], writes=[oo, cf])
        yield
        P.op("dve", R.tensor_scalar(out=cf.ap[:, 8:9], in0=cf.ap[:, 8:9], scalar1=float(1.0 / 128), scalar2=float(EPS), op0=ALU.mult, op1=ALU.add), reads=[cf], writes=[cf])
        yield
        P.op("act", R.activation(out=cf.ap[:, 8:9], in_=cf.ap[:, 8:9], func=AF.Ln), reads=[cf], writes=[cf])
        P.op("act", R.activation(out=cf.ap[:, 8:9], in_=cf.ap[:, 8:9], func=AF.Exp, scale=-0.5), reads=[cf], writes=[cf])
        yield
        P.op("dve", R.scalar_tensor_tensor(out=oo.ap[:, 128:256], in0=oo.ap[:, 0:128], scalar=cf.ap[:, 8:9], in1=onorm_bc.ap[:, l, :], op0=ALU.mult, op1=ALU.mult),
             reads=[oo, cf, onorm_bc], writes=[oo])
        yield
        P.op("pool", R.tensor_tensor(out=oo.ap[:, 128:256], in0=oo.ap[:, 128:256], in1=zs.ap[:, t, hl * 128:(hl + 1) * 128], op=ALU.mult), reads=[oo, zs], writes=[oo])
        yield
        P.op("pe", R.transpose(out=Ya.ap[:], in_=oo.ap[:, 128:256], identity=ident.ap[:]), reads=[oo, ident], writes=[Ya])
        yield
        P.op("dve", R.tensor_copy(out=oT.ap[:, h, t * 128:(t + 1) * 128], in_=Ya.ap[:]), reads=[Ya], writes=[oT])
        if l == 0 and tb == 0 and h == 0 and t == 0:
            dbg_dump("dn_o", oo, oo.ap[:, 0:128], [128, 128])
        yield

    def run_interleaved(gens):
        gens = list(gens)
        rounds = 0
        while gens:
            rounds += 1
            if getattr(cfg, "dncut", None) is not None and rounds > cfg.dncut:
                return
            alive = []
            for g in gens:
                try:
                    next(g)
                    alive.append(g)
                except StopIteration:
                    pass
            gens = alive

    for t in range(4):
        run_interleaved([chunk_local(lanes[hl], hl, t) for hl in range(4)])
        if getattr(cfg, "dncut", None) is None:
            run_interleaved([scan_step(lanes[hl], hl, t) for hl in range(4)])
    A.reset(mk)


_CACHE = {}


def _invf():
    inv = (10000.0 ** (-np.arange(0, 64, 2, dtype=np.float32) / np.float32(64))).astype(np.float32)
    return np.ascontiguousarray(np.broadcast_to(inv[None, :], (128, 32))).astype(np.float32)


def make_in_map(inputs, b, NTOK, L):
    f = lambda a: np.ascontiguousarray(np.asarray(a))
    pos = np.asarray(inputs["positions"])[b, :NTOK].astype(np.int32)
    m = {
        "x": f(np.asarray(inputs["x"])[b, :NTOK]),
        "mem": f(np.asarray(inputs["mem"])[b]),
        "pos": f(pos.reshape(NTOK // 128, 128).T),
        "invf": _invf(),
        "mem_norm": f(np.asarray(inputs["mem_norm"]).reshape(1, D)),
        "norm_final": f(np.asarray(inputs["norm_final"]).reshape(1, D)),
    }
    for k in ["norm_mix", "w_in", "dn_a_log", "dn_dt_bias", "dn_out_norm", "mla_w_qb",
              "mla_w_kvb", "w_out", "norm_xattn", "xa_wq", "xa_wk", "xa_wv", "xa_wo", "norm_ffn",
              "ffn_w_up", "ffn_w_down"]:
        m[k] = f(np.asarray(inputs[k])[:L])
    g = lambda k: np.asarray(inputs[k])[:L]
    m["mla_q_norm"] = f(g("mla_q_norm").reshape(L, 4, 128).transpose(2, 0, 1).reshape(128, L * 4))
    m["mla_kv_norm"] = f(g("mla_kv_norm").reshape(L, 2, 128).transpose(2, 0, 1).reshape(128, L * 2))
    m["dn_conv"] = f(g("dn_conv").reshape(L, 4, 24, 128).transpose(3, 0, 2, 1).reshape(128, L * 24 * 4))
    m["ffn_conv"] = f(g("ffn_conv").reshape(L, 3, 88, 128).transpose(3, 0, 2, 1).reshape(128, L * 88 * 3))
    m["ffn_conv_bias"] = f(g("ffn_conv_bias").reshape(L, 88, 128).transpose(2, 0, 1).reshape(128, L * 88))
    return m


def kernel(**inputs):
    cfg = Cfg(NTOK=2048, L=4)
    if "nc" not in _CACHE:
        _CACHE["nc"] = build_program(cfg)[0]
    nc = _CACHE["nc"]
    maps = [make_in_map(inputs, c % 4, 2048, 4) for c in range(4)]
    in_maps = [maps[c % 4] for c in range(8)]
    res = run_bass_kernel_spmd(nc, in_maps, core_ids=list(range(8)))
    out = np.stack([np.asarray(res.results[b]["y"]) for b in range(4)], axis=0).astype(np.float32)
    return out
```

```python
import numpy as np
import concourse.bass as bass
import concourse.mybir as mybir
from concourse.bass_utils import run_bass_kernel_spmd

F32 = mybir.dt.float32
BF16 = mybir.dt.bfloat16
I32 = mybir.dt.int32
U8 = mybir.dt.uint8
AF = mybir.ActivationFunctionType
ALU = mybir.AluOpType
AX = mybir.AxisListType

ENGS = ["pe", "act", "dve", "pool", "sp"]

D = 2048
NKT = 16
DFF = 5632
NIN = 4944
EPS = 1e-6
NEG = -30000.0


class T:
    __slots__ = ("ap", "name", "w", "r", "parent")

    def __init__(self, ap, name="", parent=None):
        self.ap = ap
        self.name = name
        self.w = None
        self.r = []
        self.parent = parent


def _roots(ts):
    return [t.parent if t.parent is not None else t for t in ts]


class Prog:
    def __init__(self, nc):
        self.nc = nc
        self.streams = {e: [] for e in ENGS}
        self.cnt = {}
        self.seen = {e: {} for e in ENGS}
        self.sems = {}
        self.eng_sem = {e: ("E", e, 0) for e in ENGS}
        self.dma_rr = {e: 0 for e in ENGS}
        self.NDMA = 8
        self.n_ops = 0
        self.pool_dirty = False

    def _need(self, eng, dep):
        if dep is None:
            return
        key, val = dep
        if eng == "pe" and key[0] == "E" and key[1] == "pe":
            return
        if self.seen[eng].get(key, 0) >= val:
            return
        self.seen[eng][key] = val
        self.streams[eng].append(("wait", key, val))

    def _deps(self, eng, reads, writes):
        reads = _roots(reads)
        writes = _roots(writes)
        for t in reads:
            self._need(eng, t.w)
        for t in writes:
            self._need(eng, t.w)
            for d in t.r:
                self._need(eng, d)

    def _mark(self, stamp, reads, writes):
        reads = _roots(reads)
        writes = _roots(writes)
        for t in reads:
            t.r.append(stamp)
            if len(t.r) > 48:
                best = {}
                for k, v in t.r:
                    if best.get(k, 0) < v:
                        best[k] = v
                t.r = list(best.items())
        for t in writes:
            t.w = stamp
            t.r = []

    def op(self, eng, fn, reads=(), writes=()):
        if eng == "pool":
            self.pool_dirty = True
        self._deps(eng, reads, writes)
        key = self.eng_sem[eng]
        c = self.cnt.get(key, 0) + 1
        self.cnt[key] = c
        self.streams[eng].append(("op", fn, key))
        self._mark((key, c), reads, writes)
        self.n_ops += 1
        if c >= 16000:
            self.eng_sem[eng] = ("E", eng, key[2] + 1)

    def dma(self, q, out_t, out_ap, in_t, in_ap):
        reads = [in_t] if in_t is not None else []
        writes = [out_t] if out_t is not None else []
        self._deps(q, reads, writes)
        i = self.dma_rr[q]
        self.dma_rr[q] = (i + 1) % self.NDMA
        key = ("D", q, i)
        prev = self.cnt.get(key, 0)
        if prev:
            self._need(q, (key, prev))
        c = prev + 16
        self.cnt[key] = c

        def fn(e, out_ap=out_ap, in_ap=in_ap):
            return e.dma_start(out=out_ap, in_=in_ap)
        self.streams[q].append(("dma", fn, key))
        self._mark((key, c), reads, writes)
        self.n_ops += 1

    def barrier(self):
        for e in ENGS:
            if e == "pool" and not self.pool_dirty:
                continue
            self.wait_all(e)
        self.pool_dirty = False

    def wait_all(self, eng):
        for key, val in list(self.cnt.items()):
            if val:
                self._need(eng, (key, val))

    def emit(self):
        nc = self.nc
        for k in list(self.cnt.keys()):
            self.sems[k] = nc.alloc_semaphore("s_" + "_".join(str(x) for x in k))
        streams, sems = self.streams, self.sems

        def run(e, name):
            for item in streams[name]:
                if item[0] == "wait":
                    e.wait_ge(sems[item[1]], item[2])
                elif item[0] == "op":
                    item[1](e).then_inc(sems[item[2]], 1)
                else:
                    item[1](e).then_inc(sems[item[2]], 16)

        with nc.Block() as block:
            @block.tensor
            def _(e):
                run(e, "pe")

            @block.scalar
            def _(e):
                run(e, "act")

            @block.vector
            def _(e):
                run(e, "dve")

            @block.gpsimd
            def _(e):
                run(e, "pool")

            @block.sync
            def _(e):
                run(e, "sp")


class _Rec:
    def __getattr__(self, name):
        def mk(*args, **kw):
            return lambda e: getattr(e, name)(*args, **kw)
        return mk


R = _Rec()


class Arena:
    def __init__(self, nc, size):
        self.nc = nc
        slab = nc.alloc_sbuf_tensor("slab", [128, size], U8)
        self.base = nc.lookup_mloc(slab).addr
        self.size = size
        self.top = 0
        self.peak = 0

    def alloc(self, name, shape, dtype):
        nb = int(np.prod(shape[1:])) * (4 if dtype in (F32, I32) else 2)
        nb = (nb + 31) // 32 * 32
        off = self.top
        assert off + nb <= self.size, f"arena overflow {name}: {off}+{nb} > {self.size}"
        self.top += nb
        self.peak = max(self.peak, self.top)
        h = self.nc.alloc_sbuf_tensor_at(name, list(shape), dtype, offset=self.base + off)
        return T(h, name)

    def mark(self):
        return self.top

    def reset(self, m):
        self.top = m


class Cfg:
    def __init__(self, NTOK=2048, L=4, dbg=(), stop=None):
        self.NTOK = NTOK
        self.L = L
        self.TB = 512
        self.NTB = NTOK // 512
        self.NT = NTOK // 128
        self.dbg = dbg
        self.stop = stop


def build_program(cfg):
    nc = bass.Bass("TRN2", target_bir_lowering=False)
    P = Prog(nc)
    L, NTOK, NT, NTB = cfg.L, cfg.NTOK, cfg.NT, cfg.NTB

    def din(name, shape, dt=F32):
        return nc.dram_tensor(name, list(shape), dt, kind="ExternalInput").ap()

    x_d = din("x", [NTOK, D])
    mem_d = din("mem", [256, D])
    pos_d = din("pos", [128, NT], I32)
    invf_d = din("invf", [128, 32])
    norm_mix_d = din("norm_mix", [L, D])
    w_in_d = din("w_in", [L, D, NIN])
    dn_conv_d = din("dn_conv", [128, L * 24 * 4])
    a_log_d = din("dn_a_log", [L, 8])
    dt_bias_d = din("dn_dt_bias", [L, 8])
    out_norm_d = din("dn_out_norm", [L, 128])
    q_norm_d = din("mla_q_norm", [128, L * 4])
    w_qb_d = din("mla_w_qb", [L, 512, 1536])
    kv_norm_d = din("mla_kv_norm", [128, L * 2])
    w_kvb_d = din("mla_w_kvb", [L, 256, 2048])
    w_out_d = din("w_out", [L, D, D])
    mem_norm_d = din("mem_norm", [1, D])
    norm_x_d = din("norm_xattn", [L, D])
    wq_d = din("xa_wq", [L, D, D])
    wk_d = din("xa_wk", [L, D, D])
    wv_d = din("xa_wv", [L, D, D])
    wo_d = din("xa_wo", [L, D, D])
    norm_f_d = din("norm_ffn", [L, D])
    w_up_d = din("ffn_w_up", [L, D, 2 * DFF])
    f_conv_d = din("ffn_conv", [128, L * 88 * 3])
    f_bias_d = din("ffn_conv_bias", [128, L * 88])
    w_down_d = din("ffn_w_down", [L, DFF, D])
    norm_fin_d = din("norm_final", [1, D])
    y_d = nc.dram_tensor("y", [NTOK, D], F32, kind="ExternalOutput").ap()
    h_d = nc.dram_tensor("hscr", [NTOK, D], F32, kind="Internal").ap()
    WT = T(None, "weights")
    hT = [T(None, f"h{t}") for t in range(NT)]
    yT = [T(None, f"y{t}") for t in range(NT)]
    dbg_out = {}

    def dbg_dump(name, tile, ap, shape, dt=F32):
        if name not in cfg.dbg:
            return
        d = nc.dram_tensor("dbg_" + name, list(shape), dt, kind="ExternalOutput").ap()
        dbg_out[name] = d
        P.dma("sp", T(None), d, tile, ap)

    A = Arena(nc, 198 * 1024)
    PSB = [T(nc.alloc_psum_tensor(f"psb{i}", [128, 512], F32), f"psb{i}") for i in range(8)]
    LX = [T(PSB[b].ap[:, 0:256], f"lx{b}", parent=PSB[b]) for b in range(4)]
    LYa = [T(PSB[4 + b].ap[:, 0:128], f"lya{b}", parent=PSB[4 + b]) for b in range(4)]
    LYb = [T(PSB[4 + b].ap[:, 128:256], f"lyb{b}", parent=PSB[4 + b]) for b in range(4)]
    memn_scr = nc.dram_tensor("memn_scr", [128, 16 * 256], BF16, kind="Internal").ap()
    memn_T = T(None, "memn_scr")

    ident = A.alloc("ident", [128, 128], F32)
    ones_f = A.alloc("ones_f", [128, 128], F32)
    ones_b = A.alloc("ones_b", [128, 128], BF16)
    tri_incl = A.alloc("tri_incl", [128, 128], F32)
    mask2 = A.alloc("mask2", [128, 256], F32)
    negstrict = A.alloc("negstrict", [128, 128], F32)
    sel_last = A.alloc("sel_last", [128, 1], F32)
    cos_t = A.alloc("cos_t", [128, NT, 32], F32)
    sin_t = A.alloc("sin_t", [128, NT, 32], F32)
    qn_g = A.alloc("qn_g", [128, L, 4], F32)
    kvn_g = A.alloc("kvn_g", [128, L, 2], F32)
    dnc_w = A.alloc("dnc_w", [128, L, 24, 4], F32)
    fc_w = A.alloc("fc_w", [128, L, 88, 3], F32)
    fc_b = A.alloc("fc_b", [128, L, 88], F32)
    alog_bc = A.alloc("alog_bc", [128, L, 8], F32)
    dtb_bc = A.alloc("dtb_bc", [128, L, 8], F32)
    onorm_bc = A.alloc("onorm_bc", [128, L, 128], F32)
    KmT = A.alloc("KmT", [128, 16, 256], BF16)
    Vm = A.alloc("Vm", [128, 2, D], BF16)
    ckvT = A.alloc("ckvT", [128, 2, NTOK], BF16)
    kpeT = A.alloc("kpeT", [64, NTOK], BF16)
    Sst = A.alloc("Sst", [128, 8, 128], F32)
    S_T = [T(Sst.ap[:, h, :], f"S{h}") for h in range(8)]
    dn_halo = A.alloc("dn_halo", [128, 24, 3], F32)
    f_halo = A.alloc("f_halo", [128, 88, 2], F32)
    wbuf = [A.alloc(f"wbuf{i}", [128, 16, 512], BF16) for i in range(2)]
    stat = A.alloc("stat", [128, 16], F32)
    uT = A.alloc("uT", [128, 16, 512], BF16)
    wb_i = [0]

    def sp_load(tile, ap_out, src):
        P.dma("sp", tile, ap_out, WT, src)

    P.op("pool", R.memset(ident.ap[:], 1.0), writes=[ident])
    P.op("pool", R.affine_select(out=ident.ap[:], in_=ident.ap[:], pattern=[[-1, 128]], compare_op=ALU.is_equal,
                                          fill=0.0, base=0, channel_multiplier=1), reads=[ident], writes=[ident])
    P.op("pool", R.memset(ones_f.ap[:], 1.0), writes=[ones_f])
    P.op("pool", R.memset(ones_b.ap[:], 1.0), writes=[ones_b])
    P.op("pool", R.memset(tri_incl.ap[:], 1.0), writes=[tri_incl])
    P.op("pool", R.affine_select(out=tri_incl.ap[:], in_=tri_incl.ap[:], pattern=[[1, 128]], compare_op=ALU.is_ge,
                                          fill=0.0, base=0, channel_multiplier=-1), reads=[tri_incl], writes=[tri_incl])
    P.op("pool", R.memset(mask2.ap[:], 0.0), writes=[mask2])
    for hh in range(2):
        P.op("pool", R.affine_select(out=mask2.ap[:, hh * 128:(hh + 1) * 128], in_=mask2.ap[:, hh * 128:(hh + 1) * 128],
                                                      pattern=[[1, 128]], compare_op=ALU.is_ge, fill=NEG, base=0, channel_multiplier=-1),
             reads=[mask2], writes=[mask2])
    P.op("pool", R.memset(negstrict.ap[:], -1.0), writes=[negstrict])
    P.op("pool", R.affine_select(out=negstrict.ap[:], in_=negstrict.ap[:], pattern=[[1, 128]], compare_op=ALU.is_gt,
                                          fill=0.0, base=0, channel_multiplier=-1), reads=[negstrict], writes=[negstrict])
    P.op("pool", R.memset(sel_last.ap[:], 1.0), writes=[sel_last])
    P.op("pool", R.affine_select(out=sel_last.ap[:], in_=sel_last.ap[:], pattern=[[0, 1]], compare_op=ALU.is_equal,
                                          fill=0.0, base=-127, channel_multiplier=1), reads=[sel_last], writes=[sel_last])

    nc_allow = nc.allow_non_contiguous_dma(reason="tiny param loads")
    nc_allow.__enter__()
    sp_load(qn_g, qn_g.ap[:].rearrange("p l k -> p (l k)"), q_norm_d)
    sp_load(kvn_g, kvn_g.ap[:].rearrange("p l k -> p (l k)"), kv_norm_d)
    sp_load(dnc_w, dnc_w.ap[:].rearrange("p l c k -> p (l c k)"), dn_conv_d)
    sp_load(fc_w, fc_w.ap[:].rearrange("p l c k -> p (l c k)"), f_conv_d)
    sp_load(fc_b, fc_b.ap[:].rearrange("p l c -> p (l c)"), f_bias_d)
    for l in range(L):
        sp_load(alog_bc, alog_bc.ap[:, l, :], a_log_d[l:l + 1, :].partition_broadcast(128))
        sp_load(dtb_bc, dtb_bc.ap[:, l, :], dt_bias_d[l:l + 1, :].partition_broadcast(128))
        sp_load(onorm_bc, onorm_bc.ap[:, l, :], out_norm_d[l:l + 1, :].partition_broadcast(128))
    P.op("act", R.activation(out=alog_bc.ap[:].rearrange("p l h -> p (l h)"), in_=alog_bc.ap[:].rearrange("p l h -> p (l h)"), func=AF.Exp),
         reads=[alog_bc], writes=[alog_bc])

    m0 = A.mark()
    pos_i = A.alloc("pos_i", [128, NT], I32)
    pos_f = A.alloc("pos_f", [128, NT], F32)
    invf = A.alloc("invf", [128, 32], F32)
    ang = A.alloc("ang", [128, NT, 32], F32)
    kq = A.alloc("kq", [128, NT, 32], F32)
    ki = A.alloc("ki", [128, NT, 32], I32)
    rr = A.alloc("rr", [128, NT, 32], F32)
    sp_load(pos_i, pos_i.ap[:], pos_d)
    sp_load(invf, invf.ap[:], invf_d)
    P.op("dve", R.tensor_copy(out=pos_f.ap[:], in_=pos_i.ap[:]), reads=[pos_i], writes=[pos_f])
    for t in range(NT):
        P.op("dve", R.tensor_scalar(out=ang.ap[:, t, :], in0=invf.ap[:], scalar1=pos_f.ap[:, t:t + 1], scalar2=None, op0=ALU.mult),
             reads=[invf, pos_f], writes=[ang])
    TWO_PI = 2.0 * np.pi
    C1 = 6.28125
    C2 = TWO_PI - C1
    fl = lambda ap: ap[:].rearrange("p t j -> p (t j)")
    for which, tab in ((0, sin_t), (1, cos_t)):
        shift = 0.0 if which == 0 else np.pi / 2
        P.op("dve", R.tensor_scalar(out=fl(kq.ap), in0=fl(ang.ap), scalar1=float(shift), scalar2=float(1.0 / TWO_PI), op0=ALU.add, op1=ALU.mult),
             reads=[ang], writes=[kq])
        P.op("dve", R.tensor_copy(out=fl(ki.ap), in_=fl(kq.ap)), reads=[kq], writes=[ki])
        P.op("dve", R.tensor_copy(out=fl(kq.ap), in_=fl(ki.ap)), reads=[ki], writes=[kq])
        P.op("dve", R.scalar_tensor_tensor(out=fl(rr.ap), in0=fl(kq.ap), scalar=float(-C1), in1=fl(ang.ap), op0=ALU.mult, op1=ALU.add),
             reads=[kq, ang], writes=[rr])
        P.op("dve", R.scalar_tensor_tensor(out=fl(rr.ap), in0=fl(kq.ap), scalar=float(-C2), in1=fl(rr.ap), op0=ALU.mult, op1=ALU.add),
             reads=[kq, rr], writes=[rr])
        if which == 1:
            P.op("dve", R.tensor_scalar(out=fl(rr.ap), in0=fl(rr.ap), scalar1=float(shift), scalar2=None, op0=ALU.add),
                 reads=[rr], writes=[rr])
        P.op("dve", R.tensor_scalar(out=fl(rr.ap), in0=fl(rr.ap), scalar1=float(3.1415925), scalar2=float(-3.1415925), op0=ALU.min, op1=ALU.max),
             reads=[rr], writes=[rr])
        P.op("act", R.activation(out=fl(tab.ap), in_=fl(rr.ap), func=AF.Sin), reads=[rr], writes=[tab])
    dbg_dump("cos", cos_t, cos_t.ap[:], [128, NT, 32])
    dbg_dump("sin", sin_t, sin_t.ap[:], [128, NT, 32])
    P.barrier()
    A.reset(m0)

    evac_rr = [0]

    def evac_copy(out_t, out_ap, ps_t, ps_ap, eng=None):
        if eng is None:
            eng = "act" if evac_rr[0] % 2 == 0 else "dve"
            evac_rr[0] += 1
        if eng == "act":
            P.op("act", R.copy(out=out_ap, in_=ps_ap), reads=[ps_t], writes=[out_t])
        else:
            P.op("dve", R.tensor_copy(out=out_ap, in_=ps_ap), reads=[ps_t], writes=[out_t])

    def wload(src_ap, nkt, ncols):
        wb = wbuf[wb_i[0] % 2]
        wb_i[0] += 1
        P.dma("pool", wb, wb.ap[:, 0:nkt, 0:ncols], WT, src_ap.rearrange("(kt p) c -> p kt c", p=128))
        return wb

    def rstd_from_ss(ss_ap_fn, scale, tiles):
        P.op("dve", R.tensor_scalar(out=ss_ap_fn(), in0=ss_ap_fn(), scalar1=float(scale), scalar2=float(EPS), op0=ALU.mult, op1=ALU.add),
             reads=tiles, writes=tiles)
        P.op("act", R.activation(out=ss_ap_fn(), in_=ss_ap_fn(), func=AF.Sqrt), reads=tiles, writes=tiles)
        P.op("dve", R.reciprocal(out=ss_ap_fn(), in_=ss_ap_fn()), reads=tiles, writes=tiles)

    def norm_block(src_d, src_T, row0, gain_row_ap, ntile, dst_uT, to_y=None):
        mk = A.mark()
        hx = [A.alloc(f"hx{t}", [128, D], F32) for t in range(ntile)]
        g_mix = A.alloc("g_mix", [128, D], F32)
        junk = A.alloc("junk", [128, D], BF16)
        sp_load(g_mix, g_mix.ap[:], gain_row_ap.partition_broadcast(128))
        for t in range(ntile):
            P.dma("sp", hx[t], hx[t].ap[:], src_T[row0 // 128 + t], src_d[row0 + t * 128: row0 + (t + 1) * 128, :])
            P.op("act", R.activation(out=junk.ap[:], in_=hx[t].ap[:], func=AF.Square, accum_out=stat.ap[:, t:t + 1]),
                 reads=[hx[t]], writes=[junk, stat])
        rstd_from_ss(lambda: stat.ap[:, 0:ntile], 1.0 / D, [stat])
        for t in range(ntile):
            P.op("dve", R.scalar_tensor_tensor(out=hx[t].ap[:], in0=hx[t].ap[:], scalar=stat.ap[:, t:t + 1], in1=g_mix.ap[:],
                                                              op0=ALU.mult, op1=ALU.mult), reads=[hx[t], stat, g_mix], writes=[hx[t]])
            if to_y is not None:
                gt = row0 // 128 + t
                P.dma("sp", to_y[1][gt], to_y[0][gt * 128:(gt + 1) * 128, :], hx[t], hx[t].ap[:])
                continue
            for g in range(4):
                ps = PSB[4 + (g % 4)]
                for j in range(4):
                    kt = g * 4 + j
                    P.op("pe", R.transpose(out=ps.ap[:, j * 128:(j + 1) * 128], in_=hx[t].ap[:, kt * 128:(kt + 1) * 128], identity=ident.ap[:]),
                         reads=[hx[t], ident], writes=[ps])
                evac_copy(dst_uT, dst_uT.ap[:, g * 4:(g + 1) * 4, t * 128:(t + 1) * 128], ps, ps.ap[:].rearrange("p (a b) -> p a b", a=4))
        P.barrier()
        A.reset(mk)

    def residual_add_dense(actT, nkt_total, w_d2, tb):
        mk = A.mark()
        hx = [A.alloc(f"hxr{t}", [128, D], F32) for t in range(4)]
        for t in range(4):
            P.dma("sp", hx[t], hx[t].ap[:], hT[tb * 4 + t], h_d[(tb * 4 + t) * 128:(tb * 4 + t + 1) * 128, :])
        chunks = []
        k0 = 0
        while k0 < nkt_total:
            chunks.append((k0, min(16, nkt_total - k0)))
            k0 += 16
        for cb in range(4):
            for ci, (k0, nk) in enumerate(chunks):
                wb = wload(w_d2[k0 * 128:(k0 + nk) * 128, cb * 512:(cb + 1) * 512], nk, 512)
                for t in range(4):
                    ps = PSB[t]
                    for kk in range(nk):
                        kt = k0 + kk
                        P.op("pe", R.matmul(ps.ap[:], lhsT=actT.ap[:, kt, t * 128:(t + 1) * 128], rhs=wb.ap[:, kk, :],
                                                                                     start=(kt == 0), stop=(kt == nkt_total - 1)),
                             reads=[actT, wb], writes=[ps])
            for t in range(4):
                ps = PSB[t]
                P.op("dve", R.tensor_tensor(out=hx[t].ap[:, cb * 512:(cb + 1) * 512], in0=hx[t].ap[:, cb * 512:(cb + 1) * 512], in1=ps.ap[:], op=ALU.add),
                     reads=[hx[t], ps], writes=[hx[t]])
        for t in range(4):
            P.dma("sp", hT[tb * 4 + t], h_d[(tb * 4 + t) * 128:(tb * 4 + t + 1) * 128, :], hx[t], hx[t].ap[:])
        P.barrier()
        A.reset(mk)

    for t in range(NT):
        P.dma("sp", hT[t], h_d[t * 128:(t + 1) * 128, :], WT, x_d[t * 128:(t + 1) * 128, :])

    norm_block(mem_d, [WT, WT], 0, mem_norm_d[0:1, :], 2, uT)
    P.dma("sp", memn_T, memn_scr.rearrange("p (k m) -> p k m", k=16), uT, uT.ap[:, :, 0:256])
    P.barrier()

    for l in range(L):
        mk0 = A.mark()
        memnT = A.alloc("memnT", [128, 16, 256], BF16)
        P.dma("sp", memnT, memnT.ap[:], memn_T, memn_scr.rearrange("p (k m) -> p k m", k=16))
        if l == 0:
            dbg_dump("memnT", memnT, memnT.ap[:], [128, 16, 256], BF16)
        for cb in range(4):
            wb = wload(wk_d[l][:, cb * 512:(cb + 1) * 512], 16, 512)
            for c in range(4):
                ps = PSB[c]
                for kt in range(16):
                    P.op("pe", R.matmul(ps.ap[:, 0:256], lhsT=wb.ap[:, kt, c * 128:(c + 1) * 128], rhs=memnT.ap[:, kt, :],
                                                                         start=(kt == 0), stop=(kt == 15)), reads=[wb, memnT], writes=[ps])
                evac_copy(KmT, KmT.ap[:, cb * 4 + c, :], ps, ps.ap[:, 0:256])
        for cb in range(4):
            wb = wload(wv_d[l][:, cb * 512:(cb + 1) * 512], 16, 512)
            for m in range(2):
                ps = PSB[4 + m]
                for kt in range(16):
                    P.op("pe", R.matmul(ps.ap[:], lhsT=memnT.ap[:, kt, m * 128:(m + 1) * 128], rhs=wb.ap[:, kt, :],
                                                                         start=(kt == 0), stop=(kt == 15)), reads=[wb, memnT], writes=[ps])
                evac_copy(Vm, Vm.ap[:, m, cb * 512:(cb + 1) * 512], ps, ps.ap[:])
        for h in range(8):
            P.op("dve", R.memset(Sst.ap[:, h, :], 0.0), writes=[S_T[h]])
        P.op("dve", R.memset(dn_halo.ap[:], 0.0), writes=[dn_halo])
        P.op("dve", R.memset(f_halo.ap[:], 0.0), writes=[f_halo])
        P.barrier()
        A.reset(mk0)

        for tb in range(NTB):
            tok0 = tb * 512
            norm_block(h_d, hT, tok0, norm_mix_d[l:l + 1, :], 4, uT)
            if l == 0 and tb == 0:
                dbg_dump("uT0", uT, uT.ap[:], [128, 16, 512], BF16)
            mA = A.mark()
            oT = A.alloc("oT", [128, 16, 512], BF16)
            mB = A.mark()
            lat_q = A.alloc("lat_q", [128, 4, 512], F32)
            lat_kv = A.alloc("lat_kv", [128, 4, 320], F32)
            qlatT = A.alloc("qlatT", [128, 4, 512], BF16)
            qpe = A.alloc("qpe", [128, 512], F32)
            qpe_r = A.alloc("qpe_r", [128, 4, 512], F32)
            rtmp = A.alloc("rtmp", [128, 2, 256], F32)
            qpeT = A.alloc("qpeT", [64, 8, 512], BF16)
            KhT = A.alloc("KhT", [128, NTOK], BF16)
            Vh = A.alloc("Vh", [128, NT, 128], BF16)
            QhT = A.alloc("QhT", [128, 512], BF16)
            PT = [A.alloc(f"PT{i}", [128, 512], BF16) for i in range(3)]
            rec = A.alloc("rec", [128, 512], F32)
            junkm = A.alloc("junkm", [128, 512], BF16)
            wb1 = wload(w_in_d[l][:, 4112:4624], 16, 512)
            wb2 = wload(w_in_d[l][:, 4624:4944], 16, 320)
            for t in range(4):
                ps = PSB[t % 2]
                for kt in range(16):
                    P.op("pe", R.matmul(ps.ap[:], lhsT=uT.ap[:, kt, t * 128:(t + 1) * 128], rhs=wb1.ap[:, kt, :],
                                                                    start=(kt == 0), stop=(kt == 15)), reads=[uT, wb1], writes=[ps])
                P.op("act", R.copy(out=lat_q.ap[:, t, :], in_=ps.ap[:]), reads=[ps], writes=[lat_q])
                P.op("act", R.activation(out=junkm.ap[:], in_=lat_q.ap[:, t, :], func=AF.Square, accum_out=stat.ap[:, t:t + 1]),
                     reads=[lat_q], writes=[junkm, stat])
                ps2 = PSB[2 + t % 2]
                for kt in range(16):
                    P.op("pe", R.matmul(ps2.ap[:, 0:320], lhsT=uT.ap[:, kt, t * 128:(t + 1) * 128], rhs=wb2.ap[:, kt, 0:320],
                                                                      start=(kt == 0), stop=(kt == 15)), reads=[uT, wb2], writes=[ps2])
                P.op("dve", R.tensor_copy(out=lat_kv.ap[:, t, :], in_=ps2.ap[:, 0:320]), reads=[ps2], writes=[lat_kv])
                P.op("act", R.activation(out=junkm.ap[:, 0:256], in_=lat_kv.ap[:, t, 0:256], func=AF.Square, accum_out=stat.ap[:, 4 + t:5 + t]),
                     reads=[lat_kv], writes=[junkm, stat])
            rstd_from_ss(lambda: stat.ap[:, 0:4], 1.0 / 512, [stat])
            rstd_from_ss(lambda: stat.ap[:, 4:8], 1.0 / 256, [stat])
            if l == 0 and tb == 0:
                dbg_dump("lat_q", lat_q, lat_q.ap[:], [128, 4, 512])
                dbg_dump("lat_kv", lat_kv, lat_kv.ap[:], [128, 4, 320])
            wqb = wbuf[wb_i[0] % 2]
            wb_i[0] += 1
            wkvb = wbuf[wb_i[0] % 2]
            wb_i[0] += 1
            wqb_v = wqb.ap[:].rearrange("p k c -> p (k c)")[:, 0:4 * 1536].rearrange("p (k c) -> p k c", k=4)
            wkvb_v = wkvb.ap[:].rearrange("p k c -> p (k c)")[:, 0:2 * 2048].rearrange("p (k c) -> p k c", k=2)
            P.dma("pool", wqb, wqb_v, WT, w_qb_d[l].rearrange("(kt p) c -> p kt c", p=128))
            P.dma("pool", wkvb, wkvb_v, WT, w_kvb_d[l].rearrange("(kt p) c -> p kt c", p=128))
            wq4 = wqb_v.rearrange("p k (h d) -> p k h d", h=8)
            wkv4 = wkvb_v.rearrange("p k (h d) -> p k h d", h=8)
            for t in range(4):
                gt = tb * 4 + t
                P.op("dve", R.tensor_scalar(out=lat_q.ap[:, t, :], in0=lat_q.ap[:, t, :], scalar1=stat.ap[:, t:t + 1], scalar2=None, op0=ALU.mult),
                     reads=[lat_q, stat], writes=[lat_q])
                P.op("dve", R.tensor_scalar(out=lat_kv.ap[:, t, 0:256], in0=lat_kv.ap[:, t, 0:256], scalar1=stat.ap[:, 4 + t:5 + t], scalar2=None, op0=ALU.mult),
                     reads=[lat_kv, stat], writes=[lat_kv])
                x1 = lat_kv.ap[:, t, 256:288]
                x2 = lat_kv.ap[:, t, 288:320]
                cs = cos_t.ap[:, gt, :]
                sn = sin_t.ap[:, gt, :]
                r = rtmp.ap[:, 0, :]
                P.op("dve", R.tensor_tensor(out=r[:, 0:32], in0=x1, in1=cs, op=ALU.mult), reads=[lat_kv, cos_t], writes=[rtmp])
                P.op("dve", R.tensor_tensor(out=r[:, 32:64], in0=x2, in1=sn, op=ALU.mult), reads=[lat_kv, sin_t], writes=[rtmp])
                P.op("dve", R.tensor_tensor(out=r[:, 64:96], in0=x2, in1=cs, op=ALU.mult), reads=[lat_kv, cos_t], writes=[rtmp])
                P.op("dve", R.tensor_tensor(out=r[:, 96:128], in0=x1, in1=sn, op=ALU.mult), reads=[lat_kv, sin_t], writes=[rtmp])
                P.op("dve", R.tensor_tensor(out=x1, in0=r[:, 0:32], in1=r[:, 32:64], op=ALU.subtract), reads=[rtmp], writes=[lat_kv])
                P.op("dve", R.tensor_tensor(out=x2, in0=r[:, 64:96], in1=r[:, 96:128], op=ALU.add), reads=[rtmp], writes=[lat_kv])
                ps = PSB[4 + t % 2]
                for j in range(2):
                    P.op("pe", R.transpose(out=ps.ap[:, j * 128:(j + 1) * 128], in_=lat_kv.ap[:, t, j * 128:(j + 1) * 128], identity=ident.ap[:]),
                         reads=[lat_kv, ident], writes=[ps])
                P.op("pe", R.transpose(out=ps.ap[0:64, 256:384], in_=lat_kv.ap[:, t, 256:320], identity=ident.ap[:]),
                     reads=[lat_kv, ident], writes=[ps])
                for j in range(2):
                    P.op("act", R.activation(out=ckvT.ap[:, j, gt * 128:(gt + 1) * 128], in_=ps.ap[:, j * 128:(j + 1) * 128], func=AF.Copy,
                                                                          scale=kvn_g.ap[:, l, j:j + 1]), reads=[ps, kvn_g], writes=[ckvT])
                P.op("dve", R.tensor_copy(out=kpeT.ap[:, gt * 128:(gt + 1) * 128], in_=ps.ap[0:64, 256:384]), reads=[ps], writes=[kpeT])
            for kt in range(4):
                ps = PSB[6 + kt % 2]
                for t in range(4):
                    P.op("pe", R.transpose(out=ps.ap[:, t * 128:(t + 1) * 128], in_=lat_q.ap[:, t, kt * 128:(kt + 1) * 128], identity=ident.ap[:]),
                         reads=[lat_q, ident], writes=[ps])
                P.op("act", R.activation(out=qlatT.ap[:, kt, :], in_=ps.ap[:], func=AF.Copy, scale=qn_g.ap[:, l, kt:kt + 1]),
                     reads=[ps, qn_g], writes=[qlatT])
            if l == 0 and tb == 0:
                dbg_dump("qlatT", qlatT, qlatT.ap[:], [128, 4, 512], BF16)
                dbg_dump("ckvT", ckvT, ckvT.ap[:, :, 0:512], [128, 2, 512], BF16)
                dbg_dump("kpeT", kpeT, kpeT.ap[:, 0:512], [64, 512], BF16)
            for t in range(4):
                gt = tb * 4 + t
                ps = PSB[t % 2]
                for kt in range(4):
                    P.op("pe", R.matmul(ps.ap[:].rearrange("p (h d) -> p h d", h=8), lhsT=qlatT.ap[:, kt, t * 128:(t + 1) * 128],
                                                                    rhs=wq4[:, kt, :, 128:192], start=(kt == 0), stop=(kt == 3)), reads=[qlatT, wqb], writes=[ps])
                P.op("act", R.copy(out=qpe.ap[:], in_=ps.ap[:]), reads=[ps], writes=[qpe])
                xv = qpe.ap[:].rearrange("p (h d) -> p h d", h=8)
                ov = qpe_r.ap[:, t, :].rearrange("p (h d) -> p h d", h=8)
                cb_ = cos_t.ap[:, gt, :].unsqueeze(1).broadcast_to([128, 8, 32])
                sb_ = sin_t.ap[:, gt, :].unsqueeze(1).broadcast_to([128, 8, 32])
                r0 = rtmp.ap[:, 0, :].rearrange("p (h d) -> p h d", h=8)
                r1 = rtmp.ap[:, 1, :].rearrange("p (h d) -> p h d", h=8)
                P.op("dve", R.tensor_tensor(out=r0, in0=xv[:, :, 0:32], in1=cb_, op=ALU.mult), reads=[qpe, cos_t], writes=[rtmp])
                P.op("dve", R.tensor_tensor(out=r1, in0=xv[:, :, 32:64], in1=sb_, op=ALU.mult), reads=[qpe, sin_t], writes=[rtmp])
                P.op("dve", R.tensor_tensor(out=ov[:, :, 0:32], in0=r0, in1=r1, op=ALU.subtract), reads=[rtmp], writes=[qpe_r])
                P.op("dve", R.tensor_tensor(out=r0, in0=xv[:, :, 32:64], in1=cb_, op=ALU.mult), reads=[qpe, cos_t], writes=[rtmp])
                P.op("dve", R.tensor_tensor(out=r1, in0=xv[:, :, 0:32], in1=sb_, op=ALU.mult), reads=[qpe, sin_t], writes=[rtmp])
                P.op("dve", R.tensor_tensor(out=ov[:, :, 32:64], in0=r0, in1=r1, op=ALU.add), reads=[rtmp], writes=[qpe_r])
            for h in range(8):
                ps = PSB[4 + h % 2]
                for t in range(4):
                    P.op("pe", R.transpose(out=ps.ap[0:64, t * 128:(t + 1) * 128], in_=qpe_r.ap[:, t, h * 64:(h + 1) * 64], identity=ident.ap[:]),
                         reads=[qpe_r, ident], writes=[ps])
                evac_copy(qpeT, qpeT.ap[:, h, :], ps, ps.ap[0:64, :])
            if l == 0 and tb == 0:
                dbg_dump("qpeT", qpeT, qpeT.ap[:], [64, 8, 512], BF16)
            nkt_keys = (tb + 1) * 4
            sc = float(192 ** -0.5)
            for h in range(8):
                for nb in range(tb + 1):
                    ps = PSB[nb % 2]
                    for kt in range(2):
                        P.op("pe", R.matmul(ps.ap[:], lhsT=wkv4[:, kt, h, 0:128], rhs=ckvT.ap[:, kt, nb * 512:(nb + 1) * 512],
                                                                               start=(kt == 0), stop=(kt == 1)), reads=[wkvb, ckvT], writes=[ps])
                    evac_copy(KhT, KhT.ap[:, nb * 512:(nb + 1) * 512], ps, ps.ap[:])
                for g in range(0, nkt_keys, 4):
                    ps = PSB[2 + (g // 4) % 2]
                    for j in range(4):
                        kt_ = g + j
                        for kt in range(2):
                            P.op("pe", R.matmul(ps.ap[:, j * 128:(j + 1) * 128], lhsT=ckvT.ap[:, kt, kt_ * 128:(kt_ + 1) * 128],
                                                                                         rhs=wkv4[:, kt, h, 128:256], start=(kt == 0), stop=(kt == 1)),
                                 reads=[wkvb, ckvT], writes=[ps])
                    evac_copy(Vh, Vh.ap[:, g:g + 4, :], ps, ps.ap[:].rearrange("p (a b) -> p a b", a=4))
                ps = PSB[4]
                for kt in range(4):
                    P.op("pe", R.matmul(ps.ap[:], lhsT=wq4[:, kt, h, 0:128], rhs=qlatT.ap[:, kt, :], start=(kt == 0), stop=(kt == 3)),
                         reads=[wqb, qlatT], writes=[ps])
                evac_copy(QhT, QhT.ap[:], ps, ps.ap[:])
                if l == 0 and tb == 0 and h == 0:
                    dbg_dump("KhT", KhT, KhT.ap[:, 0:512], [128, 512], BF16)
                    dbg_dump("Vh", Vh, Vh.ap[:, 0:4, :], [128, 4, 128], BF16)
                    dbg_dump("QhT", QhT, QhT.ap[:], [128, 512], BF16)
                psO = PSB[5]
                psD = PSB[6]

                def scores(kt_, h=h):
                    ps = PSB[(kt_ % 2) * 7]
                    pt = PT[kt_ % 3]
                    P.op("pe", R.matmul(ps.ap[:], lhsT=KhT.ap[:, kt_ * 128:(kt_ + 1) * 128], rhs=QhT.ap[:], start=True, stop=False),
                         reads=[KhT, QhT], writes=[ps])
                    P.op("pe", R.matmul(ps.ap[:], lhsT=kpeT.ap[:, kt_ * 128:(kt_ + 1) * 128], rhs=qpeT.ap[:, h, :], start=False, stop=True),
                         reads=[kpeT, qpeT], writes=[ps])
                    P.op("act", R.activation(out=pt.ap[:], in_=ps.ap[:], func=AF.Exp, scale=sc), reads=[ps], writes=[pt])
                    j = kt_ - tb * 4
                    if j >= 0:
                        if j > 0:
                            P.op("dve", R.memset(pt.ap[:, 0:j * 128], 0.0), reads=[pt], writes=[pt])
                        P.op("dve", R.memset(pt.ap[64:128, j * 128:j * 128 + 64], 0.0), reads=[pt], writes=[pt])

                def pv(kt_):
                    pt = PT[kt_ % 3]
                    P.op("pe", R.matmul(psO.ap[:], lhsT=Vh.ap[:, kt_, :], rhs=pt.ap[:], start=(kt_ == 0), stop=(kt_ == nkt_keys - 1)),
                         reads=[Vh, pt], writes=[psO])
                    P.op("pe", R.matmul(psD.ap[:], lhsT=ones_b.ap[:], rhs=pt.ap[:], start=(kt_ == 0), stop=(kt_ == nkt_keys - 1)),
                         reads=[ones_b, pt], writes=[psD])
                scores(0)
                for kt_ in range(nkt_keys):
                    if kt_ + 1 < nkt_keys:
                        scores(kt_ + 1)
                    pv(kt_)
                P.op("dve", R.reciprocal(out=rec.ap[:], in_=psD.ap[:]), reads=[psD], writes=[rec])
                P.op("dve", R.tensor_tensor(out=oT.ap[:, 8 + h, :], in0=psO.ap[:], in1=rec.ap[:], op=ALU.mult), reads=[psO, rec], writes=[oT])
            if l == 0 and tb == 0:
                dbg_dump("oT_mla", oT, oT.ap[:, 8:16, :], [128, 8, 512], BF16)
            P.barrier()
            A.reset(mB)
            if cfg.stop == "mla":
                A.reset(mA)
                continue
            dn_section(P, A, nc, cfg, l, tb, locals())
            P.barrier()
            A.reset(mB)
            if l == 0 and tb == 0:
                dbg_dump("oT_dn", oT, oT.ap[:, 0:8, :], [128, 8, 512], BF16)
            if cfg.stop in ("dn", "dnpre"):
                P.barrier()
                A.reset(mA)
                continue
            residual_add_dense(oT, 16, w_out_d[l], tb)
            A.reset(mA)
            if cfg.stop == "mixer":
                continue
            norm_block(h_d, hT, tok0, norm_x_d[l:l + 1, :], 4, uT)
            mA = A.mark()
            oxT = A.alloc("oxT", [128, 16, 512], BF16)
            mX = A.mark()
            qxT = A.alloc("qxT", [128, 16, 512], BF16)
            PTx = [A.alloc(f"PTx{i}", [128, 512], BF16) for i in range(2)]
            recx = A.alloc("recx", [128, 512], F32)
            for cb in range(4):
                wb = wload(wq_d[l][:, cb * 512:(cb + 1) * 512], 16, 512)
                for c in range(4):
                    ps = PSB[c]
                    for kt in range(16):
                        P.op("pe", R.matmul(ps.ap[:], lhsT=wb.ap[:, kt, c * 128:(c + 1) * 128], rhs=uT.ap[:, kt, :],
                                                                             start=(kt == 0), stop=(kt == 15)), reads=[wb, uT], writes=[ps])
                    evac_copy(qxT, qxT.ap[:, cb * 4 + c, :], ps, ps.ap[:])
            scx = float(512 ** -0.5)
            for hd in range(4):
                for m in range(2):
                    ps = PSB[4 + m]
                    for c in range(4):
                        P.op("pe", R.matmul(ps.ap[:], lhsT=KmT.ap[:, hd * 4 + c, m * 128:(m + 1) * 128], rhs=qxT.ap[:, hd * 4 + c, :],
                                                                             start=(c == 0), stop=(c == 3)), reads=[KmT, qxT], writes=[ps])
                    P.op("act", R.activation(out=PTx[m].ap[:], in_=ps.ap[:], func=AF.Exp, scale=scx), reads=[ps], writes=[PTx[m]])
                psD = PSB[6]
                for m in range(2):
                    P.op("pe", R.matmul(psD.ap[:], lhsT=ones_b.ap[:], rhs=PTx[m].ap[:], start=(m == 0), stop=(m == 1)), reads=[ones_b, PTx[m]], writes=[psD])
                P.op("dve", R.reciprocal(out=recx.ap[:], in_=psD.ap[:]), reads=[psD], writes=[recx])
                for c in range(4):
                    ps = PSB[c]
                    for m in range(2):
                        P.op("pe", R.matmul(ps.ap[:], lhsT=Vm.ap[:, m, (hd * 4 + c) * 128:(hd * 4 + c + 1) * 128], rhs=PTx[m].ap[:],
                                                                             start=(m == 0), stop=(m == 1)), reads=[Vm, PTx[m]], writes=[ps])
                    P.op("dve", R.tensor_tensor(out=oxT.ap[:, hd * 4 + c, :], in0=ps.ap[:], in1=recx.ap[:], op=ALU.mult), reads=[ps, recx], writes=[oxT])
            P.barrier()
            A.reset(mX)
            residual_add_dense(oxT, 16, wo_d[l], tb)
            A.reset(mA)
            if cfg.stop == "xattn":
                continue
            norm_block(h_d, hT, tok0, norm_f_d[l:l + 1, :], 4, uT)
            mA = A.mark()
            aT = A.alloc("aT", [128, 44, 512], BF16)
            mF = A.mark()
            sg = A.alloc("sg", [128, 4, 512], F32)
            raw = [A.alloc(f"raw{i}", [128, 514], F32) for i in range(2)]
            cacc = [A.alloc(f"cacc{i}", [128, 512], F32) for i in range(2)]
            ri = 0
            for cb in range(11):
                for part in range(2):
                    col0 = part * DFF + cb * 512
                    wb = wload(w_up_d[l][:, col0:col0 + 512], 16, 512)
                    for c in range(4):
                        ct = (col0 // 128) + c
                        ps = PSB[c + 4 * part]
                        for kt in range(16):
                            P.op("pe", R.matmul(ps.ap[:], lhsT=wb.ap[:, kt, c * 128:(c + 1) * 128], rhs=uT.ap[:, kt, :],
                                                                                 start=(kt == 0), stop=(kt == 15)), reads=[wb, uT], writes=[ps])
                        rw = raw[ri % 2]
                        ca = cacc[ri % 2]
                        ri += 1
                        P.op("act", R.copy(out=rw.ap[:, 2:514], in_=ps.ap[:]), reads=[ps], writes=[rw])
                        P.op("dve", R.tensor_copy(out=rw.ap[:, 0:2], in_=f_halo.ap[:, ct, :]), reads=[f_halo], writes=[rw])
                        P.op("dve", R.tensor_copy(out=f_halo.ap[:, ct, :], in_=rw.ap[:, 512:514]), reads=[rw], writes=[f_halo])
                        P.op("dve", R.tensor_scalar(out=ca.ap[:], in0=rw.ap[:, 0:512], scalar1=fc_w.ap[:, l, ct, 0:1], scalar2=fc_b.ap[:, l, ct:ct + 1],
                                                                                  op0=ALU.mult, op1=ALU.add), reads=[rw, fc_w, fc_b], writes=[ca])
                        P.op("dve", R.scalar_tensor_tensor(out=ca.ap[:], in0=rw.ap[:, 1:513], scalar=fc_w.ap[:, l, ct, 1:2], in1=ca.ap[:],
                                                                                         op0=ALU.mult, op1=ALU.add), reads=[rw, fc_w, ca], writes=[ca])
                        P.op("dve", R.scalar_tensor_tensor(out=ca.ap[:], in0=rw.ap[:, 2:514], scalar=fc_w.ap[:, l, ct, 2:3], in1=ca.ap[:],
                                                                                         op0=ALU.mult, op1=ALU.add), reads=[rw, fc_w, ca], writes=[ca])
                        if part == 0:
                            P.op("act", R.activation(out=sg.ap[:, c, :], in_=ca.ap[:], func=AF.Silu), reads=[ca], writes=[sg])
                        else:
                            P.op("dve", R.tensor_tensor(out=aT.ap[:, cb * 4 + c, :], in0=ca.ap[:], in1=sg.ap[:, c, :], op=ALU.mult),
                                 reads=[ca, sg], writes=[aT])
            if l == 0 and tb == 0:
                dbg_dump("aT", aT, aT.ap[:], [128, 44, 512], BF16)
            P.barrier()
            A.reset(mF)
            residual_add_dense(aT, 44, w_down_d[l], tb)
            A.reset(mA)

    if cfg.stop is None:
        for tb in range(NTB):
            norm_block(h_d, hT, tb * 512, norm_fin_d[0:1, :], 4, None, to_y=(y_d, yT))
    else:
        for t in range(NT):
            P.dma("sp", yT[t], y_d[t * 128:(t + 1) * 128, :], hT[t], h_d[t * 128:(t + 1) * 128, :])
    P.wait_all("sp")
    P.emit()
    nc_allow.__exit__(None, None, None)
    return nc, P, A, dbg_out


def dn_section(P, A, nc, cfg, l, tb, env):
    uT = env["uT"]; PSB = env["PSB"]; wload = env["wload"]; w_in_d = env["w_in_d"]
    dnc_w = env["dnc_w"]; dn_halo = env["dn_halo"]; ones_f = env["ones_f"]
    sel_last = env["sel_last"]; tri_incl = env["tri_incl"]
    alog_bc = env["alog_bc"]; dtb_bc = env["dtb_bc"]
    dbg_dump = env["dbg_dump"]

    ba = A.alloc("ba", [128, 4, 16], F32)
    beta = A.alloc("beta", [128, 4, 8], F32)
    gg = A.alloc("gg", [128, 4, 8], F32)
    Gc = A.alloc("Gc", [128, 4, 8], F32)
    negG = A.alloc("negG", [128, 4, 8], F32)
    expG = A.alloc("expG", [128, 4, 8], F32)
    edec = A.alloc("edec", [128, 4, 8], F32)
    glast = A.alloc("glast", [128, 4, 8], F32)
    gsel = A.alloc("gsel", [128, 8], F32)
    wb = wload(w_in_d[l][:, 4096:4112], 16, 16)
    for t in range(4):
        ps = PSB[4 + t % 2]
        for kt in range(16):
            P.op("pe", R.matmul(ps.ap[:, 0:16], lhsT=uT.ap[:, kt, t * 128:(t + 1) * 128], rhs=wb.ap[:, kt, 0:16],
                                                            start=(kt == 0), stop=(kt == 15)), reads=[uT, wb], writes=[ps])
        P.op("dve", R.tensor_copy(out=ba.ap[:, t, :], in_=ps.ap[:, 0:16]), reads=[ps], writes=[ba])
    P.op("act", R.activation(out=beta.ap[:], in_=ba.ap[:, :, 0:8], func=AF.Sigmoid), reads=[ba], writes=[beta])
    for t in range(4):
        P.op("dve", R.tensor_tensor(out=gg.ap[:, t, :], in0=ba.ap[:, t, 8:16], in1=dtb_bc.ap[:, l, :], op=ALU.add), reads=[ba, dtb_bc], writes=[gg])
    P.op("act", R.activation(out=gg.ap[:], in_=gg.ap[:], func=AF.Exp), reads=[gg], writes=[gg])
    P.op("act", R.activation(out=gg.ap[:], in_=gg.ap[:], func=AF.Ln, bias=1.0), reads=[gg], writes=[gg])
    for t in range(4):
        P.op("dve", R.scalar_tensor_tensor(out=gg.ap[:, t, :], in0=gg.ap[:, t, :], scalar=-1.0, in1=alog_bc.ap[:, l, :], op0=ALU.mult, op1=ALU.mult),
             reads=[gg, alog_bc], writes=[gg])
    for t in range(4):
        ps = PSB[4 + t % 2]
        P.op("pe", R.matmul(ps.ap[:, 0:8], lhsT=tri_incl.ap[:], rhs=gg.ap[:, t, :], start=True, stop=True), reads=[tri_incl, gg], writes=[ps])
        P.op("dve", R.tensor_copy(out=Gc.ap[:, t, :], in_=ps.ap[:, 0:8]), reads=[ps], writes=[Gc])
        P.op("dve", R.tensor_scalar(out=gsel.ap[:], in0=Gc.ap[:, t, :], scalar1=sel_last.ap[:, 0:1], scalar2=None, op0=ALU.mult), reads=[Gc, sel_last], writes=[gsel])
        ps2 = PSB[6 + t % 2]
        P.op("pe", R.matmul(ps2.ap[:, 0:8], lhsT=ones_f.ap[:], rhs=gsel.ap[:], start=True, stop=True), reads=[ones_f, gsel], writes=[ps2])
        P.op("act", R.activation(out=glast.ap[:, t, :], in_=ps2.ap[:, 0:8], func=AF.Exp), reads=[ps2], writes=[glast])
        P.op("dve", R.tensor_tensor(out=edec.ap[:, t, :], in0=ps2.ap[:, 0:8], in1=Gc.ap[:, t, :], op=ALU.subtract), reads=[ps2, Gc], writes=[edec])
    P.op("act", R.activation(out=edec.ap[:], in_=edec.ap[:], func=AF.Exp), reads=[edec], writes=[edec])
    P.op("act", R.activation(out=expG.ap[:], in_=Gc.ap[:], func=AF.Exp), reads=[Gc], writes=[expG])
    P.op("dve", R.tensor_scalar(out=negG.ap[:], in0=Gc.ap[:], scalar1=-1.0, scalar2=None, op0=ALU.mult), reads=[Gc], writes=[negG])
    if l == 0 and tb == 0:
        dbg_dump("beta", beta, beta.ap[:], [128, 4, 8])
        dbg_dump("Gc", Gc, Gc.ap[:], [128, 4, 8])

    mH = A.mark()
    for hh in range(2):
        zs = A.alloc("zs", [128, 4, 512], F32)
        xc = A.alloc("xc", [128, 12, 512], F32)
        mR = A.mark()
        raw = [A.alloc(f"dnraw{i}", [128, 515], F32) for i in range(2)]
        ri = 0
        for part in range(3):
            col0 = part * 1024 + hh * 512
            wbk = wload(w_in_d[l][:, col0:col0 + 512], 16, 512)
            for c in range(4):
                ct = col0 // 128 + c
                ps = PSB[4 + c]
                for kt in range(16):
                    P.op("pe", R.matmul(ps.ap[:], lhsT=wbk.ap[:, kt, c * 128:(c + 1) * 128], rhs=uT.ap[:, kt, :],
                                                                           start=(kt == 0), stop=(kt == 15)), reads=[wbk, uT], writes=[ps])
                rw = raw[ri % 2]
                ri += 1
                xo = xc.ap[:, part * 4 + c, :]
                P.op("act", R.copy(out=rw.ap[:, 3:515], in_=ps.ap[:]), reads=[ps], writes=[rw])
                P.op("dve", R.tensor_copy(out=rw.ap[:, 0:3], in_=dn_halo.ap[:, ct, :]), reads=[dn_halo], writes=[rw])
                P.op("dve", R.tensor_copy(out=dn_halo.ap[:, ct, :], in_=rw.ap[:, 512:515]), reads=[rw], writes=[dn_halo])
                P.op("dve", R.tensor_scalar(out=xo, in0=rw.ap[:, 0:512], scalar1=dnc_w.ap[:, l, ct, 0:1], scalar2=None, op0=ALU.mult),
                     reads=[rw, dnc_w], writes=[xc])
                for k in range(1, 4):
                    P.op("dve", R.scalar_tensor_tensor(out=xo, in0=rw.ap[:, k:k + 512], scalar=dnc_w.ap[:, l, ct, k:k + 1], in1=xo,
                                                                                          op0=ALU.mult, op1=ALU.add), reads=[rw, dnc_w, xc], writes=[xc])
                P.op("act", R.activation(out=xo, in_=xo, func=AF.Silu), reads=[xc], writes=[xc])
        wbz = wload(w_in_d[l][:, 3072 + hh * 512:3072 + (hh + 1) * 512], 16, 512)
        for t in range(4):
            ps = PSB[4 + t % 4]
            for kt in range(16):
                P.op("pe", R.matmul(ps.ap[:], lhsT=uT.ap[:, kt, t * 128:(t + 1) * 128], rhs=wbz.ap[:, kt, :], start=(kt == 0), stop=(kt == 15)),
                     reads=[uT, wbz], writes=[ps])
            P.op("act", R.activation(out=zs.ap[:, t, :], in_=ps.ap[:], func=AF.Silu), reads=[ps], writes=[zs])
        if l == 0 and tb == 0 and hh == 0:
            dbg_dump("xc", xc, xc.ap[:], [128, 12, 512])
            dbg_dump("zs", zs, zs.ap[:], [128, 4, 512])
        P.barrier()
        A.reset(mR)
        if cfg.stop == "dnpre":
            A.reset(mH)
            continue
        dn_heads(P, A, nc, cfg, l, tb, hh, env, dict(xc=xc, zs=zs, beta=beta, Gc=Gc, negG=negG, expG=expG, edec=edec, glast=glast))
        P.barrier()
        A.reset(mH)


def dn_heads(P, A, nc, cfg, l, tb, hh, env, d):
    oT = env["oT"]; ident = env["ident"]; ones_f = env["ones_f"]; mask2 = env["mask2"]; negstrict = env["negstrict"]
    onorm_bc = env["onorm_bc"]; Sst = env["Sst"]; S_T = env["S_T"]; dbg_dump = env["dbg_dump"]
    LX = env["LX"]; LYa = env["LYa"]; LYb = env["LYb"]
    xc = d["xc"]; zs = d["zs"]; beta = d["beta"]; Gc = d["Gc"]; negG = d["negG"]; expG = d["expG"]; edec = d["edec"]; glast = d["glast"]
    mk = A.mark()
    NL = 4
    lanes = []
    for i in range(NL):
        lanes.append(dict(
            kv_tok=A.alloc(f"kv_tok{i}", [128, 256], F32),
            sq=A.alloc(f"sq{i}", [128, 256], F32),
            dg=A.alloc(f"dg{i}", [128, 256], F32),
            E=A.alloc(f"E{i}", [128, 256], F32),
            NA=A.alloc(f"NA{i}", [128, 256], F32),
            MN0=A.alloc(f"MN0_{i}", [128, 256], F32),
            MN1=A.alloc(f"MN1_{i}", [128, 256], F32),
            P0=A.alloc(f"P0_{i}", [128, 128], F32),
            P1=A.alloc(f"P1_{i}", [128, 128], F32),
            kdec=A.alloc(f"kdec{i}", [128, 128], F32),
            cf=A.alloc(f"cf{i}", [128, 16], F32),
            X=LX[i], Ya=LYa[i], Yb=LYb[i]))

    def chunk_local(ln, hl, t):
        h = hh * 4 + hl
        kv_tok, sq, dg, E, NA, kdec, cf = ln["kv_tok"], ln["sq"], ln["dg"], ln["E"], ln["NA"], ln["kdec"], ln["cf"]
        X, Ya, Yb = ln["X"], ln["Ya"], ln["Yb"]
        xq = xc.ap[:, 0 + hl, t * 128:(t + 1) * 128]
        xk = xc.ap[:, 4 + hl, t * 128:(t + 1) * 128]
        xv = xc.ap[:, 8 + hl, t * 128:(t + 1) * 128]
        P.op("pe", R.transpose(out=X.ap[:, 0:128], in_=xk, identity=ident.ap[:]), reads=[xc, ident], writes=[X])
        P.op("pe", R.transpose(out=X.ap[:, 128:256], in_=xv, identity=ident.ap[:]), reads=[xc, ident], writes=[X])
        P.op("pool", R.tensor_tensor(out=sq.ap[:, 0:128], in0=xq, in1=xq, op=ALU.mult), reads=[xc], writes=[sq])
        P.op("pool", R.tensor_tensor(out=sq.ap[:, 128:256], in0=xk, in1=xk, op=ALU.mult), reads=[xc], writes=[sq])
        yield
        P.op("act", R.copy(out=kv_tok.ap[:], in_=X.ap[:]), reads=[X], writes=[kv_tok])
        P.op("pe", R.matmul(Ya.ap[:, 0:8], lhsT=sq.ap[:, 0:128], rhs=ones_f.ap[:, 0:8], start=True, stop=True), reads=[sq, ones_f], writes=[Ya])
        P.op("pe", R.matmul(Ya.ap[:, 8:16], lhsT=sq.ap[:, 128:256], rhs=ones_f.ap[:, 0:8], start=True, stop=True), reads=[sq, ones_f], writes=[Ya])
        P.op("pe", R.matmul(X.ap[:, 0:128], lhsT=xk, rhs=xk, start=True, stop=True), reads=[xc], writes=[X])
        P.op("pe", R.matmul(X.ap[:, 128:256], lhsT=xk, rhs=xq, start=True, stop=True), reads=[xc], writes=[X])
        yield
        P.op("act", R.activation(out=cf.ap[:, 0:2], in_=Ya.ap[:, 0:16:8], func=AF.Ln, bias=float(EPS)), reads=[Ya], writes=[cf])
        P.op("act", R.activation(out=cf.ap[:, 0:2], in_=cf.ap[:, 0:2], func=AF.Exp, scale=-0.5), reads=[cf], writes=[cf])
        yield
        P.op("dve", R.tensor_tensor(out=cf.ap[:, 2:3], in0=cf.ap[:, 1:2], in1=beta.ap[:, t, h:h + 1], op=ALU.mult), reads=[cf, beta], writes=[cf])
        P.op("dve", R.tensor_scalar(out=cf.ap[:, 9:10], in0=cf.ap[:, 0:1], scalar1=float(128 ** -0.5), scalar2=None, op0=ALU.mult), reads=[cf], writes=[cf])
        P.op("dve", R.tensor_tensor(out=cf.ap[:, 6:7], in0=cf.ap[:, 1:2], in1=edec.ap[:, t, h:h + 1], op=ALU.mult), reads=[cf, edec], writes=[cf])
        yield
        P.op("act", R.activation(out=cf.ap[:, 3:4], in_=cf.ap[:, 2:3], func=AF.Ln), reads=[cf], writes=[cf])
        P.op("act", R.activation(out=cf.ap[:, 4:5], in_=cf.ap[:, 9:10], func=AF.Ln), reads=[cf], writes=[cf])
        P.op("dve", R.tensor_tensor(out=cf.ap[:, 5:6], in0=cf.ap[:, 2:3], in1=expG.ap[:, t, h:h + 1], op=ALU.mult), reads=[cf, expG], writes=[cf])
        P.op("dve", R.tensor_tensor(out=cf.ap[:, 7:8], in0=cf.ap[:, 9:10], in1=expG.ap[:, t, h:h + 1], op=ALU.mult), reads=[cf, expG], writes=[cf])
        yield
        P.op("dve", R.tensor_scalar(out=cf.ap[:, 3:5], in0=cf.ap[:, 3:5], scalar1=Gc.ap[:, t, h:h + 1], scalar2=None, op0=ALU.add), reads=[cf, Gc], writes=[cf])
        yield
        P.op("dve", R.tensor_scalar(out=dg.ap[:, 0:128], in0=ident.ap[:], scalar1=cf.ap[:, 3:4], scalar2=None, op0=ALU.mult), reads=[cf, ident], writes=[dg])
        P.op("dve", R.tensor_scalar(out=dg.ap[:, 128:256], in0=ident.ap[:], scalar1=cf.ap[:, 4:5], scalar2=None, op0=ALU.mult), reads=[cf, ident], writes=[dg])
        yield
        P.op("pe", R.matmul(Ya.ap[:], lhsT=ones_f.ap[:], rhs=dg.ap[:, 0:128], start=True, stop=True), reads=[ones_f, dg], writes=[Ya])
        P.op("pe", R.matmul(Yb.ap[:], lhsT=ones_f.ap[:], rhs=dg.ap[:, 128:256], start=True, stop=True), reads=[ones_f, dg], writes=[Yb])
        yield
        P.op("dve", R.tensor_tensor(out=E.ap[:, 0:128], in0=Ya.ap[:], in1=mask2.ap[:, 0:128], op=ALU.add), reads=[Ya, mask2], writes=[E])
        P.op("dve", R.tensor_tensor(out=E.ap[:, 128:256], in0=Yb.ap[:], in1=mask2.ap[:, 128:256], op=ALU.add), reads=[Yb, mask2], writes=[E])
        yield
        P.op("act", R.activation(out=E.ap[:], in_=E.ap[:], func=AF.Exp, bias=negG.ap[:, t, h:h + 1], scale=1.0), reads=[E, negG], writes=[E])
        yield
        P.op("dve", R.scalar_tensor_tensor(out=NA.ap[:], in0=X.ap[:], scalar=cf.ap[:, 1:2], in1=E.ap[:], op0=ALU.mult, op1=ALU.mult),
             reads=[X, cf, E], writes=[NA])
        yield
        cur = ln["MN0"]
        P.op("pool", R.tensor_tensor(out=cur.ap[:, 128:256], in0=NA.ap[:, 0:128], in1=negstrict.ap[:], op=ALU.mult), reads=[NA, negstrict], writes=[cur])
        yield
        P.op("pe", R.transpose(out=Ya.ap[:], in_=cur.ap[:, 128:256], identity=ident.ap[:]), reads=[cur, ident], writes=[Ya])
        pc = ln["P0"]
        P.op("dve", R.tensor_tensor(out=pc.ap[:], in0=cur.ap[:, 128:256], in1=ident.ap[:], op=ALU.add), reads=[cur, ident], writes=[pc])
        yield
        P.op("act", R.copy(out=cur.ap[:, 0:128], in_=Ya.ap[:]), reads=[Ya], writes=[cur])
        yield
        for j in range(1, 7):
            nxt = ln["MN1"] if j % 2 == 1 else ln["MN0"]
            pn = ln["P1"] if j % 2 == 1 else ln["P0"]
            last = (j == 6)
            P.op("pe", R.matmul(X.ap[:, 0:128], lhsT=cur.ap[:, 128:256], rhs=cur.ap[:, 0:128], start=True, stop=True), reads=[cur], writes=[X])
            if not last:
                P.op("pe", R.matmul(X.ap[:, 128:256], lhsT=cur.ap[:, 0:128], rhs=cur.ap[:, 128:256], start=True, stop=True), reads=[cur], writes=[X])
            yield
            if not last:
                P.op("act", R.copy(out=nxt.ap[:], in_=X.ap[:]), reads=[X], writes=[nxt])
            else:
                P.op("act", R.copy(out=nxt.ap[:, 0:128], in_=X.ap[:, 0:128]), reads=[X], writes=[nxt])
            yield
            Yp = Ya if j % 2 == 1 else Yb
            P.op("pe", R.matmul(Yp.ap[:], lhsT=nxt.ap[:, 0:128], rhs=pc.ap[:], start=True, stop=True), reads=[nxt, pc], writes=[Yp])
            yield
            P.op("dve", R.tensor_tensor(out=pn.ap[:], in0=Yp.ap[:], in1=pc.ap[:], op=ALU.add), reads=[Yp, pc], writes=[pn])
            cur = nxt
            pc = pn
        rhs_t = sq
        P.op("act", R.activation(out=rhs_t.ap[:, 0:128], in_=kv_tok.ap[:, 128:256], func=AF.Copy, scale=beta.ap[:, t, h:h + 1]), reads=[kv_tok, beta], writes=[rhs_t])
        P.op("act", R.activation(out=rhs_t.ap[:, 128:256], in_=kv_tok.ap[:, 0:128], func=AF.Copy, scale=cf.ap[:, 5:6]), reads=[kv_tok, cf], writes=[rhs_t])
        P.op("pool", R.tensor_scalar(out=kdec.ap[:], in0=kv_tok.ap[:, 0:128], scalar1=cf.ap[:, 6:7], scalar2=None, op0=ALU.mult), reads=[kv_tok, cf], writes=[kdec])
        yield
        P.op("pe", R.matmul(X.ap[:, 0:128], lhsT=pc.ap[:], rhs=rhs_t.ap[:, 0:128], start=True, stop=True), reads=[pc, rhs_t], writes=[X])
        P.op("pe", R.matmul(X.ap[:, 128:256], lhsT=rhs_t.ap[:, 128:256], rhs=pc.ap[:], start=True, stop=True), reads=[pc, rhs_t], writes=[X])
        yield
        uw = dg
        P.op("act", R.copy(out=uw.ap[:], in_=X.ap[:]), reads=[X], writes=[uw])
        if l == 0 and tb == 0 and h == 0 and t == 0:
            dbg_dump("dn_NA", NA, NA.ap[:], [128, 256])
            dbg_dump("dn_P", pc, pc.ap[:], [128, 128])
            dbg_dump("dn_uw", uw, uw.ap[:], [128, 256])
        yield

    def scan_step(ln, hl, t):
        h = hh * 4 + hl
        E, NA, kdec, cf = ln["E"], ln["NA"], ln["kdec"], ln["cf"]
        X, Ya, Yb = ln["X"], ln["Ya"], ln["Yb"]
        uw = ln["dg"]
        vo = E
        oo = ln["MN1"]
        xq = xc.ap[:, 0 + hl, t * 128:(t + 1) * 128]
        Sh = Sst.ap[:, h, :]
        ST = S_T[h]
        P.op("pe", R.matmul(X.ap[:, 0:128], lhsT=uw.ap[:, 128:256], rhs=Sh, start=True, stop=True), reads=[uw, ST], writes=[X])
        P.op("pe", R.matmul(X.ap[:, 128:256], lhsT=xq, rhs=Sh, start=True, stop=True), reads=[xc, ST], writes=[X])
        yield
        P.op("dve", R.tensor_tensor(out=vo.ap[:, 0:128], in0=uw.ap[:, 0:128], in1=X.ap[:, 0:128], op=ALU.subtract), reads=[uw, X], writes=[vo])
        P.op("act", R.activation(out=vo.ap[:, 128:256], in_=X.ap[:, 128:256], func=AF.Copy, scale=cf.ap[:, 7:8]), reads=[X, cf], writes=[vo])
        yield
        P.op("pe", R.matmul(Ya.ap[:], lhsT=NA.ap[:, 128:256], rhs=vo.ap[:, 0:128], start=True, stop=True), reads=[NA, vo], writes=[Ya])
        P.op("pe", R.matmul(Yb.ap[:], lhsT=kdec.ap[:], rhs=vo.ap[:, 0:128], start=True, stop=True), reads=[kdec, vo], writes=[Yb])
        yield
        P.op("dve", R.scalar_tensor_tensor(out=Sh, in0=Sh, scalar=glast.ap[:, t, h:h + 1], in1=Yb.ap[:], op0=ALU.mult, op1=ALU.add),
             reads=[ST, glast, Yb], writes=[ST])
        P.op("dve", R.tensor_tensor(out=oo.ap[:, 0:128], in0=vo.ap[:, 128:256], in1=Ya.ap[:], op=ALU.add), reads=[vo, Ya], writes=[oo])
        yield
        P.op("act", R.activation(out=oo.ap[:, 128:256], in_=oo.ap[:, 0:128], func=AF.Square, accum_out=cf.ap[:, 8:9]), reads=[oo], writes=[oo, cf])
        yield
        P.op("dve", R.tensor_scalar(out=cf.ap[:, 8:9], in0=cf.ap[:, 8:9], scalar1=float(1.0 / 128), scalar2=float(EPS), op0=ALU.mult, op1=ALU.add), reads=[cf], writes=[cf])
        yield
        P.op("act", R.activation(out=cf.ap[:, 8:9], in_=cf.ap[:, 8:9], func=AF.Ln), reads=[cf], writes=[cf])
        P.op("act", R.activation(out=cf.ap[:, 8:9], in_=cf.ap[:, 8:9], func=AF.Exp, scale=-0.5), reads=[cf], writes=[cf])
        yield
        P.op("dve", R.scalar_tensor_tensor(out=oo.ap[:, 128:256], in0=oo.ap[:, 0:128], scalar=cf.ap[:, 8:9], in1=onorm_bc.ap[:, l, :], op0=ALU.mult, op1=ALU.mult),
             reads=[oo, cf, onorm_bc], writes=[oo])
        yield
        P.op("pool", R.tensor_tensor(out=oo.ap[:, 128:256], in0=oo.ap[:, 128:256], in1=zs.ap[:, t, hl * 128:(hl + 1) * 128], op=ALU.mult), reads=[oo, zs], writes=[oo])
        yield
        P.op("pe", R.transpose(out=Ya.ap[:], in_=oo.ap[:, 128:256], identity=ident.ap[:]), reads=[oo, ident], writes=[Ya])
        yield
        P.op("dve", R.tensor_copy(out=oT.ap[:, h, t * 128:(t + 1) * 128], in_=Ya.ap[:]), reads=[Ya], writes=[oT])
        if l == 0 and tb == 0 and h == 0 and t == 0:
            dbg_dump("dn_o", oo, oo.ap[:, 0:128], [128, 128])
        yield

    def run_interleaved(gens):
        gens = list(gens)
        rounds = 0
        while gens:
            rounds += 1
            if getattr(cfg, "dncut", None) is not None and rounds > cfg.dncut:
                return
            alive = []
            for g in gens:
                try:
                    next(g)
                    alive.append(g)
                except StopIteration:
                    pass
            gens = alive

    for t in range(4):
        run_interleaved([chunk_local(lanes[hl], hl, t) for hl in range(4)])
        if getattr(cfg, "dncut", None) is None:
            run_interleaved([scan_step(lanes[hl], hl, t) for hl in range(4)])
    A.reset(mk)


_CACHE = {}


def _invf():
    inv = (10000.0 ** (-np.arange(0, 64, 2, dtype=np.float32) / np.float32(64))).astype(np.float32)
    return np.ascontiguousarray(np.broadcast_to(inv[None, :], (128, 32))).astype(np.float32)


def make_in_map(inputs, b, NTOK, L):
    f = lambda a: np.ascontiguousarray(np.asarray(a))
    pos = np.asarray(inputs["positions"])[b, :NTOK].astype(np.int32)
    m = {
        "x": f(np.asarray(inputs["x"])[b, :NTOK]),
        "mem": f(np.asarray(inputs["mem"])[b]),
        "pos": f(pos.reshape(NTOK // 128, 128).T),
        "invf": _invf(),
        "mem_norm": f(np.asarray(inputs["mem_norm"]).reshape(1, D)),
        "norm_final": f(np.asarray(inputs["norm_final"]).reshape(1, D)),
    }
    for k in ["norm_mix", "w_in", "dn_a_log", "dn_dt_bias", "dn_out_norm", "mla_w_qb",
              "mla_w_kvb", "w_out", "norm_xattn", "xa_wq", "xa_wk", "xa_wv", "xa_wo", "norm_ffn",
              "ffn_w_up", "ffn_w_down"]:
        m[k] = f(np.asarray(inputs[k])[:L])
    g = lambda k: np.asarray(inputs[k])[:L]
    m["mla_q_norm"] = f(g("mla_q_norm").reshape(L, 4, 128).transpose(2, 0, 1).reshape(128, L * 4))
    m["mla_kv_norm"] = f(g("mla_kv_norm").reshape(L, 2, 128).transpose(2, 0, 1).reshape(128, L * 2))
    m["dn_conv"] = f(g("dn_conv").reshape(L, 4, 24, 128).transpose(3, 0, 2, 1).reshape(128, L * 24 * 4))
    m["ffn_conv"] = f(g("ffn_conv").reshape(L, 3, 88, 128).transpose(3, 0, 2, 1).reshape(128, L * 88 * 3))
    m["ffn_conv_bias"] = f(g("ffn_conv_bias").reshape(L, 88, 128).transpose(2, 0, 1).reshape(128, L * 88))
    return m


def kernel(**inputs):
    cfg = Cfg(NTOK=2048, L=4)
    if "nc" not in _CACHE:
        _CACHE["nc"] = build_program(cfg)[0]
    nc = _CACHE["nc"]
    maps = [make_in_map(inputs, c % 4, 2048, 4) for c in range(4)]
    in_maps = [maps[c % 4] for c in range(8)]
    res = run_bass_kernel_spmd(nc, in_maps, core_ids=list(range(8)))
    out = np.stack([np.asarray(res.results[b]["y"]) for b in range(4)], axis=0).astype(np.float32)
    return out
```

```python
import numpy as np
import concourse.bass as bass
import concourse.mybir as mybir
from concourse.bass_utils import run_bass_kernel_spmd

F32 = mybir.dt.float32
BF16 = mybir.dt.bfloat16
I32 = mybir.dt.int32
U8 = mybir.dt.uint8
AF = mybir.ActivationFunctionType
ALU = mybir.AluOpType
AX = mybir.AxisListType

ENGS = ["pe", "act", "dve", "pool", "sp"]

D = 2048
NKT = 16
DFF = 5632
NIN = 4944
EPS = 1e-6
NEG = -30000.0


class T:
    __slots__ = ("ap", "name", "w", "r", "parent")

    def __init__(self, ap, name="", parent=None):
        self.ap = ap
        self.name = name
        self.w = None
        self.r = []
        self.parent = parent


def _roots(ts):
    return [t.parent if t.parent is not None else t for t in ts]


class Prog:
    def __init__(self, nc):
        self.nc = nc
        self.streams = {e: [] for e in ENGS}
        self.cnt = {}
        self.seen = {e: {} for e in ENGS}
        self.sems = {}
        self.eng_sem = {e: ("E", e, 0) for e in ENGS}
        self.dma_rr = {e: 0 for e in ENGS}
        self.NDMA = 8
        self.n_ops = 0
        self.pool_dirty = False

    def _need(self, eng, dep):
        if dep is None:
            return
        key, val = dep
        if eng == "pe" and key[0] == "E" and key[1] == "pe":
            return
        if self.seen[eng].get(key, 0) >= val:
            return
        self.seen[eng][key] = val
        self.streams[eng].append(("wait", key, val))

    def _deps(self, eng, reads, writes):
        reads = _roots(reads)
        writes = _roots(writes)
        for t in reads:
            self._need(eng, t.w)
        for t in writes:
            self._need(eng, t.w)
            for d in t.r:
                self._need(eng, d)

    def _mark(self, stamp, reads, writes):
        reads = _roots(reads)
        writes = _roots(writes)
        for t in reads:
            t.r.append(stamp)
            if len(t.r) > 48:
                best = {}
                for k, v in t.r:
                    if best.get(k, 0) < v:
                        best[k] = v
                t.r = list(best.items())
        for t in writes:
            t.w = stamp
            t.r = []

    def op(self, eng, fn, reads=(), writes=()):
        if eng == "pool":
            self.pool_dirty = True
        self._deps(eng, reads, writes)
        key = self.eng_sem[eng]
        c = self.cnt.get(key, 0) + 1
        self.cnt[key] = c
        self.streams[eng].append(("op", fn, key))
        self._mark((key, c), reads, writes)
        self.n_ops += 1
        if c >= 16000:
            self.eng_sem[eng] = ("E", eng, key[2] + 1)

    def dma(self, q, out_t, out_ap, in_t, in_ap):
        reads = [in_t] if in_t is not None else []
        writes = [out_t] if out_t is not None else []
        self._deps(q, reads, writes)
        i = self.dma_rr[q]
        self.dma_rr[q] = (i + 1) % self.NDMA
        key = ("D", q, i)
        prev = self.cnt.get(key, 0)
        if prev:
            self._need(q, (key, prev))
        c = prev + 16
        self.cnt[key] = c

        def fn(e, out_ap=out_ap, in_ap=in_ap):
            return e.dma_start(out=out_ap, in_=in_ap)
        self.streams[q].append(("dma", fn, key))
        self._mark((key, c), reads, writes)
        self.n_ops += 1

    def coll(self, send_t, send_ap, recv_t, recv_ap, groups):
        q = "pool"
        self._deps(q, [send_t], [recv_t])
        key = ("C", len([k for k in self.cnt if k[0] == "C"]))
        self.cnt[key] = 1

        def fn(e):
            return e.collective_compute("AllGather", ALU.bypass, replica_groups=groups, ins=[send_ap.opt()], outs=[recv_ap.opt()])
        self.streams[q].append(("coll", fn, key))
        self._mark((key, 1), [send_t], [recv_t])
        self.n_ops += 1

    def barrier(self):
        for e in ENGS:
            if e == "pool" and not self.pool_dirty:
                continue
            self.wait_all(e)
        self.pool_dirty = False

    def wait_all(self, eng):
        for key, val in list(self.cnt.items()):
            if val:
                self._need(eng, (key, val))

    def emit(self):
        nc = self.nc
        for k in list(self.cnt.keys()):
            self.sems[k] = nc.alloc_semaphore("s_" + "_".join(str(x) for x in k))
        streams, sems = self.streams, self.sems

        def run(e, name):
            for item in streams[name]:
                if item[0] == "wait":
                    e.wait_ge(sems[item[1]], item[2])
                elif item[0] == "op":
                    item[1](e).then_inc(sems[item[2]], 1)
                elif item[0] == "coll":
                    item[1](e).then_inc(sems[item[2]])
                else:
                    item[1](e).then_inc(sems[item[2]], 16)

        with nc.Block() as block:
            @block.tensor
            def _(e):
                run(e, "pe")

            @block.scalar
            def _(e):
                run(e, "act")

            @block.vector
            def _(e):
                run(e, "dve")

            @block.gpsimd
            def _(e):
                run(e, "pool")

            @block.sync
            def _(e):
                run(e, "sp")


class _Rec:
    def __getattr__(self, name):
        def mk(*args, **kw):
            return lambda e: getattr(e, name)(*args, **kw)
        return mk


R = _Rec()


class Arena:
    def __init__(self, nc, size):
        self.nc = nc
        slab = nc.alloc_sbuf_tensor("slab", [128, size], U8)
        self.base = nc.lookup_mloc(slab).addr
        self.size = size
        self.top = 0
        self.peak = 0

    def alloc(self, name, shape, dtype):
        nb = int(np.prod(shape[1:])) * (4 if dtype in (F32, I32) else 2)
        nb = (nb + 31) // 32 * 32
        off = self.top
        assert off + nb <= self.size, f"arena overflow {name}: {off}+{nb} > {self.size}"
        self.top += nb
        self.peak = max(self.peak, self.top)
        h = self.nc.alloc_sbuf_tensor_at(name, list(shape), dtype, offset=self.base + off)
        return T(h, name)

    def mark(self):
        return self.top

    def reset(self, m):
        self.top = m


class Cfg:
    def __init__(self, NTOK=1024, L=5, NPRE=1024, n_cores=8, dbg=(), stop=None):
        self.NTOK = NTOK
        self.NPRE = NPRE
        self.NKEY = NTOK + NPRE
        self.groups = [[2 * i, 2 * i + 1] for i in range(n_cores // 2)]
        self.L = L
        self.TB = 512
        self.NTB = NTOK // 512
        self.NT = NTOK // 128
        self.dbg = dbg
        self.stop = stop


def build_program(cfg):
    nc = bass.Bass("TRN2", target_bir_lowering=False)
    P = Prog(nc)
    L, NTOK, NT, NTB = cfg.L, cfg.NTOK, cfg.NT, cfg.NTB
    NPRE, NKEY = cfg.NPRE, cfg.NKEY
    NPT = NPRE // 128
    SW = 1024 + 72 + 176 + 1024 + 512

    def din(name, shape, dt=F32):
        return nc.dram_tensor(name, list(shape), dt, kind="ExternalInput").ap()

    x_d = din("x", [NTOK, D])
    mem_d = din("mem", [256, D])
    pos_d = din("pos", [128, NT], I32)
    invf_d = din("invf", [128, 32])
    flag_d = din("flag", [128, 2])
    norm_mix_d = din("norm_mix", [L, D])
    w_in_d = din("w_in", [L, D, NIN])
    dn_conv_d = din("dn_conv", [128, L * 24 * 4])
    a_log_d = din("dn_a_log", [L, 8])
    dt_bias_d = din("dn_dt_bias", [L, 8])
    out_norm_d = din("dn_out_norm", [L, 128])
    q_norm_d = din("mla_q_norm", [128, L * 4])
    w_qb_d = din("mla_w_qb", [L, 512, 1536])
    kv_norm_d = din("mla_kv_norm", [128, L * 2])
    w_kvb_d = din("mla_w_kvb", [L, 256, 2048])
    w_out_d = din("w_out", [L, D, D])
    mem_norm_d = din("mem_norm", [1, D])
    norm_x_d = din("norm_xattn", [L, D])
    wq_d = din("xa_wq", [L, D, D])
    wk_d = din("xa_wk", [L, D, D])
    wv_d = din("xa_wv", [L, D, D])
    wo_d = din("xa_wo", [L, D, D])
    norm_f_d = din("norm_ffn", [L, D])
    w_up_d = din("ffn_w_up", [L, D, 2 * DFF])
    f_conv_d = din("ffn_conv", [128, L * 88 * 3])
    f_bias_d = din("ffn_conv_bias", [128, L * 88])
    w_down_d = din("ffn_w_down", [L, DFF, D])
    norm_fin_d = din("norm_final", [1, D])
    y_d = nc.dram_tensor("y", [NTOK, D], F32, kind="ExternalOutput").ap()
    h_d = nc.dram_tensor("hscr", [NTOK, D], F32, kind="Internal").ap()
    WT = T(None, "weights")
    hT = [T(None, f"h{t}") for t in range(NT)]
    yT = [T(None, f"y{t}") for t in range(NT)]
    dbg_out = {}

    def dbg_dump(name, tile, ap, shape, dt=F32):
        if name not in cfg.dbg:
            return
        d = nc.dram_tensor("dbg_" + name, list(shape), dt, kind="ExternalOutput").ap()
        dbg_out[name] = d
        P.dma("sp", T(None), d, tile, ap)

    A = Arena(nc, 198 * 1024)
    PSB = [T(nc.alloc_psum_tensor(f"psb{i}", [128, 512], F32), f"psb{i}") for i in range(8)]
    LX = [T(PSB[b].ap[:, 0:256], f"lx{b}", parent=PSB[b]) for b in range(4)]
    LYa = [T(PSB[4 + b].ap[:, 0:128], f"lya{b}", parent=PSB[4 + b]) for b in range(4)]
    LYb = [T(PSB[4 + b].ap[:, 128:256], f"lyb{b}", parent=PSB[4 + b]) for b in range(4)]
    memn_scr = nc.dram_tensor("memn_scr", [128, 16 * 256], BF16, kind="Internal").ap()
    memn_T = T(None, "memn_scr")
    send_d = nc.dram_tensor("st_send", [128, SW], F32, kind="Internal").ap()
    recv_d = nc.dram_tensor("st_recv", [256, SW], F32, kind="Internal").ap()
    send_T = T(None, "st_send")
    recv_T = T(None, "st_recv")

    ident = A.alloc("ident", [128, 128], F32)
    ones_f = A.alloc("ones_f", [128, 128], F32)
    ones_b = A.alloc("ones_b", [128, 128], BF16)
    tri_incl = A.alloc("tri_incl", [128, 128], F32)
    mask2 = A.alloc("mask2", [128, 256], F32)
    negstrict = A.alloc("negstrict", [128, 128], F32)
    sel_last = A.alloc("sel_last", [128, 1], F32)
    cos_t = A.alloc("cos_t", [128, NT, 32], F32)
    sin_t = A.alloc("sin_t", [128, NT, 32], F32)
    qn_g = A.alloc("qn_g", [128, L, 4], F32)
    kvn_g = A.alloc("kvn_g", [128, L, 2], F32)
    dnc_w = A.alloc("dnc_w", [128, L, 24, 4], F32)
    fc_w = A.alloc("fc_w", [128, L, 88, 3], F32)
    fc_b = A.alloc("fc_b", [128, L, 88], F32)
    alog_bc = A.alloc("alog_bc", [128, L, 8], F32)
    dtb_bc = A.alloc("dtb_bc", [128, L, 8], F32)
    onorm_bc = A.alloc("onorm_bc", [128, L, 128], F32)
    KmT = A.alloc("KmT", [128, 16, 256], BF16)
    Vm = A.alloc("Vm", [128, 2, D], BF16)
    ckvT = A.alloc("ckvT", [128, 2, NKEY], BF16)
    kpeT = A.alloc("kpeT", [64, NKEY], BF16)
    flag = A.alloc("flag", [128, 2], F32)
    Sst = A.alloc("Sst", [128, 8, 128], F32)
    S_T = [T(Sst.ap[:, h, :], f"S{h}") for h in range(8)]
    dn_halo = A.alloc("dn_halo", [128, 24, 3], F32)
    f_halo = A.alloc("f_halo", [128, 88, 2], F32)
    wbuf = [A.alloc(f"wbuf{i}", [128, 16, 512], BF16) for i in range(2)]
    stat = A.alloc("stat", [128, 16], F32)
    uT = A.alloc("uT", [128, 16, 512], BF16)
    wb_i = [0]

    def sp_load(tile, ap_out, src):
        P.dma("sp", tile, ap_out, WT, src)

    P.op("pool", R.memset(ident.ap[:], 1.0), writes=[ident])
    P.op("pool", R.affine_select(out=ident.ap[:], in_=ident.ap[:], pattern=[[-1, 128]], compare_op=ALU.is_equal,
                                          fill=0.0, base=0, channel_multiplier=1), reads=[ident], writes=[ident])
    P.op("pool", R.memset(ones_f.ap[:], 1.0), writes=[ones_f])
    P.op("pool", R.memset(ones_b.ap[:], 1.0), writes=[ones_b])
    P.op("pool", R.memset(tri_incl.ap[:], 1.0), writes=[tri_incl])
    P.op("pool", R.affine_select(out=tri_incl.ap[:], in_=tri_incl.ap[:], pattern=[[1, 128]], compare_op=ALU.is_ge,
                                          fill=0.0, base=0, channel_multiplier=-1), reads=[tri_incl], writes=[tri_incl])
    P.op("pool", R.memset(mask2.ap[:], 0.0), writes=[mask2])
    for hh in range(2):
        P.op("pool", R.affine_select(out=mask2.ap[:, hh * 128:(hh + 1) * 128], in_=mask2.ap[:, hh * 128:(hh + 1) * 128],
                                                      pattern=[[1, 128]], compare_op=ALU.is_ge, fill=NEG, base=0, channel_multiplier=-1),
             reads=[mask2], writes=[mask2])
    P.op("pool", R.memset(negstrict.ap[:], -1.0), writes=[negstrict])
    P.op("pool", R.affine_select(out=negstrict.ap[:], in_=negstrict.ap[:], pattern=[[1, 128]], compare_op=ALU.is_gt,
                                          fill=0.0, base=0, channel_multiplier=-1), reads=[negstrict], writes=[negstrict])
    P.op("pool", R.memset(sel_last.ap[:], 1.0), writes=[sel_last])
    P.op("pool", R.affine_select(out=sel_last.ap[:], in_=sel_last.ap[:], pattern=[[0, 1]], compare_op=ALU.is_equal,
                                          fill=0.0, base=-127, channel_multiplier=1), reads=[sel_last], writes=[sel_last])

    nc_allow = nc.allow_non_contiguous_dma(reason="tiny param loads")
    nc_allow.__enter__()
    sp_load(flag, flag.ap[:], flag_d)
    sp_load(qn_g, qn_g.ap[:].rearrange("p l k -> p (l k)"), q_norm_d)
    sp_load(kvn_g, kvn_g.ap[:].rearrange("p l k -> p (l k)"), kv_norm_d)
    sp_load(dnc_w, dnc_w.ap[:].rearrange("p l c k -> p (l c k)"), dn_conv_d)
    sp_load(fc_w, fc_w.ap[:].rearrange("p l c k -> p (l c k)"), f_conv_d)
    sp_load(fc_b, fc_b.ap[:].rearrange("p l c -> p (l c)"), f_bias_d)
    for l in range(L):
        sp_load(alog_bc, alog_bc.ap[:, l, :], a_log_d[l:l + 1, :].partition_broadcast(128))
        sp_load(dtb_bc, dtb_bc.ap[:, l, :], dt_bias_d[l:l + 1, :].partition_broadcast(128))
        sp_load(onorm_bc, onorm_bc.ap[:, l, :], out_norm_d[l:l + 1, :].partition_broadcast(128))
    P.op("act", R.activation(out=alog_bc.ap[:].rearrange("p l h -> p (l h)"), in_=alog_bc.ap[:].rearrange("p l h -> p (l h)"), func=AF.Exp),
         reads=[alog_bc], writes=[alog_bc])

    m0 = A.mark()
    pos_i = A.alloc("pos_i", [128, NT], I32)
    pos_f = A.alloc("pos_f", [128, NT], F32)
    invf = A.alloc("invf", [128, 32], F32)
    ang = A.alloc("ang", [128, NT, 32], F32)
    kq = A.alloc("kq", [128, NT, 32], F32)
    ki = A.alloc("ki", [128, NT, 32], I32)
    rr = A.alloc("rr", [128, NT, 32], F32)
    sp_load(pos_i, pos_i.ap[:], pos_d)
    sp_load(invf, invf.ap[:], invf_d)
    P.op("dve", R.tensor_copy(out=pos_f.ap[:], in_=pos_i.ap[:]), reads=[pos_i], writes=[pos_f])
    for t in range(NT):
        P.op("dve", R.tensor_scalar(out=ang.ap[:, t, :], in0=invf.ap[:], scalar1=pos_f.ap[:, t:t + 1], scalar2=None, op0=ALU.mult),
             reads=[invf, pos_f], writes=[ang])
    TWO_PI = 2.0 * np.pi
    C1 = 6.28125
    C2 = TWO_PI - C1
    fl = lambda ap: ap[:].rearrange("p t j -> p (t j)")
    for which, tab in ((0, sin_t), (1, cos_t)):
        shift = 0.0 if which == 0 else np.pi / 2
        P.op("dve", R.tensor_scalar(out=fl(kq.ap), in0=fl(ang.ap), scalar1=float(shift), scalar2=float(1.0 / TWO_PI), op0=ALU.add, op1=ALU.mult),
             reads=[ang], writes=[kq])
        P.op("dve", R.tensor_copy(out=fl(ki.ap), in_=fl(kq.ap)), reads=[kq], writes=[ki])
        P.op("dve", R.tensor_copy(out=fl(kq.ap), in_=fl(ki.ap)), reads=[ki], writes=[kq])
        P.op("dve", R.scalar_tensor_tensor(out=fl(rr.ap), in0=fl(kq.ap), scalar=float(-C1), in1=fl(ang.ap), op0=ALU.mult, op1=ALU.add),
             reads=[kq, ang], writes=[rr])
        P.op("dve", R.scalar_tensor_tensor(out=fl(rr.ap), in0=fl(kq.ap), scalar=float(-C2), in1=fl(rr.ap), op0=ALU.mult, op1=ALU.add),
             reads=[kq, rr], writes=[rr])
        if which == 1:
            P.op("dve", R.tensor_scalar(out=fl(rr.ap), in0=fl(rr.ap), scalar1=float(shift), scalar2=None, op0=ALU.add),
                 reads=[rr], writes=[rr])
        P.op("dve", R.tensor_scalar(out=fl(rr.ap), in0=fl(rr.ap), scalar1=float(3.1415925), scalar2=float(-3.1415925), op0=ALU.min, op1=ALU.max),
             reads=[rr], writes=[rr])
        P.op("act", R.activation(out=fl(tab.ap), in_=fl(rr.ap), func=AF.Sin), reads=[rr], writes=[tab])
    dbg_dump("cos", cos_t, cos_t.ap[:], [128, NT, 32])
    dbg_dump("sin", sin_t, sin_t.ap[:], [128, NT, 32])
    P.barrier()
    A.reset(m0)

    evac_rr = [0]

    def evac_copy(out_t, out_ap, ps_t, ps_ap, eng=None):
        if eng is None:
            eng = "act" if evac_rr[0] % 2 == 0 else "dve"
            evac_rr[0] += 1
        if eng == "act":
            P.op("act", R.copy(out=out_ap, in_=ps_ap), reads=[ps_t], writes=[out_t])
        else:
            P.op("dve", R.tensor_copy(out=out_ap, in_=ps_ap), reads=[ps_t], writes=[out_t])

    def wload(src_ap, nkt, ncols):
        wb = wbuf[wb_i[0] % 2]
        wb_i[0] += 1
        P.dma("pool", wb, wb.ap[:, 0:nkt, 0:ncols], WT, src_ap.rearrange("(kt p) c -> p kt c", p=128))
        return wb

    def rstd_from_ss(ss_ap_fn, scale, tiles):
        P.op("dve", R.tensor_scalar(out=ss_ap_fn(), in0=ss_ap_fn(), scalar1=float(scale), scalar2=float(EPS), op0=ALU.mult, op1=ALU.add),
             reads=tiles, writes=tiles)
        P.op("act", R.activation(out=ss_ap_fn(), in_=ss_ap_fn(), func=AF.Sqrt), reads=tiles, writes=tiles)
        P.op("dve", R.reciprocal(out=ss_ap_fn(), in_=ss_ap_fn()), reads=tiles, writes=tiles)

    def norm_block(src_d, src_T, row0, gain_row_ap, ntile, dst_uT, to_y=None):
        mk = A.mark()
        hx = [A.alloc(f"hx{t}", [128, D], F32) for t in range(ntile)]
        g_mix = A.alloc("g_mix", [128, D], F32)
        junk = A.alloc("junk", [128, D], BF16)
        sp_load(g_mix, g_mix.ap[:], gain_row_ap.partition_broadcast(128))
        for t in range(ntile):
            P.dma("sp", hx[t], hx[t].ap[:], src_T[row0 // 128 + t], src_d[row0 + t * 128: row0 + (t + 1) * 128, :])
            P.op("act", R.activation(out=junk.ap[:], in_=hx[t].ap[:], func=AF.Square, accum_out=stat.ap[:, t:t + 1]),
                 reads=[hx[t]], writes=[junk, stat])
        rstd_from_ss(lambda: stat.ap[:, 0:ntile], 1.0 / D, [stat])
        for t in range(ntile):
            P.op("dve", R.scalar_tensor_tensor(out=hx[t].ap[:], in0=hx[t].ap[:], scalar=stat.ap[:, t:t + 1], in1=g_mix.ap[:],
                                                              op0=ALU.mult, op1=ALU.mult), reads=[hx[t], stat, g_mix], writes=[hx[t]])
            if to_y is not None:
                gt = row0 // 128 + t
                P.dma("sp", to_y[1][gt], to_y[0][gt * 128:(gt + 1) * 128, :], hx[t], hx[t].ap[:])
                continue
            for g in range(4):
                ps = PSB[4 + (g % 4)]
                for j in range(4):
                    kt = g * 4 + j
                    P.op("pe", R.transpose(out=ps.ap[:, j * 128:(j + 1) * 128], in_=hx[t].ap[:, kt * 128:(kt + 1) * 128], identity=ident.ap[:]),
                         reads=[hx[t], ident], writes=[ps])
                evac_copy(dst_uT, dst_uT.ap[:, g * 4:(g + 1) * 4, t * 128:(t + 1) * 128], ps, ps.ap[:].rearrange("p (a b) -> p a b", a=4))
        P.barrier()
        A.reset(mk)

    def residual_add_dense(actT, nkt_total, w_d2, tb):
        mk = A.mark()
        hx = [A.alloc(f"hxr{t}", [128, D], F32) for t in range(4)]
        for t in range(4):
            P.dma("sp", hx[t], hx[t].ap[:], hT[tb * 4 + t], h_d[(tb * 4 + t) * 128:(tb * 4 + t + 1) * 128, :])
        chunks = []
        k0 = 0
        while k0 < nkt_total:
            chunks.append((k0, min(16, nkt_total - k0)))
            k0 += 16
        for cb in range(4):
            for ci, (k0, nk) in enumerate(chunks):
                wb = wload(w_d2[k0 * 128:(k0 + nk) * 128, cb * 512:(cb + 1) * 512], nk, 512)
                for t in range(4):
                    ps = PSB[t]
                    for kk in range(nk):
                        kt = k0 + kk
                        P.op("pe", R.matmul(ps.ap[:], lhsT=actT.ap[:, kt, t * 128:(t + 1) * 128], rhs=wb.ap[:, kk, :],
                                                                                     start=(kt == 0), stop=(kt == nkt_total - 1)),
                             reads=[actT, wb], writes=[ps])
            for t in range(4):
                ps = PSB[t]
                P.op("dve", R.tensor_tensor(out=hx[t].ap[:, cb * 512:(cb + 1) * 512], in0=hx[t].ap[:, cb * 512:(cb + 1) * 512], in1=ps.ap[:], op=ALU.add),
                     reads=[hx[t], ps], writes=[hx[t]])
        for t in range(4):
            P.dma("sp", hT[tb * 4 + t], h_d[(tb * 4 + t) * 128:(tb * 4 + t + 1) * 128, :], hx[t], hx[t].ap[:])
        P.barrier()
        A.reset(mk)

    for t in range(NT):
        P.dma("sp", hT[t], h_d[t * 128:(t + 1) * 128, :], WT, x_d[t * 128:(t + 1) * 128, :])

    mz = A.mark()
    ztile = A.alloc("ztile", [128, SW], F32)
    P.op("dve", R.memset(ztile.ap[:], 0.0), writes=[ztile])
    P.dma("sp", send_T, send_d, ztile, ztile.ap[:])
    P.barrier()
    A.reset(mz)

    norm_block(mem_d, [WT, WT], 0, mem_norm_d[0:1, :], 2, uT)
    P.dma("sp", memn_T, memn_scr.rearrange("p (k m) -> p k m", k=16), uT, uT.ap[:, :, 0:256])
    P.barrier()

    for l in range(L):
        mk0 = A.mark()
        memnT = A.alloc("memnT", [128, 16, 256], BF16)
        P.dma("sp", memnT, memnT.ap[:], memn_T, memn_scr.rearrange("p (k m) -> p k m", k=16))
        if l == 0:
            dbg_dump("memnT", memnT, memnT.ap[:], [128, 16, 256], BF16)
        for cb in range(4):
            wb = wload(wk_d[l][:, cb * 512:(cb + 1) * 512], 16, 512)
            for c in range(4):
                ps = PSB[c]
                for kt in range(16):
                    P.op("pe", R.matmul(ps.ap[:, 0:256], lhsT=wb.ap[:, kt, c * 128:(c + 1) * 128], rhs=memnT.ap[:, kt, :],
                                                                         start=(kt == 0), stop=(kt == 15)), reads=[wb, memnT], writes=[ps])
                evac_copy(KmT, KmT.ap[:, cb * 4 + c, :], ps, ps.ap[:, 0:256])
        for cb in range(4):
            wb = wload(wv_d[l][:, cb * 512:(cb + 1) * 512], 16, 512)
            for m in range(2):
                ps = PSB[4 + m]
                for kt in range(16):
                    P.op("pe", R.matmul(ps.ap[:], lhsT=memnT.ap[:, kt, m * 128:(m + 1) * 128], rhs=wb.ap[:, kt, :],
                                                                         start=(kt == 0), stop=(kt == 15)), reads=[wb, memnT], writes=[ps])
                evac_copy(Vm, Vm.ap[:, m, cb * 512:(cb + 1) * 512], ps, ps.ap[:])
        P.barrier()
        A.reset(mk0)
        P.coll(send_T, send_d, recv_T, recv_d, cfg.groups)
        P.dma("sp", Sst, Sst.ap[:].rearrange("p h d -> p (h d)"), recv_T, recv_d[0:128, 0:1024])
        P.dma("sp", dn_halo, dn_halo.ap[:].rearrange("p c k -> p (c k)"), recv_T, recv_d[0:128, 1024:1096])
        P.dma("sp", f_halo, f_halo.ap[:].rearrange("p c k -> p (c k)"), recv_T, recv_d[0:128, 1096:1272])
        P.dma("sp", ckvT, ckvT.ap[:, :, 0:NPRE].bitcast(F32), recv_T, recv_d[0:128, 1272:2296].rearrange("p (k t) -> p k t", k=2))
        P.dma("sp", kpeT, kpeT.ap[:, 0:NPRE].bitcast(F32), recv_T, recv_d[0:64, 2296:2808])
        P.op("dve", R.tensor_scalar(out=Sst.ap[:].rearrange("p h d -> p (h d)"), in0=Sst.ap[:].rearrange("p h d -> p (h d)"), scalar1=flag.ap[:, 0:1], scalar2=None, op0=ALU.mult),
             reads=[Sst, flag], writes=[Sst] + S_T)
        P.op("dve", R.tensor_scalar(out=dn_halo.ap[:].rearrange("p c k -> p (c k)"), in0=dn_halo.ap[:].rearrange("p c k -> p (c k)"), scalar1=flag.ap[:, 0:1], scalar2=None, op0=ALU.mult),
             reads=[dn_halo, flag], writes=[dn_halo])
        P.op("dve", R.tensor_scalar(out=f_halo.ap[:].rearrange("p c k -> p (c k)"), in0=f_halo.ap[:].rearrange("p c k -> p (c k)"), scalar1=flag.ap[:, 0:1], scalar2=None, op0=ALU.mult),
             reads=[f_halo, flag], writes=[f_halo])

        for tb in range(NTB):
            tok0 = tb * 512
            norm_block(h_d, hT, tok0, norm_mix_d[l:l + 1, :], 4, uT)
            if l == 0 and tb == 0:
                dbg_dump("uT0", uT, uT.ap[:], [128, 16, 512], BF16)
            mA = A.mark()
            oT = A.alloc("oT", [128, 16, 512], BF16)
            mB = A.mark()
            lat_q = A.alloc("lat_q", [128, 4, 512], F32)
            lat_kv = A.alloc("lat_kv", [128, 4, 320], F32)
            qlatT = A.alloc("qlatT", [128, 4, 512], BF16)
            qpe = A.alloc("qpe", [128, 512], F32)
            qpe_r = A.alloc("qpe_r", [128, 4, 512], F32)
            rtmp = A.alloc("rtmp", [128, 2, 256], F32)
            qpeT = A.alloc("qpeT", [64, 8, 512], BF16)
            KhT = A.alloc("KhT", [128, NKEY], BF16)
            Vh = A.alloc("Vh", [128, NKEY // 128, 128], BF16)
            QhT = A.alloc("QhT", [128, 512], BF16)
            PT = [A.alloc(f"PT{i}", [128, 512], BF16) for i in range(3)]
            rec = A.alloc("rec", [128, 512], F32)
            junkm = A.alloc("junkm", [128, 512], BF16)
            wb1 = wload(w_in_d[l][:, 4112:4624], 16, 512)
            wb2 = wload(w_in_d[l][:, 4624:4944], 16, 320)
            for t in range(4):
                ps = PSB[t % 2]
                for kt in range(16):
                    P.op("pe", R.matmul(ps.ap[:], lhsT=uT.ap[:, kt, t * 128:(t + 1) * 128], rhs=wb1.ap[:, kt, :],
                                                                    start=(kt == 0), stop=(kt == 15)), reads=[uT, wb1], writes=[ps])
                P.op("act", R.copy(out=lat_q.ap[:, t, :], in_=ps.ap[:]), reads=[ps], writes=[lat_q])
                P.op("act", R.activation(out=junkm.ap[:], in_=lat_q.ap[:, t, :], func=AF.Square, accum_out=stat.ap[:, t:t + 1]),
                     reads=[lat_q], writes=[junkm, stat])
                ps2 = PSB[2 + t % 2]
                for kt in range(16):
                    P.op("pe", R.matmul(ps2.ap[:, 0:320], lhsT=uT.ap[:, kt, t * 128:(t + 1) * 128], rhs=wb2.ap[:, kt, 0:320],
                                                                      start=(kt == 0), stop=(kt == 15)), reads=[uT, wb2], writes=[ps2])
                P.op("dve", R.tensor_copy(out=lat_kv.ap[:, t, :], in_=ps2.ap[:, 0:320]), reads=[ps2], writes=[lat_kv])
                P.op("act", R.activation(out=junkm.ap[:, 0:256], in_=lat_kv.ap[:, t, 0:256], func=AF.Square, accum_out=stat.ap[:, 4 + t:5 + t]),
                     reads=[lat_kv], writes=[junkm, stat])
            rstd_from_ss(lambda: stat.ap[:, 0:4], 1.0 / 512, [stat])
            rstd_from_ss(lambda: stat.ap[:, 4:8], 1.0 / 256, [stat])
            if l == 0 and tb == 0:
                dbg_dump("lat_q", lat_q, lat_q.ap[:], [128, 4, 512])
                dbg_dump("lat_kv", lat_kv, lat_kv.ap[:], [128, 4, 320])
            wqb = wbuf[wb_i[0] % 2]
            wb_i[0] += 1
            wkvb = wbuf[wb_i[0] % 2]
            wb_i[0] += 1
            wqb_v = wqb.ap[:].rearrange("p k c -> p (k c)")[:, 0:4 * 1536].rearrange("p (k c) -> p k c", k=4)
            wkvb_v = wkvb.ap[:].rearrange("p k c -> p (k c)")[:, 0:2 * 2048].rearrange("p (k c) -> p k c", k=2)
            P.dma("pool", wqb, wqb_v, WT, w_qb_d[l].rearrange("(kt p) c -> p kt c", p=128))
            P.dma("pool", wkvb, wkvb_v, WT, w_kvb_d[l].rearrange("(kt p) c -> p kt c", p=128))
            wq4 = wqb_v.rearrange("p k (h d) -> p k h d", h=8)
            wkv4 = wkvb_v.rearrange("p k (h d) -> p k h d", h=8)
            for t in range(4):
                gt = tb * 4 + t
                P.op("dve", R.tensor_scalar(out=lat_q.ap[:, t, :], in0=lat_q.ap[:, t, :], scalar1=stat.ap[:, t:t + 1], scalar2=None, op0=ALU.mult),
                     reads=[lat_q, stat], writes=[lat_q])
                P.op("dve", R.tensor_scalar(out=lat_kv.ap[:, t, 0:256], in0=lat_kv.ap[:, t, 0:256], scalar1=stat.ap[:, 4 + t:5 + t], scalar2=None, op0=ALU.mult),
                     reads=[lat_kv, stat], writes=[lat_kv])
                x1 = lat_kv.ap[:, t, 256:288]
                x2 = lat_kv.ap[:, t, 288:320]
                cs = cos_t.ap[:, gt, :]
                sn = sin_t.ap[:, gt, :]
                r = rtmp.ap[:, 0, :]
                P.op("dve", R.tensor_tensor(out=r[:, 0:32], in0=x1, in1=cs, op=ALU.mult), reads=[lat_kv, cos_t], writes=[rtmp])
                P.op("dve", R.tensor_tensor(out=r[:, 32:64], in0=x2, in1=sn, op=ALU.mult), reads=[lat_kv, sin_t], writes=[rtmp])
                P.op("dve", R.tensor_tensor(out=r[:, 64:96], in0=x2, in1=cs, op=ALU.mult), reads=[lat_kv, cos_t], writes=[rtmp])
                P.op("dve", R.tensor_tensor(out=r[:, 96:128], in0=x1, in1=sn, op=ALU.mult), reads=[lat_kv, sin_t], writes=[rtmp])
                P.op("dve", R.tensor_tensor(out=x1, in0=r[:, 0:32], in1=r[:, 32:64], op=ALU.subtract), reads=[rtmp], writes=[lat_kv])
                P.op("dve", R.tensor_tensor(out=x2, in0=r[:, 64:96], in1=r[:, 96:128], op=ALU.add), reads=[rtmp], writes=[lat_kv])
                ps = PSB[4 + t % 2]
                for j in range(2):
                    P.op("pe", R.transpose(out=ps.ap[:, j * 128:(j + 1) * 128], in_=lat_kv.ap[:, t, j * 128:(j + 1) * 128], identity=ident.ap[:]),
                         reads=[lat_kv, ident], writes=[ps])
                P.op("pe", R.transpose(out=ps.ap[0:64, 256:384], in_=lat_kv.ap[:, t, 256:320], identity=ident.ap[:]),
                     reads=[lat_kv, ident], writes=[ps])
                for j in range(2):
                    P.op("act", R.activation(out=ckvT.ap[:, j, (NPT + gt) * 128:(NPT + gt + 1) * 128], in_=ps.ap[:, j * 128:(j + 1) * 128], func=AF.Copy,
                                                                          scale=kvn_g.ap[:, l, j:j + 1]), reads=[ps, kvn_g], writes=[ckvT])
                P.op("dve", R.tensor_copy(out=kpeT.ap[:, (NPT + gt) * 128:(NPT + gt + 1) * 128], in_=ps.ap[0:64, 256:384]), reads=[ps], writes=[kpeT])
            for kt in range(4):
                ps = PSB[6 + kt % 2]
                for t in range(4):
                    P.op("pe", R.transpose(out=ps.ap[:, t * 128:(t + 1) * 128], in_=lat_q.ap[:, t, kt * 128:(kt + 1) * 128], identity=ident.ap[:]),
                         reads=[lat_q, ident], writes=[ps])
                P.op("act", R.activation(out=qlatT.ap[:, kt, :], in_=ps.ap[:], func=AF.Copy, scale=qn_g.ap[:, l, kt:kt + 1]),
                     reads=[ps, qn_g], writes=[qlatT])
            if l == 0 and tb == 0:
                dbg_dump("qlatT", qlatT, qlatT.ap[:], [128, 4, 512], BF16)
                dbg_dump("ckvT", ckvT, ckvT.ap[:, :, NPRE:NPRE + 512], [128, 2, 512], BF16)
                dbg_dump("kpeT", kpeT, kpeT.ap[:, NPRE:NPRE + 512], [64, 512], BF16)
            for t in range(4):
                gt = tb * 4 + t
                ps = PSB[t % 2]
                for kt in range(4):
                    P.op("pe", R.matmul(ps.ap[:].rearrange("p (h d) -> p h d", h=8), lhsT=qlatT.ap[:, kt, t * 128:(t + 1) * 128],
                                                                    rhs=wq4[:, kt, :, 128:192], start=(kt == 0), stop=(kt == 3)), reads=[qlatT, wqb], writes=[ps])
                P.op("act", R.copy(out=qpe.ap[:], in_=ps.ap[:]), reads=[ps], writes=[qpe])
                xv = qpe.ap[:].rearrange("p (h d) -> p h d", h=8)
                ov = qpe_r.ap[:, t, :].rearrange("p (h d) -> p h d", h=8)
                cb_ = cos_t.ap[:, gt, :].unsqueeze(1).broadcast_to([128, 8, 32])
                sb_ = sin_t.ap[:, gt, :].unsqueeze(1).broadcast_to([128, 8, 32])
                r0 = rtmp.ap[:, 0, :].rearrange("p (h d) -> p h d", h=8)
                r1 = rtmp.ap[:, 1, :].rearrange("p (h d) -> p h d", h=8)
                P.op("dve", R.tensor_tensor(out=r0, in0=xv[:, :, 0:32], in1=cb_, op=ALU.mult), reads=[qpe, cos_t], writes=[rtmp])
                P.op("dve", R.tensor_tensor(out=r1, in0=xv[:, :, 32:64], in1=sb_, op=ALU.mult), reads=[qpe, sin_t], writes=[rtmp])
                P.op("dve", R.tensor_tensor(out=ov[:, :, 0:32], in0=r0, in1=r1, op=ALU.subtract), reads=[rtmp], writes=[qpe_r])
                P.op("dve", R.tensor_tensor(out=r0, in0=xv[:, :, 32:64], in1=cb_, op=ALU.mult), reads=[qpe, cos_t], writes=[rtmp])
                P.op("dve", R.tensor_tensor(out=r1, in0=xv[:, :, 0:32], in1=sb_, op=ALU.mult), reads=[qpe, sin_t], writes=[rtmp])
                P.op("dve", R.tensor_tensor(out=ov[:, :, 32:64], in0=r0, in1=r1, op=ALU.add), reads=[rtmp], writes=[qpe_r])
            for h in range(8):
                ps = PSB[4 + h % 2]
                for t in range(4):
                    P.op("pe", R.transpose(out=ps.ap[0:64, t * 128:(t + 1) * 128], in_=qpe_r.ap[:, t, h * 64:(h + 1) * 64], identity=ident.ap[:]),
                         reads=[qpe_r, ident], writes=[ps])
                evac_copy(qpeT, qpeT.ap[:, h, :], ps, ps.ap[0:64, :])
            if l == 0 and tb == 0:
                dbg_dump("qpeT", qpeT, qpeT.ap[:], [64, 8, 512], BF16)
            nkt_keys = NPT + (tb + 1) * 4
            sc = float(192 ** -0.5)
            for h in range(8):
                for nb in range(nkt_keys // 4):
                    ps = PSB[nb % 2]
                    for kt in range(2):
                        P.op("pe", R.matmul(ps.ap[:], lhsT=wkv4[:, kt, h, 0:128], rhs=ckvT.ap[:, kt, nb * 512:(nb + 1) * 512],
                                                                               start=(kt == 0), stop=(kt == 1)), reads=[wkvb, ckvT], writes=[ps])
                    evac_copy(KhT, KhT.ap[:, nb * 512:(nb + 1) * 512], ps, ps.ap[:])
                for g in range(0, nkt_keys, 4):
                    ps = PSB[2 + (g // 4) % 2]
                    for j in range(4):
                        kt_ = g + j
                        for kt in range(2):
                            P.op("pe", R.matmul(ps.ap[:, j * 128:(j + 1) * 128], lhsT=ckvT.ap[:, kt, kt_ * 128:(kt_ + 1) * 128],
                                                                                         rhs=wkv4[:, kt, h, 128:256], start=(kt == 0), stop=(kt == 1)),
                                 reads=[wkvb, ckvT], writes=[ps])
                    evac_copy(Vh, Vh.ap[:, g:g + 4, :], ps, ps.ap[:].rearrange("p (a b) -> p a b", a=4))
                ps = PSB[4]
                for kt in range(4):
                    P.op("pe", R.matmul(ps.ap[:], lhsT=wq4[:, kt, h, 0:128], rhs=qlatT.ap[:, kt, :], start=(kt == 0), stop=(kt == 3)),
                         reads=[wqb, qlatT], writes=[ps])
                evac_copy(QhT, QhT.ap[:], ps, ps.ap[:])
                if l == 0 and tb == 0 and h == 0:
                    dbg_dump("KhT", KhT, KhT.ap[:, 0:512], [128, 512], BF16)
                    dbg_dump("Vh", Vh, Vh.ap[:, 0:4, :], [128, 4, 128], BF16)
                    dbg_dump("QhT", QhT, QhT.ap[:], [128, 512], BF16)
                psO = PSB[5]
                psD = PSB[6]

                def scores(kt_, h=h):
                    ps = PSB[(kt_ % 2) * 7]
                    pt = PT[kt_ % 3]
                    P.op("pe", R.matmul(ps.ap[:], lhsT=KhT.ap[:, kt_ * 128:(kt_ + 1) * 128], rhs=QhT.ap[:], start=True, stop=False),
                         reads=[KhT, QhT], writes=[ps])
                    P.op("pe", R.matmul(ps.ap[:], lhsT=kpeT.ap[:, kt_ * 128:(kt_ + 1) * 128], rhs=qpeT.ap[:, h, :], start=False, stop=True),
                         reads=[kpeT, qpeT], writes=[ps])
                    if kt_ < NPT:
                        P.op("act", R.activation(out=pt.ap[:], in_=ps.ap[:], func=AF.Exp, scale=sc, bias=flag.ap[:, 1:2]), reads=[ps, flag], writes=[pt])
                    else:
                        P.op("act", R.activation(out=pt.ap[:], in_=ps.ap[:], func=AF.Exp, scale=sc), reads=[ps], writes=[pt])
                    j = kt_ - NPT - tb * 4
                    if j >= 0:
                        if j > 0:
                            P.op("dve", R.memset(pt.ap[:, 0:j * 128], 0.0), reads=[pt], writes=[pt])
                        P.op("dve", R.memset(pt.ap[64:128, j * 128:j * 128 + 64], 0.0), reads=[pt], writes=[pt])

                def pv(kt_):
                    pt = PT[kt_ % 3]
                    P.op("pe", R.matmul(psO.ap[:], lhsT=Vh.ap[:, kt_, :], rhs=pt.ap[:], start=(kt_ == 0), stop=(kt_ == nkt_keys - 1)),
                         reads=[Vh, pt], writes=[psO])
                    P.op("pe", R.matmul(psD.ap[:], lhsT=ones_b.ap[:], rhs=pt.ap[:], start=(kt_ == 0), stop=(kt_ == nkt_keys - 1)),
                         reads=[ones_b, pt], writes=[psD])
                scores(0)
                for kt_ in range(nkt_keys):
                    if kt_ + 1 < nkt_keys:
                        scores(kt_ + 1)
                    pv(kt_)
                P.op("dve", R.reciprocal(out=rec.ap[:], in_=psD.ap[:]), reads=[psD], writes=[rec])
                P.op("dve", R.tensor_tensor(out=oT.ap[:, 8 + h, :], in0=psO.ap[:], in1=rec.ap[:], op=ALU.mult), reads=[psO, rec], writes=[oT])
            if l == 0 and tb == 0:
                dbg_dump("oT_mla", oT, oT.ap[:, 8:16, :], [128, 8, 512], BF16)
            P.barrier()
            A.reset(mB)
            if cfg.stop == "mla":
                A.reset(mA)
                continue
            dn_section(P, A, nc, cfg, l, tb, locals())
            P.barrier()
            A.reset(mB)
            if l == 0 and tb == 0:
                dbg_dump("oT_dn", oT, oT.ap[:, 0:8, :], [128, 8, 512], BF16)
            if cfg.stop in ("dn", "dnpre"):
                P.barrier()
                A.reset(mA)
                continue
            residual_add_dense(oT, 16, w_out_d[l], tb)
            A.reset(mA)
            if cfg.stop == "mixer":
                continue
            norm_block(h_d, hT, tok0, norm_x_d[l:l + 1, :], 4, uT)
            mA = A.mark()
            oxT = A.alloc("oxT", [128, 16, 512], BF16)
            mX = A.mark()
            qxT = A.alloc("qxT", [128, 16, 512], BF16)
            PTx = [A.alloc(f"PTx{i}", [128, 512], BF16) for i in range(2)]
            recx = A.alloc("recx", [128, 512], F32)
            for cb in range(4):
                wb = wload(wq_d[l][:, cb * 512:(cb + 1) * 512], 16, 512)
                for c in range(4):
                    ps = PSB[c]
                    for kt in range(16):
                        P.op("pe", R.matmul(ps.ap[:], lhsT=wb.ap[:, kt, c * 128:(c + 1) * 128], rhs=uT.ap[:, kt, :],
                                                                             start=(kt == 0), stop=(kt == 15)), reads=[wb, uT], writes=[ps])
                    evac_copy(qxT, qxT.ap[:, cb * 4 + c, :], ps, ps.ap[:])
            scx = float(512 ** -0.5)
            for hd in range(4):
                for m in range(2):
                    ps = PSB[4 + m]
                    for c in range(4):
                        P.op("pe", R.matmul(ps.ap[:], lhsT=KmT.ap[:, hd * 4 + c, m * 128:(m + 1) * 128], rhs=qxT.ap[:, hd * 4 + c, :],
                                                                             start=(c == 0), stop=(c == 3)), reads=[KmT, qxT], writes=[ps])
                    P.op("act", R.activation(out=PTx[m].ap[:], in_=ps.ap[:], func=AF.Exp, scale=scx), reads=[ps], writes=[PTx[m]])
                psD = PSB[6]
                for m in range(2):
                    P.op("pe", R.matmul(psD.ap[:], lhsT=ones_b.ap[:], rhs=PTx[m].ap[:], start=(m == 0), stop=(m == 1)), reads=[ones_b, PTx[m]], writes=[psD])
                P.op("dve", R.reciprocal(out=recx.ap[:], in_=psD.ap[:]), reads=[psD], writes=[recx])
                for c in range(4):
                    ps = PSB[c]
                    for m in range(2):
                        P.op("pe", R.matmul(ps.ap[:], lhsT=Vm.ap[:, m, (hd * 4 + c) * 128:(hd * 4 + c + 1) * 128], rhs=PTx[m].ap[:],
                                                                             start=(m == 0), stop=(m == 1)), reads=[Vm, PTx[m]], writes=[ps])
                    P.op("dve", R.tensor_tensor(out=oxT.ap[:, hd * 4 + c, :], in0=ps.ap[:], in1=recx.ap[:], op=ALU.mult), reads=[ps, recx], writes=[oxT])
            P.barrier()
            A.reset(mX)
            residual_add_dense(oxT, 16, wo_d[l], tb)
            A.reset(mA)
            if cfg.stop == "xattn":
                continue
            norm_block(h_d, hT, tok0, norm_f_d[l:l + 1, :], 4, uT)
            mA = A.mark()
            aT = A.alloc("aT", [128, 44, 512], BF16)
            mF = A.mark()
            sg = A.alloc("sg", [128, 4, 512], F32)
            raw = [A.alloc(f"raw{i}", [128, 514], F32) for i in range(2)]
            cacc = [A.alloc(f"cacc{i}", [128, 512], F32) for i in range(2)]
            ri = 0
            for cb in range(11):
                for part in range(2):
                    col0 = part * DFF + cb * 512
                    wb = wload(w_up_d[l][:, col0:col0 + 512], 16, 512)
                    for c in range(4):
                        ct = (col0 // 128) + c
                        ps = PSB[c + 4 * part]
                        for kt in range(16):
                            P.op("pe", R.matmul(ps.ap[:], lhsT=wb.ap[:, kt, c * 128:(c + 1) * 128], rhs=uT.ap[:, kt, :],
                                                                                 start=(kt == 0), stop=(kt == 15)), reads=[wb, uT], writes=[ps])
                        rw = raw[ri % 2]
                        ca = cacc[ri % 2]
                        ri += 1
                        P.op("act", R.copy(out=rw.ap[:, 2:514], in_=ps.ap[:]), reads=[ps], writes=[rw])
                        P.op("dve", R.tensor_copy(out=rw.ap[:, 0:2], in_=f_halo.ap[:, ct, :]), reads=[f_halo], writes=[rw])
                        P.op("dve", R.tensor_copy(out=f_halo.ap[:, ct, :], in_=rw.ap[:, 512:514]), reads=[rw], writes=[f_halo])
                        P.op("dve", R.tensor_scalar(out=ca.ap[:], in0=rw.ap[:, 0:512], scalar1=fc_w.ap[:, l, ct, 0:1], scalar2=fc_b.ap[:, l, ct:ct + 1],
                                                                                  op0=ALU.mult, op1=ALU.add), reads=[rw, fc_w, fc_b], writes=[ca])
                        P.op("dve", R.scalar_tensor_tensor(out=ca.ap[:], in0=rw.ap[:, 1:513], scalar=fc_w.ap[:, l, ct, 1:2], in1=ca.ap[:],
                                                                                         op0=ALU.mult, op1=ALU.add), reads=[rw, fc_w, ca], writes=[ca])
                        P.op("dve", R.scalar_tensor_tensor(out=ca.ap[:], in0=rw.ap[:, 2:514], scalar=fc_w.ap[:, l, ct, 2:3], in1=ca.ap[:],
                                                                                         op0=ALU.mult, op1=ALU.add), reads=[rw, fc_w, ca], writes=[ca])
                        if part == 0:
                            P.op("act", R.activation(out=sg.ap[:, c, :], in_=ca.ap[:], func=AF.Silu), reads=[ca], writes=[sg])
                        else:
                            P.op("dve", R.tensor_tensor(out=aT.ap[:, cb * 4 + c, :], in0=ca.ap[:], in1=sg.ap[:, c, :], op=ALU.mult),
                                 reads=[ca, sg], writes=[aT])
            if l == 0 and tb == 0:
                dbg_dump("aT", aT, aT.ap[:], [128, 44, 512], BF16)
            P.barrier()
            A.reset(mF)
            residual_add_dense(aT, 44, w_down_d[l], tb)
            A.reset(mA)

        P.dma("sp", send_T, send_d[:, 0:1024], Sst, Sst.ap[:].rearrange("p h d -> p (h d)"))
        P.dma("sp", send_T, send_d[:, 1024:1096], dn_halo, dn_halo.ap[:].rearrange("p c k -> p (c k)"))
        P.dma("sp", send_T, send_d[:, 1096:1272], f_halo, f_halo.ap[:].rearrange("p c k -> p (c k)"))
        P.dma("sp", send_T, send_d[:, 1272:2296].rearrange("p (k t) -> p k t", k=2), ckvT, ckvT.ap[:, :, NPRE:NKEY].bitcast(F32))
        P.dma("sp", send_T, send_d[0:64, 2296:2808], kpeT, kpeT.ap[:, NPRE:NKEY].bitcast(F32))

    if cfg.stop is None:
        for tb in range(NTB):
            norm_block(h_d, hT, tb * 512, norm_fin_d[0:1, :], 4, None, to_y=(y_d, yT))
    else:
        for t in range(NT):
            P.dma("sp", yT[t], y_d[t * 128:(t + 1) * 128, :], hT[t], h_d[t * 128:(t + 1) * 128, :])
    P.wait_all("sp")
    P.emit()
    nc_allow.__exit__(None, None, None)
    return nc, P, A, dbg_out


def dn_section(P, A, nc, cfg, l, tb, env):
    uT = env["uT"]; PSB = env["PSB"]; wload = env["wload"]; w_in_d = env["w_in_d"]
    dnc_w = env["dnc_w"]; dn_halo = env["dn_halo"]; ones_f = env["ones_f"]
    sel_last = env["sel_last"]; tri_incl = env["tri_incl"]
    alog_bc = env["alog_bc"]; dtb_bc = env["dtb_bc"]
    dbg_dump = env["dbg_dump"]

    ba = A.alloc("ba", [128, 4, 16], F32)
    beta = A.alloc("beta", [128, 4, 8], F32)
    gg = A.alloc("gg", [128, 4, 8], F32)
    Gc = A.alloc("Gc", [128, 4, 8], F32)
    negG = A.alloc("negG", [128, 4, 8], F32)
    expG = A.alloc("expG", [128, 4, 8], F32)
    edec = A.alloc("edec", [128, 4, 8], F32)
    glast = A.alloc("glast", [128, 4, 8], F32)
    gsel = A.alloc("gsel", [128, 8], F32)
    wb = wload(w_in_d[l][:, 4096:4112], 16, 16)
    for t in range(4):
        ps = PSB[4 + t % 2]
        for kt in range(16):
            P.op("pe", R.matmul(ps.ap[:, 0:16], lhsT=uT.ap[:, kt, t * 128:(t + 1) * 128], rhs=wb.ap[:, kt, 0:16],
                                                            start=(kt == 0), stop=(kt == 15)), reads=[uT, wb], writes=[ps])
        P.op("dve", R.tensor_copy(out=ba.ap[:, t, :], in_=ps.ap[:, 0:16]), reads=[ps], writes=[ba])
    P.op("act", R.activation(out=beta.ap[:], in_=ba.ap[:, :, 0:8], func=AF.Sigmoid), reads=[ba], writes=[beta])
    for t in range(4):
        P.op("dve", R.tensor_tensor(out=gg.ap[:, t, :], in0=ba.ap[:, t, 8:16], in1=dtb_bc.ap[:, l, :], op=ALU.add), reads=[ba, dtb_bc], writes=[gg])
    P.op("act", R.activation(out=gg.ap[:], in_=gg.ap[:], func=AF.Exp), reads=[gg], writes=[gg])
    P.op("act", R.activation(out=gg.ap[:], in_=gg.ap[:], func=AF.Ln, bias=1.0), reads=[gg], writes=[gg])
    for t in range(4):
        P.op("dve", R.scalar_tensor_tensor(out=gg.ap[:, t, :], in0=gg.ap[:, t, :], scalar=-1.0, in1=alog_bc.ap[:, l, :], op0=ALU.mult, op1=ALU.mult),
             reads=[gg, alog_bc], writes=[gg])
    for t in range(4):
        ps = PSB[4 + t % 2]
        P.op("pe", R.matmul(ps.ap[:, 0:8], lhsT=tri_incl.ap[:], rhs=gg.ap[:, t, :], start=True, stop=True), reads=[tri_incl, gg], writes=[ps])
        P.op("dve", R.tensor_copy(out=Gc.ap[:, t, :], in_=ps.ap[:, 0:8]), reads=[ps], writes=[Gc])
        P.op("dve", R.tensor_scalar(out=gsel.ap[:], in0=Gc.ap[:, t, :], scalar1=sel_last.ap[:, 0:1], scalar2=None, op0=ALU.mult), reads=[Gc, sel_last], writes=[gsel])
        ps2 = PSB[6 + t % 2]
        P.op("pe", R.matmul(ps2.ap[:, 0:8], lhsT=ones_f.ap[:], rhs=gsel.ap[:], start=True, stop=True), reads=[ones_f, gsel], writes=[ps2])
        P.op("act", R.activation(out=glast.ap[:, t, :], in_=ps2.ap[:, 0:8], func=AF.Exp), reads=[ps2], writes=[glast])
        P.op("dve", R.tensor_tensor(out=edec.ap[:, t, :], in0=ps2.ap[:, 0:8], in1=Gc.ap[:, t, :], op=ALU.subtract), reads=[ps2, Gc], writes=[edec])
    P.op("act", R.activation(out=edec.ap[:], in_=edec.ap[:], func=AF.Exp), reads=[edec], writes=[edec])
    P.op("act", R.activation(out=expG.ap[:], in_=Gc.ap[:], func=AF.Exp), reads=[Gc], writes=[expG])
    P.op("dve", R.tensor_scalar(out=negG.ap[:], in0=Gc.ap[:], scalar1=-1.0, scalar2=None, op0=ALU.mult), reads=[Gc], writes=[negG])
    if l == 0 and tb == 0:
        dbg_dump("beta", beta, beta.ap[:], [128, 4, 8])
        dbg_dump("Gc", Gc, Gc.ap[:], [128, 4, 8])

    mH = A.mark()
    for hh in range(2):
        zs = A.alloc("zs", [128, 4, 512], F32)
        xc = A.alloc("xc", [128, 12, 512], F32)
        mR = A.mark()
        raw = [A.alloc(f"dnraw{i}", [128, 515], F32) for i in range(2)]
        ri = 0
        for part in range(3):
            col0 = part * 1024 + hh * 512
            wbk = wload(w_in_d[l][:, col0:col0 + 512], 16, 512)
            for c in range(4):
                ct = col0 // 128 + c
                ps = PSB[4 + c]
                for kt in range(16):
                    P.op("pe", R.matmul(ps.ap[:], lhsT=wbk.ap[:, kt, c * 128:(c + 1) * 128], rhs=uT.ap[:, kt, :],
                                                                           start=(kt == 0), stop=(kt == 15)), reads=[wbk, uT], writes=[ps])
                rw = raw[ri % 2]
                ri += 1
                xo = xc.ap[:, part * 4 + c, :]
                P.op("act", R.copy(out=rw.ap[:, 3:515], in_=ps.ap[:]), reads=[ps], writes=[rw])
                P.op("dve", R.tensor_copy(out=rw.ap[:, 0:3], in_=dn_halo.ap[:, ct, :]), reads=[dn_halo], writes=[rw])
                P.op("dve", R.tensor_copy(out=dn_halo.ap[:, ct, :], in_=rw.ap[:, 512:515]), reads=[rw], writes=[dn_halo])
                P.op("dve", R.tensor_scalar(out=xo, in0=rw.ap[:, 0:512], scalar1=dnc_w.ap[:, l, ct, 0:1], scalar2=None, op0=ALU.mult),
                     reads=[rw, dnc_w], writes=[xc])
                for k in range(1, 4):
                    P.op("dve", R.scalar_tensor_tensor(out=xo, in0=rw.ap[:, k:k + 512], scalar=dnc_w.ap[:, l, ct, k:k + 1], in1=xo,
                                                                                          op0=ALU.mult, op1=ALU.add), reads=[rw, dnc_w, xc], writes=[xc])
                P.op("act", R.activation(out=xo, in_=xo, func=AF.Silu), reads=[xc], writes=[xc])
        wbz = wload(w_in_d[l][:, 3072 + hh * 512:3072 + (hh + 1) * 512], 16, 512)
        for t in range(4):
            ps = PSB[4 + t % 4]
            for kt in range(16):
                P.op("pe", R.matmul(ps.ap[:], lhsT=uT.ap[:, kt, t * 128:(t + 1) * 128], rhs=wbz.ap[:, kt, :], start=(kt == 0), stop=(kt == 15)),
                     reads=[uT, wbz], writes=[ps])
            P.op("act", R.activation(out=zs.ap[:, t, :], in_=ps.ap[:], func=AF.Silu), reads=[ps], writes=[zs])
        if l == 0 and tb == 0 and hh == 0:
            dbg_dump("xc", xc, xc.ap[:], [128, 12, 512])
            dbg_dump("zs", zs, zs.ap[:], [128, 4, 512])
        P.barrier()
        A.reset(mR)
        if cfg.stop == "dnpre":
            A.reset(mH)
            continue
        dn_heads(P, A, nc, cfg, l, tb, hh, env, dict(xc=xc, zs=zs, beta=beta, Gc=Gc, negG=negG, expG=expG, edec=edec, glast=glast))
        P.barrier()
        A.reset(mH)


def dn_heads(P, A, nc, cfg, l, tb, hh, env, d):
    oT = env["oT"]; ident = env["ident"]; ones_f = env["ones_f"]; mask2 = env["mask2"]; negstrict = env["negstrict"]
    onorm_bc = env["onorm_bc"]; Sst = env["Sst"]; S_T = env["S_T"]; dbg_dump = env["dbg_dump"]
    LX = env["LX"]; LYa = env["LYa"]; LYb = env["LYb"]
    xc = d["xc"]; zs = d["zs"]; beta = d["beta"]; Gc = d["Gc"]; negG = d["negG"]; expG = d["expG"]; edec = d["edec"]; glast = d["glast"]
    mk = A.mark()
    NL = 4
    lanes = []
    for i in range(NL):
        lanes.append(dict(
            kv_tok=A.alloc(f"kv_tok{i}", [128, 256], F32),
            sq=A.alloc(f"sq{i}", [128, 256], F32),
            dg=A.alloc(f"dg{i}", [128, 256], F32),
            E=A.alloc(f"E{i}", [128, 256], F32),
            NA=A.alloc(f"NA{i}", [128, 256], F32),
            MN0=A.alloc(f"MN0_{i}", [128, 256], F32),
            MN1=A.alloc(f"MN1_{i}", [128, 256], F32),
            P0=A.alloc(f"P0_{i}", [128, 128], F32),
            P1=A.alloc(f"P1_{i}", [128, 128], F32),
            kdec=A.alloc(f"kdec{i}", [128, 128], F32),
            cf=A.alloc(f"cf{i}", [128, 16], F32),
            X=LX[i], Ya=LYa[i], Yb=LYb[i]))

    def chunk_local(ln, hl, t):
        h = hh * 4 + hl
        kv_tok, sq, dg, E, NA, kdec, cf = ln["kv_tok"], ln["sq"], ln["dg"], ln["E"], ln["NA"], ln["kdec"], ln["cf"]
        X, Ya, Yb = ln["X"], ln["Ya"], ln["Yb"]
        xq = xc.ap[:, 0 + hl, t * 128:(t + 1) * 128]
        xk = xc.ap[:, 4 + hl, t * 128:(t + 1) * 128]
        xv = xc.ap[:, 8 + hl, t * 128:(t + 1) * 128]
        P.op("pe", R.transpose(out=X.ap[:, 0:128], in_=xk, identity=ident.ap[:]), reads=[xc, ident], writes=[X])
        P.op("pe", R.transpose(out=X.ap[:, 128:256], in_=xv, identity=ident.ap[:]), reads=[xc, ident], writes=[X])
        P.op("pool", R.tensor_tensor(out=sq.ap[:, 0:128], in0=xq, in1=xq, op=ALU.mult), reads=[xc], writes=[sq])
        P.op("pool", R.tensor_tensor(out=sq.ap[:, 128:256], in0=xk, in1=xk, op=ALU.mult), reads=[xc], writes=[sq])
        yield
        P.op("act", R.copy(out=kv_tok.ap[:], in_=X.ap[:]), reads=[X], writes=[kv_tok])
        P.op("pe", R.matmul(Ya.ap[:, 0:8], lhsT=sq.ap[:, 0:128], rhs=ones_f.ap[:, 0:8], start=True, stop=True), reads=[sq, ones_f], writes=[Ya])
        P.op("pe", R.matmul(Ya.ap[:, 8:16], lhsT=sq.ap[:, 128:256], rhs=ones_f.ap[:, 0:8], start=True, stop=True), reads=[sq, ones_f], writes=[Ya])
        P.op("pe", R.matmul(X.ap[:, 0:128], lhsT=xk, rhs=xk, start=True, stop=True), reads=[xc], writes=[X])
        P.op("pe", R.matmul(X.ap[:, 128:256], lhsT=xk, rhs=xq, start=True, stop=True), reads=[xc], writes=[X])
        yield
        P.op("act", R.activation(out=cf.ap[:, 0:2], in_=Ya.ap[:, 0:16:8], func=AF.Ln, bias=float(EPS)), reads=[Ya], writes=[cf])
        P.op("act", R.activation(out=cf.ap[:, 0:2], in_=cf.ap[:, 0:2], func=AF.Exp, scale=-0.5), reads=[cf], writes=[cf])
        yield
        P.op("dve", R.tensor_tensor(out=cf.ap[:, 2:3], in0=cf.ap[:, 1:2], in1=beta.ap[:, t, h:h + 1], op=ALU.mult), reads=[cf, beta], writes=[cf])
        P.op("dve", R.tensor_scalar(out=cf.ap[:, 9:10], in0=cf.ap[:, 0:1], scalar1=float(128 ** -0.5), scalar2=None, op0=ALU.mult), reads=[cf], writes=[cf])
        P.op("dve", R.tensor_tensor(out=cf.ap[:, 6:7], in0=cf.ap[:, 1:2], in1=edec.ap[:, t, h:h + 1], op=ALU.mult), reads=[cf, edec], writes=[cf])
        yield
        P.op("act", R.activation(out=cf.ap[:, 3:4], in_=cf.ap[:, 2:3], func=AF.Ln), reads=[cf], writes=[cf])
        P.op("act", R.activation(out=cf.ap[:, 4:5], in_=cf.ap[:, 9:10], func=AF.Ln), reads=[cf], writes=[cf])
        P.op("dve", R.tensor_tensor(out=cf.ap[:, 5:6], in0=cf.ap[:, 2:3], in1=expG.ap[:, t, h:h + 1], op=ALU.mult), reads=[cf, expG], writes=[cf])
        P.op("dve", R.tensor_tensor(out=cf.ap[:, 7:8], in0=cf.ap[:, 9:10], in1=expG.ap[:, t, h:h + 1], op=ALU.mult), reads=[cf, expG], writes=[cf])
        yield
        P.op("dve", R.tensor_scalar(out=cf.ap[:, 3:5], in0=cf.ap[:, 3:5], scalar1=Gc.ap[:, t, h:h + 1], scalar2=None, op0=ALU.add), reads=[cf, Gc], writes=[cf])
        yield
        P.op("dve", R.tensor_scalar(out=dg.ap[:, 0:128], in0=ident.ap[:], scalar1=cf.ap[:, 3:4], scalar2=None, op0=ALU.mult), reads=[cf, ident], writes=[dg])
        P.op("dve", R.tensor_scalar(out=dg.ap[:, 128:256], in0=ident.ap[:], scalar1=cf.ap[:, 4:5], scalar2=None, op0=ALU.mult), reads=[cf, ident], writes=[dg])
        yield
        P.op("pe", R.matmul(Ya.ap[:], lhsT=ones_f.ap[:], rhs=dg.ap[:, 0:128], start=True, stop=True), reads=[ones_f, dg], writes=[Ya])
        P.op("pe", R.matmul(Yb.ap[:], lhsT=ones_f.ap[:], rhs=dg.ap[:, 128:256], start=True, stop=True), reads=[ones_f, dg], writes=[Yb])
        yield
        P.op("dve", R.tensor_tensor(out=E.ap[:, 0:128], in0=Ya.ap[:], in1=mask2.ap[:, 0:128], op=ALU.add), reads=[Ya, mask2], writes=[E])
        P.op("dve", R.tensor_tensor(out=E.ap[:, 128:256], in0=Yb.ap[:], in1=mask2.ap[:, 128:256], op=ALU.add), reads=[Yb, mask2], writes=[E])
        yield
        P.op("act", R.activation(out=E.ap[:], in_=E.ap[:], func=AF.Exp, bias=negG.ap[:, t, h:h + 1], scale=1.0), reads=[E, negG], writes=[E])
        yield
        P.op("dve", R.scalar_tensor_tensor(out=NA.ap[:], in0=X.ap[:], scalar=cf.ap[:, 1:2], in1=E.ap[:], op0=ALU.mult, op1=ALU.mult),
             reads=[X, cf, E], writes=[NA])
        yield
        cur = ln["MN0"]
        P.op("pool", R.tensor_tensor(out=cur.ap[:, 128:256], in0=NA.ap[:, 0:128], in1=negstrict.ap[:], op=ALU.mult), reads=[NA, negstrict], writes=[cur])
        yield
        P.op("pe", R.transpose(out=Ya.ap[:], in_=cur.ap[:, 128:256], identity=ident.ap[:]), reads=[cur, ident], writes=[Ya])
        pc = ln["P0"]
        P.op("dve", R.tensor_tensor(out=pc.ap[:], in0=cur.ap[:, 128:256], in1=ident.ap[:], op=ALU.add), reads=[cur, ident], writes=[pc])
        yield
        P.op("act", R.copy(out=cur.ap[:, 0:128], in_=Ya.ap[:]), reads=[Ya], writes=[cur])
        yield
        for j in range(1, 7):
            nxt = ln["MN1"] if j % 2 == 1 else ln["MN0"]
            pn = ln["P1"] if j % 2 == 1 else ln["P0"]
            last = (j == 6)
            P.op("pe", R.matmul(X.ap[:, 0:128], lhsT=cur.ap[:, 128:256], rhs=cur.ap[:, 0:128], start=True, stop=True), reads=[cur], writes=[X])
            if not last:
                P.op("pe", R.matmul(X.ap[:, 128:256], lhsT=cur.ap[:, 0:128], rhs=cur.ap[:, 128:256], start=True, stop=True), reads=[cur], writes=[X])
            yield
            if not last:
                P.op("act", R.copy(out=nxt.ap[:], in_=X.ap[:]), reads=[X], writes=[nxt])
            else:
                P.op("act", R.copy(out=nxt.ap[:, 0:128], in_=X.ap[:, 0:128]), reads=[X], writes=[nxt])
            yield
            Yp = Ya if j % 2 == 1 else Yb
            P.op("pe", R.matmul(Yp.ap[:], lhsT=nxt.ap[:, 0:128], rhs=pc.ap[:], start=True, stop=True), reads=[nxt, pc], writes=[Yp])
            yield
            P.op("dve", R.tensor_tensor(out=pn.ap[:], in0=Yp.ap[:], in1=pc.ap[:], op=ALU.add), reads=[Yp, pc], writes=[pn])
            cur = nxt
            pc = pn
        rhs_t = sq
        P.op("act", R.activation(out=rhs_t.ap[:, 0:128], in_=kv_tok.ap[:, 128:256], func=AF.Copy, scale=beta.ap[:, t, h:h + 1]), reads=[kv_tok, beta], writes=[rhs_t])
        P.op("act", R.activation(out=rhs_t.ap[:, 128:256], in_=kv_tok.ap[:, 0:128], func=AF.Copy, scale=cf.ap[:, 5:6]), reads=[kv_tok, cf], writes=[rhs_t])
        P.op("pool", R.tensor_scalar(out=kdec.ap[:], in0=kv_tok.ap[:, 0:128], scalar1=cf.ap[:, 6:7], scalar2=None, op0=ALU.mult), reads=[kv_tok, cf], writes=[kdec])
        yield
        P.op("pe", R.matmul(X.ap[:, 0:128], lhsT=pc.ap[:], rhs=rhs_t.ap[:, 0:128], start=True, stop=True), reads=[pc, rhs_t], writes=[X])
        P.op("pe", R.matmul(X.ap[:, 128:256], lhsT=rhs_t.ap[:, 128:256], rhs=pc.ap[:], start=True, stop=True), reads=[pc, rhs_t], writes=[X])
        yield
        uw = dg
        P.op("act", R.copy(out=uw.ap[:], in_=X.ap[:]), reads=[X], writes=[uw])
        if l == 0 and tb == 0 and h == 0 and t == 0:
            dbg_dump("dn_NA", NA, NA.ap[:], [128, 256])
            dbg_dump("dn_P", pc, pc.ap[:], [128, 128])
            dbg_dump("dn_uw", uw, uw.ap[:], [128, 256])
        yield

    def scan_step(ln, hl, t):
        h = hh * 4 + hl
        E, NA, kdec, cf = ln["E"], ln["NA"], ln["kdec"], ln["cf"]
        X, Ya, Yb = ln["X"], ln["Ya"], ln["Yb"]
        uw = ln["dg"]
        vo = E
        oo = ln["MN1"]
        xq = xc.ap[:, 0 + hl, t * 128:(t + 1) * 128]
        Sh = Sst.ap[:, h, :]
        ST = S_T[h]
        P.op("pe", R.matmul(X.ap[:, 0:128], lhsT=uw.ap[:, 128:256], rhs=Sh, start=True, stop=True), reads=[uw, ST], writes=[X])
        P.op("pe", R.matmul(X.ap[:, 128:256], lhsT=xq, rhs=Sh, start=True, stop=True), reads=[xc, ST], writes=[X])
        yield
        P.op("dve", R.tensor_tensor(out=vo.ap[:, 0:128], in0=uw.ap[:, 0:128], in1=X.ap[:, 0:128], op=ALU.subtract), reads=[uw, X], writes=[vo])
        P.op("act", R.activation(out=vo.ap[:, 128:256], in_=X.ap[:, 128:256], func=AF.Copy, scale=cf.ap[:, 7:8]), reads=[X, cf], writes=[vo])
        yield
        P.op("pe", R.matmul(Ya.ap[:], lhsT=NA.ap[:, 128:256], rhs=vo.ap[:, 0:128], start=True, stop=True), reads=[NA, vo], writes=[Ya])
        P.op("pe", R.matmul(Yb.ap[:], lhsT=kdec.ap[:], rhs=vo.ap[:, 0:128], start=True, stop=True), reads=[kdec, vo], writes=[Yb])
        yield
        P.op("dve", R.scalar_tensor_tensor(out=Sh, in0=Sh, scalar=glast.ap[:, t, h:h + 1], in1=Yb.ap[:], op0=ALU.mult, op1=ALU.add),
             reads=[ST, glast, Yb], writes=[ST])
        P.op("dve", R.tensor_tensor(out=oo.ap[:, 0:128], in0=vo.ap[:, 128:256], in1=Ya.ap[:], op=ALU.add), reads=[vo, Ya], writes=[oo])
        yield
        P.op("act", R.activation(out=oo.ap[:, 128:256], in_=oo.ap[:, 0:128], func=AF.Square, accum_out=cf.ap[:, 8:9]), reads=[oo], writes=[oo, cf])
        yield
        P.op("dve", R.tensor_scalar(out=cf.ap[:, 8:9], in0=cf.ap[:, 8:9], scalar1=float(1.0 / 128), scalar2=float(EPS), op0=ALU.mult, op1=ALU.add), reads=[cf], writes=[cf])
        yield
        P.op("act", R.activation(out=cf.ap[:, 8:9], in_=cf.ap[:, 8:9], func=AF.Ln), reads=[cf], writes=[cf])
        P.op("act", R.activation(out=cf.ap[:, 8:9], in_=cf.ap[:, 8:9], func=AF.Exp, scale=-0.5), reads=[cf], writes=[cf])
        yield
        P.op("dve", R.scalar_tensor_tensor(out=oo.ap[:, 128:256], in0=oo.ap[:, 0:128], scalar=cf.ap[:, 8:9], in1=onorm_bc.ap[:, l, :], op0=ALU.mult, op1=ALU.mult),
             reads=[oo, cf, onorm_bc], writes=[oo])
        yield
        P.op("pool", R.tensor_tensor(out=oo.ap[:, 128:256], in0=oo.ap[:, 128:256], in1=zs.ap[:, t, hl * 128:(hl + 1) * 128], op=ALU.mult), reads=[oo, zs], writes=[oo])
        yield
        P.op("pe", R.transpose(out=Ya.ap[:], in_=oo.ap[:, 128:256], identity=ident.ap[:]), reads=[oo, ident], writes=[Ya])
        yield
        P.op("dve", R.tensor_copy(out=oT.ap[:, h, t * 128:(t + 1) * 128], in_=Ya.ap[:]), reads=[Ya], writes=[oT])
        if l == 0 and tb == 0 and h == 0 and t == 0:
            dbg_dump("dn_o", oo, oo.ap[:, 0:128], [128, 128])
        yield

    def run_interleaved(gens):
        gens = list(gens)
        rounds = 0
        while gens:
            rounds += 1
            if getattr(cfg, "dncut", None) is not None and rounds > cfg.dncut:
                return
            alive = []
            for g in gens:
                try:
                    next(g)
                    alive.append(g)
                except StopIteration:
                    pass
            gens = alive

    for t in range(4):
        run_interleaved([chunk_local(lanes[hl], hl, t) for hl in range(4)])
        if getattr(cfg, "dncut", None) is None:
            run_interleaved([scan_step(lanes[hl], hl, t) for hl in range(4)])
    A.reset(mk)


_CACHE = {}


def _invf():
    inv = (10000.0 ** (-np.arange(0, 64, 2, dtype=np.float32) / np.float32(64))).astype(np.float32)
    return np.ascontiguousarray(np.broadcast_to(inv[None, :], (128, 32))).astype(np.float32)


_W_KEYS = ["norm_mix", "w_in", "dn_a_log", "dn_dt_bias", "dn_out_norm", "mla_w_qb", "mla_w_kvb", "w_out", "norm_xattn",
           "xa_wq", "xa_wk", "xa_wv", "xa_wo", "norm_ffn", "ffn_w_up", "ffn_w_down"]


def _layout_weights(inputs, nlayers):
    f = lambda a: np.ascontiguousarray(np.asarray(a))
    g = lambda k: np.asarray(inputs[k])[:nlayers]
    Ln = nlayers
    w = {k: f(g(k)) for k in _W_KEYS}
    w["mla_q_norm"] = f(g("mla_q_norm").reshape(Ln, 4, 128).transpose(0, 2, 1))
    w["mla_kv_norm"] = f(g("mla_kv_norm").reshape(Ln, 2, 128).transpose(0, 2, 1))
    w["dn_conv"] = f(g("dn_conv").reshape(Ln, 4, 24, 128).transpose(0, 3, 2, 1))
    w["ffn_conv"] = f(g("ffn_conv").reshape(Ln, 3, 88, 128).transpose(0, 3, 2, 1))
    w["ffn_conv_bias"] = f(g("ffn_conv_bias").reshape(Ln, 88, 128).transpose(0, 2, 1))
    return w


def _slot_weights(w, role):
    out = {}
    for k, a in w.items():
        z = np.zeros((1,) + a.shape[1:], a.dtype)
        b = np.concatenate([a, z], 0) if role == 0 else np.concatenate([z, a], 0)
        if k in ("mla_q_norm", "mla_kv_norm", "dn_conv", "ffn_conv", "ffn_conv_bias"):
            nl = b.shape[0]
            b = np.moveaxis(b, 0, 1).reshape(128, -1)
        out[k] = np.ascontiguousarray(b)
    return out


def make_in_map(inputs, b, role, wslots, NTOK):
    f = lambda a: np.ascontiguousarray(np.asarray(a))
    t0 = role * NTOK
    pos = np.asarray(inputs["positions"])[b, t0:t0 + NTOK].astype(np.int32)
    fl = np.zeros((128, 2), np.float32)
    fl[:, 0] = float(role)
    fl[:, 1] = (float(role) - 1.0) * 30000.0
    m = {
        "x": f(np.asarray(inputs["x"])[b, t0:t0 + NTOK]),
        "mem": f(np.asarray(inputs["mem"])[b]),
        "pos": f(pos.reshape(NTOK // 128, 128).T),
        "invf": _invf(),
        "flag": fl,
        "mem_norm": f(np.asarray(inputs["mem_norm"]).reshape(1, D)),
        "norm_final": f(np.asarray(inputs["norm_final"]).reshape(1, D)),
    }
    m.update(wslots)
    return m


def kernel(**inputs):
    cfg = Cfg(NTOK=1024, L=5, NPRE=1024, n_cores=8)
    if "nc" not in _CACHE:
        _CACHE["nc"] = build_program(cfg)[0]
    nc = _CACHE["nc"]
    w = _layout_weights(inputs, 4)
    slots = [_slot_weights(w, 0), _slot_weights(w, 1)]
    in_maps = [make_in_map(inputs, c // 2, c % 2, slots[c % 2], 1024) for c in range(8)]
    res = run_bass_kernel_spmd(nc, in_maps, core_ids=list(range(8)))
    out = np.stack([np.concatenate([np.asarray(res.results[2 * b]["y"]), np.asarray(res.results[2 * b + 1]["y"])], axis=0)
                    for b in range(4)], axis=0).astype(np.float32)
    return out
```

```python
import numpy as np
import concourse.bass as bass
import concourse.mybir as mybir
from concourse.bass_utils import run_bass_kernel_spmd

F32 = mybir.dt.float32
BF16 = mybir.dt.bfloat16
I32 = mybir.dt.int32
U8 = mybir.dt.uint8
AF = mybir.ActivationFunctionType
ALU = mybir.AluOpType
AX = mybir.AxisListType

ENGS = ["pe", "act", "dve", "pool", "sp"]

D = 2048
NKT = 16
DFF = 5632
NIN = 4944
EPS = 1e-6
NEG = -30000.0


class T:
    __slots__ = ("ap", "name", "w", "r", "parent")

    def __init__(self, ap, name="", parent=None):
        self.ap = ap
        self.name = name
        self.w = None
        self.r = []
        self.parent = parent


def _roots(ts):
    return [t.parent if t.parent is not None else t for t in ts]


class Prog:
    def __init__(self, nc):
        self.nc = nc
        self.streams = {e: [] for e in ENGS}
        self.cnt = {}
        self.seen = {e: {} for e in ENGS}
        self.sems = {}
        self.eng_sem = {e: ("E", e, 0) for e in ENGS}
        self.dma_rr = {e: 0 for e in ENGS}
        self.NDMA = 8
        self.n_ops = 0
        self.pool_dirty = False

    def _need(self, eng, dep):
        if dep is None:
            return
        key, val = dep
        if eng == "pe" and key[0] == "E" and key[1] == "pe":
            return
        if self.seen[eng].get(key, 0) >= val:
            return
        self.seen[eng][key] = val
        self.streams[eng].append(("wait", key, val))

    def _deps(self, eng, reads, writes):
        reads = _roots(reads)
        writes = _roots(writes)
        for t in reads:
            self._need(eng, t.w)
        for t in writes:
            self._need(eng, t.w)
            for d in t.r:
                self._need(eng, d)

    def _mark(self, stamp, reads, writes):
        reads = _roots(reads)
        writes = _roots(writes)
        for t in reads:
            t.r.append(stamp)
            if len(t.r) > 48:
                best = {}
                for k, v in t.r:
                    if best.get(k, 0) < v:
                        best[k] = v
                t.r = list(best.items())
        for t in writes:
            t.w = stamp
            t.r = []

    def op(self, eng, fn, reads=(), writes=()):
        if eng == "pool":
            self.pool_dirty = True
        self._deps(eng, reads, writes)
        key = self.eng_sem[eng]
        c = self.cnt.get(key, 0) + 1
        self.cnt[key] = c
        self.streams[eng].append(("op", fn, key))
        self._mark((key, c), reads, writes)
        self.n_ops += 1
        if c >= 16000:
            self.eng_sem[eng] = ("E", eng, key[2] + 1)

    def dma(self, q, out_t, out_ap, in_t, in_ap):
        reads = [in_t] if in_t is not None else []
        writes = [out_t] if out_t is not None else []
        self._deps(q, reads, writes)
        i = self.dma_rr[q]
        self.dma_rr[q] = (i + 1) % self.NDMA
        key = ("D", q, i)
        prev = self.cnt.get(key, 0)
        if prev:
            self._need(q, (key, prev))
        c = prev + 16
        self.cnt[key] = c

        def fn(e, out_ap=out_ap, in_ap=in_ap):
            return e.dma_start(out=out_ap, in_=in_ap)
        self.streams[q].append(("dma", fn, key))
        self._mark((key, c), reads, writes)
        self.n_ops += 1

    def coll(self, send_t, send_ap, recv_t, recv_ap, groups):
        q = "pool"
        self._deps(q, [send_t], [recv_t])
        key = ("C", len([k for k in self.cnt if k[0] == "C"]))
        self.cnt[key] = 1

        def fn(e):
            return e.collective_compute("AllGather", ALU.bypass, replica_groups=groups, ins=[send_ap.opt()], outs=[recv_ap.opt()])
        self.streams[q].append(("coll", fn, key))
        self._mark((key, 1), [send_t], [recv_t])
        self.n_ops += 1

    def barrier(self):
        for e in ENGS:
            if e == "pool" and not self.pool_dirty:
                continue
            self.wait_all(e)
        self.pool_dirty = False

    def wait_all(self, eng):
        for key, val in list(self.cnt.items()):
            if val:
                self._need(eng, (key, val))

    def emit(self):
        nc = self.nc
        for k in list(self.cnt.keys()):
            self.sems[k] = nc.alloc_semaphore("s_" + "_".join(str(x) for x in k))
        streams, sems = self.streams, self.sems

        def run(e, name):
            for item in streams[name]:
                if item[0] == "wait":
                    e.wait_ge(sems[item[1]], item[2])
                elif item[0] == "op":
                    item[1](e).then_inc(sems[item[2]], 1)
                elif item[0] == "coll":
                    item[1](e).then_inc(sems[item[2]])
                else:
                    item[1](e).then_inc(sems[item[2]], 16)

        with nc.Block() as block:
            @block.tensor
            def _(e):
                run(e, "pe")

            @block.scalar
            def _(e):
                run(e, "act")

            @block.vector
            def _(e):
                run(e, "dve")

            @block.gpsimd
            def _(e):
                run(e, "pool")

            @block.sync
            def _(e):
                run(e, "sp")


class _Rec:
    def __getattr__(self, name):
        def mk(*args, **kw):
            return lambda e: getattr(e, name)(*args, **kw)
        return mk


R = _Rec()


class Arena:
    def __init__(self, nc, size):
        self.nc = nc
        slab = nc.alloc_sbuf_tensor("slab", [128, size], U8)
        self.base = nc.lookup_mloc(slab).addr
        self.size = size
        self.top = 0
        self.peak = 0

    def alloc(self, name, shape, dtype):
        nb = int(np.prod(shape[1:])) * (4 if dtype in (F32, I32) else 2)
        nb = (nb + 31) // 32 * 32
        off = self.top
        assert off + nb <= self.size, f"arena overflow {name}: {off}+{nb} > {self.size}"
        self.top += nb
        self.peak = max(self.peak, self.top)
        h = self.nc.alloc_sbuf_tensor_at(name, list(shape), dtype, offset=self.base + off)
        return T(h, name)

    def mark(self):
        return self.top

    def reset(self, m):
        self.top = m


class Cfg:
    def __init__(self, NTOK=1024, L=5, NPRE=1024, n_cores=8, dbg=(), stop=None):
        self.NTOK = NTOK
        self.NPRE = NPRE
        self.NKEY = NTOK + NPRE
        self.groups = [[2 * i, 2 * i + 1] for i in range(n_cores // 2)]
        self.L = L
        self.TB = 512
        self.NTB = NTOK // 512
        self.NT = NTOK // 128
        self.dbg = dbg
        self.stop = stop


def build_program(cfg):
    nc = bass.Bass("TRN2", target_bir_lowering=False)
    P = Prog(nc)
    L, NTOK, NT, NTB = cfg.L, cfg.NTOK, cfg.NT, cfg.NTB
    NPRE, NKEY = cfg.NPRE, cfg.NKEY
    NPT = NPRE // 128
    SW = 1024 + 72 + 176 + 1024 + 512

    def din(name, shape, dt=F32):
        return nc.dram_tensor(name, list(shape), dt, kind="ExternalInput").ap()

    x_d = din("x", [NTOK, D])
    mem_d = din("mem", [256, D])
    pos_d = din("pos", [128, NT], I32)
    invf_d = din("invf", [128, 32])
    flag_d = din("flag", [128, 2])
    norm_mix_d = din("norm_mix", [L, D])
    w_in_d = din("w_in", [L, D, NIN])
    dn_conv_d = din("dn_conv", [128, L * 24 * 4])
    a_log_d = din("dn_a_log", [L, 8])
    dt_bias_d = din("dn_dt_bias", [L, 8])
    out_norm_d = din("dn_out_norm", [L, 128])
    q_norm_d = din("mla_q_norm", [128, L * 4])
    w_qb_d = din("mla_w_qb", [L, 512, 1536])
    kv_norm_d = din("mla_kv_norm", [128, L * 2])
    w_kvb_d = din("mla_w_kvb", [L, 256, 2048])
    w_out_d = din("w_out", [L, D, D])
    mem_norm_d = din("mem_norm", [1, D])
    norm_x_d = din("norm_xattn", [L, D])
    wq_d = din("xa_wq", [L, D, D])
    wk_d = din("xa_wk", [L, D, D])
    wv_d = din("xa_wv", [L, D, D])
    wo_d = din("xa_wo", [L, D, D])
    norm_f_d = din("norm_ffn", [L, D])
    w_up_d = din("ffn_w_up", [L, D, 2 * DFF])
    f_conv_d = din("ffn_conv", [128, L * 88 * 3])
    f_bias_d = din("ffn_conv_bias", [128, L * 88])
    w_down_d = din("ffn_w_down", [L, DFF, D])
    norm_fin_d = din("norm_final", [1, D])
    y_d = nc.dram_tensor("y", [NTOK, D], F32, kind="ExternalOutput").ap()
    h_d = nc.dram_tensor("hscr", [NTOK, D], F32, kind="Internal").ap()
    WT = T(None, "weights")
    hT = [T(None, f"h{t}") for t in range(NT)]
    yT = [T(None, f"y{t}") for t in range(NT)]
    dbg_out = {}

    def dbg_dump(name, tile, ap, shape, dt=F32):
        if name not in cfg.dbg:
            return
        d = nc.dram_tensor("dbg_" + name, list(shape), dt, kind="ExternalOutput").ap()
        dbg_out[name] = d
        P.dma("sp", T(None), d, tile, ap)

    A = Arena(nc, 198 * 1024)
    PSB = [T(nc.alloc_psum_tensor(f"psb{i}", [128, 512], F32), f"psb{i}") for i in range(8)]
    LX = [T(PSB[b].ap[:, 0:256], f"lx{b}", parent=PSB[b]) for b in range(4)]
    LYa = [T(PSB[4 + b].ap[:, 0:128], f"lya{b}", parent=PSB[4 + b]) for b in range(4)]
    LYb = [T(PSB[4 + b].ap[:, 128:256], f"lyb{b}", parent=PSB[4 + b]) for b in range(4)]
    memn_scr = nc.dram_tensor("memn_scr", [128, 16 * 256], BF16, kind="Internal").ap()
    memn_T = T(None, "memn_scr")
    send_d = nc.dram_tensor("st_send", [128, SW], F32, kind="Internal").ap()
    recv_d = nc.dram_tensor("st_recv", [256, SW], F32, kind="Internal").ap()
    send_T = T(None, "st_send")
    recv_T = T(None, "st_recv")

    ident = A.alloc("ident", [128, 128], F32)
    ones_f = A.alloc("ones_f", [128, 128], F32)
    ones_b = A.alloc("ones_b", [128, 128], BF16)
    tri_incl = A.alloc("tri_incl", [128, 128], F32)
    mask2 = A.alloc("mask2", [128, 256], F32)
    negstrict = A.alloc("negstrict", [128, 128], F32)
    sel_last = A.alloc("sel_last", [128, 1], F32)
    cos_t = A.alloc("cos_t", [128, NT, 32], F32)
    sin_t = A.alloc("sin_t", [128, NT, 32], F32)
    qn_g = A.alloc("qn_g", [128, L, 4], F32)
    kvn_g = A.alloc("kvn_g", [128, L, 2], F32)
    dnc_w = A.alloc("dnc_w", [128, L, 24, 4], F32)
    fc_w = A.alloc("fc_w", [128, L, 88, 3], F32)
    fc_b = A.alloc("fc_b", [128, L, 88], F32)
    alog_bc = A.alloc("alog_bc", [128, L, 8], F32)
    dtb_bc = A.alloc("dtb_bc", [128, L, 8], F32)
    onorm_bc = A.alloc("onorm_bc", [128, L, 128], F32)
    KmT = A.alloc("KmT", [128, 16, 256], BF16)
    Vm = A.alloc("Vm", [128, 2, D], BF16)
    ckvT = A.alloc("ckvT", [128, 2, NKEY], BF16)
    kpeT = A.alloc("kpeT", [64, NKEY], BF16)
    flag = A.alloc("flag", [128, 2], F32)
    Sst = A.alloc("Sst", [128, 8, 128], F32)
    S_T = [T(Sst.ap[:, h, :], f"S{h}") for h in range(8)]
    dn_halo = A.alloc("dn_halo", [128, 24, 3], F32)
    f_halo = A.alloc("f_halo", [128, 88, 2], F32)
    wbuf = [A.alloc(f"wbuf{i}", [128, 16, 512], BF16) for i in range(2)]
    stat = A.alloc("stat", [128, 16], F32)
    uT = A.alloc("uT", [128, 16, 512], BF16)
    wb_i = [0]

    def sp_load(tile, ap_out, src):
        P.dma("sp", tile, ap_out, WT, src)

    P.op("pool", R.memset(ident.ap[:], 1.0), writes=[ident])
    P.op("pool", R.affine_select(out=ident.ap[:], in_=ident.ap[:], pattern=[[-1, 128]], compare_op=ALU.is_equal,
                                          fill=0.0, base=0, channel_multiplier=1), reads=[ident], writes=[ident])
    P.op("pool", R.memset(ones_f.ap[:], 1.0), writes=[ones_f])
    P.op("pool", R.memset(ones_b.ap[:], 1.0), writes=[ones_b])
    P.op("pool", R.memset(tri_incl.ap[:], 1.0), writes=[tri_incl])
    P.op("pool", R.affine_select(out=tri_incl.ap[:], in_=tri_incl.ap[:], pattern=[[1, 128]], compare_op=ALU.is_ge,
                                          fill=0.0, base=0, channel_multiplier=-1), reads=[tri_incl], writes=[tri_incl])
    P.op("pool", R.memset(mask2.ap[:], 0.0), writes=[mask2])
    for hh in range(2):
        P.op("pool", R.affine_select(out=mask2.ap[:, hh * 128:(hh + 1) * 128], in_=mask2.ap[:, hh * 128:(hh + 1) * 128],
                                                      pattern=[[1, 128]], compare_op=ALU.is_ge, fill=NEG, base=0, channel_multiplier=-1),
             reads=[mask2], writes=[mask2])
    P.op("pool", R.memset(negstrict.ap[:], -1.0), writes=[negstrict])
    P.op("pool", R.affine_select(out=negstrict.ap[:], in_=negstrict.ap[:], pattern=[[1, 128]], compare_op=ALU.is_gt,
                                          fill=0.0, base=0, channel_multiplier=-1), reads=[negstrict], writes=[negstrict])
    P.op("pool", R.memset(sel_last.ap[:], 1.0), writes=[sel_last])
    P.op("pool", R.affine_select(out=sel_last.ap[:], in_=sel_last.ap[:], pattern=[[0, 1]], compare_op=ALU.is_equal,
                                          fill=0.0, base=-127, channel_multiplier=1), reads=[sel_last], writes=[sel_last])

    nc_allow = nc.allow_non_contiguous_dma(reason="tiny param loads")
    nc_allow.__enter__()
    sp_load(flag, flag.ap[:], flag_d)
    sp_load(qn_g, qn_g.ap[:].rearrange("p l k -> p (l k)"), q_norm_d)
    sp_load(kvn_g, kvn_g.ap[:].rearrange("p l k -> p (l k)"), kv_norm_d)
    sp_load(dnc_w, dnc_w.ap[:].rearrange("p l c k -> p (l c k)"), dn_conv_d)
    sp_load(fc_w, fc_w.ap[:].rearrange("p l c k -> p (l c k)"), f_conv_d)
    sp_load(fc_b, fc_b.ap[:].rearrange("p l c -> p (l c)"), f_bias_d)
    for l in range(L):
        sp_load(alog_bc, alog_bc.ap[:, l, :], a_log_d[l:l + 1, :].partition_broadcast(128))
        sp_load(dtb_bc, dtb_bc.ap[:, l, :], dt_bias_d[l:l + 1, :].partition_broadcast(128))
        sp_load(onorm_bc, onorm_bc.ap[:, l, :], out_norm_d[l:l + 1, :].partition_broadcast(128))
    P.op("act", R.activation(out=alog_bc.ap[:].rearrange("p l h -> p (l h)"), in_=alog_bc.ap[:].rearrange("p l h -> p (l h)"), func=AF.Exp),
         reads=[alog_bc], writes=[alog_bc])

    m0 = A.mark()
    pos_i = A.alloc("pos_i", [128, NT], I32)
    pos_f = A.alloc("pos_f", [128, NT], F32)
    invf = A.alloc("invf", [128, 32], F32)
    ang = A.alloc("ang", [128, NT, 32], F32)
    kq = A.alloc("kq", [128, NT, 32], F32)
    ki = A.alloc("ki", [128, NT, 32], I32)
    rr = A.alloc("rr", [128, NT, 32], F32)
    sp_load(pos_i, pos_i.ap[:], pos_d)
    sp_load(invf, invf.ap[:], invf_d)
    P.op("dve", R.tensor_copy(out=pos_f.ap[:], in_=pos_i.ap[:]), reads=[pos_i], writes=[pos_f])
    for t in range(NT):
        P.op("dve", R.tensor_scalar(out=ang.ap[:, t, :], in0=invf.ap[:], scalar1=pos_f.ap[:, t:t + 1], scalar2=None, op0=ALU.mult),
             reads=[invf, pos_f], writes=[ang])
    TWO_PI = 2.0 * np.pi
    C1 = 6.28125
    C2 = TWO_PI - C1
    fl = lambda ap: ap[:].rearrange("p t j -> p (t j)")
    for which, tab in ((0, sin_t), (1, cos_t)):
        shift = 0.0 if which == 0 else np.pi / 2
        P.op("dve", R.tensor_scalar(out=fl(kq.ap), in0=fl(ang.ap), scalar1=float(shift), scalar2=float(1.0 / TWO_PI), op0=ALU.add, op1=ALU.mult),
             reads=[ang], writes=[kq])
        P.op("dve", R.tensor_copy(out=fl(ki.ap), in_=fl(kq.ap)), reads=[kq], writes=[ki])
        P.op("dve", R.tensor_copy(out=fl(kq.ap), in_=fl(ki.ap)), reads=[ki], writes=[kq])
        P.op("dve", R.scalar_tensor_tensor(out=fl(rr.ap), in0=fl(kq.ap), scalar=float(-C1), in1=fl(ang.ap), op0=ALU.mult, op1=ALU.add),
             reads=[kq, ang], writes=[rr])
        P.op("dve", R.scalar_tensor_tensor(out=fl(rr.ap), in0=fl(kq.ap), scalar=float(-C2), in1=fl(rr.ap), op0=ALU.mult, op1=ALU.add),
             reads=[kq, rr], writes=[rr])
        if which == 1:
            P.op("dve", R.tensor_scalar(out=fl(rr.ap), in0=fl(rr.ap), scalar1=float(shift), scalar2=None, op0=ALU.add),
                 reads=[rr], writes=[rr])
        P.op("dve", R.tensor_scalar(out=fl(rr.ap), in0=fl(rr.ap), scalar1=float(3.1415925), scalar2=float(-3.1415925), op0=ALU.min, op1=ALU.max),
             reads=[rr], writes=[rr])
        P.op("act", R.activation(out=fl(tab.ap), in_=fl(rr.ap), func=AF.Sin), reads=[rr], writes=[tab])
    dbg_dump("cos", cos_t, cos_t.ap[:], [128, NT, 32])
    dbg_dump("sin", sin_t, sin_t.ap[:], [128, NT, 32])
    P.barrier()
    A.reset(m0)

    evac_rr = [0]

    def evac_copy(out_t, out_ap, ps_t, ps_ap, eng=None):
        if eng is None:
            eng = "act" if evac_rr[0] % 2 == 0 else "dve"
            evac_rr[0] += 1
        if eng == "act":
            P.op("act", R.copy(out=out_ap, in_=ps_ap), reads=[ps_t], writes=[out_t])
        else:
            P.op("dve", R.tensor_copy(out=out_ap, in_=ps_ap), reads=[ps_t], writes=[out_t])

    def wload(src_ap, nkt, ncols):
        wb = wbuf[wb_i[0] % 2]
        wb_i[0] += 1
        P.dma("pool", wb, wb.ap[:, 0:nkt, 0:ncols], WT, src_ap.rearrange("(kt p) c -> p kt c", p=128))
        return wb

    def rstd_from_ss(ss_ap_fn, scale, tiles):
        P.op("dve", R.tensor_scalar(out=ss_ap_fn(), in0=ss_ap_fn(), scalar1=float(scale), scalar2=float(EPS), op0=ALU.mult, op1=ALU.add),
             reads=tiles, writes=tiles)
        P.op("act", R.activation(out=ss_ap_fn(), in_=ss_ap_fn(), func=AF.Sqrt), reads=tiles, writes=tiles)
        P.op("dve", R.reciprocal(out=ss_ap_fn(), in_=ss_ap_fn()), reads=tiles, writes=tiles)

    def norm_block(src_d, src_T, row0, gain_row_ap, ntile, dst_uT, to_y=None):
        mk = A.mark()
        hx = [A.alloc(f"hx{t}", [128, D], F32) for t in range(ntile)]
        g_mix = A.alloc("g_mix", [128, D], F32)
        junk = A.alloc("junk", [128, D], BF16)
        sp_load(g_mix, g_mix.ap[:], gain_row_ap.partition_broadcast(128))
        for t in range(ntile):
            P.dma("sp", hx[t], hx[t].ap[:], src_T[row0 // 128 + t], src_d[row0 + t * 128: row0 + (t + 1) * 128, :])
        norm_core(hx, g_mix, junk, row0, ntile, dst_uT, to_y)
        P.barrier()
        A.reset(mk)

    def norm_core(hx, g_mix, junk, row0, ntile, dst_uT, to_y):
        for t in range(ntile):
            P.op("act", R.activation(out=junk.ap[:], in_=hx[t].ap[:], func=AF.Square, accum_out=stat.ap[:, t:t + 1]),
                 reads=[hx[t]], writes=[junk, stat])
        rstd_from_ss(lambda: stat.ap[:, 0:ntile], 1.0 / D, [stat])
        for t in range(ntile):
            P.op("dve", R.scalar_tensor_tensor(out=hx[t].ap[:], in0=hx[t].ap[:], scalar=stat.ap[:, t:t + 1], in1=g_mix.ap[:],
                                                              op0=ALU.mult, op1=ALU.mult), reads=[hx[t], stat, g_mix], writes=[hx[t]])
            if to_y is not None:
                gt = row0 // 128 + t
                P.dma("sp", to_y[1][gt], to_y[0][gt * 128:(gt + 1) * 128, :], hx[t], hx[t].ap[:])
                continue
            for g in range(4):
                ps = PSB[4 + (g % 4)]
                for j in range(4):
                    kt = g * 4 + j
                    P.op("pe", R.transpose(out=ps.ap[:, j * 128:(j + 1) * 128], in_=hx[t].ap[:, kt * 128:(kt + 1) * 128], identity=ident.ap[:]),
                         reads=[hx[t], ident], writes=[ps])
                evac_copy(dst_uT, dst_uT.ap[:, g * 4:(g + 1) * 4, t * 128:(t + 1) * 128], ps, ps.ap[:].rearrange("p (a b) -> p a b", a=4))

    def residual_add_dense(actT, nkt_total, w_d2, tb, nxt=None):
        mk = A.mark()
        hx = [A.alloc(f"hxr{t}", [128, D], F32) for t in range(4)]
        if nxt is not None:
            g_mix = A.alloc("g_mixr", [128, D], F32)
            junk = A.alloc("junkr", [128, D], BF16)
            sp_load(g_mix, g_mix.ap[:], nxt["gain"].partition_broadcast(128))
        for t in range(4):
            P.dma("sp", hx[t], hx[t].ap[:], hT[tb * 4 + t], h_d[(tb * 4 + t) * 128:(tb * 4 + t + 1) * 128, :])
        chunks = []
        k0 = 0
        while k0 < nkt_total:
            chunks.append((k0, min(16, nkt_total - k0)))
            k0 += 16
        for cb in range(4):
            for ci, (k0, nk) in enumerate(chunks):
                wb = wload(w_d2[k0 * 128:(k0 + nk) * 128, cb * 512:(cb + 1) * 512], nk, 512)
                for t in range(4):
                    ps = PSB[t]
                    for kk in range(nk):
                        kt = k0 + kk
                        P.op("pe", R.matmul(ps.ap[:], lhsT=actT.ap[:, kt, t * 128:(t + 1) * 128], rhs=wb.ap[:, kk, :],
                                                                                     start=(kt == 0), stop=(kt == nkt_total - 1)),
                             reads=[actT, wb], writes=[ps])
            for t in range(4):
                ps = PSB[t]
                P.op("dve", R.tensor_tensor(out=hx[t].ap[:, cb * 512:(cb + 1) * 512], in0=hx[t].ap[:, cb * 512:(cb + 1) * 512], in1=ps.ap[:], op=ALU.add),
                     reads=[hx[t], ps], writes=[hx[t]])
        if nxt is None or not nxt.get("skip_store"):
            for t in range(4):
                P.dma("sp", hT[tb * 4 + t], h_d[(tb * 4 + t) * 128:(tb * 4 + t + 1) * 128, :], hx[t], hx[t].ap[:])
        if nxt is not None:
            norm_core(hx, g_mix, junk, tb * 512, 4, nxt.get("dst"), nxt.get("to_y"))
        P.barrier()
        A.reset(mk)

    for t in range(NT):
        P.dma("sp", hT[t], h_d[t * 128:(t + 1) * 128, :], WT, x_d[t * 128:(t + 1) * 128, :])

    mz = A.mark()
    ztile = A.alloc("ztile", [128, SW], F32)
    P.op("dve", R.memset(ztile.ap[:], 0.0), writes=[ztile])
    P.dma("sp", send_T, send_d, ztile, ztile.ap[:])
    P.barrier()
    A.reset(mz)

    norm_block(mem_d, [WT, WT], 0, mem_norm_d[0:1, :], 2, uT)
    P.dma("sp", memn_T, memn_scr.rearrange("p (k m) -> p k m", k=16), uT, uT.ap[:, :, 0:256])
    P.barrier()

    for l in range(L):
        mk0 = A.mark()
        memnT = A.alloc("memnT", [128, 16, 256], BF16)
        P.dma("sp", memnT, memnT.ap[:], memn_T, memn_scr.rearrange("p (k m) -> p k m", k=16))
        if l == 0:
            dbg_dump("memnT", memnT, memnT.ap[:], [128, 16, 256], BF16)
        for cb in range(4):
            wb = wload(wk_d[l][:, cb * 512:(cb + 1) * 512], 16, 512)
            for c in range(4):
                ps = PSB[c]
                for kt in range(16):
                    P.op("pe", R.matmul(ps.ap[:, 0:256], lhsT=wb.ap[:, kt, c * 128:(c + 1) * 128], rhs=memnT.ap[:, kt, :],
                                                                         start=(kt == 0), stop=(kt == 15)), reads=[wb, memnT], writes=[ps])
                evac_copy(KmT, KmT.ap[:, cb * 4 + c, :], ps, ps.ap[:, 0:256])
        for cb in range(4):
            wb = wload(wv_d[l][:, cb * 512:(cb + 1) * 512], 16, 512)
            for m in range(2):
                ps = PSB[4 + m]
                for kt in range(16):
                    P.op("pe", R.matmul(ps.ap[:], lhsT=memnT.ap[:, kt, m * 128:(m + 1) * 128], rhs=wb.ap[:, kt, :],
                                                                         start=(kt == 0), stop=(kt == 15)), reads=[wb, memnT], writes=[ps])
                evac_copy(Vm, Vm.ap[:, m, cb * 512:(cb + 1) * 512], ps, ps.ap[:])
        P.barrier()
        A.reset(mk0)
        P.coll(send_T, send_d, recv_T, recv_d, cfg.groups)
        P.dma("sp", Sst, Sst.ap[:].rearrange("p h d -> p (h d)"), recv_T, recv_d[0:128, 0:1024])
        P.dma("sp", dn_halo, dn_halo.ap[:].rearrange("p c k -> p (c k)"), recv_T, recv_d[0:128, 1024:1096])
        P.dma("sp", f_halo, f_halo.ap[:].rearrange("p c k -> p (c k)"), recv_T, recv_d[0:128, 1096:1272])
        P.dma("sp", ckvT, ckvT.ap[:, :, 0:NPRE].bitcast(F32), recv_T, recv_d[0:128, 1272:2296].rearrange("p (k t) -> p k t", k=2))
        P.dma("sp", kpeT, kpeT.ap[:, 0:NPRE].bitcast(F32), recv_T, recv_d[0:64, 2296:2808])
        P.op("dve", R.tensor_scalar(out=Sst.ap[:].rearrange("p h d -> p (h d)"), in0=Sst.ap[:].rearrange("p h d -> p (h d)"), scalar1=flag.ap[:, 0:1], scalar2=None, op0=ALU.mult),
             reads=[Sst, flag], writes=[Sst] + S_T)
        P.op("dve", R.tensor_scalar(out=dn_halo.ap[:].rearrange("p c k -> p (c k)"), in0=dn_halo.ap[:].rearrange("p c k -> p (c k)"), scalar1=flag.ap[:, 0:1], scalar2=None, op0=ALU.mult),
             reads=[dn_halo, flag], writes=[dn_halo])
        P.op("dve", R.tensor_scalar(out=f_halo.ap[:].rearrange("p c k -> p (c k)"), in0=f_halo.ap[:].rearrange("p c k -> p (c k)"), scalar1=flag.ap[:, 0:1], scalar2=None, op0=ALU.mult),
             reads=[f_halo, flag], writes=[f_halo])

        for tb in range(NTB):
            tok0 = tb * 512
            norm_block(h_d, hT, tok0, norm_mix_d[l:l + 1, :], 4, uT)
            if l == 0 and tb == 0:
                dbg_dump("uT0", uT, uT.ap[:], [128, 16, 512], BF16)
            mA = A.mark()
            oT = A.alloc("oT", [128, 16, 512], BF16)
            mB = A.mark()
            lat_q = A.alloc("lat_q", [128, 4, 512], F32)
            lat_kv = A.alloc("lat_kv", [128, 4, 320], F32)
            qlatT = A.alloc("qlatT", [128, 4, 512], BF16)
            qpe = A.alloc("qpe", [128, 512], F32)
            qpe_r = A.alloc("qpe_r", [128, 4, 512], F32)
            rtmp = A.alloc("rtmp", [128, 2, 256], F32)
            qpeT = A.alloc("qpeT", [64, 8, 512], BF16)
            KhT = A.alloc("KhT", [128, NKEY], BF16)
            Vh = A.alloc("Vh", [128, NKEY // 128, 128], BF16)
            QhT = A.alloc("QhT", [128, 512], BF16)
            PT = [A.alloc(f"PT{i}", [128, 512], BF16) for i in range(3)]
            rec = A.alloc("rec", [128, 512], F32)
            junkm = A.alloc("junkm", [128, 512], BF16)
            wb1 = wload(w_in_d[l][:, 4112:4624], 16, 512)
            wb2 = wload(w_in_d[l][:, 4624:4944], 16, 320)
            for t in range(4):
                ps = PSB[t % 2]
                for kt in range(16):
                    P.op("pe", R.matmul(ps.ap[:], lhsT=uT.ap[:, kt, t * 128:(t + 1) * 128], rhs=wb1.ap[:, kt, :],
                                                                    start=(kt == 0), stop=(kt == 15)), reads=[uT, wb1], writes=[ps])
                P.op("act", R.copy(out=lat_q.ap[:, t, :], in_=ps.ap[:]), reads=[ps], writes=[lat_q])
                P.op("act", R.activation(out=junkm.ap[:], in_=lat_q.ap[:, t, :], func=AF.Square, accum_out=stat.ap[:, t:t + 1]),
                     reads=[lat_q], writes=[junkm, stat])
                ps2 = PSB[2 + t % 2]
                for kt in range(16):
                    P.op("pe", R.matmul(ps2.ap[:, 0:320], lhsT=uT.ap[:, kt, t * 128:(t + 1) * 128], rhs=wb2.ap[:, kt, 0:320],
                                                                      start=(kt == 0), stop=(kt == 15)), reads=[uT, wb2], writes=[ps2])
                P.op("dve", R.tensor_copy(out=lat_kv.ap[:, t, :], in_=ps2.ap[:, 0:320]), reads=[ps2], writes=[lat_kv])
                P.op("act", R.activation(out=junkm.ap[:, 0:256], in_=lat_kv.ap[:, t, 0:256], func=AF.Square, accum_out=stat.ap[:, 4 + t:5 + t]),
                     reads=[lat_kv], writes=[junkm, stat])
            rstd_from_ss(lambda: stat.ap[:, 0:4], 1.0 / 512, [stat])
            rstd_from_ss(lambda: stat.ap[:, 4:8], 1.0 / 256, [stat])
            if l == 0 and tb == 0:
                dbg_dump("lat_q", lat_q, lat_q.ap[:], [128, 4, 512])
                dbg_dump("lat_kv", lat_kv, lat_kv.ap[:], [128, 4, 320])
            wqb = wbuf[wb_i[0] % 2]
            wb_i[0] += 1
            wkvb = wbuf[wb_i[0] % 2]
            wb_i[0] += 1
            wqb_v = wqb.ap[:].rearrange("p k c -> p (k c)")[:, 0:4 * 1536].rearrange("p (k c) -> p k c", k=4)
            wkvb_v = wkvb.ap[:].rearrange("p k c -> p (k c)")[:, 0:2 * 2048].rearrange("p (k c) -> p k c", k=2)
            P.dma("pool", wqb, wqb_v, WT, w_qb_d[l].rearrange("(kt p) c -> p kt c", p=128))
            P.dma("pool", wkvb, wkvb_v, WT, w_kvb_d[l].rearrange("(kt p) c -> p kt c", p=128))
            wq4 = wqb_v.rearrange("p k (h d) -> p k h d", h=8)
            wkv4 = wkvb_v.rearrange("p k (h d) -> p k h d", h=8)
            for t in range(4):
                gt = tb * 4 + t
                P.op("dve", R.tensor_scalar(out=lat_q.ap[:, t, :], in0=lat_q.ap[:, t, :], scalar1=stat.ap[:, t:t + 1], scalar2=None, op0=ALU.mult),
                     reads=[lat_q, stat], writes=[lat_q])
                P.op("dve", R.tensor_scalar(out=lat_kv.ap[:, t, 0:256], in0=lat_kv.ap[:, t, 0:256], scalar1=stat.ap[:, 4 + t:5 + t], scalar2=None, op0=ALU.mult),
                     reads=[lat_kv, stat], writes=[lat_kv])
                x1 = lat_kv.ap[:, t, 256:288]
                x2 = lat_kv.ap[:, t, 288:320]
                cs = cos_t.ap[:, gt, :]
                sn = sin_t.ap[:, gt, :]
                r = rtmp.ap[:, 0, :]
                P.op("dve", R.tensor_tensor(out=r[:, 0:32], in0=x1, in1=cs, op=ALU.mult), reads=[lat_kv, cos_t], writes=[rtmp])
                P.op("dve", R.tensor_tensor(out=r[:, 32:64], in0=x2, in1=sn, op=ALU.mult), reads=[lat_kv, sin_t], writes=[rtmp])
                P.op("dve", R.tensor_tensor(out=r[:, 64:96], in0=x2, in1=cs, op=ALU.mult), reads=[lat_kv, cos_t], writes=[rtmp])
                P.op("dve", R.tensor_tensor(out=r[:, 96:128], in0=x1, in1=sn, op=ALU.mult), reads=[lat_kv, sin_t], writes=[rtmp])
                P.op("dve", R.tensor_tensor(out=x1, in0=r[:, 0:32], in1=r[:, 32:64], op=ALU.subtract), reads=[rtmp], writes=[lat_kv])
                P.op("dve", R.tensor_tensor(out=x2, in0=r[:, 64:96], in1=r[:, 96:128], op=ALU.add), reads=[rtmp], writes=[lat_kv])
                ps = PSB[4 + t % 2]
                for j in range(2):
                    P.op("pe", R.transpose(out=ps.ap[:, j * 128:(j + 1) * 128], in_=lat_kv.ap[:, t, j * 128:(j + 1) * 128], identity=ident.ap[:]),
                         reads=[lat_kv, ident], writes=[ps])
                P.op("pe", R.transpose(out=ps.ap[0:64, 256:384], in_=lat_kv.ap[:, t, 256:320], identity=ident.ap[:]),
                     reads=[lat_kv, ident], writes=[ps])
                for j in range(2):
                    P.op("act", R.activation(out=ckvT.ap[:, j, (NPT + gt) * 128:(NPT + gt + 1) * 128], in_=ps.ap[:, j * 128:(j + 1) * 128], func=AF.Copy,
                                                                          scale=kvn_g.ap[:, l, j:j + 1]), reads=[ps, kvn_g], writes=[ckvT])
                P.op("dve", R.tensor_copy(out=kpeT.ap[:, (NPT + gt) * 128:(NPT + gt + 1) * 128], in_=ps.ap[0:64, 256:384]), reads=[ps], writes=[kpeT])
            for kt in range(4):
                ps = PSB[6 + kt % 2]
                for t in range(4):
                    P.op("pe", R.transpose(out=ps.ap[:, t * 128:(t + 1) * 128], in_=lat_q.ap[:, t, kt * 128:(kt + 1) * 128], identity=ident.ap[:]),
                         reads=[lat_q, ident], writes=[ps])
                P.op("act", R.activation(out=qlatT.ap[:, kt, :], in_=ps.ap[:], func=AF.Copy, scale=qn_g.ap[:, l, kt:kt + 1]),
                     reads=[ps, qn_g], writes=[qlatT])
            if l == 0 and tb == 0:
                dbg_dump("qlatT", qlatT, qlatT.ap[:], [128, 4, 512], BF16)
                dbg_dump("ckvT", ckvT, ckvT.ap[:, :, NPRE:NPRE + 512], [128, 2, 512], BF16)
                dbg_dump("kpeT", kpeT, kpeT.ap[:, NPRE:NPRE + 512], [64, 512], BF16)
            for t in range(4):
                gt = tb * 4 + t
                ps = PSB[t % 2]
                for kt in range(4):
                    P.op("pe", R.matmul(ps.ap[:].rearrange("p (h d) -> p h d", h=8), lhsT=qlatT.ap[:, kt, t * 128:(t + 1) * 128],
                                                                    rhs=wq4[:, kt, :, 128:192], start=(kt == 0), stop=(kt == 3)), reads=[qlatT, wqb], writes=[ps])
                P.op("act", R.copy(out=qpe.ap[:], in_=ps.ap[:]), reads=[ps], writes=[qpe])
                xv = qpe.ap[:].rearrange("p (h d) -> p h d", h=8)
                ov = qpe_r.ap[:, t, :].rearrange("p (h d) -> p h d", h=8)
                cb_ = cos_t.ap[:, gt, :].unsqueeze(1).broadcast_to([128, 8, 32])
                sb_ = sin_t.ap[:, gt, :].unsqueeze(1).broadcast_to([128, 8, 32])
                r0 = rtmp.ap[:, 0, :].rearrange("p (h d) -> p h d", h=8)
                r1 = rtmp.ap[:, 1, :].rearrange("p (h d) -> p h d", h=8)
                P.op("dve", R.tensor_tensor(out=r0, in0=xv[:, :, 0:32], in1=cb_, op=ALU.mult), reads=[qpe, cos_t], writes=[rtmp])
                P.op("dve", R.tensor_tensor(out=r1, in0=xv[:, :, 32:64], in1=sb_, op=ALU.mult), reads=[qpe, sin_t], writes=[rtmp])
                P.op("dve", R.tensor_tensor(out=ov[:, :, 0:32], in0=r0, in1=r1, op=ALU.subtract), reads=[rtmp], writes=[qpe_r])
                P.op("dve", R.tensor_tensor(out=r0, in0=xv[:, :, 32:64], in1=cb_, op=ALU.mult), reads=[qpe, cos_t], writes=[rtmp])
                P.op("dve", R.tensor_tensor(out=r1, in0=xv[:, :, 0:32], in1=sb_, op=ALU.mult), reads=[qpe, sin_t], writes=[rtmp])
                P.op("dve", R.tensor_tensor(out=ov[:, :, 32:64], in0=r0, in1=r1, op=ALU.add), reads=[rtmp], writes=[qpe_r])
            for h in range(8):
                ps = PSB[4 + h % 2]
                for t in range(4):
                    P.op("pe", R.transpose(out=ps.ap[0:64, t * 128:(t + 1) * 128], in_=qpe_r.ap[:, t, h * 64:(h + 1) * 64], identity=ident.ap[:]),
                         reads=[qpe_r, ident], writes=[ps])
                evac_copy(qpeT, qpeT.ap[:, h, :], ps, ps.ap[0:64, :])
            if l == 0 and tb == 0:
                dbg_dump("qpeT", qpeT, qpeT.ap[:], [64, 8, 512], BF16)
            nkt_keys = NPT + (tb + 1) * 4
            sc = float(192 ** -0.5)
            for h in range(8):
                for nb in range(nkt_keys // 4):
                    ps = PSB[nb % 2]
                    for kt in range(2):
                        P.op("pe", R.matmul(ps.ap[:], lhsT=wkv4[:, kt, h, 0:128], rhs=ckvT.ap[:, kt, nb * 512:(nb + 1) * 512],
                                                                               start=(kt == 0), stop=(kt == 1)), reads=[wkvb, ckvT], writes=[ps])
                    evac_copy(KhT, KhT.ap[:, nb * 512:(nb + 1) * 512], ps, ps.ap[:])
                for g in range(0, nkt_keys, 4):
                    ps = PSB[2 + (g // 4) % 2]
                    for j in range(4):
                        kt_ = g + j
                        for kt in range(2):
                            P.op("pe", R.matmul(ps.ap[:, j * 128:(j + 1) * 128], lhsT=ckvT.ap[:, kt, kt_ * 128:(kt_ + 1) * 128],
                                                                                         rhs=wkv4[:, kt, h, 128:256], start=(kt == 0), stop=(kt == 1)),
                                 reads=[wkvb, ckvT], writes=[ps])
                    evac_copy(Vh, Vh.ap[:, g:g + 4, :], ps, ps.ap[:].rearrange("p (a b) -> p a b", a=4))
                ps = PSB[4]
                for kt in range(4):
                    P.op("pe", R.matmul(ps.ap[:], lhsT=wq4[:, kt, h, 0:128], rhs=qlatT.ap[:, kt, :], start=(kt == 0), stop=(kt == 3)),
                         reads=[wqb, qlatT], writes=[ps])
                evac_copy(QhT, QhT.ap[:], ps, ps.ap[:])
                if l == 0 and tb == 0 and h == 0:
                    dbg_dump("KhT", KhT, KhT.ap[:, 0:512], [128, 512], BF16)
                    dbg_dump("Vh", Vh, Vh.ap[:, 0:4, :], [128, 4, 128], BF16)
                    dbg_dump("QhT", QhT, QhT.ap[:], [128, 512], BF16)
                psO = PSB[5]
                psD = PSB[6]

                def scores(kt_, h=h):
                    ps = PSB[(kt_ % 2) * 7]
                    pt = PT[kt_ % 3]
                    P.op("pe", R.matmul(ps.ap[:], lhsT=KhT.ap[:, kt_ * 128:(kt_ + 1) * 128], rhs=QhT.ap[:], start=True, stop=False),
                         reads=[KhT, QhT], writes=[ps])
                    P.op("pe", R.matmul(ps.ap[:], lhsT=kpeT.ap[:, kt_ * 128:(kt_ + 1) * 128], rhs=qpeT.ap[:, h, :], start=False, stop=True),
                         reads=[kpeT, qpeT], writes=[ps])
                    if kt_ < NPT:
                        P.op("act", R.activation(out=pt.ap[:], in_=ps.ap[:], func=AF.Exp, scale=sc, bias=flag.ap[:, 1:2]), reads=[ps, flag], writes=[pt])
                    else:
                        P.op("act", R.activation(out=pt.ap[:], in_=ps.ap[:], func=AF.Exp, scale=sc), reads=[ps], writes=[pt])
                    j = kt_ - NPT - tb * 4
                    if j >= 0:
                        if j > 0:
                            P.op("dve", R.memset(pt.ap[:, 0:j * 128], 0.0), reads=[pt], writes=[pt])
                        P.op("dve", R.memset(pt.ap[64:128, j * 128:j * 128 + 64], 0.0), reads=[pt], writes=[pt])

                def pv(kt_):
                    pt = PT[kt_ % 3]
                    P.op("pe", R.matmul(psO.ap[:], lhsT=Vh.ap[:, kt_, :], rhs=pt.ap[:], start=(kt_ == 0), stop=(kt_ == nkt_keys - 1)),
                         reads=[Vh, pt], writes=[psO])
                    P.op("pe", R.matmul(psD.ap[:], lhsT=ones_b.ap[:], rhs=pt.ap[:], start=(kt_ == 0), stop=(kt_ == nkt_keys - 1)),
                         reads=[ones_b, pt], writes=[psD])
                scores(0)
                for kt_ in range(nkt_keys):
                    if kt_ + 1 < nkt_keys:
                        scores(kt_ + 1)
                    pv(kt_)
                P.op("dve", R.reciprocal(out=rec.ap[:], in_=psD.ap[:]), reads=[psD], writes=[rec])
                P.op("dve", R.tensor_tensor(out=oT.ap[:, 8 + h, :], in0=psO.ap[:], in1=rec.ap[:], op=ALU.mult), reads=[psO, rec], writes=[oT])
            if l == 0 and tb == 0:
                dbg_dump("oT_mla", oT, oT.ap[:, 8:16, :], [128, 8, 512], BF16)
            P.barrier()
            A.reset(mB)
            if cfg.stop == "mla":
                A.reset(mA)
                continue
            dn_section(P, A, nc, cfg, l, tb, locals())
            P.barrier()
            A.reset(mB)
            if l == 0 and tb == 0:
                dbg_dump("oT_dn", oT, oT.ap[:, 0:8, :], [128, 8, 512], BF16)
            if cfg.stop in ("dn", "dnpre"):
                P.barrier()
                A.reset(mA)
                continue
            residual_add_dense(oT, 16, w_out_d[l], tb, nxt=dict(gain=norm_x_d[l:l + 1, :], dst=uT))
            A.reset(mA)
            mA = A.mark()
            oxT = A.alloc("oxT", [128, 16, 512], BF16)
            mX = A.mark()
            qxT = A.alloc("qxT", [128, 16, 512], BF16)
            PTx = [A.alloc(f"PTx{i}", [128, 512], BF16) for i in range(2)]
            recx = A.alloc("recx", [128, 512], F32)
            for cb in range(4):
                wb = wload(wq_d[l][:, cb * 512:(cb + 1) * 512], 16, 512)
                for c in range(4):
                    ps = PSB[c]
                    for kt in range(16):
                        P.op("pe", R.matmul(ps.ap[:], lhsT=wb.ap[:, kt, c * 128:(c + 1) * 128], rhs=uT.ap[:, kt, :],
                                                                             start=(kt == 0), stop=(kt == 15)), reads=[wb, uT], writes=[ps])
                    evac_copy(qxT, qxT.ap[:, cb * 4 + c, :], ps, ps.ap[:])
            scx = float(512 ** -0.5)
            for hd in range(4):
                for m in range(2):
                    ps = PSB[4 + m]
                    for c in range(4):
                        P.op("pe", R.matmul(ps.ap[:], lhsT=KmT.ap[:, hd * 4 + c, m * 128:(m + 1) * 128], rhs=qxT.ap[:, hd * 4 + c, :],
                                                                             start=(c == 0), stop=(c == 3)), reads=[KmT, qxT], writes=[ps])
                    P.op("act", R.activation(out=PTx[m].ap[:], in_=ps.ap[:], func=AF.Exp, scale=scx), reads=[ps], writes=[PTx[m]])
                psD = PSB[6]
                for m in range(2):
                    P.op("pe", R.matmul(psD.ap[:], lhsT=ones_b.ap[:], rhs=PTx[m].ap[:], start=(m == 0), stop=(m == 1)), reads=[ones_b, PTx[m]], writes=[psD])
                P.op("dve", R.reciprocal(out=recx.ap[:], in_=psD.ap[:]), reads=[psD], writes=[recx])
                for c in range(4):
                    ps = PSB[c]
                    for m in range(2):
                        P.op("pe", R.matmul(ps.ap[:], lhsT=Vm.ap[:, m, (hd * 4 + c) * 128:(hd * 4 + c + 1) * 128], rhs=PTx[m].ap[:],
                                                                             start=(m == 0), stop=(m == 1)), reads=[Vm, PTx[m]], writes=[ps])
                    P.op("dve", R.tensor_tensor(out=oxT.ap[:, hd * 4 + c, :], in0=ps.ap[:], in1=recx.ap[:], op=ALU.mult), reads=[ps, recx], writes=[oxT])
            P.barrier()
            A.reset(mX)
            residual_add_dense(oxT, 16, wo_d[l], tb, nxt=dict(gain=norm_f_d[l:l + 1, :], dst=uT))
            A.reset(mA)
            mA = A.mark()
            aT = A.alloc("aT", [128, 44, 512], BF16)
            mF = A.mark()
            sg = A.alloc("sg", [128, 4, 512], F32)
            raw = [A.alloc(f"raw{i}", [128, 514], F32) for i in range(2)]
            cacc = [A.alloc(f"cacc{i}", [128, 512], F32) for i in range(2)]
            ri = 0
            for cb in range(11):
                for part in range(2):
                    col0 = part * DFF + cb * 512
                    wb = wload(w_up_d[l][:, col0:col0 + 512], 16, 512)
                    for c in range(4):
                        ct = (col0 // 128) + c
                        ps = PSB[c + 4 * part]
                        for kt in range(16):
                            P.op("pe", R.matmul(ps.ap[:], lhsT=wb.ap[:, kt, c * 128:(c + 1) * 128], rhs=uT.ap[:, kt, :],
                                                                                 start=(kt == 0), stop=(kt == 15)), reads=[wb, uT], writes=[ps])
                        rw = raw[ri % 2]
                        ca = cacc[ri % 2]
                        ri += 1
                        P.op("act", R.copy(out=rw.ap[:, 2:514], in_=ps.ap[:]), reads=[ps], writes=[rw])
                        P.op("dve", R.tensor_copy(out=rw.ap[:, 0:2], in_=f_halo.ap[:, ct, :]), reads=[f_halo], writes=[rw])
                        P.op("dve", R.tensor_copy(out=f_halo.ap[:, ct, :], in_=rw.ap[:, 512:514]), reads=[rw], writes=[f_halo])
                        P.op("dve", R.tensor_scalar(out=ca.ap[:], in0=rw.ap[:, 0:512], scalar1=fc_w.ap[:, l, ct, 0:1], scalar2=fc_b.ap[:, l, ct:ct + 1],
                                                                                  op0=ALU.mult, op1=ALU.add), reads=[rw, fc_w, fc_b], writes=[ca])
                        P.op("dve", R.scalar_tensor_tensor(out=ca.ap[:], in0=rw.ap[:, 1:513], scalar=fc_w.ap[:, l, ct, 1:2], in1=ca.ap[:],
                                                                                         op0=ALU.mult, op1=ALU.add), reads=[rw, fc_w, ca], writes=[ca])
                        P.op("dve", R.scalar_tensor_tensor(out=ca.ap[:], in0=rw.ap[:, 2:514], scalar=fc_w.ap[:, l, ct, 2:3], in1=ca.ap[:],
                                                                                         op0=ALU.mult, op1=ALU.add), reads=[rw, fc_w, ca], writes=[ca])
                        if part == 0:
                            P.op("act", R.activation(out=sg.ap[:, c, :], in_=ca.ap[:], func=AF.Silu), reads=[ca], writes=[sg])
                        else:
                            P.op("dve", R.tensor_tensor(out=aT.ap[:, cb * 4 + c, :], in0=ca.ap[:], in1=sg.ap[:, c, :], op=ALU.mult),
                                 reads=[ca, sg], writes=[aT])
            if l == 0 and tb == 0:
                dbg_dump("aT", aT, aT.ap[:], [128, 44, 512], BF16)
            P.barrier()
            A.reset(mF)
            if l == L - 1 and cfg.stop is None:
                residual_add_dense(aT, 44, w_down_d[l], tb, nxt=dict(gain=norm_fin_d[0:1, :], dst=None, to_y=(y_d, yT), skip_store=True))
            else:
                residual_add_dense(aT, 44, w_down_d[l], tb)
            A.reset(mA)

        P.dma("sp", send_T, send_d[:, 0:1024], Sst, Sst.ap[:].rearrange("p h d -> p (h d)"))
        P.dma("sp", send_T, send_d[:, 1024:1096], dn_halo, dn_halo.ap[:].rearrange("p c k -> p (c k)"))
        P.dma("sp", send_T, send_d[:, 1096:1272], f_halo, f_halo.ap[:].rearrange("p c k -> p (c k)"))
        P.dma("sp", send_T, send_d[:, 1272:2296].rearrange("p (k t) -> p k t", k=2), ckvT, ckvT.ap[:, :, NPRE:NKEY].bitcast(F32))
        P.dma("sp", send_T, send_d[0:64, 2296:2808], kpeT, kpeT.ap[:, NPRE:NKEY].bitcast(F32))

    if cfg.stop is None:
        pass
    else:
        for t in range(NT):
            P.dma("sp", yT[t], y_d[t * 128:(t + 1) * 128, :], hT[t], h_d[t * 128:(t + 1) * 128, :])
    P.wait_all("sp")
    P.emit()
    nc_allow.__exit__(None, None, None)
    return nc, P, A, dbg_out


def dn_section(P, A, nc, cfg, l, tb, env):
    uT = env["uT"]; PSB = env["PSB"]; wload = env["wload"]; w_in_d = env["w_in_d"]
    dnc_w = env["dnc_w"]; dn_halo = env["dn_halo"]; ones_f = env["ones_f"]
    sel_last = env["sel_last"]; tri_incl = env["tri_incl"]
    alog_bc = env["alog_bc"]; dtb_bc = env["dtb_bc"]
    dbg_dump = env["dbg_dump"]

    ba = A.alloc("ba", [128, 4, 16], F32)
    beta = A.alloc("beta", [128, 4, 8], F32)
    gg = A.alloc("gg", [128, 4, 8], F32)
    Gc = A.alloc("Gc", [128, 4, 8], F32)
    negG = A.alloc("negG", [128, 4, 8], F32)
    expG = A.alloc("expG", [128, 4, 8], F32)
    edec = A.alloc("edec", [128, 4, 8], F32)
    glast = A.alloc("glast", [128, 4, 8], F32)
    gsel = A.alloc("gsel", [128, 8], F32)
    wb = wload(w_in_d[l][:, 4096:4112], 16, 16)
    for t in range(4):
        ps = PSB[4 + t % 2]
        for kt in range(16):
            P.op("pe", R.matmul(ps.ap[:, 0:16], lhsT=uT.ap[:, kt, t * 128:(t + 1) * 128], rhs=wb.ap[:, kt, 0:16],
                                                            start=(kt == 0), stop=(kt == 15)), reads=[uT, wb], writes=[ps])
        P.op("dve", R.tensor_copy(out=ba.ap[:, t, :], in_=ps.ap[:, 0:16]), reads=[ps], writes=[ba])
    P.op("act", R.activation(out=beta.ap[:], in_=ba.ap[:, :, 0:8], func=AF.Sigmoid), reads=[ba], writes=[beta])
    for t in range(4):
        P.op("dve", R.tensor_tensor(out=gg.ap[:, t, :], in0=ba.ap[:, t, 8:16], in1=dtb_bc.ap[:, l, :], op=ALU.add), reads=[ba, dtb_bc], writes=[gg])
    P.op("act", R.activation(out=gg.ap[:], in_=gg.ap[:], func=AF.Exp), reads=[gg], writes=[gg])
    P.op("act", R.activation(out=gg.ap[:], in_=gg.ap[:], func=AF.Ln, bias=1.0), reads=[gg], writes=[gg])
    for t in range(4):
        P.op("dve", R.scalar_tensor_tensor(out=gg.ap[:, t, :], in0=gg.ap[:, t, :], scalar=-1.0, in1=alog_bc.ap[:, l, :], op0=ALU.mult, op1=ALU.mult),
             reads=[gg, alog_bc], writes=[gg])
    for t in range(4):
        ps = PSB[4 + t % 2]
        P.op("pe", R.matmul(ps.ap[:, 0:8], lhsT=tri_incl.ap[:], rhs=gg.ap[:, t, :], start=True, stop=True), reads=[tri_incl, gg], writes=[ps])
        P.op("dve", R.tensor_copy(out=Gc.ap[:, t, :], in_=ps.ap[:, 0:8]), reads=[ps], writes=[Gc])
        P.op("dve", R.tensor_scalar(out=gsel.ap[:], in0=Gc.ap[:, t, :], scalar1=sel_last.ap[:, 0:1], scalar2=None, op0=ALU.mult), reads=[Gc, sel_last], writes=[gsel])
        ps2 = PSB[6 + t % 2]
        P.op("pe", R.matmul(ps2.ap[:, 0:8], lhsT=ones_f.ap[:], rhs=gsel.ap[:], start=True, stop=True), reads=[ones_f, gsel], writes=[ps2])
        P.op("act", R.activation(out=glast.ap[:, t, :], in_=ps2.ap[:, 0:8], func=AF.Exp), reads=[ps2], writes=[glast])
        P.op("dve", R.tensor_tensor(out=edec.ap[:, t, :], in0=ps2.ap[:, 0:8], in1=Gc.ap[:, t, :], op=ALU.subtract), reads=[ps2, Gc], writes=[edec])
    P.op("act", R.activation(out=edec.ap[:], in_=edec.ap[:], func=AF.Exp), reads=[edec], writes=[edec])
    P.op("act", R.activation(out=expG.ap[:], in_=Gc.ap[:], func=AF.Exp), reads=[Gc], writes=[expG])
    P.op("dve", R.tensor_scalar(out=negG.ap[:], in0=Gc.ap[:], scalar1=-1.0, scalar2=None, op0=ALU.mult), reads=[Gc], writes=[negG])
    if l == 0 and tb == 0:
        dbg_dump("beta", beta, beta.ap[:], [128, 4, 8])
        dbg_dump("Gc", Gc, Gc.ap[:], [128, 4, 8])

    mH = A.mark()
    for hh in range(2):
        zs = A.alloc("zs", [128, 4, 512], F32)
        xc = A.alloc("xc", [128, 12, 512], F32)
        mR = A.mark()
        raw = [A.alloc(f"dnraw{i}", [128, 515], F32) for i in range(2)]
        ri = 0
        for part in range(3):
            col0 = part * 1024 + hh * 512
            wbk = wload(w_in_d[l][:, col0:col0 + 512], 16, 512)
            for c in range(4):
                ct = col0 // 128 + c
                ps = PSB[4 + c]
                for kt in range(16):
                    P.op("pe", R.matmul(ps.ap[:], lhsT=wbk.ap[:, kt, c * 128:(c + 1) * 128], rhs=uT.ap[:, kt, :],
                                                                           start=(kt == 0), stop=(kt == 15)), reads=[wbk, uT], writes=[ps])
                rw = raw[ri % 2]
                ri += 1
                xo = xc.ap[:, part * 4 + c, :]
                P.op("act", R.copy(out=rw.ap[:, 3:515], in_=ps.ap[:]), reads=[ps], writes=[rw])
                P.op("dve", R.tensor_copy(out=rw.ap[:, 0:3], in_=dn_halo.ap[:, ct, :]), reads=[dn_halo], writes=[rw])
                P.op("dve", R.tensor_copy(out=dn_halo.ap[:, ct, :], in_=rw.ap[:, 512:515]), reads=[rw], writes=[dn_halo])
                P.op("dve", R.tensor_scalar(out=xo, in0=rw.ap[:, 0:512], scalar1=dnc_w.ap[:, l, ct, 0:1], scalar2=None, op0=ALU.mult),
                     reads=[rw, dnc_w], writes=[xc])
                for k in range(1, 4):
                    P.op("dve", R.scalar_tensor_tensor(out=xo, in0=rw.ap[:, k:k + 512], scalar=dnc_w.ap[:, l, ct, k:k + 1], in1=xo,
                                                                                          op0=ALU.mult, op1=ALU.add), reads=[rw, dnc_w, xc], writes=[xc])
                P.op("act", R.activation(out=xo, in_=xo, func=AF.Silu), reads=[xc], writes=[xc])
        wbz = wload(w_in_d[l][:, 3072 + hh * 512:3072 + (hh + 1) * 512], 16, 512)
        for t in range(4):
            ps = PSB[4 + t % 4]
            for kt in range(16):
                P.op("pe", R.matmul(ps.ap[:], lhsT=uT.ap[:, kt, t * 128:(t + 1) * 128], rhs=wbz.ap[:, kt, :], start=(kt == 0), stop=(kt == 15)),
                     reads=[uT, wbz], writes=[ps])
            P.op("act", R.activation(out=zs.ap[:, t, :], in_=ps.ap[:], func=AF.Silu), reads=[ps], writes=[zs])
        if l == 0 and tb == 0 and hh == 0:
            dbg_dump("xc", xc, xc.ap[:], [128, 12, 512])
            dbg_dump("zs", zs, zs.ap[:], [128, 4, 512])
        P.barrier()
        A.reset(mR)
        if cfg.stop == "dnpre":
            A.reset(mH)
            continue
        dn_heads(P, A, nc, cfg, l, tb, hh, env, dict(xc=xc, zs=zs, beta=beta, Gc=Gc, negG=negG, expG=expG, edec=edec, glast=glast))
        P.barrier()
        A.reset(mH)


def dn_heads(P, A, nc, cfg, l, tb, hh, env, d):
    oT = env["oT"]; ident = env["ident"]; ones_f = env["ones_f"]; mask2 = env["mask2"]; negstrict = env["negstrict"]
    onorm_bc = env["onorm_bc"]; Sst = env["Sst"]; S_T = env["S_T"]; dbg_dump = env["dbg_dump"]
    LX = env["LX"]; LYa = env["LYa"]; LYb = env["LYb"]
    xc = d["xc"]; zs = d["zs"]; beta = d["beta"]; Gc = d["Gc"]; negG = d["negG"]; expG = d["expG"]; edec = d["edec"]; glast = d["glast"]
    mk = A.mark()
    NL = 4
    lanes = []
    for i in range(NL):
        lanes.append(dict(
            kv_tok=A.alloc(f"kv_tok{i}", [128, 256], F32),
            sq=A.alloc(f"sq{i}", [128, 256], F32),
            dg=A.alloc(f"dg{i}", [128, 256], F32),
            E=A.alloc(f"E{i}", [128, 256], F32),
            NA=A.alloc(f"NA{i}", [128, 256], F32),
            MN0=A.alloc(f"MN0_{i}", [128, 256], F32),
            MN1=A.alloc(f"MN1_{i}", [128, 256], F32),
            P0=A.alloc(f"P0_{i}", [128, 128], F32),
            P1=A.alloc(f"P1_{i}", [128, 128], F32),
            kdec=A.alloc(f"kdec{i}", [128, 128], F32),
            cf=A.alloc(f"cf{i}", [128, 16], F32),
            X=LX[i], Ya=LYa[i], Yb=LYb[i]))

    def chunk_local(ln, hl, t):
        h = hh * 4 + hl
        kv_tok, sq, dg, E, NA, kdec, cf = ln["kv_tok"], ln["sq"], ln["dg"], ln["E"], ln["NA"], ln["kdec"], ln["cf"]
        X, Ya, Yb = ln["X"], ln["Ya"], ln["Yb"]
        xq = xc.ap[:, 0 + hl, t * 128:(t + 1) * 128]
        xk = xc.ap[:, 4 + hl, t * 128:(t + 1) * 128]
        xv = xc.ap[:, 8 + hl, t * 128:(t + 1) * 128]
        P.op("pe", R.transpose(out=X.ap[:, 0:128], in_=xk, identity=ident.ap[:]), reads=[xc, ident], writes=[X])
        P.op("pe", R.transpose(out=X.ap[:, 128:256], in_=xv, identity=ident.ap[:]), reads=[xc, ident], writes=[X])
        P.op("dve", R.tensor_tensor(out=sq.ap[:, 0:128], in0=xq, in1=xq, op=ALU.mult), reads=[xc], writes=[sq])
        P.op("dve", R.tensor_tensor(out=sq.ap[:, 128:256], in0=xk, in1=xk, op=ALU.mult), reads=[xc], writes=[sq])
        yield
        P.op("act", R.copy(out=kv_tok.ap[:], in_=X.ap[:]), reads=[X], writes=[kv_tok])
        P.op("pe", R.matmul(Ya.ap[:, 0:8], lhsT=sq.ap[:, 0:128], rhs=ones_f.ap[:, 0:8], start=True, stop=True), reads=[sq, ones_f], writes=[Ya])
        P.op("pe", R.matmul(Ya.ap[:, 8:16], lhsT=sq.ap[:, 128:256], rhs=ones_f.ap[:, 0:8], start=True, stop=True), reads=[sq, ones_f], writes=[Ya])
        P.op("pe", R.matmul(X.ap[:, 0:128], lhsT=xk, rhs=xk, start=True, stop=True), reads=[xc], writes=[X])
        P.op("pe", R.matmul(X.ap[:, 128:256], lhsT=xk, rhs=xq, start=True, stop=True), reads=[xc], writes=[X])
        yield
        P.op("act", R.activation(out=cf.ap[:, 0:2], in_=Ya.ap[:, 0:16:8], func=AF.Ln, bias=float(EPS)), reads=[Ya], writes=[cf])
        P.op("act", R.activation(out=cf.ap[:, 0:2], in_=cf.ap[:, 0:2], func=AF.Exp, scale=-0.5), reads=[cf], writes=[cf])
        yield
        P.op("dve", R.tensor_tensor(out=cf.ap[:, 2:3], in0=cf.ap[:, 1:2], in1=beta.ap[:, t, h:h + 1], op=ALU.mult), reads=[cf, beta], writes=[cf])
        P.op("dve", R.tensor_scalar(out=cf.ap[:, 9:10], in0=cf.ap[:, 0:1], scalar1=float(128 ** -0.5), scalar2=None, op0=ALU.mult), reads=[cf], writes=[cf])
        P.op("dve", R.tensor_tensor(out=cf.ap[:, 6:7], in0=cf.ap[:, 1:2], in1=edec.ap[:, t, h:h + 1], op=ALU.mult), reads=[cf, edec], writes=[cf])
        yield
        P.op("act", R.activation(out=cf.ap[:, 3:4], in_=cf.ap[:, 2:3], func=AF.Ln), reads=[cf], writes=[cf])
        P.op("act", R.activation(out=cf.ap[:, 4:5], in_=cf.ap[:, 9:10], func=AF.Ln), reads=[cf], writes=[cf])
        P.op("dve", R.tensor_tensor(out=cf.ap[:, 5:6], in0=cf.ap[:, 2:3], in1=expG.ap[:, t, h:h + 1], op=ALU.mult), reads=[cf, expG], writes=[cf])
        P.op("dve", R.tensor_tensor(out=cf.ap[:, 7:8], in0=cf.ap[:, 9:10], in1=expG.ap[:, t, h:h + 1], op=ALU.mult), reads=[cf, expG], writes=[cf])
        yield
        P.op("dve", R.tensor_scalar(out=cf.ap[:, 3:5], in0=cf.ap[:, 3:5], scalar1=Gc.ap[:, t, h:h + 1], scalar2=None, op0=ALU.add), reads=[cf, Gc], writes=[cf])
        yield
        P.op("dve", R.tensor_scalar(out=dg.ap[:, 0:128], in0=ident.ap[:], scalar1=cf.ap[:, 3:4], scalar2=None, op0=ALU.mult), reads=[cf, ident], writes=[dg])
        P.op("dve", R.tensor_scalar(out=dg.ap[:, 128:256], in0=ident.ap[:], scalar1=cf.ap[:, 4:5], scalar2=None, op0=ALU.mult), reads=[cf, ident], writes=[dg])
        yield
        P.op("pe", R.matmul(Ya.ap[:], lhsT=ones_f.ap[:], rhs=dg.ap[:, 0:128], start=True, stop=True), reads=[ones_f, dg], writes=[Ya])
        P.op("pe", R.matmul(Yb.ap[:], lhsT=ones_f.ap[:], rhs=dg.ap[:, 128:256], start=True, stop=True), reads=[ones_f, dg], writes=[Yb])
        yield
        P.op("dve", R.tensor_tensor(out=E.ap[:, 0:128], in0=Ya.ap[:], in1=mask2.ap[:, 0:128], op=ALU.add), reads=[Ya, mask2], writes=[E])
        P.op("dve", R.tensor_tensor(out=E.ap[:, 128:256], in0=Yb.ap[:], in1=mask2.ap[:, 128:256], op=ALU.add), reads=[Yb, mask2], writes=[E])
        yield
        P.op("act", R.activation(out=E.ap[:], in_=E.ap[:], func=AF.Exp, bias=negG.ap[:, t, h:h + 1], scale=1.0), reads=[E, negG], writes=[E])
        yield
        P.op("dve", R.scalar_tensor_tensor(out=NA.ap[:], in0=X.ap[:], scalar=cf.ap[:, 1:2], in1=E.ap[:], op0=ALU.mult, op1=ALU.mult),
             reads=[X, cf, E], writes=[NA])
        yield
        cur = ln["MN0"]
        P.op("dve", R.tensor_tensor(out=cur.ap[:, 128:256], in0=NA.ap[:, 0:128], in1=negstrict.ap[:], op=ALU.mult), reads=[NA, negstrict], writes=[cur])
        yield
        P.op("pe", R.transpose(out=Ya.ap[:], in_=cur.ap[:, 128:256], identity=ident.ap[:]), reads=[cur, ident], writes=[Ya])
        pc = ln["P0"]
        P.op("dve", R.tensor_tensor(out=pc.ap[:], in0=cur.ap[:, 128:256], in1=ident.ap[:], op=ALU.add), reads=[cur, ident], writes=[pc])
        yield
        P.op("act", R.copy(out=cur.ap[:, 0:128], in_=Ya.ap[:]), reads=[Ya], writes=[cur])
        yield
        for j in range(1, 7):
            nxt = ln["MN1"] if j % 2 == 1 else ln["MN0"]
            pn = ln["P1"] if j % 2 == 1 else ln["P0"]
            last = (j == 6)
            P.op("pe", R.matmul(X.ap[:, 0:128], lhsT=cur.ap[:, 128:256], rhs=cur.ap[:, 0:128], start=True, stop=True), reads=[cur], writes=[X])
            if not last:
                P.op("pe", R.matmul(X.ap[:, 128:256], lhsT=cur.ap[:, 0:128], rhs=cur.ap[:, 128:256], start=True, stop=True), reads=[cur], writes=[X])
            yield
            if not last:
                P.op("act", R.copy(out=nxt.ap[:], in_=X.ap[:]), reads=[X], writes=[nxt])
            else:
                P.op("act", R.copy(out=nxt.ap[:, 0:128], in_=X.ap[:, 0:128]), reads=[X], writes=[nxt])
            yield
            Yp = Ya if j % 2 == 1 else Yb
            P.op("pe", R.matmul(Yp.ap[:], lhsT=nxt.ap[:, 0:128], rhs=pc.ap[:], start=True, stop=True), reads=[nxt, pc], writes=[Yp])
            yield
            P.op("dve", R.tensor_tensor(out=pn.ap[:], in0=Yp.ap[:], in1=pc.ap[:], op=ALU.add), reads=[Yp, pc], writes=[pn])
            cur = nxt
            pc = pn
        rhs_t = sq
        P.op("act", R.activation(out=rhs_t.ap[:, 0:128], in_=kv_tok.ap[:, 128:256], func=AF.Copy, scale=beta.ap[:, t, h:h + 1]), reads=[kv_tok, beta], writes=[rhs_t])
        P.op("act", R.activation(out=rhs_t.ap[:, 128:256], in_=kv_tok.ap[:, 0:128], func=AF.Copy, scale=cf.ap[:, 5:6]), reads=[kv_tok, cf], writes=[rhs_t])
        P.op("dve", R.tensor_scalar(out=kdec.ap[:], in0=kv_tok.ap[:, 0:128], scalar1=cf.ap[:, 6:7], scalar2=None, op0=ALU.mult), reads=[kv_tok, cf], writes=[kdec])
        yield
        P.op("pe", R.matmul(X.ap[:, 0:128], lhsT=pc.ap[:], rhs=rhs_t.ap[:, 0:128], start=True, stop=True), reads=[pc, rhs_t], writes=[X])
        P.op("pe", R.matmul(X.ap[:, 128:256], lhsT=rhs_t.ap[:, 128:256], rhs=pc.ap[:], start=True, stop=True), reads=[pc, rhs_t], writes=[X])
        yield
        uw = dg
        P.op("act", R.copy(out=uw.ap[:], in_=X.ap[:]), reads=[X], writes=[uw])
        if l == 0 and tb == 0 and h == 0 and t == 0:
            dbg_dump("dn_NA", NA, NA.ap[:], [128, 256])
            dbg_dump("dn_P", pc, pc.ap[:], [128, 128])
            dbg_dump("dn_uw", uw, uw.ap[:], [128, 256])
        yield

    def scan_step(ln, hl, t):
        h = hh * 4 + hl
        E, NA, kdec, cf = ln["E"], ln["NA"], ln["kdec"], ln["cf"]
        X, Ya, Yb = ln["X"], ln["Ya"], ln["Yb"]
        uw = ln["dg"]
        vo = E
        oo = ln["MN1"]
        xq = xc.ap[:, 0 + hl, t * 128:(t + 1) * 128]
        Sh = Sst.ap[:, h, :]
        ST = S_T[h]
        P.op("pe", R.matmul(X.ap[:, 0:128], lhsT=uw.ap[:, 128:256], rhs=Sh, start=True, stop=True), reads=[uw, ST], writes=[X])
        P.op("pe", R.matmul(X.ap[:, 128:256], lhsT=xq, rhs=Sh, start=True, stop=True), reads=[xc, ST], writes=[X])
        yield
        P.op("dve", R.tensor_tensor(out=vo.ap[:, 0:128], in0=uw.ap[:, 0:128], in1=X.ap[:, 0:128], op=ALU.subtract), reads=[uw, X], writes=[vo])
        P.op("act", R.activation(out=vo.ap[:, 128:256], in_=X.ap[:, 128:256], func=AF.Copy, scale=cf.ap[:, 7:8]), reads=[X, cf], writes=[vo])
        yield
        P.op("pe", R.matmul(Ya.ap[:], lhsT=NA.ap[:, 128:256], rhs=vo.ap[:, 0:128], start=True, stop=True), reads=[NA, vo], writes=[Ya])
        P.op("pe", R.matmul(Yb.ap[:], lhsT=kdec.ap[:], rhs=vo.ap[:, 0:128], start=True, stop=True), reads=[kdec, vo], writes=[Yb])
        yield
        P.op("dve", R.scalar_tensor_tensor(out=Sh, in0=Sh, scalar=glast.ap[:, t, h:h + 1], in1=Yb.ap[:], op0=ALU.mult, op1=ALU.add),
             reads=[ST, glast, Yb], writes=[ST])
        P.op("dve", R.tensor_tensor(out=oo.ap[:, 0:128], in0=vo.ap[:, 128:256], in1=Ya.ap[:], op=ALU.add), reads=[vo, Ya], writes=[oo])
        yield
        P.op("act", R.activation(out=oo.ap[:, 128:256], in_=oo.ap[:, 0:128], func=AF.Square, accum_out=cf.ap[:, 8:9]), reads=[oo], writes=[oo, cf])
        yield
        P.op("dve", R.tensor_scalar(out=cf.ap[:, 8:9], in0=cf.ap[:, 8:9], scalar1=float(1.0 / 128), scalar2=float(EPS), op0=ALU.mult, op1=ALU.add), reads=[cf], writes=[cf])
        yield
        P.op("act", R.activation(out=cf.ap[:, 8:9], in_=cf.ap[:, 8:9], func=AF.Ln), reads=[cf], writes=[cf])
        P.op("act", R.activation(out=cf.ap[:, 8:9], in_=cf.ap[:, 8:9], func=AF.Exp, scale=-0.5), reads=[cf], writes=[cf])
        yield
        P.op("dve", R.scalar_tensor_tensor(out=oo.ap[:, 128:256], in0=oo.ap[:, 0:128], scalar=cf.ap[:, 8:9], in1=onorm_bc.ap[:, l, :], op0=ALU.mult, op1=ALU.mult),
             reads=[oo, cf, onorm_bc], writes=[oo])
        yield
        P.op("dve", R.tensor_tensor(out=oo.ap[:, 128:256], in0=oo.ap[:, 128:256], in1=zs.ap[:, t, hl * 128:(hl + 1) * 128], op=ALU.mult), reads=[oo, zs], writes=[oo])
        yield
        P.op("pe", R.transpose(out=Ya.ap[:], in_=oo.ap[:, 128:256], identity=ident.ap[:]), reads=[oo, ident], writes=[Ya])
        yield
        P.op("dve", R.tensor_copy(out=oT.ap[:, h, t * 128:(t + 1) * 128], in_=Ya.ap[:]), reads=[Ya], writes=[oT])
        if l == 0 and tb == 0 and h == 0 and t == 0:
            dbg_dump("dn_o", oo, oo.ap[:, 0:128], [128, 128])
        yield

    def run_interleaved(gens):
        gens = list(gens)
        rounds = 0
        while gens:
            rounds += 1
            if getattr(cfg, "dncut", None) is not None and rounds > cfg.dncut:
                return
            alive = []
            for g in gens:
                try:
                    next(g)
                    alive.append(g)
                except StopIteration:
                    pass
            gens = alive

    for t in range(4):
        run_interleaved([chunk_local(lanes[hl], hl, t) for hl in range(4)])
        if getattr(cfg, "dncut", None) is None:
            run_interleaved([scan_step(lanes[hl], hl, t) for hl in range(4)])
    A.reset(mk)


_CACHE = {}


def _invf():
    inv = (10000.0 ** (-np.arange(0, 64, 2, dtype=np.float32) / np.float32(64))).astype(np.float32)
    return np.ascontiguousarray(np.broadcast_to(inv[None, :], (128, 32))).astype(np.float32)


_W_KEYS = ["norm_mix", "w_in", "dn_a_log", "dn_dt_bias", "dn_out_norm", "mla_w_qb", "mla_w_kvb", "w_out", "norm_xattn",
           "xa_wq", "xa_wk", "xa_wv", "xa_wo", "norm_ffn", "ffn_w_up", "ffn_w_down"]


def _layout_weights(inputs, nlayers):
    f = lambda a: np.ascontiguousarray(np.asarray(a))
    g = lambda k: np.asarray(inputs[k])[:nlayers]
    Ln = nlayers
    w = {k: f(g(k)) for k in _W_KEYS}
    w["mla_q_norm"] = f(g("mla_q_norm").reshape(Ln, 4, 128).transpose(0, 2, 1))
    w["mla_kv_norm"] = f(g("mla_kv_norm").reshape(Ln, 2, 128).transpose(0, 2, 1))
    w["dn_conv"] = f(g("dn_conv").reshape(Ln, 4, 24, 128).transpose(0, 3, 2, 1))
    w["ffn_conv"] = f(g("ffn_conv").reshape(Ln, 3, 88, 128).transpose(0, 3, 2, 1))
    w["ffn_conv_bias"] = f(g("ffn_conv_bias").reshape(Ln, 88, 128).transpose(0, 2, 1))
    return w


def _slot_weights(w, role):
    out = {}
    for k, a in w.items():
        z = np.zeros((1,) + a.shape[1:], a.dtype)
        b = np.concatenate([a, z], 0) if role == 0 else np.concatenate([z, a], 0)
        if k in ("mla_q_norm", "mla_kv_norm", "dn_conv", "ffn_conv", "ffn_conv_bias"):
            nl = b.shape[0]
            b = np.moveaxis(b, 0, 1).reshape(128, -1)
        out[k] = np.ascontiguousarray(b)
    return out


def make_in_map(inputs, b, role, wslots, NTOK):
    f = lambda a: np.ascontiguousarray(np.asarray(a))
    t0 = role * NTOK
    pos = np.asarray(inputs["positions"])[b, t0:t0 + NTOK].astype(np.int32)
    fl = np.zeros((128, 2), np.float32)
    fl[:, 0] = float(role)
    fl[:, 1] = (float(role) - 1.0) * 30000.0
    m = {
        "x": f(np.asarray(inputs["x"])[b, t0:t0 + NTOK]),
        "mem": f(np.asarray(inputs["mem"])[b]),
        "pos": f(pos.reshape(NTOK // 128, 128).T),
        "invf": _invf(),
        "flag": fl,
        "mem_norm": f(np.asarray(inputs["mem_norm"]).reshape(1, D)),
        "norm_final": f(np.asarray(inputs["norm_final"]).reshape(1, D)),
    }
    m.update(wslots)
    return m


def kernel(**inputs):
    cfg = Cfg(NTOK=1024, L=5, NPRE=1024, n_cores=8)
    if "nc" not in _CACHE:
        _CACHE["nc"] = build_program(cfg)[0]
    nc = _CACHE["nc"]
    w = _layout_weights(inputs, 4)
    slots = [_slot_weights(w, 0), _slot_weights(w, 1)]
    in_maps = [make_in_map(inputs, c // 2, c % 2, slots[c % 2], 1024) for c in range(8)]
    res = run_bass_kernel_spmd(nc, in_maps, core_ids=list(range(8)))
    out = np.stack([np.concatenate([np.asarray(res.results[2 * b]["y"]), np.asarray(res.results[2 * b + 1]["y"])], axis=0)
                    for b in range(4)], axis=0).astype(np.float32)
    return out
```

```python
import numpy as np
import concourse.bass as bass
import concourse.mybir as mybir
from concourse.bass_utils import run_bass_kernel_spmd

F32 = mybir.dt.float32
BF16 = mybir.dt.bfloat16
I32 = mybir.dt.int32
U8 = mybir.dt.uint8
AF = mybir.ActivationFunctionType
ALU = mybir.AluOpType
AX = mybir.AxisListType

ENGS = ["pe", "act", "dve", "pool", "sp"]

D = 2048
NKT = 16
DFF = 5632
NIN = 4944
EPS = 1e-6
NEG = -30000.0


class T:
    __slots__ = ("ap", "name", "w", "r", "parent")

    def __init__(self, ap, name="", parent=None):
        self.ap = ap
        self.name = name
        self.w = None
        self.r = []
        self.parent = parent


def _roots(ts):
    return [t.parent if t.parent is not None else t for t in ts]


class Prog:
    def __init__(self, nc):
        self.nc = nc
        self.streams = {e: [] for e in ENGS}
        self.cnt = {}
        self.seen = {e: {} for e in ENGS}
        self.sems = {}
        self.eng_sem = {e: ("E", e, 0) for e in ENGS}
        self.dma_rr = {e: 0 for e in ENGS}
        self.NDMA = 8
        self.n_ops = 0
        self.pool_dirty = False

    def _need(self, eng, dep):
        if dep is None:
            return
        key, val = dep
        if eng == "pe" and key[0] == "E" and key[1] == "pe":
            return
        if self.seen[eng].get(key, 0) >= val:
            return
        self.seen[eng][key] = val
        self.streams[eng].append(("wait", key, val))

    def _deps(self, eng, reads, writes):
        reads = _roots(reads)
        writes = _roots(writes)
        for t in reads:
            self._need(eng, t.w)
        for t in writes:
            self._need(eng, t.w)
            for d in t.r:
                self._need(eng, d)

    def _mark(self, stamp, reads, writes):
        reads = _roots(reads)
        writes = _roots(writes)
        for t in reads:
            t.r.append(stamp)
            if len(t.r) > 48:
                best = {}
                for k, v in t.r:
                    if best.get(k, 0) < v:
                        best[k] = v
                t.r = list(best.items())
        for t in writes:
            t.w = stamp
            t.r = []

    def op(self, eng, fn, reads=(), writes=()):
        if eng == "pool":
            self.pool_dirty = True
        self._deps(eng, reads, writes)
        key = self.eng_sem[eng]
        c = self.cnt.get(key, 0) + 1
        self.cnt[key] = c
        self.streams[eng].append(("op", fn, key))
        self._mark((key, c), reads, writes)
        self.n_ops += 1
        if c >= 16000:
            self.eng_sem[eng] = ("E", eng, key[2] + 1)

    def dma(self, q, out_t, out_ap, in_t, in_ap):
        reads = [in_t] if in_t is not None else []
        writes = [out_t] if out_t is not None else []
        self._deps(q, reads, writes)
        i = self.dma_rr[q]
        self.dma_rr[q] = (i + 1) % self.NDMA
        key = ("D", q, i)
        prev = self.cnt.get(key, 0)
        if prev:
            self._need(q, (key, prev))
        c = prev + 16
        self.cnt[key] = c

        def fn(e, out_ap=out_ap, in_ap=in_ap):
            return e.dma_start(out=out_ap, in_=in_ap)
        self.streams[q].append(("dma", fn, key))
        self._mark((key, c), reads, writes)
        self.n_ops += 1

    def coll(self, send_t, send_ap, recv_t, recv_ap, groups):
        q = "pool"
        self._deps(q, [send_t], [recv_t])
        key = ("C", len([k for k in self.cnt if k[0] == "C"]))
        self.cnt[key] = 1

        def fn(e):
            return e.collective_compute("AllGather", ALU.bypass, replica_groups=groups, ins=[send_ap.opt()], outs=[recv_ap.opt()])
        self.streams[q].append(("coll", fn, key))
        self._mark((key, 1), [send_t], [recv_t])
        self.n_ops += 1

    def barrier(self):
        for e in ENGS:
            if e == "pool" and not self.pool_dirty:
                continue
            self.wait_all(e)
        self.pool_dirty = False

    def wait_all(self, eng):
        for key, val in list(self.cnt.items()):
            if val:
                self._need(eng, (key, val))

    def emit(self):
        nc = self.nc
        for k in list(self.cnt.keys()):
            self.sems[k] = nc.alloc_semaphore("s_" + "_".join(str(x) for x in k))
        streams, sems = self.streams, self.sems

        def run(e, name):
            for item in streams[name]:
                if item[0] == "wait":
                    e.wait_ge(sems[item[1]], item[2])
                elif item[0] == "op":
                    item[1](e).then_inc(sems[item[2]], 1)
                elif item[0] == "coll":
                    item[1](e).then_inc(sems[item[2]])
                else:
                    item[1](e).then_inc(sems[item[2]], 16)

        with nc.Block() as block:
            @block.tensor
            def _(e):
                run(e, "pe")

            @block.scalar
            def _(e):
                run(e, "act")

            @block.vector
            def _(e):
                run(e, "dve")

            @block.gpsimd
            def _(e):
                run(e, "pool")

            @block.sync
            def _(e):
                run(e, "sp")


class _Rec:
    def __getattr__(self, name):
        def mk(*args, **kw):
            return lambda e: getattr(e, name)(*args, **kw)
        return mk


R = _Rec()


class Arena:
    def __init__(self, nc, size):
        self.nc = nc
        slab = nc.alloc_sbuf_tensor("slab", [128, size], U8)
        self.base = nc.lookup_mloc(slab).addr
        self.size = size
        self.top = 0
        self.peak = 0
        self.hist = []

    def alloc(self, name, shape, dtype):
        nb = int(np.prod(shape[1:])) * (4 if dtype in (F32, I32) else 2)
        nb = (nb + 31) // 32 * 32
        off = self.top
        assert off + nb <= self.size, f"arena overflow {name}: {off}+{nb} > {self.size}"
        self.top += nb
        self.peak = max(self.peak, self.top)
        h = self.nc.alloc_sbuf_tensor_at(name, list(shape), dtype, offset=self.base + off)
        t = T(h, name)
        best = {}
        keep = []
        for (o, n, old) in self.hist:
            if o < off + nb and off < o + n:
                stamps = list(old.r)
                if old.w is not None:
                    stamps.append(old.w)
                for k, v in stamps:
                    if best.get(k, 0) < v:
                        best[k] = v
                if o >= off and o + n <= off + nb:
                    continue
            keep.append((o, n, old))
        t.r = list(best.items())
        keep.append((off, nb, t))
        self.hist = keep
        return t

    def mark(self):
        return self.top

    def reset(self, m):
        self.top = m


class Cfg:
    def __init__(self, NTOK=1024, L=5, NPRE=1024, n_cores=8, dbg=(), stop=None):
        self.NTOK = NTOK
        self.NPRE = NPRE
        self.NKEY = NTOK + NPRE
        self.groups = [[2 * i, 2 * i + 1] for i in range(n_cores // 2)]
        self.L = L
        self.TB = 512
        self.NTB = NTOK // 512
        self.NT = NTOK // 128
        self.dbg = dbg
        self.stop = stop


def build_program(cfg):
    nc = bass.Bass("TRN2", target_bir_lowering=False)
    P = Prog(nc)
    L, NTOK, NT, NTB = cfg.L, cfg.NTOK, cfg.NT, cfg.NTB
    NPRE, NKEY = cfg.NPRE, cfg.NKEY
    NPT = NPRE // 128
    SW = 1024 + 72 + 176 + 1024 + 512

    def din(name, shape, dt=F32):
        return nc.dram_tensor(name, list(shape), dt, kind="ExternalInput").ap()

    x_d = din("x", [NTOK, D])
    mem_d = din("mem", [256, D])
    pos_d = din("pos", [128, NT], I32)
    invf_d = din("invf", [128, 32])
    flag_d = din("flag", [128, 2])
    norm_mix_d = din("norm_mix", [L, D])
    w_in_d = din("w_in", [L, D, NIN])
    dn_conv_d = din("dn_conv", [128, L * 24 * 4])
    a_log_d = din("dn_a_log", [L, 8])
    dt_bias_d = din("dn_dt_bias", [L, 8])
    out_norm_d = din("dn_out_norm", [L, 128])
    q_norm_d = din("mla_q_norm", [128, L * 4])
    w_qb_d = din("mla_w_qb", [L, 512, 1536])
    kv_norm_d = din("mla_kv_norm", [128, L * 2])
    w_kvb_d = din("mla_w_kvb", [L, 256, 2048])
    w_out_d = din("w_out", [L, D, D])
    mem_norm_d = din("mem_norm", [1, D])
    norm_x_d = din("norm_xattn", [L, D])
    wq_d = din("xa_wq", [L, D, D])
    wk_d = din("xa_wk", [L, D, D])
    wv_d = din("xa_wv", [L, D, D])
    wo_d = din("xa_wo", [L, D, D])
    norm_f_d = din("norm_ffn", [L, D])
    w_up_d = din("ffn_w_up", [L, D, 2 * DFF])
    f_conv_d = din("ffn_conv", [128, L * 88 * 3])
    f_bias_d = din("ffn_conv_bias", [128, L * 88])
    w_down_d = din("ffn_w_down", [L, DFF, D])
    norm_fin_d = din("norm_final", [1, D])
    y_d = nc.dram_tensor("y", [NTOK, D], F32, kind="ExternalOutput").ap()
    h_d = nc.dram_tensor("hscr", [NTOK, D], F32, kind="Internal").ap()
    WT = T(None, "weights")
    hT = [T(None, f"h{t}") for t in range(NT)]
    yT = [T(None, f"y{t}") for t in range(NT)]
    dbg_out = {}

    def dbg_dump(name, tile, ap, shape, dt=F32):
        if name not in cfg.dbg:
            return
        d = nc.dram_tensor("dbg_" + name, list(shape), dt, kind="ExternalOutput").ap()
        dbg_out[name] = d
        P.dma("sp", T(None), d, tile, ap)

    A = Arena(nc, 206 * 1024)
    if getattr(cfg, "nobarrier", True):
        P.barrier = lambda: None
    PSB = [T(nc.alloc_psum_tensor(f"psb{i}", [128, 512], F32), f"psb{i}") for i in range(8)]
    LX = [T(PSB[b].ap[:, 0:256], f"lx{b}", parent=PSB[b]) for b in range(4)]
    LYa = [T(PSB[4 + b].ap[:, 0:128], f"lya{b}", parent=PSB[4 + b]) for b in range(4)]
    LYb = [T(PSB[4 + b].ap[:, 128:256], f"lyb{b}", parent=PSB[4 + b]) for b in range(4)]
    memn_scr = nc.dram_tensor("memn_scr", [128, 16 * 256], BF16, kind="Internal").ap()
    memn_T = T(None, "memn_scr")
    send_d = nc.dram_tensor("st_send", [128, SW], F32, kind="Internal").ap()
    recv_d = nc.dram_tensor("st_recv", [256, SW], F32, kind="Internal").ap()
    send_T = T(None, "st_send")
    recv_T = T(None, "st_recv")

    ident = A.alloc("ident", [128, 128], F32)
    ones_f = A.alloc("ones_f", [128, 128], F32)
    ones_b = A.alloc("ones_b", [128, 128], BF16)
    ident_b = A.alloc("ident_b", [128, 128], BF16)
    tri_incl = A.alloc("tri_incl", [128, 128], F32)
    mask2 = A.alloc("mask2", [128, 256], F32)
    negstrict = A.alloc("negstrict", [128, 128], F32)
    sel_last = A.alloc("sel_last", [128, 1], F32)
    cos_t = A.alloc("cos_t", [128, NT, 32], F32)
    sin_t = A.alloc("sin_t", [128, NT, 32], F32)
    qn_g = A.alloc("qn_g", [128, L, 4], F32)
    kvn_g = A.alloc("kvn_g", [128, L, 2], F32)
    dnc_w = A.alloc("dnc_w", [128, L, 24, 4], F32)
    fc_w = A.alloc("fc_w", [128, L, 88, 3], F32)
    fc_b = A.alloc("fc_b", [128, L, 88], F32)
    alog_bc = A.alloc("alog_bc", [128, L, 8], F32)
    dtb_bc = A.alloc("dtb_bc", [128, L, 8], F32)
    onorm_bc = A.alloc("onorm_bc", [128, L, 128], F32)
    KmT = A.alloc("KmT", [128, 16, 256], BF16)
    Vm = A.alloc("Vm", [128, 2, D], BF16)
    ckvT = A.alloc("ckvT", [128, 2, NKEY], BF16)
    kpeT = A.alloc("kpeT", [64, NKEY], BF16)
    flag = A.alloc("flag", [128, 2], F32)
    Sst = A.alloc("Sst", [128, 8, 128], F32)
    S_T = [T(Sst.ap[:, h, :], f"S{h}", parent=Sst) for h in range(8)]
    dn_halo = A.alloc("dn_halo", [128, 24, 3], F32)
    f_halo = A.alloc("f_halo", [128, 88, 2], F32)
    NWB = 3
    wbuf = [A.alloc(f"wbuf{i}", [128, 16, 512], BF16) for i in range(NWB)]
    stat = A.alloc("stat", [128, 16], F32)
    uT = A.alloc("uT", [128, 16, 512], BF16)
    wb_i = [0]

    def sp_load(tile, ap_out, src):
        P.dma("sp", tile, ap_out, WT, src)

    P.op("pool", R.memset(ident.ap[:], 1.0), writes=[ident])
    P.op("pool", R.affine_select(out=ident.ap[:], in_=ident.ap[:], pattern=[[-1, 128]], compare_op=ALU.is_equal,
                                          fill=0.0, base=0, channel_multiplier=1), reads=[ident], writes=[ident])
    P.op("pool", R.tensor_copy(out=ident_b.ap[:], in_=ident.ap[:]), reads=[ident], writes=[ident_b])
    P.op("pool", R.memset(ones_f.ap[:], 1.0), writes=[ones_f])
    P.op("pool", R.memset(ones_b.ap[:], 1.0), writes=[ones_b])
    P.op("pool", R.memset(tri_incl.ap[:], 1.0), writes=[tri_incl])
    P.op("pool", R.affine_select(out=tri_incl.ap[:], in_=tri_incl.ap[:], pattern=[[1, 128]], compare_op=ALU.is_ge,
                                          fill=0.0, base=0, channel_multiplier=-1), reads=[tri_incl], writes=[tri_incl])
    P.op("pool", R.memset(mask2.ap[:], 0.0), writes=[mask2])
    for hh in range(2):
        P.op("pool", R.affine_select(out=mask2.ap[:, hh * 128:(hh + 1) * 128], in_=mask2.ap[:, hh * 128:(hh + 1) * 128],
                                                      pattern=[[1, 128]], compare_op=ALU.is_ge, fill=NEG, base=0, channel_multiplier=-1),
             reads=[mask2], writes=[mask2])
    P.op("pool", R.memset(negstrict.ap[:], -1.0), writes=[negstrict])
    P.op("pool", R.affine_select(out=negstrict.ap[:], in_=negstrict.ap[:], pattern=[[1, 128]], compare_op=ALU.is_gt,
                                          fill=0.0, base=0, channel_multiplier=-1), reads=[negstrict], writes=[negstrict])
    P.op("pool", R.memset(sel_last.ap[:], 1.0), writes=[sel_last])
    P.op("pool", R.affine_select(out=sel_last.ap[:], in_=sel_last.ap[:], pattern=[[0, 1]], compare_op=ALU.is_equal,
                                          fill=0.0, base=-127, channel_multiplier=1), reads=[sel_last], writes=[sel_last])

    nc_allow = nc.allow_non_contiguous_dma(reason="tiny param loads")
    nc_allow.__enter__()
    sp_load(flag, flag.ap[:], flag_d)
    sp_load(qn_g, qn_g.ap[:].rearrange("p l k -> p (l k)"), q_norm_d)
    sp_load(kvn_g, kvn_g.ap[:].rearrange("p l k -> p (l k)"), kv_norm_d)
    sp_load(dnc_w, dnc_w.ap[:].rearrange("p l c k -> p (l c k)"), dn_conv_d)
    sp_load(fc_w, fc_w.ap[:].rearrange("p l c k -> p (l c k)"), f_conv_d)
    sp_load(fc_b, fc_b.ap[:].rearrange("p l c -> p (l c)"), f_bias_d)
    for l in range(L):
        sp_load(alog_bc, alog_bc.ap[:, l, :], a_log_d[l:l + 1, :].partition_broadcast(128))
        sp_load(dtb_bc, dtb_bc.ap[:, l, :], dt_bias_d[l:l + 1, :].partition_broadcast(128))
        sp_load(onorm_bc, onorm_bc.ap[:, l, :], out_norm_d[l:l + 1, :].partition_broadcast(128))
    P.op("act", R.activation(out=alog_bc.ap[:].rearrange("p l h -> p (l h)"), in_=alog_bc.ap[:].rearrange("p l h -> p (l h)"), func=AF.Exp),
         reads=[alog_bc], writes=[alog_bc])

    m0 = A.mark()
    pos_i = A.alloc("pos_i", [128, NT], I32)
    pos_f = A.alloc("pos_f", [128, NT], F32)
    invf = A.alloc("invf", [128, 32], F32)
    ang = A.alloc("ang", [128, NT, 32], F32)
    kq = A.alloc("kq", [128, NT, 32], F32)
    ki = A.alloc("ki", [128, NT, 32], I32)
    rr = A.alloc("rr", [128, NT, 32], F32)
    sp_load(pos_i, pos_i.ap[:], pos_d)
    sp_load(invf, invf.ap[:], invf_d)
    P.op("dve", R.tensor_copy(out=pos_f.ap[:], in_=pos_i.ap[:]), reads=[pos_i], writes=[pos_f])
    for t in range(NT):
        P.op("dve", R.tensor_scalar(out=ang.ap[:, t, :], in0=invf.ap[:], scalar1=pos_f.ap[:, t:t + 1], scalar2=None, op0=ALU.mult),
             reads=[invf, pos_f], writes=[ang])
    TWO_PI = 2.0 * np.pi
    C1 = 6.28125
    C2 = TWO_PI - C1
    fl = lambda ap: ap[:].rearrange("p t j -> p (t j)")
    for which, tab in ((0, sin_t), (1, cos_t)):
        shift = 0.0 if which == 0 else np.pi / 2
        P.op("dve", R.tensor_scalar(out=fl(kq.ap), in0=fl(ang.ap), scalar1=float(shift), scalar2=float(1.0 / TWO_PI), op0=ALU.add, op1=ALU.mult),
             reads=[ang], writes=[kq])
        P.op("dve", R.tensor_copy(out=fl(ki.ap), in_=fl(kq.ap)), reads=[kq], writes=[ki])
        P.op("dve", R.tensor_copy(out=fl(kq.ap), in_=fl(ki.ap)), reads=[ki], writes=[kq])
        P.op("dve", R.scalar_tensor_tensor(out=fl(rr.ap), in0=fl(kq.ap), scalar=float(-C1), in1=fl(ang.ap), op0=ALU.mult, op1=ALU.add),
             reads=[kq, ang], writes=[rr])
        P.op("dve", R.scalar_tensor_tensor(out=fl(rr.ap), in0=fl(kq.ap), scalar=float(-C2), in1=fl(rr.ap), op0=ALU.mult, op1=ALU.add),
             reads=[kq, rr], writes=[rr])
        if which == 1:
            P.op("dve", R.tensor_scalar(out=fl(rr.ap), in0=fl(rr.ap), scalar1=float(shift), scalar2=None, op0=ALU.add),
                 reads=[rr], writes=[rr])
        P.op("dve", R.tensor_scalar(out=fl(rr.ap), in0=fl(rr.ap), scalar1=float(3.1415925), scalar2=float(-3.1415925), op0=ALU.min, op1=ALU.max),
             reads=[rr], writes=[rr])
        P.op("act", R.activation(out=fl(tab.ap), in_=fl(rr.ap), func=AF.Sin), reads=[rr], writes=[tab])
    dbg_dump("cos", cos_t, cos_t.ap[:], [128, NT, 32])
    dbg_dump("sin", sin_t, sin_t.ap[:], [128, NT, 32])
    P.barrier()
    A.reset(m0)

    evac_rr = [0]

    def evac_copy(out_t, out_ap, ps_t, ps_ap, eng=None):
        if eng is None:
            eng = "act" if evac_rr[0] % 2 == 0 else "dve"
            evac_rr[0] += 1
        if eng == "act":
            P.op("act", R.copy(out=out_ap, in_=ps_ap), reads=[ps_t], writes=[out_t])
        else:
            P.op("dve", R.tensor_copy(out=out_ap, in_=ps_ap), reads=[ps_t], writes=[out_t])

    def wload(src_ap, nkt, ncols):
        wb = wbuf[wb_i[0] % NWB]
        wb_i[0] += 1
        P.dma("pool", wb, wb.ap[:, 0:nkt, 0:ncols], WT, src_ap.rearrange("(kt p) c -> p kt c", p=128))
        return wb

    def rstd_from_ss(ss_ap_fn, scale, tiles):
        P.op("dve", R.tensor_scalar(out=ss_ap_fn(), in0=ss_ap_fn(), scalar1=float(scale), scalar2=float(EPS), op0=ALU.mult, op1=ALU.add),
             reads=tiles, writes=tiles)
        P.op("act", R.activation(out=ss_ap_fn(), in_=ss_ap_fn(), func=AF.Sqrt), reads=tiles, writes=tiles)
        P.op("dve", R.reciprocal(out=ss_ap_fn(), in_=ss_ap_fn()), reads=tiles, writes=tiles)

    def norm_block(src_d, src_T, row0, gain_row_ap, ntile, dst_uT, to_y=None):
        mk = A.mark()
        hx = [A.alloc(f"hx{t}", [128, D], F32) for t in range(ntile)]
        g_mix = A.alloc("g_mix", [128, D], F32)
        junk = A.alloc("junk", [128, D], BF16)
        hxb2 = A.alloc("hxb2", [128, D], BF16)
        sp_load(g_mix, g_mix.ap[:], gain_row_ap.partition_broadcast(128))
        for t in range(ntile):
            P.dma("sp", hx[t], hx[t].ap[:], src_T[row0 // 128 + t], src_d[row0 + t * 128: row0 + (t + 1) * 128, :])
        norm_core(hx, g_mix, junk, row0, ntile, dst_uT, to_y, hxb2)
        P.barrier()
        A.reset(mk)

    def norm_core(hx, g_mix, junk, row0, ntile, dst_uT, to_y, hxb2):
        hxb = [junk, hxb2]
        for t in range(ntile):
            P.op("act", R.activation(out=junk.ap[:], in_=hx[t].ap[:], func=AF.Square, accum_out=stat.ap[:, t:t + 1]),
                 reads=[hx[t]], writes=[junk, stat])
        rstd_from_ss(lambda: stat.ap[:, 0:ntile], 1.0 / D, [stat])
        for t in range(ntile):
            if to_y is not None:
                P.op("dve", R.scalar_tensor_tensor(out=hx[t].ap[:], in0=hx[t].ap[:], scalar=stat.ap[:, t:t + 1], in1=g_mix.ap[:],
                                                   op0=ALU.mult, op1=ALU.mult), reads=[hx[t], stat, g_mix], writes=[hx[t]])
                gt = row0 // 128 + t
                P.dma("sp", to_y[1][gt], to_y[0][gt * 128:(gt + 1) * 128, :], hx[t], hx[t].ap[:])
                continue
            hb = hxb[t % 2]
            P.op("dve", R.scalar_tensor_tensor(out=hb.ap[:], in0=hx[t].ap[:], scalar=stat.ap[:, t:t + 1], in1=g_mix.ap[:],
                                               op0=ALU.mult, op1=ALU.mult), reads=[hx[t], stat, g_mix], writes=[hb])
            for g in range(4):
                ps = PSB[4 + (g % 4)]
                psb16 = ps.ap[:].bitcast(BF16)
                for j in range(4):
                    kt = g * 4 + j
                    P.op("pe", R.transpose(out=psb16[:, j * 128:(j + 1) * 128], in_=hb.ap[:, kt * 128:(kt + 1) * 128], identity=ident_b.ap[:]),
                         reads=[hb, ident_b], writes=[ps])
                evac_copy(dst_uT, dst_uT.ap[:, g * 4:(g + 1) * 4, t * 128:(t + 1) * 128], ps, psb16[:, 0:512].rearrange("p (a b) -> p a b", a=4))

    def residual_add_dense(actT, nkt_total, w_d2, tb, nxt=None):
        mk = A.mark()
        hx = [A.alloc(f"hxr{t}", [128, D], F32) for t in range(4)]
        if nxt is not None:
            g_mix = A.alloc("g_mixr", [128, D], F32)
            junk = A.alloc("junkr", [128, D], BF16)
            hxb2 = A.alloc("hxb2r", [128, D], BF16) if nxt.get("to_y") is None else None
            sp_load(g_mix, g_mix.ap[:], nxt["gain"].partition_broadcast(128))
        for t in range(4):
            P.dma("sp", hx[t], hx[t].ap[:], hT[tb * 4 + t], h_d[(tb * 4 + t) * 128:(tb * 4 + t + 1) * 128, :])
        chunks = []
        k0 = 0
        while k0 < nkt_total:
            chunks.append((k0, min(16, nkt_total - k0)))
            k0 += 16
        for cb in range(4):
            for ci, (k0, nk) in enumerate(chunks):
                wb = wload(w_d2[k0 * 128:(k0 + nk) * 128, cb * 512:(cb + 1) * 512], nk, 512)
                for t in range(4):
                    ps = PSB[t]
                    for kk in range(nk):
                        kt = k0 + kk
                        P.op("pe", R.matmul(ps.ap[:], lhsT=actT.ap[:, kt, t * 128:(t + 1) * 128], rhs=wb.ap[:, kk, :],
                                                                                     start=(kt == 0), stop=(kt == nkt_total - 1)),
                             reads=[actT, wb], writes=[ps])
            for t in range(4):
                ps = PSB[t]
                P.op("dve", R.tensor_tensor(out=hx[t].ap[:, cb * 512:(cb + 1) * 512], in0=hx[t].ap[:, cb * 512:(cb + 1) * 512], in1=ps.ap[:], op=ALU.add),
                     reads=[hx[t], ps], writes=[hx[t]])
        if nxt is None or not nxt.get("skip_store"):
            for t in range(4):
                P.dma("sp", hT[tb * 4 + t], h_d[(tb * 4 + t) * 128:(tb * 4 + t + 1) * 128, :], hx[t], hx[t].ap[:])
        if nxt is not None:
            norm_core(hx, g_mix, junk, tb * 512, 4, nxt.get("dst"), nxt.get("to_y"), hxb2)
        P.barrier()
        A.reset(mk)

    for t in range(NT):
        P.dma("sp", hT[t], h_d[t * 128:(t + 1) * 128, :], WT, x_d[t * 128:(t + 1) * 128, :])

    mz = A.mark()
    ztile = A.alloc("ztile", [128, SW], F32)
    P.op("dve", R.memset(ztile.ap[:], 0.0), writes=[ztile])
    P.dma("sp", send_T, send_d, ztile, ztile.ap[:])
    P.barrier()
    A.reset(mz)

    norm_block(mem_d, [WT, WT], 0, mem_norm_d[0:1, :], 2, uT)
    P.dma("sp", memn_T, memn_scr.rearrange("p (k m) -> p k m", k=16), uT, uT.ap[:, :, 0:256])
    P.barrier()

    for l in range(L):
        mk0 = A.mark()
        memnT = A.alloc("memnT", [128, 16, 256], BF16)
        P.dma("sp", memnT, memnT.ap[:], memn_T, memn_scr.rearrange("p (k m) -> p k m", k=16))
        if l == 0:
            dbg_dump("memnT", memnT, memnT.ap[:], [128, 16, 256], BF16)
        for cb in range(4):
            wb = wload(wk_d[l][:, cb * 512:(cb + 1) * 512], 16, 512)
            for c in range(4):
                ps = PSB[c]
                for kt in range(16):
                    P.op("pe", R.matmul(ps.ap[:, 0:256], lhsT=wb.ap[:, kt, c * 128:(c + 1) * 128], rhs=memnT.ap[:, kt, :],
                                                                         start=(kt == 0), stop=(kt == 15)), reads=[wb, memnT], writes=[ps])
                evac_copy(KmT, KmT.ap[:, cb * 4 + c, :], ps, ps.ap[:, 0:256])
        for cb in range(4):
            wb = wload(wv_d[l][:, cb * 512:(cb + 1) * 512], 16, 512)
            for m in range(2):
                ps = PSB[4 + m]
                for kt in range(16):
                    P.op("pe", R.matmul(ps.ap[:], lhsT=memnT.ap[:, kt, m * 128:(m + 1) * 128], rhs=wb.ap[:, kt, :],
                                                                         start=(kt == 0), stop=(kt == 15)), reads=[wb, memnT], writes=[ps])
                evac_copy(Vm, Vm.ap[:, m, cb * 512:(cb + 1) * 512], ps, ps.ap[:])
        P.barrier()
        A.reset(mk0)
        P.coll(send_T, send_d, recv_T, recv_d, cfg.groups)
        P.dma("sp", Sst, Sst.ap[:].rearrange("p h d -> p (h d)"), recv_T, recv_d[0:128, 0:1024])
        P.dma("sp", dn_halo, dn_halo.ap[:].rearrange("p c k -> p (c k)"), recv_T, recv_d[0:128, 1024:1096])
        P.dma("sp", f_halo, f_halo.ap[:].rearrange("p c k -> p (c k)"), recv_T, recv_d[0:128, 1096:1272])
        P.dma("sp", ckvT, ckvT.ap[:, :, 0:NPRE].bitcast(F32), recv_T, recv_d[0:128, 1272:2296].rearrange("p (k t) -> p k t", k=2))
        P.dma("sp", kpeT, kpeT.ap[:, 0:NPRE].bitcast(F32), recv_T, recv_d[0:64, 2296:2808])
        P.op("dve", R.tensor_scalar(out=Sst.ap[:].rearrange("p h d -> p (h d)"), in0=Sst.ap[:].rearrange("p h d -> p (h d)"), scalar1=flag.ap[:, 0:1], scalar2=None, op0=ALU.mult),
             reads=[Sst, flag], writes=[Sst] + S_T)
        P.op("dve", R.tensor_scalar(out=dn_halo.ap[:].rearrange("p c k -> p (c k)"), in0=dn_halo.ap[:].rearrange("p c k -> p (c k)"), scalar1=flag.ap[:, 0:1], scalar2=None, op0=ALU.mult),
             reads=[dn_halo, flag], writes=[dn_halo])
        P.op("dve", R.tensor_scalar(out=f_halo.ap[:].rearrange("p c k -> p (c k)"), in0=f_halo.ap[:].rearrange("p c k -> p (c k)"), scalar1=flag.ap[:, 0:1], scalar2=None, op0=ALU.mult),
             reads=[f_halo, flag], writes=[f_halo])

        for tb in range(NTB):
            tok0 = tb * 512
            norm_block(h_d, hT, tok0, norm_mix_d[l:l + 1, :], 4, uT)
            if l == 0 and tb == 0:
                dbg_dump("uT0", uT, uT.ap[:], [128, 16, 512], BF16)
            mA = A.mark()
            oT = A.alloc("oT", [128, 16, 512], BF16)
            mB = A.mark()
            lat_q = A.alloc("lat_q", [128, 4, 512], F32)
            lat_kv = A.alloc("lat_kv", [128, 4, 320], F32)
            qlatT = A.alloc("qlatT", [128, 4, 512], BF16)
            qpe = A.alloc("qpe", [128, 512], F32)
            qpe_r = A.alloc("qpe_r", [128, 4, 512], F32)
            rtmp = A.alloc("rtmp", [128, 2, 256], F32)
            qpeT = A.alloc("qpeT", [64, 8, 512], BF16)
            KhT2 = [A.alloc(f"KhT{i}", [128, NKEY], BF16) for i in range(2)]
            Vh2 = [A.alloc(f"Vh{i}", [128, NKEY // 128, 128], BF16) for i in range(2)]
            QhT2 = [A.alloc(f"QhT{i}", [128, 512], BF16) for i in range(2)]
            PT = [A.alloc(f"PT{i}", [128, 512], BF16) for i in range(3)]
            rec = A.alloc("rec", [128, 512], F32)
            junkm = A.alloc("junkm", [128, 512], BF16)
            wb1 = wload(w_in_d[l][:, 4112:4624], 16, 512)
            wb2 = wload(w_in_d[l][:, 4624:4944], 16, 320)
            for t in range(4):
                ps = PSB[t % 2]
                for kt in range(16):
                    P.op("pe", R.matmul(ps.ap[:], lhsT=uT.ap[:, kt, t * 128:(t + 1) * 128], rhs=wb1.ap[:, kt, :],
                                                                    start=(kt == 0), stop=(kt == 15)), reads=[uT, wb1], writes=[ps])
                P.op("act", R.copy(out=lat_q.ap[:, t, :], in_=ps.ap[:]), reads=[ps], writes=[lat_q])
                P.op("act", R.activation(out=junkm.ap[:], in_=lat_q.ap[:, t, :], func=AF.Square, accum_out=stat.ap[:, t:t + 1]),
                     reads=[lat_q], writes=[junkm, stat])
                ps2 = PSB[2 + t % 2]
                for kt in range(16):
                    P.op("pe", R.matmul(ps2.ap[:, 0:320], lhsT=uT.ap[:, kt, t * 128:(t + 1) * 128], rhs=wb2.ap[:, kt, 0:320],
                                                                      start=(kt == 0), stop=(kt == 15)), reads=[uT, wb2], writes=[ps2])
                P.op("dve", R.tensor_copy(out=lat_kv.ap[:, t, :], in_=ps2.ap[:, 0:320]), reads=[ps2], writes=[lat_kv])
                P.op("act", R.activation(out=junkm.ap[:, 0:256], in_=lat_kv.ap[:, t, 0:256], func=AF.Square, accum_out=stat.ap[:, 4 + t:5 + t]),
                     reads=[lat_kv], writes=[junkm, stat])
            rstd_from_ss(lambda: stat.ap[:, 0:4], 1.0 / 512, [stat])
            rstd_from_ss(lambda: stat.ap[:, 4:8], 1.0 / 256, [stat])
            if l == 0 and tb == 0:
                dbg_dump("lat_q", lat_q, lat_q.ap[:], [128, 4, 512])
                dbg_dump("lat_kv", lat_kv, lat_kv.ap[:], [128, 4, 320])
            wqb = wbuf[wb_i[0] % NWB]
            wb_i[0] += 1
            wkvb = wbuf[wb_i[0] % NWB]
            wb_i[0] += 1
            wqb_v = wqb.ap[:].rearrange("p k c -> p (k c)")[:, 0:4 * 1536].rearrange("p (k c) -> p k c", k=4)
            wkvb_v = wkvb.ap[:].rearrange("p k c -> p (k c)")[:, 0:2 * 2048].rearrange("p (k c) -> p k c", k=2)
            P.dma("pool", wqb, wqb_v, WT, w_qb_d[l].rearrange("(kt p) c -> p kt c", p=128))
            P.dma("pool", wkvb, wkvb_v, WT, w_kvb_d[l].rearrange("(kt p) c -> p kt c", p=128))
            wq4 = wqb_v.rearrange("p k (h d) -> p k h d", h=8)
            wkv4 = wkvb_v.rearrange("p k (h d) -> p k h d", h=8)
            for t in range(4):
                gt = tb * 4 + t
                P.op("dve", R.tensor_scalar(out=lat_q.ap[:, t, :], in0=lat_q.ap[:, t, :], scalar1=stat.ap[:, t:t + 1], scalar2=None, op0=ALU.mult),
                     reads=[lat_q, stat], writes=[lat_q])
                P.op("dve", R.tensor_scalar(out=lat_kv.ap[:, t, 0:256], in0=lat_kv.ap[:, t, 0:256], scalar1=stat.ap[:, 4 + t:5 + t], scalar2=None, op0=ALU.mult),
                     reads=[lat_kv, stat], writes=[lat_kv])
                x1 = lat_kv.ap[:, t, 256:288]
                x2 = lat_kv.ap[:, t, 288:320]
                cs = cos_t.ap[:, gt, :]
                sn = sin_t.ap[:, gt, :]
                r = rtmp.ap[:, 0, :]
                P.op("dve", R.tensor_tensor(out=r[:, 0:32], in0=x1, in1=cs, op=ALU.mult), reads=[lat_kv, cos_t], writes=[rtmp])
                P.op("dve", R.tensor_tensor(out=r[:, 32:64], in0=x2, in1=sn, op=ALU.mult), reads=[lat_kv, sin_t], writes=[rtmp])
                P.op("dve", R.tensor_tensor(out=r[:, 64:96], in0=x2, in1=cs, op=ALU.mult), reads=[lat_kv, cos_t], writes=[rtmp])
                P.op("dve", R.tensor_tensor(out=r[:, 96:128], in0=x1, in1=sn, op=ALU.mult), reads=[lat_kv, sin_t], writes=[rtmp])
                P.op("dve", R.tensor_tensor(out=x1, in0=r[:, 0:32], in1=r[:, 32:64], op=ALU.subtract), reads=[rtmp], writes=[lat_kv])
                P.op("dve", R.tensor_tensor(out=x2, in0=r[:, 64:96], in1=r[:, 96:128], op=ALU.add), reads=[rtmp], writes=[lat_kv])
                ps = PSB[4 + t % 2]
                for j in range(2):
                    P.op("pe", R.transpose(out=ps.ap[:, j * 128:(j + 1) * 128], in_=lat_kv.ap[:, t, j * 128:(j + 1) * 128], identity=ident.ap[:]),
                         reads=[lat_kv, ident], writes=[ps])
                P.op("pe", R.transpose(out=ps.ap[0:64, 256:384], in_=lat_kv.ap[:, t, 256:320], identity=ident.ap[:]),
                     reads=[lat_kv, ident], writes=[ps])
                for j in range(2):
                    P.op("act", R.activation(out=ckvT.ap[:, j, (NPT + gt) * 128:(NPT + gt + 1) * 128], in_=ps.ap[:, j * 128:(j + 1) * 128], func=AF.Copy,
                                                                          scale=kvn_g.ap[:, l, j:j + 1]), reads=[ps, kvn_g], writes=[ckvT])
                P.op("dve", R.tensor_copy(out=kpeT.ap[:, (NPT + gt) * 128:(NPT + gt + 1) * 128], in_=ps.ap[0:64, 256:384]), reads=[ps], writes=[kpeT])
            for kt in range(4):
                ps = PSB[6 + kt % 2]
                for t in range(4):
                    P.op("pe", R.transpose(out=ps.ap[:, t * 128:(t + 1) * 128], in_=lat_q.ap[:, t, kt * 128:(kt + 1) * 128], identity=ident.ap[:]),
                         reads=[lat_q, ident], writes=[ps])
                P.op("act", R.activation(out=qlatT.ap[:, kt, :], in_=ps.ap[:], func=AF.Copy, scale=qn_g.ap[:, l, kt:kt + 1]),
                     reads=[ps, qn_g], writes=[qlatT])
            if l == 0 and tb == 0:
                dbg_dump("qlatT", qlatT, qlatT.ap[:], [128, 4, 512], BF16)
                dbg_dump("ckvT", ckvT, ckvT.ap[:, :, NPRE:NPRE + 512], [128, 2, 512], BF16)
                dbg_dump("kpeT", kpeT, kpeT.ap[:, NPRE:NPRE + 512], [64, 512], BF16)
            for t in range(4):
                gt = tb * 4 + t
                ps = PSB[t % 2]
                for kt in range(4):
                    P.op("pe", R.matmul(ps.ap[:].rearrange("p (h d) -> p h d", h=8), lhsT=qlatT.ap[:, kt, t * 128:(t + 1) * 128],
                                                                    rhs=wq4[:, kt, :, 128:192], start=(kt == 0), stop=(kt == 3)), reads=[qlatT, wqb], writes=[ps])
                P.op("act", R.copy(out=qpe.ap[:], in_=ps.ap[:]), reads=[ps], writes=[qpe])
                xv = qpe.ap[:].rearrange("p (h d) -> p h d", h=8)
                ov = qpe_r.ap[:, t, :].rearrange("p (h d) -> p h d", h=8)
                cb_ = cos_t.ap[:, gt, :].unsqueeze(1).broadcast_to([128, 8, 32])
                sb_ = sin_t.ap[:, gt, :].unsqueeze(1).broadcast_to([128, 8, 32])
                r0 = rtmp.ap[:, 0, :].rearrange("p (h d) -> p h d", h=8)
                r1 = rtmp.ap[:, 1, :].rearrange("p (h d) -> p h d", h=8)
                P.op("dve", R.tensor_tensor(out=r0, in0=xv[:, :, 0:32], in1=cb_, op=ALU.mult), reads=[qpe, cos_t], writes=[rtmp])
                P.op("dve", R.tensor_tensor(out=r1, in0=xv[:, :, 32:64], in1=sb_, op=ALU.mult), reads=[qpe, sin_t], writes=[rtmp])
                P.op("dve", R.tensor_tensor(out=ov[:, :, 0:32], in0=r0, in1=r1, op=ALU.subtract), reads=[rtmp], writes=[qpe_r])
                P.op("dve", R.tensor_tensor(out=r0, in0=xv[:, :, 32:64], in1=cb_, op=ALU.mult), reads=[qpe, cos_t], writes=[rtmp])
                P.op("dve", R.tensor_tensor(out=r1, in0=xv[:, :, 0:32], in1=sb_, op=ALU.mult), reads=[qpe, sin_t], writes=[rtmp])
                P.op("dve", R.tensor_tensor(out=ov[:, :, 32:64], in0=r0, in1=r1, op=ALU.add), reads=[rtmp], writes=[qpe_r])
            for h in range(8):
                ps = PSB[4 + h % 2]
                for t in range(4):
                    P.op("pe", R.transpose(out=ps.ap[0:64, t * 128:(t + 1) * 128], in_=qpe_r.ap[:, t, h * 64:(h + 1) * 64], identity=ident.ap[:]),
                         reads=[qpe_r, ident], writes=[ps])
                evac_copy(qpeT, qpeT.ap[:, h, :], ps, ps.ap[0:64, :])
            if l == 0 and tb == 0:
                dbg_dump("qpeT", qpeT, qpeT.ap[:], [64, 8, 512], BF16)
            nkt_keys = NPT + (tb + 1) * 4
            sc = float(192 ** -0.5)
            for h in range(8):
                KhT, Vh, QhT = KhT2[h % 2], Vh2[h % 2], QhT2[h % 2]
                for nb in range(nkt_keys // 4):
                    ps = PSB[nb % 2]
                    for kt in range(2):
                        P.op("pe", R.matmul(ps.ap[:], lhsT=wkv4[:, kt, h, 0:128], rhs=ckvT.ap[:, kt, nb * 512:(nb + 1) * 512],
                                                                               start=(kt == 0), stop=(kt == 1)), reads=[wkvb, ckvT], writes=[ps])
                    evac_copy(KhT, KhT.ap[:, nb * 512:(nb + 1) * 512], ps, ps.ap[:])
                for g in range(0, nkt_keys, 4):
                    ps = PSB[2 + (g // 4) % 2]
                    for j in range(4):
                        kt_ = g + j
                        for kt in range(2):
                            P.op("pe", R.matmul(ps.ap[:, j * 128:(j + 1) * 128], lhsT=ckvT.ap[:, kt, kt_ * 128:(kt_ + 1) * 128],
                                                                                         rhs=wkv4[:, kt, h, 128:256], start=(kt == 0), stop=(kt == 1)),
                                 reads=[wkvb, ckvT], writes=[ps])
                    evac_copy(Vh, Vh.ap[:, g:g + 4, :], ps, ps.ap[:].rearrange("p (a b) -> p a b", a=4))
                ps = PSB[4]
                for kt in range(4):
                    P.op("pe", R.matmul(ps.ap[:], lhsT=wq4[:, kt, h, 0:128], rhs=qlatT.ap[:, kt, :], start=(kt == 0), stop=(kt == 3)),
                         reads=[wqb, qlatT], writes=[ps])
                evac_copy(QhT, QhT.ap[:], ps, ps.ap[:])
                if l == 0 and tb == 0 and h == 0:
                    dbg_dump("KhT", KhT, KhT.ap[:, 0:512], [128, 512], BF16)
                    dbg_dump("Vh", Vh, Vh.ap[:, 0:4, :], [128, 4, 128], BF16)
                    dbg_dump("QhT", QhT, QhT.ap[:], [128, 512], BF16)
                psO = PSB[5]
                psD = PSB[6]

                def scores(kt_, h=h):
                    ps = PSB[(kt_ % 2) * 7]
                    pt = PT[kt_ % 3]
                    P.op("pe", R.matmul(ps.ap[:], lhsT=KhT.ap[:, kt_ * 128:(kt_ + 1) * 128], rhs=QhT.ap[:], start=True, stop=False),
                         reads=[KhT, QhT], writes=[ps])
                    P.op("pe", R.matmul(ps.ap[:], lhsT=kpeT.ap[:, kt_ * 128:(kt_ + 1) * 128], rhs=qpeT.ap[:, h, :], start=False, stop=True),
                         reads=[kpeT, qpeT], writes=[ps])
                    if kt_ < NPT:
                        P.op("act", R.activation(out=pt.ap[:], in_=ps.ap[:], func=AF.Exp, scale=sc, bias=flag.ap[:, 1:2]), reads=[ps, flag], writes=[pt])
                    else:
                        P.op("act", R.activation(out=pt.ap[:], in_=ps.ap[:], func=AF.Exp, scale=sc), reads=[ps], writes=[pt])
                    j = kt_ - NPT - tb * 4
                    if j >= 0:
                        if j > 0:
                            P.op("dve", R.memset(pt.ap[:, 0:j * 128], 0.0), reads=[pt], writes=[pt])
                        P.op("dve", R.memset(pt.ap[64:128, j * 128:j * 128 + 64], 0.0), reads=[pt], writes=[pt])

                def pv(kt_):
                    pt = PT[kt_ % 3]
                    P.op("pe", R.matmul(psO.ap[:], lhsT=Vh.ap[:, kt_, :], rhs=pt.ap[:], start=(kt_ == 0), stop=(kt_ == nkt_keys - 1)),
                         reads=[Vh, pt], writes=[psO])
                    P.op("pe", R.matmul(psD.ap[:], lhsT=ones_b.ap[:], rhs=pt.ap[:], start=(kt_ == 0), stop=(kt_ == nkt_keys - 1)),
                         reads=[ones_b, pt], writes=[psD])
                scores(0)
                for kt_ in range(nkt_keys):
                    if kt_ + 1 < nkt_keys:
                        scores(kt_ + 1)
                    pv(kt_)
                P.op("dve", R.reciprocal(out=rec.ap[:], in_=psD.ap[:]), reads=[psD], writes=[rec])
                P.op("dve", R.tensor_tensor(out=oT.ap[:, 8 + h, :], in0=psO.ap[:], in1=rec.ap[:], op=ALU.mult), reads=[psO, rec], writes=[oT])
            if l == 0 and tb == 0:
                dbg_dump("oT_mla", oT, oT.ap[:, 8:16, :], [128, 8, 512], BF16)
            P.barrier()
            A.reset(mB)
            if cfg.stop == "mla":
                A.reset(mA)
                continue
            dn_section(P, A, nc, cfg, l, tb, locals())
            P.barrier()
            A.reset(mB)
            if l == 0 and tb == 0:
                dbg_dump("oT_dn", oT, oT.ap[:, 0:8, :], [128, 8, 512], BF16)
            if cfg.stop in ("dn", "dnpre"):
                P.barrier()
                A.reset(mA)
                continue
            residual_add_dense(oT, 16, w_out_d[l], tb, nxt=dict(gain=norm_x_d[l:l + 1, :], dst=uT))
            A.reset(mA)
            mA = A.mark()
            oxT = A.alloc("oxT", [128, 16, 512], BF16)
            mX = A.mark()
            qxT = A.alloc("qxT", [128, 16, 512], BF16)
            PTx = [A.alloc(f"PTx{i}", [128, 512], BF16) for i in range(2)]
            recx = A.alloc("recx", [128, 512], F32)
            for cb in range(4):
                wb = wload(wq_d[l][:, cb * 512:(cb + 1) * 512], 16, 512)
                for c in range(4):
                    ps = PSB[c]
                    for kt in range(16):
                        P.op("pe", R.matmul(ps.ap[:], lhsT=wb.ap[:, kt, c * 128:(c + 1) * 128], rhs=uT.ap[:, kt, :],
                                                                             start=(kt == 0), stop=(kt == 15)), reads=[wb, uT], writes=[ps])
                    evac_copy(qxT, qxT.ap[:, cb * 4 + c, :], ps, ps.ap[:])
            scx = float(512 ** -0.5)
            for hd in range(4):
                for m in range(2):
                    ps = PSB[4 + m]
                    for c in range(4):
                        P.op("pe", R.matmul(ps.ap[:], lhsT=KmT.ap[:, hd * 4 + c, m * 128:(m + 1) * 128], rhs=qxT.ap[:, hd * 4 + c, :],
                                                                             start=(c == 0), stop=(c == 3)), reads=[KmT, qxT], writes=[ps])
                    P.op("act", R.activation(out=PTx[m].ap[:], in_=ps.ap[:], func=AF.Exp, scale=scx), reads=[ps], writes=[PTx[m]])
                psD = PSB[6]
                for m in range(2):
                    P.op("pe", R.matmul(psD.ap[:], lhsT=ones_b.ap[:], rhs=PTx[m].ap[:], start=(m == 0), stop=(m == 1)), reads=[ones_b, PTx[m]], writes=[psD])
                P.op("dve", R.reciprocal(out=recx.ap[:], in_=psD.ap[:]), reads=[psD], writes=[recx])
                for c in range(4):
                    ps = PSB[c]
                    for m in range(2):
                        P.op("pe", R.matmul(ps.ap[:], lhsT=Vm.ap[:, m, (hd * 4 + c) * 128:(hd * 4 + c + 1) * 128], rhs=PTx[m].ap[:],
                                                                             start=(m == 0), stop=(m == 1)), reads=[Vm, PTx[m]], writes=[ps])
                    P.op("dve", R.tensor_tensor(out=oxT.ap[:, hd * 4 + c, :], in0=ps.ap[:], in1=recx.ap[:], op=ALU.mult), reads=[ps, recx], writes=[oxT])
            P.barrier()
            A.reset(mX)
            residual_add_dense(oxT, 16, wo_d[l], tb, nxt=dict(gain=norm_f_d[l:l + 1, :], dst=uT))
            A.reset(mA)
            mA = A.mark()
            aT = A.alloc("aT", [128, 44, 512], BF16)
            mF = A.mark()
            sg = A.alloc("sg", [128, 4, 512], F32)
            raw = [A.alloc(f"raw{i}", [128, 514], F32) for i in range(2)]
            cacc = [A.alloc(f"cacc{i}", [128, 512], F32) for i in range(2)]
            ri = 0
            for cb in range(11):
                for part in range(2):
                    col0 = part * DFF + cb * 512
                    wb = wload(w_up_d[l][:, col0:col0 + 512], 16, 512)
                    for c in range(4):
                        ct = (col0 // 128) + c
                        ps = PSB[c + 4 * part]
                        for kt in range(16):
                            P.op("pe", R.matmul(ps.ap[:], lhsT=wb.ap[:, kt, c * 128:(c + 1) * 128], rhs=uT.ap[:, kt, :],
                                                                                 start=(kt == 0), stop=(kt == 15)), reads=[wb, uT], writes=[ps])
                        rw = raw[ri % 2]
                        ca = cacc[ri % 2]
                        ri += 1
                        P.op("act", R.copy(out=rw.ap[:, 2:514], in_=ps.ap[:]), reads=[ps], writes=[rw])
                        P.op("dve", R.tensor_copy(out=rw.ap[:, 0:2], in_=f_halo.ap[:, ct, :]), reads=[f_halo], writes=[rw])
                        P.op("dve", R.tensor_copy(out=f_halo.ap[:, ct, :], in_=rw.ap[:, 512:514]), reads=[rw], writes=[f_halo])
                        P.op("dve", R.tensor_scalar(out=ca.ap[:], in0=rw.ap[:, 0:512], scalar1=fc_w.ap[:, l, ct, 0:1], scalar2=fc_b.ap[:, l, ct:ct + 1],
                                                                                  op0=ALU.mult, op1=ALU.add), reads=[rw, fc_w, fc_b], writes=[ca])
                        P.op("dve", R.scalar_tensor_tensor(out=ca.ap[:], in0=rw.ap[:, 1:513], scalar=fc_w.ap[:, l, ct, 1:2], in1=ca.ap[:],
                                                                                         op0=ALU.mult, op1=ALU.add), reads=[rw, fc_w, ca], writes=[ca])
                        P.op("dve", R.scalar_tensor_tensor(out=ca.ap[:], in0=rw.ap[:, 2:514], scalar=fc_w.ap[:, l, ct, 2:3], in1=ca.ap[:],
                                                                                         op0=ALU.mult, op1=ALU.add), reads=[rw, fc_w, ca], writes=[ca])
                        if part == 0:
                            P.op("act", R.activation(out=sg.ap[:, c, :], in_=ca.ap[:], func=AF.Silu), reads=[ca], writes=[sg])
                        else:
                            P.op("dve", R.tensor_tensor(out=aT.ap[:, cb * 4 + c, :], in0=ca.ap[:], in1=sg.ap[:, c, :], op=ALU.mult),
                                 reads=[ca, sg], writes=[aT])
            if l == 0 and tb == 0:
                dbg_dump("aT", aT, aT.ap[:], [128, 44, 512], BF16)
            P.barrier()
            A.reset(mF)
            if l == L - 1 and cfg.stop is None:
                residual_add_dense(aT, 44, w_down_d[l], tb, nxt=dict(gain=norm_fin_d[0:1, :], dst=None, to_y=(y_d, yT), skip_store=True))
            else:
                residual_add_dense(aT, 44, w_down_d[l], tb)
            A.reset(mA)

        P.dma("sp", send_T, send_d[:, 0:1024], Sst, Sst.ap[:].rearrange("p h d -> p (h d)"))
        P.dma("sp", send_T, send_d[:, 1024:1096], dn_halo, dn_halo.ap[:].rearrange("p c k -> p (c k)"))
        P.dma("sp", send_T, send_d[:, 1096:1272], f_halo, f_halo.ap[:].rearrange("p c k -> p (c k)"))
        P.dma("sp", send_T, send_d[:, 1272:2296].rearrange("p (k t) -> p k t", k=2), ckvT, ckvT.ap[:, :, NPRE:NKEY].bitcast(F32))
        P.dma("sp", send_T, send_d[0:64, 2296:2808], kpeT, kpeT.ap[:, NPRE:NKEY].bitcast(F32))

    if cfg.stop is None:
        pass
    else:
        for t in range(NT):
            P.dma("sp", yT[t], y_d[t * 128:(t + 1) * 128, :], hT[t], h_d[t * 128:(t + 1) * 128, :])
    P.wait_all("sp")
    P.emit()
    nc_allow.__exit__(None, None, None)
    return nc, P, A, dbg_out


def dn_section(P, A, nc, cfg, l, tb, env):
    uT = env["uT"]; PSB = env["PSB"]; wload = env["wload"]; w_in_d = env["w_in_d"]
    dnc_w = env["dnc_w"]; dn_halo = env["dn_halo"]; ones_f = env["ones_f"]
    sel_last = env["sel_last"]; tri_incl = env["tri_incl"]
    alog_bc = env["alog_bc"]; dtb_bc = env["dtb_bc"]
    dbg_dump = env["dbg_dump"]

    ba = A.alloc("ba", [128, 4, 16], F32)
    beta = A.alloc("beta", [128, 4, 8], F32)
    gg = A.alloc("gg", [128, 4, 8], F32)
    Gc = A.alloc("Gc", [128, 4, 8], F32)
    negG = A.alloc("negG", [128, 4, 8], F32)
    expG = A.alloc("expG", [128, 4, 8], F32)
    edec = A.alloc("edec", [128, 4, 8], F32)
    glast = A.alloc("glast", [128, 4, 8], F32)
    gsel = A.alloc("gsel", [128, 8], F32)
    wb = wload(w_in_d[l][:, 4096:4112], 16, 16)
    for t in range(4):
        ps = PSB[4 + t % 2]
        for kt in range(16):
            P.op("pe", R.matmul(ps.ap[:, 0:16], lhsT=uT.ap[:, kt, t * 128:(t + 1) * 128], rhs=wb.ap[:, kt, 0:16],
                                                            start=(kt == 0), stop=(kt == 15)), reads=[uT, wb], writes=[ps])
        P.op("dve", R.tensor_copy(out=ba.ap[:, t, :], in_=ps.ap[:, 0:16]), reads=[ps], writes=[ba])
    P.op("act", R.activation(out=beta.ap[:], in_=ba.ap[:, :, 0:8], func=AF.Sigmoid), reads=[ba], writes=[beta])
    for t in range(4):
        P.op("dve", R.tensor_tensor(out=gg.ap[:, t, :], in0=ba.ap[:, t, 8:16], in1=dtb_bc.ap[:, l, :], op=ALU.add), reads=[ba, dtb_bc], writes=[gg])
    P.op("act", R.activation(out=gg.ap[:], in_=gg.ap[:], func=AF.Exp), reads=[gg], writes=[gg])
    P.op("act", R.activation(out=gg.ap[:], in_=gg.ap[:], func=AF.Ln, bias=1.0), reads=[gg], writes=[gg])
    for t in range(4):
        P.op("dve", R.scalar_tensor_tensor(out=gg.ap[:, t, :], in0=gg.ap[:, t, :], scalar=-1.0, in1=alog_bc.ap[:, l, :], op0=ALU.mult, op1=ALU.mult),
             reads=[gg, alog_bc], writes=[gg])
    for t in range(4):
        ps = PSB[4 + t % 2]
        P.op("pe", R.matmul(ps.ap[:, 0:8], lhsT=tri_incl.ap[:], rhs=gg.ap[:, t, :], start=True, stop=True), reads=[tri_incl, gg], writes=[ps])
        P.op("dve", R.tensor_copy(out=Gc.ap[:, t, :], in_=ps.ap[:, 0:8]), reads=[ps], writes=[Gc])
        P.op("dve", R.tensor_scalar(out=gsel.ap[:], in0=Gc.ap[:, t, :], scalar1=sel_last.ap[:, 0:1], scalar2=None, op0=ALU.mult), reads=[Gc, sel_last], writes=[gsel])
        ps2 = PSB[6 + t % 2]
        P.op("pe", R.matmul(ps2.ap[:, 0:8], lhsT=ones_f.ap[:], rhs=gsel.ap[:], start=True, stop=True), reads=[ones_f, gsel], writes=[ps2])
        P.op("act", R.activation(out=glast.ap[:, t, :], in_=ps2.ap[:, 0:8], func=AF.Exp), reads=[ps2], writes=[glast])
        P.op("dve", R.tensor_tensor(out=edec.ap[:, t, :], in0=ps2.ap[:, 0:8], in1=Gc.ap[:, t, :], op=ALU.subtract), reads=[ps2, Gc], writes=[edec])
    P.op("act", R.activation(out=edec.ap[:], in_=edec.ap[:], func=AF.Exp), reads=[edec], writes=[edec])
    P.op("act", R.activation(out=expG.ap[:], in_=Gc.ap[:], func=AF.Exp), reads=[Gc], writes=[expG])
    P.op("dve", R.tensor_scalar(out=negG.ap[:], in0=Gc.ap[:], scalar1=-1.0, scalar2=None, op0=ALU.mult), reads=[Gc], writes=[negG])
    if l == 0 and tb == 0:
        dbg_dump("beta", beta, beta.ap[:], [128, 4, 8])
        dbg_dump("Gc", Gc, Gc.ap[:], [128, 4, 8])

    mH = A.mark()
    for hh in range(2):
        zs = A.alloc("zs", [128, 4, 512], F32)
        xc = A.alloc("xc", [128, 12, 512], F32)
        mR = A.mark()
        raw = [A.alloc(f"dnraw{i}", [128, 515], F32) for i in range(2)]
        ri = 0
        for part in range(3):
            col0 = part * 1024 + hh * 512
            wbk = wload(w_in_d[l][:, col0:col0 + 512], 16, 512)
            for c in range(4):
                ct = col0 // 128 + c
                ps = PSB[4 + c]
                for kt in range(16):
                    P.op("pe", R.matmul(ps.ap[:], lhsT=wbk.ap[:, kt, c * 128:(c + 1) * 128], rhs=uT.ap[:, kt, :],
                                                                           start=(kt == 0), stop=(kt == 15)), reads=[wbk, uT], writes=[ps])
                rw = raw[ri % 2]
                ri += 1
                xo = xc.ap[:, part * 4 + c, :]
                P.op("act", R.copy(out=rw.ap[:, 3:515], in_=ps.ap[:]), reads=[ps], writes=[rw])
                P.op("dve", R.tensor_copy(out=rw.ap[:, 0:3], in_=dn_halo.ap[:, ct, :]), reads=[dn_halo], writes=[rw])
                P.op("dve", R.tensor_copy(out=dn_halo.ap[:, ct, :], in_=rw.ap[:, 512:515]), reads=[rw], writes=[dn_halo])
                P.op("dve", R.tensor_scalar(out=xo, in0=rw.ap[:, 0:512], scalar1=dnc_w.ap[:, l, ct, 0:1], scalar2=None, op0=ALU.mult),
                     reads=[rw, dnc_w], writes=[xc])
                for k in range(1, 4):
                    P.op("dve", R.scalar_tensor_tensor(out=xo, in0=rw.ap[:, k:k + 512], scalar=dnc_w.ap[:, l, ct, k:k + 1], in1=xo,
                                                                                          op0=ALU.mult, op1=ALU.add), reads=[rw, dnc_w, xc], writes=[xc])
                P.op("act", R.activation(out=xo, in_=xo, func=AF.Silu), reads=[xc], writes=[xc])
        wbz = wload(w_in_d[l][:, 3072 + hh * 512:3072 + (hh + 1) * 512], 16, 512)
        for t in range(4):
            ps = PSB[4 + t % 4]
            for kt in range(16):
                P.op("pe", R.matmul(ps.ap[:], lhsT=uT.ap[:, kt, t * 128:(t + 1) * 128], rhs=wbz.ap[:, kt, :], start=(kt == 0), stop=(kt == 15)),
                     reads=[uT, wbz], writes=[ps])
            P.op("act", R.activation(out=zs.ap[:, t, :], in_=ps.ap[:], func=AF.Silu), reads=[ps], writes=[zs])
        if l == 0 and tb == 0 and hh == 0:
            dbg_dump("xc", xc, xc.ap[:], [128, 12, 512])
            dbg_dump("zs", zs, zs.ap[:], [128, 4, 512])
        P.barrier()
        A.reset(mR)
        if cfg.stop == "dnpre":
            A.reset(mH)
            continue
        dn_heads(P, A, nc, cfg, l, tb, hh, env, dict(xc=xc, zs=zs, beta=beta, Gc=Gc, negG=negG, expG=expG, edec=edec, glast=glast))
        P.barrier()
        A.reset(mH)


def dn_heads(P, A, nc, cfg, l, tb, hh, env, d):
    oT = env["oT"]; ident = env["ident"]; ones_f = env["ones_f"]; mask2 = env["mask2"]; negstrict = env["negstrict"]
    onorm_bc = env["onorm_bc"]; Sst = env["Sst"]; S_T = env["S_T"]; dbg_dump = env["dbg_dump"]
    LX = env["LX"]; LYa = env["LYa"]; LYb = env["LYb"]
    xc = d["xc"]; zs = d["zs"]; beta = d["beta"]; Gc = d["Gc"]; negG = d["negG"]; expG = d["expG"]; edec = d["edec"]; glast = d["glast"]
    mk = A.mark()
    NL = 4
    lanes = []
    for i in range(NL):
        lanes.append(dict(
            kv_tok=A.alloc(f"kv_tok{i}", [128, 256], F32),
            sq=A.alloc(f"sq{i}", [128, 256], F32),
            dg=A.alloc(f"dg{i}", [128, 256], F32),
            E=A.alloc(f"E{i}", [128, 256], F32),
            NA=A.alloc(f"NA{i}", [128, 256], F32),
            MN0=A.alloc(f"MN0_{i}", [128, 256], F32),
            MN1=A.alloc(f"MN1_{i}", [128, 256], F32),
            P0=A.alloc(f"P0_{i}", [128, 128], F32),
            P1=A.alloc(f"P1_{i}", [128, 128], F32),
            kdec=A.alloc(f"kdec{i}", [128, 128], F32),
            cf=A.alloc(f"cf{i}", [128, 16], F32),
            X=LX[i], Ya=LYa[i], Yb=LYb[i]))

    def chunk_local(ln, hl, t):
        h = hh * 4 + hl
        kv_tok, sq, dg, E, NA, kdec, cf = ln["kv_tok"], ln["sq"], ln["dg"], ln["E"], ln["NA"], ln["kdec"], ln["cf"]
        X, Ya, Yb = ln["X"], ln["Ya"], ln["Yb"]
        xq = xc.ap[:, 0 + hl, t * 128:(t + 1) * 128]
        xk = xc.ap[:, 4 + hl, t * 128:(t + 1) * 128]
        xv = xc.ap[:, 8 + hl, t * 128:(t + 1) * 128]
        P.op("pe", R.transpose(out=X.ap[:, 0:128], in_=xk, identity=ident.ap[:]), reads=[xc, ident], writes=[X])
        P.op("pe", R.transpose(out=X.ap[:, 128:256], in_=xv, identity=ident.ap[:]), reads=[xc, ident], writes=[X])
        P.op("dve", R.tensor_tensor(out=sq.ap[:, 0:128], in0=xq, in1=xq, op=ALU.mult), reads=[xc], writes=[sq])
        P.op("dve", R.tensor_tensor(out=sq.ap[:, 128:256], in0=xk, in1=xk, op=ALU.mult), reads=[xc], writes=[sq])
        yield
        P.op("act", R.copy(out=kv_tok.ap[:], in_=X.ap[:]), reads=[X], writes=[kv_tok])
        P.op("pe", R.matmul(Ya.ap[:, 0:8], lhsT=sq.ap[:, 0:128], rhs=ones_f.ap[:, 0:8], start=True, stop=True), reads=[sq, ones_f], writes=[Ya])
        P.op("pe", R.matmul(Ya.ap[:, 8:16], lhsT=sq.ap[:, 128:256], rhs=ones_f.ap[:, 0:8], start=True, stop=True), reads=[sq, ones_f], writes=[Ya])
        P.op("pe", R.matmul(X.ap[:, 0:128], lhsT=xk, rhs=xk, start=True, stop=True), reads=[xc], writes=[X])
        P.op("pe", R.matmul(X.ap[:, 128:256], lhsT=xk, rhs=xq, start=True, stop=True), reads=[xc], writes=[X])
        yield
        P.op("act", R.activation(out=cf.ap[:, 0:2], in_=Ya.ap[:, 0:16:8], func=AF.Ln, bias=float(EPS)), reads=[Ya], writes=[cf])
        P.op("act", R.activation(out=cf.ap[:, 0:2], in_=cf.ap[:, 0:2], func=AF.Exp, scale=-0.5), reads=[cf], writes=[cf])
        yield
        P.op("dve", R.tensor_tensor(out=cf.ap[:, 2:3], in0=cf.ap[:, 1:2], in1=beta.ap[:, t, h:h + 1], op=ALU.mult), reads=[cf, beta], writes=[cf])
        P.op("dve", R.tensor_scalar(out=cf.ap[:, 9:10], in0=cf.ap[:, 0:1], scalar1=float(128 ** -0.5), scalar2=None, op0=ALU.mult), reads=[cf], writes=[cf])
        P.op("dve", R.tensor_tensor(out=cf.ap[:, 6:7], in0=cf.ap[:, 1:2], in1=edec.ap[:, t, h:h + 1], op=ALU.mult), reads=[cf, edec], writes=[cf])
        yield
        P.op("act", R.activation(out=cf.ap[:, 3:4], in_=cf.ap[:, 2:3], func=AF.Ln), reads=[cf], writes=[cf])
        P.op("act", R.activation(out=cf.ap[:, 4:5], in_=cf.ap[:, 9:10], func=AF.Ln), reads=[cf], writes=[cf])
        P.op("dve", R.tensor_tensor(out=cf.ap[:, 5:6], in0=cf.ap[:, 2:3], in1=expG.ap[:, t, h:h + 1], op=ALU.mult), reads=[cf, expG], writes=[cf])
        P.op("dve", R.tensor_tensor(out=cf.ap[:, 7:8], in0=cf.ap[:, 9:10], in1=expG.ap[:, t, h:h + 1], op=ALU.mult), reads=[cf, expG], writes=[cf])
        yield
        P.op("dve", R.tensor_scalar(out=cf.ap[:, 3:5], in0=cf.ap[:, 3:5], scalar1=Gc.ap[:, t, h:h + 1], scalar2=None, op0=ALU.add), reads=[cf, Gc], writes=[cf])
        yield
        P.op("dve", R.tensor_scalar(out=dg.ap[:, 0:128], in0=ident.ap[:], scalar1=cf.ap[:, 3:4], scalar2=None, op0=ALU.mult), reads=[cf, ident], writes=[dg])
        P.op("dve", R.tensor_scalar(out=dg.ap[:, 128:256], in0=ident.ap[:], scalar1=cf.ap[:, 4:5], scalar2=None, op0=ALU.mult), reads=[cf, ident], writes=[dg])
        yield
        P.op("pe", R.matmul(Ya.ap[:], lhsT=ones_f.ap[:], rhs=dg.ap[:, 0:128], start=True, stop=True), reads=[ones_f, dg], writes=[Ya])
        P.op("pe", R.matmul(Yb.ap[:], lhsT=ones_f.ap[:], rhs=dg.ap[:, 128:256], start=True, stop=True), reads=[ones_f, dg], writes=[Yb])
        yield
        P.op("dve", R.tensor_tensor(out=E.ap[:, 0:128], in0=Ya.ap[:], in1=mask2.ap[:, 0:128], op=ALU.add), reads=[Ya, mask2], writes=[E])
        P.op("dve", R.tensor_tensor(out=E.ap[:, 128:256], in0=Yb.ap[:], in1=mask2.ap[:, 128:256], op=ALU.add), reads=[Yb, mask2], writes=[E])
        yield
        P.op("act", R.activation(out=E.ap[:], in_=E.ap[:], func=AF.Exp, bias=negG.ap[:, t, h:h + 1], scale=1.0), reads=[E, negG], writes=[E])
        yield
        P.op("dve", R.scalar_tensor_tensor(out=NA.ap[:], in0=X.ap[:], scalar=cf.ap[:, 1:2], in1=E.ap[:], op0=ALU.mult, op1=ALU.mult),
             reads=[X, cf, E], writes=[NA])
        yield
        cur = ln["MN0"]
        P.op("dve", R.tensor_tensor(out=cur.ap[:, 128:256], in0=NA.ap[:, 0:128], in1=negstrict.ap[:], op=ALU.mult), reads=[NA, negstrict], writes=[cur])
        yield
        P.op("pe", R.transpose(out=Ya.ap[:], in_=cur.ap[:, 128:256], identity=ident.ap[:]), reads=[cur, ident], writes=[Ya])
        pc = ln["P0"]
        P.op("dve", R.tensor_tensor(out=pc.ap[:], in0=cur.ap[:, 128:256], in1=ident.ap[:], op=ALU.add), reads=[cur, ident], writes=[pc])
        yield
        P.op("act", R.copy(out=cur.ap[:, 0:128], in_=Ya.ap[:]), reads=[Ya], writes=[cur])
        yield
        for j in range(1, 7):
            nxt = ln["MN1"] if j % 2 == 1 else ln["MN0"]
            pn = ln["P1"] if j % 2 == 1 else ln["P0"]
            last = (j == 6)
            P.op("pe", R.matmul(X.ap[:, 0:128], lhsT=cur.ap[:, 128:256], rhs=cur.ap[:, 0:128], start=True, stop=True), reads=[cur], writes=[X])
            if not last:
                P.op("pe", R.matmul(X.ap[:, 128:256], lhsT=cur.ap[:, 0:128], rhs=cur.ap[:, 128:256], start=True, stop=True), reads=[cur], writes=[X])
            yield
            if not last:
                P.op("act", R.copy(out=nxt.ap[:], in_=X.ap[:]), reads=[X], writes=[nxt])
            else:
                P.op("act", R.copy(out=nxt.ap[:, 0:128], in_=X.ap[:, 0:128]), reads=[X], writes=[nxt])
            yield
            Yp = Ya if j % 2 == 1 else Yb
            P.op("pe", R.matmul(Yp.ap[:], lhsT=nxt.ap[:, 0:128], rhs=pc.ap[:], start=True, stop=True), reads=[nxt, pc], writes=[Yp])
            yield
            P.op("dve", R.tensor_tensor(out=pn.ap[:], in0=Yp.ap[:], in1=pc.ap[:], op=ALU.add), reads=[Yp, pc], writes=[pn])
            cur = nxt
            pc = pn
        rhs_t = sq
        P.op("act", R.activation(out=rhs_t.ap[:, 0:128], in_=kv_tok.ap[:, 128:256], func=AF.Copy, scale=beta.ap[:, t, h:h + 1]), reads=[kv_tok, beta], writes=[rhs_t])
        P.op("act", R.activation(out=rhs_t.ap[:, 128:256], in_=kv_tok.ap[:, 0:128], func=AF.Copy, scale=cf.ap[:, 5:6]), reads=[kv_tok, cf], writes=[rhs_t])
        P.op("dve", R.tensor_scalar(out=kdec.ap[:], in0=kv_tok.ap[:, 0:128], scalar1=cf.ap[:, 6:7], scalar2=None, op0=ALU.mult), reads=[kv_tok, cf], writes=[kdec])
        yield
        P.op("pe", R.matmul(X.ap[:, 0:128], lhsT=pc.ap[:], rhs=rhs_t.ap[:, 0:128], start=True, stop=True), reads=[pc, rhs_t], writes=[X])
        P.op("pe", R.matmul(X.ap[:, 128:256], lhsT=rhs_t.ap[:, 128:256], rhs=pc.ap[:], start=True, stop=True), reads=[pc, rhs_t], writes=[X])
        yield
        uw = dg
        P.op("act", R.copy(out=uw.ap[:], in_=X.ap[:]), reads=[X], writes=[uw])
        if l == 0 and tb == 0 and h == 0 and t == 0:
            dbg_dump("dn_NA", NA, NA.ap[:], [128, 256])
            dbg_dump("dn_P", pc, pc.ap[:], [128, 128])
            dbg_dump("dn_uw", uw, uw.ap[:], [128, 256])
        yield

    def scan_step(ln, hl, t):
        h = hh * 4 + hl
        E, NA, kdec, cf = ln["E"], ln["NA"], ln["kdec"], ln["cf"]
        X, Ya, Yb = ln["X"], ln["Ya"], ln["Yb"]
        uw = ln["dg"]
        vo = E
        oo = ln["MN1"]
        xq = xc.ap[:, 0 + hl, t * 128:(t + 1) * 128]
        Sh = Sst.ap[:, h, :]
        ST = S_T[h]
        P.op("pe", R.matmul(X.ap[:, 0:128], lhsT=uw.ap[:, 128:256], rhs=Sh, start=True, stop=True), reads=[uw, ST], writes=[X])
        P.op("pe", R.matmul(X.ap[:, 128:256], lhsT=xq, rhs=Sh, start=True, stop=True), reads=[xc, ST], writes=[X])
        yield
        P.op("dve", R.tensor_tensor(out=vo.ap[:, 0:128], in0=uw.ap[:, 0:128], in1=X.ap[:, 0:128], op=ALU.subtract), reads=[uw, X], writes=[vo])
        P.op("act", R.activation(out=vo.ap[:, 128:256], in_=X.ap[:, 128:256], func=AF.Copy, scale=cf.ap[:, 7:8]), reads=[X, cf], writes=[vo])
        yield
        P.op("pe", R.matmul(Ya.ap[:], lhsT=NA.ap[:, 128:256], rhs=vo.ap[:, 0:128], start=True, stop=True), reads=[NA, vo], writes=[Ya])
        P.op("pe", R.matmul(Yb.ap[:], lhsT=kdec.ap[:], rhs=vo.ap[:, 0:128], start=True, stop=True), reads=[kdec, vo], writes=[Yb])
        yield
        P.op("dve", R.scalar_tensor_tensor(out=Sh, in0=Sh, scalar=glast.ap[:, t, h:h + 1], in1=Yb.ap[:], op0=ALU.mult, op1=ALU.add),
             reads=[ST, glast, Yb], writes=[ST])
        P.op("dve", R.tensor_tensor(out=oo.ap[:, 0:128], in0=vo.ap[:, 128:256], in1=Ya.ap[:], op=ALU.add), reads=[vo, Ya], writes=[oo])
        yield
        P.op("act", R.activation(out=oo.ap[:, 128:256], in_=oo.ap[:, 0:128], func=AF.Square, accum_out=cf.ap[:, 8:9]), reads=[oo], writes=[oo, cf])
        yield
        P.op("dve", R.tensor_scalar(out=cf.ap[:, 8:9], in0=cf.ap[:, 8:9], scalar1=float(1.0 / 128), scalar2=float(EPS), op0=ALU.mult, op1=ALU.add), reads=[cf], writes=[cf])
        yield
        P.op("act", R.activation(out=cf.ap[:, 8:9], in_=cf.ap[:, 8:9], func=AF.Ln), reads=[cf], writes=[cf])
        P.op("act", R.activation(out=cf.ap[:, 8:9], in_=cf.ap[:, 8:9], func=AF.Exp, scale=-0.5), reads=[cf], writes=[cf])
        yield
        P.op("dve", R.scalar_tensor_tensor(out=oo.ap[:, 128:256], in0=oo.ap[:, 0:128], scalar=cf.ap[:, 8:9], in1=onorm_bc.ap[:, l, :], op0=ALU.mult, op1=ALU.mult),
             reads=[oo, cf, onorm_bc], writes=[oo])
        yield
        P.op("dve", R.tensor_tensor(out=oo.ap[:, 128:256], in0=oo.ap[:, 128:256], in1=zs.ap[:, t, hl * 128:(hl + 1) * 128], op=ALU.mult), reads=[oo, zs], writes=[oo])
        yield
        P.op("pe", R.transpose(out=Ya.ap[:], in_=oo.ap[:, 128:256], identity=ident.ap[:]), reads=[oo, ident], writes=[Ya])
        yield
        P.op("dve", R.tensor_copy(out=oT.ap[:, h, t * 128:(t + 1) * 128], in_=Ya.ap[:]), reads=[Ya], writes=[oT])
        if l == 0 and tb == 0 and h == 0 and t == 0:
            dbg_dump("dn_o", oo, oo.ap[:, 0:128], [128, 128])
        yield

    def skewed(g, n):
        for _ in range(n):
            yield
        yield from g

    def run_interleaved(gens, skew=0):
        gens = [skewed(g, i * skew) for i, g in enumerate(gens)] if skew else list(gens)
        rounds = 0
        while gens:
            rounds += 1
            if getattr(cfg, "dncut", None) is not None and rounds > cfg.dncut:
                return
            alive = []
            for g in gens:
                try:
                    next(g)
                    alive.append(g)
                except StopIteration:
                    pass
            gens = alive

    for t in range(4):
        run_interleaved([chunk_local(lanes[hl], hl, t) for hl in range(4)], skew=getattr(cfg, "skew", 0))
        if getattr(cfg, "dncut", None) is None:
            run_interleaved([scan_step(lanes[hl], hl, t) for hl in range(4)])
    A.reset(mk)


_CACHE = {}


def _invf():
    inv = (10000.0 ** (-np.arange(0, 64, 2, dtype=np.float32) / np.float32(64))).astype(np.float32)
    return np.ascontiguousarray(np.broadcast_to(inv[None, :], (128, 32))).astype(np.float32)


_W_KEYS = ["norm_mix", "w_in", "dn_a_log", "dn_dt_bias", "dn_out_norm", "mla_w_qb", "mla_w_kvb", "w_out", "norm_xattn",
           "xa_wq", "xa_wk", "xa_wv", "xa_wo", "norm_ffn", "ffn_w_up", "ffn_w_down"]


def _layout_weights(inputs, nlayers):
    f = lambda a: np.ascontiguousarray(np.asarray(a))
    g = lambda k: np.asarray(inputs[k])[:nlayers]
    Ln = nlayers
    w = {k: f(g(k)) for k in _W_KEYS}
    w["mla_q_norm"] = f(g("mla_q_norm").reshape(Ln, 4, 128).transpose(0, 2, 1))
    w["mla_kv_norm"] = f(g("mla_kv_norm").reshape(Ln, 2, 128).transpose(0, 2, 1))
    w["dn_conv"] = f(g("dn_conv").reshape(Ln, 4, 24, 128).transpose(0, 3, 2, 1))
    w["ffn_conv"] = f(g("ffn_conv").reshape(Ln, 3, 88, 128).transpose(0, 3, 2, 1))
    w["ffn_conv_bias"] = f(g("ffn_conv_bias").reshape(Ln, 88, 128).transpose(0, 2, 1))
    return w


def _slot_weights(w, role):
    out = {}
    for k, a in w.items():
        z = np.zeros((1,) + a.shape[1:], a.dtype)
        b = np.concatenate([a, z], 0) if role == 0 else np.concatenate([z, a], 0)
        if k in ("mla_q_norm", "mla_kv_norm", "dn_conv", "ffn_conv", "ffn_conv_bias"):
            nl = b.shape[0]
            b = np.moveaxis(b, 0, 1).reshape(128, -1)
        out[k] = np.ascontiguousarray(b)
    return out


def make_in_map(inputs, b, role, wslots, NTOK):
    f = lambda a: np.ascontiguousarray(np.asarray(a))
    t0 = role * NTOK
    pos = np.asarray(inputs["positions"])[b, t0:t0 + NTOK].astype(np.int32)
    fl = np.zeros((128, 2), np.float32)
    fl[:, 0] = float(role)
    fl[:, 1] = (float(role) - 1.0) * 30000.0
    m = {
        "x": f(np.asarray(inputs["x"])[b, t0:t0 + NTOK]),
        "mem": f(np.asarray(inputs["mem"])[b]),
        "pos": f(pos.reshape(NTOK // 128, 128).T),
        "invf": _invf(),
        "flag": fl,
        "mem_norm": f(np.asarray(inputs["mem_norm"]).reshape(1, D)),
        "norm_final": f(np.asarray(inputs["norm_final"]).reshape(1, D)),
    }
    m.update(wslots)
    return m


def kernel(**inputs):
    cfg = Cfg(NTOK=1024, L=5, NPRE=1024, n_cores=8)
    if "nc" not in _CACHE:
        _CACHE["nc"] = build_program(cfg)[0]
    nc = _CACHE["nc"]
    w = _layout_weights(inputs, 4)
    slots = [_slot_weights(w, 0), _slot_weights(w, 1)]
    in_maps = [make_in_map(inputs, c // 2, c % 2, slots[c % 2], 1024) for c in range(8)]
    res = run_bass_kernel_spmd(nc, in_maps, core_ids=list(range(8)))
    out = np.stack([np.concatenate([np.asarray(res.results[2 * b]["y"]), np.asarray(res.results[2 * b + 1]["y"])], axis=0)
                    for b in range(4)], axis=0).astype(np.float32)
    return out
```
